# Optimizing a Trainium2 kernel written in Bass

```python
import jax, jax.numpy as jnp
from jax import lax
import numpy as np

D_MODEL = 1024
BATCH = 8
SEQ = 2048
DEPTH = 2

GRID_W = 64
CTX_LEN = 256
MIX_W = D_MODEL
FN_W = MIX_W // 4
FN_GROUPS = 4
FN_GD = FN_W // FN_GROUPS
NA_HD = 64
NA_W = 3 * MIX_W // 8
NA_HEADS = NA_W // NA_HD
NA_KH = 8
NA_KW = 16
GLA_HEADS = 4
GLA_V_W = MIX_W - FN_W - NA_W
GLA_DV = GLA_V_W // GLA_HEADS
GLA_QK_W = GLA_V_W // 2
GLA_DK = GLA_QK_W // GLA_HEADS
GLA_RANK = 16
GLA_GATE_NORM = 16.0
GLA_CHUNK = 64
ROPE_BASE = 10000.0
D_FF = ((8 * D_MODEL // 3 + 255) // 256) * 256
N_IN = FN_W + 3 * NA_W + 2 * GLA_QK_W + 2 * GLA_V_W + 2 * GLA_RANK
EPS = 1e-6

kernel_name = "hybrid_fnet_natten_gla_prefix_block"

F32 = jnp.float32


def rms_norm(x, gain):
    xf = x.astype(F32)
    y = xf * lax.rsqrt(jnp.mean(xf * xf, axis=-1, keepdims=True) + EPS)
    return (y * gain.astype(F32)).astype(x.dtype)


def ada_modulate(x, gain, shift, scale):
    return rms_norm(x, gain) * (1.0 + scale) + shift


def split_in(z):
    sizes = (FN_W, NA_W, NA_W, NA_W, GLA_QK_W, GLA_QK_W, GLA_V_W, GLA_V_W, 2 * GLA_RANK)
    points = [int(p) for p in np.cumsum(sizes)[:-1]]
    return jnp.split(z, points, axis=-1)


def rope_axis(x, pos):
    m = x.shape[-1] // 2
    inv = ROPE_BASE ** (-jnp.arange(m, dtype=F32) / m)
    ang = pos[:, None] * inv[None, :]
    cos = jnp.cos(ang)[None, :, None, :]
    sin = jnp.sin(ang)[None, :, None, :]
    x1, x2 = x[..., :m], x[..., m:]
    return jnp.concatenate([x1 * cos - x2 * sin, x2 * cos + x1 * sin], axis=-1)


def rope_2d(x):
    L = x.shape[1]
    t = jnp.arange(L)
    row = (t // GRID_W).astype(F32)
    col = (t % GRID_W).astype(F32)
    h = x.shape[-1] // 2
    xf = x.astype(F32)
    return jnp.concatenate([rope_axis(xf[..., :h], row), rope_axis(xf[..., h:], col)], axis=-1).astype(x.dtype)


def fourier_mix(u, w):
    B, L, _ = u.shape
    z = u.astype(F32).reshape(B, L, FN_GROUPS, FN_GD)
    f = jnp.fft.fftn(z, axes=(1, 3), norm="ortho").real
    return f.reshape(B, L, FN_W).astype(u.dtype) @ w


def na_latent(q, k, v, kc, vc, rpb):
    B, L, H, d = q.shape
    R = L // GRID_W
    kh = min(NA_KH, R)
    scale = d ** -0.5
    to_grid = lambda t: t.reshape(B, R, GRID_W, H, d).transpose(0, 3, 1, 2, 4)
    qg, kg, vg = to_grid(q), to_grid(k), to_grid(v)
    r = jnp.arange(R)
    r0 = jnp.clip(r - kh // 2, 0, R - kh)
    row_idx = r0[:, None] + jnp.arange(kh)[None, :]
    kb = kg[:, :, row_idx]
    vb = vg[:, :, row_idx]
    cq = jnp.arange(GRID_W)
    c0 = jnp.clip(cq - NA_KW // 2, 0, GRID_W - NA_KW)
    col_ok = (cq[None, :] >= c0[:, None]) & (cq[None, :] < c0[:, None] + NA_KW)
    dr = row_idx - r[:, None] + NA_KH - 1
    dc = jnp.clip(cq[None, :] - cq[:, None], -(NA_KW - 1), NA_KW - 1) + NA_KW - 1
    bias = rpb[:, dr[:, None, :, None], dc[None, :, None, :]]
    s_loc = jnp.einsum('bhrqd,bhrakd->bhrqak', qg, kb, preferred_element_type=F32) * scale + bias.astype(F32)
    s_loc = jnp.where(col_ok[:, None, :], s_loc, -jnp.inf)
    s_ctx = jnp.einsum('bhrqd,bnhd->bhrqn', qg, kc, preferred_element_type=F32) * scale
    n_loc = kh * GRID_W
    s = jnp.concatenate([s_loc.reshape(B, H, R, GRID_W, n_loc), s_ctx], axis=-1)
    p = jax.nn.softmax(s, axis=-1).astype(v.dtype)
    p_loc = p[..., :n_loc].reshape(B, H, R, GRID_W, kh, GRID_W)
    o = (jnp.einsum('bhrqak,bhrakd->bhrqd', p_loc, vb)
         + jnp.einsum('bhrqn,bnhd->bhrqd', p[..., n_loc:], vc))
    return o.transpose(0, 2, 3, 1, 4).reshape(B, L, H * d)


def ctx_attn(q, k, v):
    B, N, H, d = q.shape
    s = jnp.einsum('bnhd,bmhd->bhnm', q, k, preferred_element_type=F32) * (d ** -0.5)
    p = jax.nn.softmax(s, axis=-1).astype(v.dtype)
    return jnp.einsum('bhnm,bmhd->bnhd', p, v).reshape(B, N, H * d)


def gla_chunked(q, k, v, glog, s0, need_out):
    B, L, H, dk = k.shape
    dv = v.shape[-1]
    n = L // GLA_CHUNK
    blk = lambda t: t.reshape(B, n, GLA_CHUNK, H, t.shape[-1]).transpose(1, 0, 3, 2, 4)
    kb, vb, gb = blk(k), blk(v), blk(glog)
    b = jnp.cumsum(gb, axis=3)
    b_last = b[:, :, :, -1:, :]
    decay = jnp.exp(b_last)
    k_end = kb * jnp.exp(b_last - b)
    if not need_out:
        def step_state(S, xs):
            ke, vi, dl = xs
            return S * jnp.swapaxes(dl, -1, -2) + jnp.einsum('bhcd,bhcv->bhdv', ke, vi), None
        S, _ = lax.scan(step_state, s0, (k_end, vb, decay))
        return None, S
    qb = blk(q)
    q_in = qb * jnp.exp(b)
    k_in = kb * jnp.exp(-b)
    mask = jnp.tril(jnp.ones((GLA_CHUNK, GLA_CHUNK), dtype=bool))
    att = jnp.where(mask, jnp.einsum('nbhcd,nbhsd->nbhcs', q_in, k_in), 0.0)
    o_intra = jnp.einsum('nbhcs,nbhsv->nbhcv', att, vb)

    def step(S, xs):
        qi, ke, vi, dl = xs
        o = jnp.einsum('bhcd,bhdv->bhcv', qi, S)
        return S * jnp.swapaxes(dl, -1, -2) + jnp.einsum('bhcd,bhcv->bhdv', ke, vi), o

    S, o_inter = lax.scan(step, s0, (q_in, k_end, vb, decay))
    o = (o_intra + o_inter).transpose(1, 0, 3, 2, 4).reshape(B, L, H, dv)
    return o, S


def gla_prep(q, k, v, a, alpha_w, alpha_b, rope):
    B, L, _ = q.shape
    q = q.astype(F32).reshape(B, L, GLA_HEADS, GLA_DK)
    k = k.astype(F32).reshape(B, L, GLA_HEADS, GLA_DK)
    v = v.astype(F32).reshape(B, L, GLA_HEADS, GLA_DV)
    if rope:
        q = rope_2d(q)
        k = rope_2d(k)
    q = q * (GLA_DK ** -0.5)
    a = a.astype(F32).reshape(B, L, 2, GLA_RANK)
    logits = jnp.einsum('bldr,drk->bldk', a, alpha_w.astype(F32)) + alpha_b.astype(F32)
    glog = (jax.nn.log_sigmoid(logits) / GLA_GATE_NORM).reshape(B, L, 2, GLA_HEADS, GLA_DK)
    return q, k, v, glog[:, :, 0], glog[:, :, 1]


def gla_out(o, g, gain):
    B, L = o.shape[:2]
    on = rms_norm(o, gain).reshape(B, L, GLA_V_W)
    return (on * jax.nn.silu(g.astype(F32))).astype(g.dtype)


def gla_mix(zx, zc, alpha_w, alpha_b, o_norm, need_ctx):
    qx, kx, vx, gx, ax = zx
    qc, kc, vc, gc, ac = zc
    q, k, v, gf, gb = gla_prep(qx, kx, vx, ax, alpha_w, alpha_b, rope=True)
    q_c, k_c, v_c, gfc, gbc = gla_prep(qc, kc, vc, ac, alpha_w, alpha_b, rope=False)
    s0 = jnp.zeros((q.shape[0], GLA_HEADS, GLA_DK, GLA_DV), F32)
    rev = lambda t: t[:, ::-1]
    oc_f, sc_f = gla_chunked(q_c, k_c, v_c, gfc, s0, need_ctx)
    oc_b, sc_b = gla_chunked(rev(q_c), rev(k_c), rev(v_c), rev(gbc), s0, need_ctx)
    ox_f, _ = gla_chunked(q, k, v, gf, sc_f, True)
    ox_b, _ = gla_chunked(rev(q), rev(k), rev(v), rev(gb), sc_b, True)
    y_x = gla_out(ox_f + rev(ox_b), gx, o_norm)
    y_c = gla_out(oc_f + rev(oc_b), gc, o_norm) if need_ctx else None
    return y_x, y_c


def swiglu(h, w1, w3, w2):
    return (jax.nn.silu(h @ w1) * (h @ w3)) @ w2


def hybrid_layer(x, cx, c, c_ctx, ada_w, ada_b, norm_mix, norm_ffn, w_in, fnet_w,
                 na_q_norm, na_k_norm, na_rpb, gla_alpha_w, gla_alpha_b, gla_o_norm,
                 w_out, ffn_w1, ffn_w3, ffn_w2, need_ctx):
    mod_x = (jax.nn.silu(c) @ ada_w + ada_b)[:, None, :]
    mod_c = (jax.nn.silu(c_ctx) @ ada_w + ada_b)[None, None, :]
    sh_m, sc_m, g_m, sh_f, sc_f, g_f = jnp.split(mod_x, 6, axis=-1)
    csh_m, csc_m, cg_m, csh_f, csc_f, cg_f = jnp.split(mod_c, 6, axis=-1)

    hx = ada_modulate(x, norm_mix, sh_m, sc_m)
    hc = ada_modulate(cx, norm_mix, csh_m, csc_m)
    fx, nqx, nkx, nvx, gqx, gkx, gvx, ggx, gax = split_in(hx @ w_in)
    fc, nqc, nkc, nvc, gqc, gkc, gvc, ggc, gac = split_in(hc @ w_in)

    heads = lambda t: t.reshape(t.shape[0], t.shape[1], NA_HEADS, NA_HD)
    y_fn = fourier_mix(fx, fnet_w)
    qxh, kxh, vxh = rms_norm(heads(nqx), na_q_norm), rms_norm(heads(nkx), na_k_norm), heads(nvx)
    qch, kch, vch = rms_norm(heads(nqc), na_q_norm), rms_norm(heads(nkc), na_k_norm), heads(nvc)
    y_na = na_latent(qxh, kxh, vxh, kch, vch, na_rpb)
    y_gla, yc_gla = gla_mix((gqx, gkx, gvx, ggx, gax), (gqc, gkc, gvc, ggc, gac),
                            gla_alpha_w, gla_alpha_b, gla_o_norm, need_ctx)

    y = jnp.concatenate([y_fn, y_na, y_gla.astype(y_fn.dtype)], axis=-1) @ w_out
    x = x + (g_m * y).astype(x.dtype)
    x = x + (g_f * swiglu(ada_modulate(x, norm_ffn, sh_f, sc_f), ffn_w1, ffn_w3, ffn_w2)).astype(x.dtype)

    if need_ctx:
        yc = jnp.concatenate([fourier_mix(fc, fnet_w), ctx_attn(qch, kch, vch),
                              yc_gla.astype(fc.dtype)], axis=-1) @ w_out
        cx = cx + (cg_m * yc).astype(cx.dtype)
        cx = cx + (cg_f * swiglu(ada_modulate(cx, norm_ffn, csh_f, csc_f), ffn_w1, ffn_w3, ffn_w2)).astype(cx.dtype)
    return x, cx


def setup_inputs(seed: int = 0) -> dict:
    key = jax.random.key(seed)
    ks = jax.random.split(key, 20)
    nrm = lambda k, shape, s: jax.random.normal(k, shape, F32) * s
    return {
        "x": nrm(ks[0], (BATCH, SEQ, D_MODEL), 1.0),
        "c": nrm(ks[1], (BATCH, D_MODEL), 1.0),
        "ctx": nrm(ks[2], (BATCH, CTX_LEN, D_MODEL), 1.0),
        "c_ctx": nrm(ks[3], (D_MODEL,), 1.0),
        "ada_w": nrm(ks[4], (DEPTH, D_MODEL, 6 * D_MODEL), 0.5 * D_MODEL ** -0.5),
        "ada_b": nrm(ks[5], (DEPTH, 6 * D_MODEL), 0.01),
        "norm_mix": 1.0 + nrm(ks[6], (DEPTH, D_MODEL), 0.02),
        "norm_ffn": 1.0 + nrm(ks[7], (DEPTH, D_MODEL), 0.02),
        "w_in": nrm(ks[8], (DEPTH, D_MODEL, N_IN), D_MODEL ** -0.5),
        "fnet_w": nrm(ks[9], (DEPTH, FN_W, FN_W), FN_W ** -0.5),
        "na_q_norm": 1.0 + nrm(ks[10], (DEPTH, NA_HD), 0.02),
        "na_k_norm": 1.0 + nrm(ks[11], (DEPTH, NA_HD), 0.02),
        "na_rpb": nrm(ks[12], (DEPTH, NA_HEADS, 2 * NA_KH - 1, 2 * NA_KW - 1), 0.1),
        "gla_alpha_w": nrm(ks[13], (DEPTH, 2, GLA_RANK, GLA_QK_W), GLA_RANK ** -0.5),
        "gla_alpha_b": nrm(ks[14], (DEPTH, 2, GLA_QK_W), 0.1),
        "gla_o_norm": 1.0 + nrm(ks[15], (DEPTH, GLA_DV), 0.02),
        "w_out": nrm(ks[16], (DEPTH, MIX_W, D_MODEL), MIX_W ** -0.5),
        "ffn_w1": nrm(ks[17], (DEPTH, D_MODEL, D_FF), D_MODEL ** -0.5),
        "ffn_w3": nrm(ks[18], (DEPTH, D_MODEL, D_FF), D_MODEL ** -0.5),
        "ffn_w2": nrm(ks[19], (DEPTH, D_FF, D_MODEL), D_FF ** -0.5),
    }


def reference(x, c, ctx, c_ctx, ada_w, ada_b, norm_mix, norm_ffn, w_in, fnet_w,
              na_q_norm, na_k_norm, na_rpb, gla_alpha_w, gla_alpha_b, gla_o_norm,
              w_out, ffn_w1, ffn_w3, ffn_w2):
    cx = ctx
    for l in range(DEPTH):
        x, cx = hybrid_layer(x, cx, c, c_ctx, ada_w[l], ada_b[l], norm_mix[l], norm_ffn[l],
                             w_in[l], fnet_w[l], na_q_norm[l], na_k_norm[l], na_rpb[l],
                             gla_alpha_w[l], gla_alpha_b[l], gla_o_norm[l], w_out[l],
                             ffn_w1[l], ffn_w3[l], ffn_w2[l], need_ctx=(l < DEPTH - 1))
    return x
```

```python
import contextlib
import os
import numpy as np
import ml_dtypes
import concourse.bass as bass
import concourse.mybir as mybir
from concourse.bass_utils import run_bass_kernel_spmd

F32 = mybir.dt.float32
BF16 = mybir.dt.bfloat16
AF = mybir.ActivationFunctionType
ALU = mybir.AluOpType

D = 1024
T = 2048
NCTX = 256
NT = T + NCTX
DFF = 2816
NIN = 2592
EPS = 1e-6
NEG = -30000.0
SAME_ENGINE_RAW = True


class Tok:
    __slots__ = ("w", "r", "dsem", "name")

    def __init__(self, name=""):
        self.w = []
        self.r = {}
        self.dsem = None
        self.name = name


class Src:
    def __init__(self, name, sem, unit, h=None):
        self.name = name
        self.sem = sem
        self.unit = unit
        self.h = h
        self.count = 0
        self.seen = {}


class Ctx:
    def __init__(self):
        self.nc = bass.Bass("TRN2", target_bir_lowering=False)
        nc = self.nc
        self.es = contextlib.ExitStack()
        self.engs = {}
        for nm, h in (("pe", nc.tensor), ("act", nc.scalar), ("dve", nc.vector),
                      ("pool", nc.gpsimd), ("sp", nc.sync)):
            sem = self.es.enter_context(nc.semaphore("sem_" + nm))
            self.engs[nm] = Src(nm, sem, 1, h)
        self.dsems = []
        self.outs = {}
        self.n_inst = 0

    def dram(self, name, shape, dt, kind="ExternalInput"):
        return self.nc.dram_tensor(name, list(shape), dt, kind=kind).ap()

    def sb(self, stack, name, shape, dt):
        self.n_sb = getattr(self, "n_sb", 0) + 1
        return stack.enter_context(self.nc.sbuf_tensor("sb%d_%s" % (self.n_sb, name), list(shape), dt))

    def new_dsem(self, name):
        sem = self.es.enter_context(self.nc.semaphore("d_" + name + str(len(self.dsems))))
        s = Src("dma_" + name, sem, 16)
        self.dsems.append(s)
        return s

    def _wait_deps(self, E, reads, writes):
        deps = {}
        for t in reads:
            for (s, n) in t.w:
                deps[s] = max(deps.get(s, 0), n)
        for t in writes:
            for (s, n) in t.w:
                deps[s] = max(deps.get(s, 0), n)
            for s, n in t.r.items():
                deps[s] = max(deps.get(s, 0), n)
        for s, n in deps.items():
            if s is E and (E.name == "pe" or not SAME_ENGINE_RAW):
                continue
            if E.seen.get(s, 0) < n:
                E.h.wait_ge(s.sem, n * s.unit)
                E.seen[s] = n

    def op(self, eng, reads, writes, emit):
        E = self.engs[eng]
        self._wait_deps(E, reads, writes)
        ins = emit(E.h)
        E.count += 1
        ins.then_inc(E.sem, 1)
        self.n_inst += 1
        for t in reads:
            t.r[E] = E.count
        for t in writes:
            t.w = [(E, E.count)]
            t.r = {}
        return ins

    def mm(self, reads, writes, out, pairs, first=True, last=True):
        E = self.engs["pe"]
        self._wait_deps(E, reads, writes)
        n = len(pairs)
        ins = None
        for i, (l, r) in enumerate(pairs):
            ins = E.h.matmul(out, l, r, start=(first and i == 0), stop=(last and i == n - 1))
            self.n_inst += 1
        E.count += 1
        ins.then_inc(E.sem, 1)
        for t in reads:
            t.r[E] = E.count
        for t in writes:
            t.w = [(E, E.count)]
            t.r = {}

    def transpose(self, reads, writes, out, in_, ident):
        return self.op("pe", reads, writes, lambda h: h.transpose(out, in_, ident))

    def dma(self, q, out, in_, reads, writes, name="x"):
        E = self.engs[q]
        self._wait_deps(E, reads, writes)
        wt = writes[0]
        if wt.dsem is None:
            wt.dsem = {}
        if q not in wt.dsem:
            wt.dsem[q] = self.new_dsem(name + q)
        S = wt.dsem[q]
        ins = E.h.dma_start(out=out, in_=in_)
        S.count += 1
        ins.then_inc(S.sem, 16)
        self.n_inst += 1
        for t in reads:
            t.r[S] = S.count
        for t in writes:
            t.w = [(s_, n_) for (s_, n_) in t.w if (s_.unit == 16 and s_ is not S)] + [(S, S.count)]
            t.r = {}

    def barrier(self):
        allsrc = list(self.engs.values()) + self.dsems
        for E in self.engs.values():
            for s in allsrc:
                if s is E or s.count == 0:
                    continue
                if E.seen.get(s, 0) < s.count:
                    E.h.wait_ge(s.sem, s.count * s.unit)
                    E.seen[s] = s.count


def build_program(debug=()):
    K = Ctx()
    nc = K.nc
    es = K.es
    dbg = {}

    x_d = K.dram("x", [T, D], F32)
    ctx_d = K.dram("ctx", [NCTX, D], F32)
    cc_d = K.dram("cc", [128, 8, 2], F32)
    adaw_d = K.dram("ada_w", [2, 128, 8, 6144], F32)
    adab_d = K.dram("ada_b", [2, 128, 48], F32)
    nmix_d = K.dram("norm_mix", [2, 128, 8], F32)
    nffn_d = K.dram("norm_ffn", [2, 128, 8], F32)
    out_d = K.dram("out", [T, D], F32, kind="ExternalOutput")
    ident_d = K.dram("ident", [128, 128], F32)
    w1_d = K.dram("ffn_w1", [2, 128, 8, DFF], F32)
    win_d = K.dram("w_in", [2, 128, 8, NIN], F32)
    wg_d = K.dram("w_g", [2, 128, 8, 1024], F32)
    fnetw_d = K.dram("fnet_w", [2, 128, 2, 256], F32)
    wout_d = K.dram("w_out", [2, 128, 8, D], F32)
    woutg_d = K.dram("w_out_g", [2, 96, 4, D], F32)
    naq_d = K.dram("na_q", [2, 128, 1], F32)
    nak_d = K.dram("na_k", [2, 128, 1], F32)
    bt_d = K.dram("na_bt", [2, 128, 54, 128], F32)
    aw_d = K.dram("gla_aw", [2, 32, 2, 256], F32)
    ab_d = K.dram("gla_ab", [2, 128, 2, 2], F32)
    gon_d = K.dram("gla_gon", [2, 96, 1], F32)
    ct_d = K.dram("c_ct", [128, 16, T], BF16)
    st_d = K.dram("c_st", [128, 16, T], BF16)
    c256_d = K.dram("c_c256", [128, 2, 256], BF16)
    s256_d = K.dram("c_s256", [128, 2, 256], BF16)
    c64_d = K.dram("c_c64", [128, 4, 128], BF16)
    cos_d = K.dram("c_cos", [128, T], F32)
    sin_d = K.dram("c_sin", [128, T], F32)
    gmask_d = K.dram("c_gmask", [128, 2, 4, 128], F32)
    mscan_d = K.dram("c_mscan", [128, 256], F32)
    w3_d = K.dram("ffn_w3", [2, 128, 8, DFF], F32)
    w2_d = K.dram("ffn_w2", [2, 128, 22, D], F32)

    top = es
    xT = K.sb(top, "xT", [128, 8, NT], F32)
    xT_t = [Tok("xT%d" % i) for i in range(5)]
    identf = K.sb(top, "identf", [128, 128], F32)
    identb = K.sb(top, "identb", [128, 128], BF16)
    onesb = K.sb(top, "onesb", [128, 128], BF16)
    cc = K.sb(top, "cc", [128, 8, 2], F32)
    scb = K.sb(top, "scb", [128, 8, 2], BF16)
    modT = K.sb(top, "modT", [128, 48, 2], F32)
    adab = K.sb(top, "adab", [128, 48], F32)
    nmix = K.sb(top, "nmix", [128, 8], F32)
    nffn = K.sb(top, "nffn", [128, 8], F32)
    amix = K.sb(top, "amix", [128, 8, 2], F32)
    affn = K.sb(top, "affn", [128, 8, 2], F32)
    t_const = Tok("const")
    t_mod = Tok("mod")
    t_small = Tok("small")

    ps = [es.enter_context(nc.psum_tensor("ps%d" % i, [128, 512], F32)) for i in range(7)]
    pt = [Tok("ps%d" % i) for i in range(7)]
    psb = es.enter_context(nc.psum_tensor("psb", [128, 1024], BF16))
    ptb = Tok("psb")

    def blk_cols(b):
        return (b * 512, 512) if b < 4 else (T, NCTX)

    K.dma("sp", identf[:], ident_d, [], [t_const], "c")
    K.op("dve", [t_const], [t_const], lambda h: h.tensor_copy(out=identb[:], in_=identf[:]))
    K.op("dve", [], [t_const], lambda h: h.memset(onesb[:], 1.0))
    K.dma("sp", cc[:], cc_d, [], [t_small], "s")
    K.op("act", [t_small], [t_small], lambda h: h.activation(out=scb[:], in_=cc[:], func=AF.Silu))

    with contextlib.ExitStack() as ph:
        xin = [K.sb(ph, "xin%d" % i, [128, D], F32) for i in range(3)]
        xin_t = [Tok("xin%d" % i) for i in range(3)]
        for ti in range(18):
            s = ti % 3
            src = x_d[ti * 128:(ti + 1) * 128, :] if ti < 16 else ctx_d[(ti - 16) * 128:(ti - 15) * 128, :]
            K.dma("sp", xin[s][:], src, [], [xin_t[s]], "xin")
            b = min(ti // 4, 4)
            for half in range(2):
                pb = (ti * 2 + half) % 7
                for j in range(4):
                    kc = half * 4 + j
                    K.op("pe", [xin_t[s], t_const], [pt[pb]],
                         lambda h, kc=kc, j=j, pb=pb, s=s: h.transpose(
                             ps[pb][:, j * 128:(j + 1) * 128], xin[s][:, kc * 128:(kc + 1) * 128], identf[:]))
                eng = "dve" if half == 0 else "act"
                dst = xT[:, half * 4:(half + 1) * 4, ti * 128:(ti + 1) * 128]
                srcp = ps[pb][:, :].rearrange("p (a b) -> p a b", a=4)
                if eng == "dve":
                    K.op("dve", [pt[pb]], [xT_t[b]], lambda h, dst=dst, srcp=srcp: h.tensor_copy(out=dst, in_=srcp))
                else:
                    K.op("act", [pt[pb]], [xT_t[b]], lambda h, dst=dst, srcp=srcp: h.activation(out=dst, in_=srcp, func=AF.Copy))
        K.barrier()


    rot = {"ps": 0}

    def norm_mod(l, which, hT, hT_t, blocks):
        a = amix if which == "mix" else affn
        sec = 0 if which == "mix" else 3
        with contextlib.ExitStack() as ph:
            sq = [K.sb(ph, "nm_sq%d" % i, [128, 8, 512], BF16) for i in range(2)]
            sq_t = [Tok() for i in range(2)]
            rt = [K.sb(ph, "nm_rt%d" % i, [128, 512], F32) for i in range(2)]
            rs = [K.sb(ph, "nm_rs%d" % i, [128, 512], F32) for i in range(2)]
            rs_t = [Tok() for i in range(2)]
            tmp = [K.sb(ph, "nm_tmp%d" % i, [128, 512], F32) for i in range(3)]
            tmp_t = [Tok() for i in range(3)]
            epsb = K.sb(ph, "nm_eps", [128, 1], F32)
            eps_t = Tok()
            K.op("dve", [], [eps_t], lambda h: h.memset(epsb[:], EPS))
            ti = 0
            for bi, b in enumerate(blocks):
                c0, n = blk_cols(b)
                r = 0 if b < 4 else 1
                s = bi % 2
                pb = bi % 2
                K.op("act", [xT_t[b]], [sq_t[s]], lambda h, s=s, c0=c0, n=n: h.activation(
                    out=sq[s][:, :, :n], in_=xT[:, :, c0:c0 + n], func=AF.Square))
                K.mm([sq_t[s], t_const], [pt[pb]], ps[pb][:, :n],
                     [(onesb[:], sq[s][:, kc, :n]) for kc in range(8)])
                K.op("act", [pt[pb], eps_t], [rs_t[s]], lambda h, s=s, pb=pb, n=n: h.activation(
                    out=rt[s][:, :n], in_=ps[pb][:, :n], func=AF.Sqrt, bias=epsb[:], scale=1.0 / D))
                K.op("dve", [rs_t[s]], [rs_t[s]], lambda h, s=s, n=n: h.reciprocal(out=rs[s][:, :n], in_=rt[s][:, :n]))
                for kc in range(8):
                    u = ti % 3
                    ti += 1
                    K.op("dve", [xT_t[b], rs_t[s], t_mod], [tmp_t[u]], lambda h, u=u, kc=kc, c0=c0, n=n, r=r, s=s: h.scalar_tensor_tensor(
                        out=tmp[u][:, :n], in0=xT[:, kc, c0:c0 + n], scalar=a[:, kc, r:r + 1], in1=rs[s][:, :n],
                        op0=ALU.mult, op1=ALU.mult))
                    K.op("act", [tmp_t[u], t_mod], [hT_t[b]], lambda h, u=u, kc=kc, c0=c0, n=n, r=r: h.activation(
                        out=hT[:, kc, c0:c0 + n], in_=tmp[u][:, :n], func=AF.Identity,
                        bias=modT[:, sec * 8 + kc, r:r + 1], scale=1.0))
            K.barrier()

    def ffn(l, hT, hT_t, blocks):
        with contextlib.ExitStack() as ph:
            w1s = [K.sb(ph, "w1s%d" % i, [128, 8, 512], BF16) for i in range(2)]
            w3s = [K.sb(ph, "w3s%d" % i, [128, 8, 512], BF16) for i in range(2)]
            w2s = [K.sb(ph, "w2s%d" % i, [128, 4, D], BF16) for i in range(2)]
            w_t = [Tok() for i in range(2)]
            s1 = [K.sb(ph, "ff_s1%d" % i, [128, 512], F32) for i in range(2)]
            s1_t = [Tok() for i in range(2)]
            g = [K.sb(ph, "ff_g%d" % i, [128, 4, 512], BF16) for i in range(2)]
            g_t = [Tok() for i in range(2)]
            chunks = [(i * 512, 512) for i in range(5)] + [(2560, 256)]

            def load(ch):
                f0, nf = chunks[ch]
                s = ch % 2
                K.dma("pool", w1s[s][:, :, :nf], w1_d[l][:, :, f0:f0 + nf], [], [w_t[s]], "ffw")
                K.dma("pool", w3s[s][:, :, :nf], w3_d[l][:, :, f0:f0 + nf], [], [w_t[s]], "ffw")
                K.dma("pool", w2s[s][:, :nf // 128, :], w2_d[l][:, f0 // 128:(f0 + nf) // 128, :], [], [w_t[s]], "ffw")
            load(0)
            cnt = 0
            gi = 0
            oi = 0
            for ch in range(6):
                if ch + 1 < 6:
                    load(ch + 1)
                f0, nf = chunks[ch]
                s = ch % 2
                nft = nf // 128
                for b in blocks:
                    c0, n = blk_cols(b)
                    r = 0 if b < 4 else 1
                    gs = gi % 2
                    gi += 1
                    for ft in range(nft):
                        pa = cnt % 2
                        pbk = 2 + cnt % 2
                        u = cnt % 2
                        cnt += 1
                        K.mm([w_t[s], hT_t[b]], [pt[pa]], ps[pa][:, :n],
                             [(w1s[s][:, kc, ft * 128:(ft + 1) * 128], hT[:, kc, c0:c0 + n]) for kc in range(8)])
                        K.mm([w_t[s], hT_t[b]], [pt[pbk]], ps[pbk][:, :n],
                             [(w3s[s][:, kc, ft * 128:(ft + 1) * 128], hT[:, kc, c0:c0 + n]) for kc in range(8)])
                        K.op("act", [pt[pa]], [s1_t[u]], lambda h, u=u, pa=pa, n=n: h.activation(
                            out=s1[u][:, :n], in_=ps[pa][:, :n], func=AF.Silu))
                        K.op("dve", [s1_t[u], pt[pbk]], [g_t[gs]], lambda h, u=u, pbk=pbk, n=n, gs=gs, ft=ft: h.tensor_tensor(
                            out=g[gs][:, ft, :n], in0=s1[u][:, :n], in1=ps[pbk][:, :n], op=ALU.mult))
                    for dt in range(8):
                        pc = 4 + oi % 3
                        oi += 1
                        K.mm([w_t[s], g_t[gs]], [pt[pc]], ps[pc][:, :n],
                             [(w2s[s][:, ft, dt * 128:(dt + 1) * 128], g[gs][:, ft, :n]) for ft in range(nft)])
                        K.op("dve", [pt[pc], t_mod, xT_t[b]], [xT_t[b]], lambda h, pc=pc, dt=dt, c0=c0, n=n, r=r: h.scalar_tensor_tensor(
                            out=xT[:, dt, c0:c0 + n], in0=ps[pc][:, :n], scalar=modT[:, 40 + dt, r:r + 1],
                            in1=xT[:, dt, c0:c0 + n], op0=ALU.mult, op1=ALU.add))
            K.barrier()

    def lin_fm(wt, w, col0, M, hT, hT_t, b, c0, n, pb, wtok):
        K.mm([wtok, hT_t[b]], [pt[pb]], ps[pb][:M, :n],
             [(w[:, kc, col0:col0 + M], hT[:, kc, c0:c0 + n]) for kc in range(8)])

    def out_proj_update(pairs_fn, reads, b, c0, n, r, pbs):
        for dt in range(8):
            pc = pbs[dt % len(pbs)]
            K.mm(reads, [pt[pc]], ps[pc][:, :n], pairs_fn(dt))
            K.op("dve", [pt[pc], t_mod, xT_t[b]], [xT_t[b]], lambda h, pc=pc, dt=dt: h.scalar_tensor_tensor(
                out=xT[:, dt, c0:c0 + n], in0=ps[pc][:, :n], scalar=modT[:, 16 + dt, r:r + 1],
                in1=xT[:, dt, c0:c0 + n], op0=ALU.mult, op1=ALU.add))

    def rstd_from_ps(pb, P, n, inv, rt, rs, rs_tok, eps_ap, eps_t):
        K.op("act", [pt[pb], eps_t], [rs_tok], lambda h: h.activation(
            out=rt[:P, :n], in_=ps[pb][:P, :n], func=AF.Sqrt, bias=eps_ap[:P, :], scale=inv))
        K.op("dve", [rs_tok], [rs_tok], lambda h: h.reciprocal(out=rs[:P, :n], in_=rt[:P, :n]))

    def fnet(l, hT, hT_t, need_ctx):
        with contextlib.ExitStack() as ph:
            Z = K.sb(ph, "fn_Z", [128, 18, 256], BF16)
            Z_t = Tok()
            wfn = K.sb(ph, "fn_w", [128, 8, 256], BF16)
            fnw = K.sb(ph, "fn_fw", [128, 2, 256], BF16)
            wo = K.sb(ph, "fn_wo", [128, 2, D], BF16)
            c64 = K.sb(ph, "fn_c64", [128, 4, 128], BF16)
            w_t = Tok()
            K.dma("pool", wfn[:], win_d[l][:, :, 0:256], [], [w_t], "fnw")
            K.dma("pool", fnw[:], fnetw_d[l], [], [w_t], "fnw")
            K.dma("pool", wo[:], wout_d[l][:, 0:2, :], [], [w_t], "fnw")
            K.dma("sp", c64[:], c64_d, [], [w_t], "fnw")
            cts = [K.sb(ph, "fn_ct%d" % i, [128, 2, 16, 512], BF16) for i in range(2)]
            ct_t = [Tok() for i in range(2)]
            Psb = [K.sb(ph, "fn_P%d" % i, [128, 2, 512], BF16) for i in range(2)]
            P_t = [Tok() for i in range(2)]
            Ysb = K.sb(ph, "fn_Y", [128, 2, 512], BF16)
            Y_t = Tok()
            yf = K.sb(ph, "fn_yf", [128, 2, 512], BF16)
            yf_t = Tok()
            for ti in range(18):
                pb = ti % 2
                b = min(ti // 4, 4)
                K.mm([w_t, hT_t[b]], [pt[pb]], ps[pb][:, :256],
                     [(hT[:, kc, ti * 128:(ti + 1) * 128], wfn[:, kc, :]) for kc in range(8)])
                if ti % 2 == 0:
                    K.op("dve", [pt[pb]], [Z_t], lambda h, ti=ti, pb=pb: h.tensor_copy(out=Z[:, ti, :], in_=ps[pb][:, :256]))
                else:
                    K.op("act", [pt[pb]], [Z_t], lambda h, ti=ti, pb=pb: h.activation(out=Z[:, ti, :], in_=ps[pb][:, :256], func=AF.Copy))
            jobs = [(kb, 0) for kb in range(4)] + ([(0, 1)] if need_ctx else [])
            for ji, (kb, isctx) in enumerate(jobs):
                s = ji % 2
                if not isctx:
                    ntt, n, tt0, c0, b, r = 16, 512, 0, kb * 512, kb, 0
                    K.dma("sp", cts[s][:, 0, :, :], ct_d[:, :, kb * 512:(kb + 1) * 512], [], [ct_t[s]], "ct")
                    K.dma("sp", cts[s][:, 1, :, :], st_d[:, :, kb * 512:(kb + 1) * 512], [], [ct_t[s]], "ct")
                else:
                    ntt, n, tt0, c0, b, r = 2, 256, 16, T, 4, 1
                    K.dma("sp", cts[s][:, 0, 0:2, 0:256], c256_d, [], [ct_t[s]], "ct")
                    K.dma("sp", cts[s][:, 1, 0:2, 0:256], s256_d, [], [ct_t[s]], "ct")
                for chc in range(2):
                    u = chc
                    for cs in range(2):
                        pb = cs
                        K.mm([Z_t, ct_t[s]], [pt[pb]], ps[pb][:, :n],
                             [(Z[:, tt0 + tt, chc * 128:(chc + 1) * 128], cts[s][:, cs, tt, :n]) for tt in range(ntt)])
                        if cs == 0:
                            K.op("dve", [pt[pb]], [P_t[u]], lambda h, u=u, pb=pb, cs=cs: h.tensor_copy(out=Psb[u][:, cs, :n], in_=ps[pb][:, :n]))
                        else:
                            K.op("act", [pt[pb]], [P_t[u]], lambda h, u=u, pb=pb, cs=cs: h.activation(out=Psb[u][:, cs, :n], in_=ps[pb][:, :n], func=AF.Copy))
                    pb = 2 + chc
                    K.mm([P_t[u], w_t], [pt[pb]], ps[pb][:, :n],
                         [(c64[:, 2 * isctx + 0, :], Psb[u][:, 0, :n]), (c64[:, 2 * isctx + 1, :], Psb[u][:, 1, :n])])
                    K.op("dve", [pt[pb]], [Y_t], lambda h, pb=pb, chc=chc: h.tensor_copy(out=Ysb[:, chc, :n], in_=ps[pb][:, :n]))
                for c2 in range(2):
                    pb = 4 + c2
                    K.mm([Y_t, w_t], [pt[pb]], ps[pb][:, :n],
                         [(fnw[:, c1, c2 * 128:(c2 + 1) * 128], Ysb[:, c1, :n]) for c1 in range(2)])
                    K.op("act", [pt[pb]], [yf_t], lambda h, pb=pb, c2=c2: h.activation(out=yf[:, c2, :n], in_=ps[pb][:, :n], func=AF.Copy))
                out_proj_update(lambda dt: [(wo[:, c2, dt * 128:(dt + 1) * 128], yf[:, c2, :n]) for c2 in range(2)],
                                [yf_t, w_t], b, c0, n, r, [0, 1, 2, 3, 4, 5, 6])
            K.barrier()

    def natt(l, hT, hT_t, need_ctx):
        with contextlib.ExitStack() as ph:
            kT = K.sb(ph, "na_kT", [128, 3, NT], BF16)
            kT_t = Tok()
            V = K.sb(ph, "na_V", [128, 18, 6, 65], BF16)
            V_t = Tok()
            gq = K.sb(ph, "na_gq", [128, 1], F32)
            gk = K.sb(ph, "na_gk", [128, 1], F32)
            epsb = K.sb(ph, "na_eps", [128, 1], F32)
            bones = K.sb(ph, "na_bones", [128, 128], BF16)
            g_t = Tok()
            K.dma("sp", gq[:], naq_d[l], [], [g_t], "nag")
            K.dma("sp", gk[:], nak_d[l], [], [g_t], "nag")
            K.op("dve", [g_t], [g_t], lambda h: h.tensor_scalar(out=gq[:], in0=gq[:], scalar1=0.125, scalar2=None, op0=ALU.mult))
            K.op("dve", [], [g_t], lambda h: h.memset(epsb[:], EPS))
            K.op("dve", [], [g_t], lambda h: h.memset(bones[:], 0.0))
            K.op("dve", [g_t], [g_t], lambda h: h.memset(bones[0:64, 0:64], 1.0))
            K.op("dve", [g_t], [g_t], lambda h: h.memset(bones[64:128, 64:128], 1.0))
            K.op("dve", [], [V_t], lambda h: h.memset(V[:, :, :, 64:65], 1.0))
            sq = [K.sb(ph, "na_sq%d" % i, [128, 512], BF16) for i in range(2)]
            sq_t = [Tok() for i in range(2)]
            rtl = [K.sb(ph, "na_rt%d" % i, [128, 512], F32) for i in range(2)]
            rs = [K.sb(ph, "na_rs%d" % i, [128, 512], F32) for i in range(2)]
            rs_t = [Tok() for i in range(2)]

            def qk_norm(w, wtok, gain, dst, dst_t, dcol, b, c0, n, cnt):
                for hp in range(3):
                    u = (cnt[0]) % 2
                    cnt[0] += 1
                    pa, pbk = u, 2 + u
                    lin_fm(None, w, hp * 128, 128, hT, hT_t, b, c0, n, pa, wtok)
                    K.op("act", [pt[pa]], [sq_t[u]], lambda h, u=u, pa=pa: h.activation(out=sq[u][:, :n], in_=ps[pa][:, :n], func=AF.Square))
                    K.mm([sq_t[u], g_t], [pt[pbk]], ps[pbk][:, :n], [(bones[:], sq[u][:, :n])])
                    K.op("act", [pt[pbk], g_t], [rs_t[u]], lambda h, u=u, pbk=pbk: h.activation(
                        out=rtl[u][:, :n], in_=ps[pbk][:, :n], func=AF.Sqrt, bias=epsb[:], scale=1.0 / 64))
                    K.op("dve", [rs_t[u]], [rs_t[u]], lambda h, u=u: h.reciprocal(out=rs[u][:, :n], in_=rtl[u][:, :n]))
                    K.op("dve", [pt[pa], rs_t[u], g_t], [dst_t], lambda h, u=u, pa=pa, hp=hp: h.scalar_tensor_tensor(
                        out=dst[:, hp, dcol:dcol + n], in0=ps[pa][:, :n], scalar=gain[:, 0:1], in1=rs[u][:, :n],
                        op0=ALU.mult, op1=ALU.mult))

            cnt = [0]
            with contextlib.ExitStack() as p1:
                wk = K.sb(p1, "na_wk", [128, 8, 384], BF16)
                wv = K.sb(p1, "na_wv", [128, 8, 384], BF16)
                wk_t = Tok()
                K.dma("pool", wk[:], win_d[l][:, :, 640:1024], [], [wk_t], "naw")
                K.dma("pool", wv[:], win_d[l][:, :, 1024:1408], [], [wk_t], "naw")
                for b in range(5):
                    c0, n = blk_cols(b)
                    qk_norm(wk, wk_t, gk, kT, kT_t, c0, b, c0, n, cnt)
                for ti in range(18):
                    pb = 4 + ti % 2
                    b = min(ti // 4, 4)
                    K.mm([wk_t, hT_t[b]], [pt[pb]], ps[pb][:, :384],
                         [(hT[:, kc, ti * 128:(ti + 1) * 128], wv[:, kc, :]) for kc in range(8)])
                    K.op("act" if ti % 2 else "dve", [pt[pb]], [V_t],
                         (lambda h, ti=ti, pb=pb: h.activation(out=V[:, ti, :, 0:64], in_=ps[pb][:, :384].rearrange("p (a b) -> p a b", b=64), func=AF.Copy))
                         if ti % 2 else
                         (lambda h, ti=ti, pb=pb: h.tensor_copy(out=V[:, ti, :, 0:64], in_=ps[pb][:, :384].rearrange("p (a b) -> p a b", b=64))))
                K.barrier()
            wq = K.sb(ph, "na_wq", [128, 8, 384], BF16)
            wo = K.sb(ph, "na_wo", [128, 3, D], BF16)
            BT = K.sb(ph, "na_BT", [128, 54, 128], BF16)
            w_t = Tok()
            K.dma("pool", wq[:], win_d[l][:, :, 256:640], [], [w_t], "naw2")
            K.dma("pool", wo[:], wout_d[l][:, 2:5, :], [], [w_t], "naw2")
            K.dma("pool", BT[:, 0:27, :], bt_d[l][:, 0:27, :], [], [w_t], "naw2")
            K.dma("pool", BT[:, 27:54, :], bt_d[l][:, 27:54, :], [], [w_t], "naw2")
            qT = K.sb(ph, "na_qT", [128, 3, 512], BF16)
            qT_t = Tok()
            PT = [K.sb(ph, "na_PT%d" % i, [128, 7, 128], BF16) for i in range(2)]
            PT_t = [Tok() for i in range(2)]
            rec = K.sb(ph, "na_rec", [128, 6], F32)
            rec_t = Tok()
            Osb = K.sb(ph, "na_O", [128, 6, 64], BF16)
            O_t = Tok()
            yna = K.sb(ph, "na_y", [128, 3, 512], BF16)
            yna_t = Tok()
            hcount = 0
            for b in (range(5) if need_ctx else range(4)):
                c0, n = blk_cols(b)
                r = 0 if b < 4 else 1
                qk_norm(wq, w_t, gq, qT, qT_t, 0, b, c0, n, cnt)
                for qi in range(n // 128):
                    if b < 4:
                        i = b * 4 + qi
                        if i <= 1:
                            js = list(range(0, 4))
                        elif i >= 14:
                            js = list(range(12, 16))
                        else:
                            js = list(range(i - 2, i + 3))
                        chunks = []
                        for j in js:
                            dl = j - i
                            if 2 <= i <= 13 and abs(dl) == 2:
                                v = 7 if dl < 0 else 8
                            else:
                                v = dl + 3
                            chunks.append((j, v))
                        chunks += [(16, None), (17, None)]
                    else:
                        chunks = [(16, None), (17, None)]
                    nch = len(chunks)
                    for hd in range(6):
                        hp, r0 = hd // 2, (hd % 2) * 64
                        u = hcount % 2
                        hcount += 1
                        pS = (0 + 2 * u, 1 + 2 * u)
                        for c, (kt, v) in enumerate(chunks):
                            pb = pS[c // 4]
                            pairs = [(kT[r0:r0 + 64, hp, kt * 128:(kt + 1) * 128], qT[r0:r0 + 64, hp, qi * 128:(qi + 1) * 128])]
                            if v is not None:
                                pairs.append((identb[:], BT[:, hd * 9 + v, :]))
                            K.mm([kT_t, qT_t, w_t, t_const], [pt[pb]], ps[pb][:, (c % 4) * 128:(c % 4 + 1) * 128], pairs)
                        n0 = min(nch, 4)
                        K.op("act", [pt[pS[0]]], [PT_t[u]], lambda h, u=u, pS=pS, n0=n0: h.activation(
                            out=PT[u][:, 0:n0, :], in_=ps[pS[0]][:, 0:n0 * 128].rearrange("p (a b) -> p a b", b=128), func=AF.Exp))
                        if nch > 4:
                            n1 = nch - 4
                            K.op("act", [pt[pS[1]]], [PT_t[u]], lambda h, u=u, pS=pS, n1=n1: h.activation(
                                out=PT[u][:, 4:4 + n1, :], in_=ps[pS[1]][:, 0:n1 * 128].rearrange("p (a b) -> p a b", b=128), func=AF.Exp))
                        K.mm([PT_t[u], V_t], [pt[4]], ps[4][:, hd * 65:(hd + 1) * 65],
                             [(PT[u][:, c, :], V[:, kt, hd, :]) for c, (kt, v) in enumerate(chunks)])
                    ov = ps[4][:, 0:390].rearrange("p (a b) -> p a b", b=65)
                    K.op("dve", [pt[4]], [rec_t], lambda h, ov=ov: h.reciprocal(out=rec[:], in_=ov[:, :, 64]))
                    K.op("dve", [pt[4], rec_t], [O_t], lambda h, ov=ov: h.tensor_tensor(
                        out=Osb[:], in0=ov[:, :, 0:64], in1=rec[:].unsqueeze(2).to_broadcast([128, 6, 64]), op=ALU.mult))
                    Of = Osb[:].rearrange("p a b -> p (a b)")
                    for hp in range(3):
                        K.op("pe", [O_t, t_const], [ptb], lambda h, hp=hp, Of=Of: h.transpose(
                            psb[:, hp * 128:(hp + 1) * 128], Of[:, hp * 128:(hp + 1) * 128], identb[:]))
                    K.op("act", [ptb], [yna_t], lambda h, qi=qi: h.activation(
                        out=yna[:, :, qi * 128:(qi + 1) * 128], in_=psb[:, 0:384].rearrange("p (a b) -> p a b", b=128), func=AF.Copy))
                out_proj_update(lambda dt: [(wo[:, hp, dt * 128:(dt + 1) * 128], yna[:, hp, :n]) for hp in range(3)],
                                [yna_t, w_t], b, c0, n, r, [5, 6])
            K.barrier()

    def gla(l, hT, hT_t, need_ctx):
        NB = 256
        with contextlib.ExitStack() as ph:
            wg = K.sb(ph, "gl_wg", [128, 8, 1024], BF16)
            wgv = K.sb(ph, "gl_wgv", [128, 8, 384], BF16)
            wga = K.sb(ph, "gl_wga", [128, 8, 32], BF16)
            aw = K.sb(ph, "gl_aw", [32, 2, 256], BF16)
            nab = K.sb(ph, "gl_nab", [128, 2, 2], F32)
            gon = K.sb(ph, "gl_gon", [96, 1], F32)
            msk = K.sb(ph, "gl_msk", [128, 2, 4, 128], F32)
            mscan = K.sb(ph, "gl_mscan", [128, NB], F32)
            epsb = K.sb(ph, "gl_eps", [128, 1], F32)
            oneb = K.sb(ph, "gl_one", [128, 1], F32)
            w_t = Tok()
            K.dma("pool", wg[:], wg_d[l], [], [w_t], "glw")
            K.dma("pool", wgv[:], win_d[l][:, :, 1792:2176], [], [w_t], "glw")
            K.dma("pool", wga[:], win_d[l][:, :, 2560:2592], [], [w_t], "glw")
            K.dma("pool", aw[:], aw_d[l], [], [w_t], "glw")
            K.dma("sp", nab[:], ab_d[l], [], [w_t], "glw")
            K.dma("sp", gon[:], gon_d[l], [], [w_t], "glw")
            K.dma("sp", msk[:], gmask_d, [], [w_t], "glw")
            K.dma("sp", mscan[:], mscan_d, [], [w_t], "glw")
            K.op("dve", [w_t], [w_t], lambda h: h.tensor_scalar(out=nab[:], in0=nab[:], scalar1=-1.0, scalar2=None, op0=ALU.mult))
            K.op("dve", [], [w_t], lambda h: h.memset(epsb[:], EPS))
            K.op("dve", [], [w_t], lambda h: h.memset(oneb[:], 1.0))
            oF = K.sb(ph, "gl_oF", [96, 4, NT], BF16)
            oF_t = Tok()
            S = K.sb(ph, "gl_S", [128, 2, 192], F32)
            Sb = K.sb(ph, "gl_Sb", [128, 2, 192], BF16)
            S_t = Tok()
            Sb_t = Tok()
            qin = K.sb(ph, "gl_qin", [128, 2, NB], BF16)
            kin = K.sb(ph, "gl_kin", [128, 2, NB], BF16)
            qk_t = Tok()
            ktok = K.sb(ph, "gl_ktok", [128, 2, 2, 128], BF16)
            ktok_t = Tok()
            gv = K.sb(ph, "gl_gv", [128, 2, 384], BF16)
            gv_t = Tok()
            dec = K.sb(ph, "gl_dec", [128, 2, 2], F32)
            dec_t = Tok()
            gaT = K.sb(ph, "gl_gaT", [32, NB], BF16)
            gaT_t = Tok()
            cosb = K.sb(ph, "gl_cos", [128, NB], F32)
            sinb = K.sb(ph, "gl_sin", [128, NB], F32)
            cs_t = Tok()
            e1 = K.sb(ph, "gl_e1", [128, NB], F32)
            lsb = K.sb(ph, "gl_l", [128, NB], F32)
            bpos = K.sb(ph, "gl_bpos", [128, NB], F32)
            bb = K.sb(ph, "gl_bb", [128, NB], F32)
            eb = K.sb(ph, "gl_eb", [128, NB], F32)
            enb = K.sb(ph, "gl_enb", [128, NB], F32)
            g_t = Tok()
            t1 = K.sb(ph, "gl_t1", [128, NB], F32)
            t2 = K.sb(ph, "gl_t2", [128, NB], F32)
            r_t = Tok()
            AM = K.sb(ph, "gl_AM", [128, 4, 128], BF16)
            AM_t = Tok()
            QS = 48 ** -0.5

            def block_prep(b, dirn):
                c0 = b * NB if b < 8 else T
                n = NB
                bb5 = min(c0 // 512, 4)
                lat = b < 8
                if lat:
                    K.dma("sp", cosb[:], cos_d[:, c0:c0 + n], [], [cs_t], "cs")
                    K.dma("sp", sinb[:], sin_d[:, c0:c0 + n], [], [cs_t], "cs")
                lin_fm(None, wga, 0, 32, hT, hT_t, bb5, c0, n, 3, w_t)
                K.op("act", [pt[3]], [gaT_t], lambda h: h.activation(out=gaT[:, :n], in_=ps[3][:32, :n], func=AF.Copy))
                for hp in range(2):
                    K.mm([gaT_t, w_t], [pt[4]], ps[4][:, :n], [(aw[:, dirn, hp * 128:(hp + 1) * 128], gaT[:, :n])])
                    K.op("act", [pt[4], w_t], [g_t], lambda h, hp=hp: h.activation(
                        out=e1[:, :n], in_=ps[4][:, :n], func=AF.Exp, bias=nab[:, dirn, hp:hp + 1], scale=-1.0))
                    K.op("act", [g_t, w_t], [g_t], lambda h: h.activation(
                        out=lsb[:, :n], in_=e1[:, :n], func=AF.Ln, bias=oneb[:], scale=1.0))
                    K.op("dve", [g_t, w_t], [g_t], lambda h: h.tensor_tensor_scan(
                        out=bpos[:, :n], data0=mscan[:, :n], data1=lsb[:, :n], initial=0.0, op0=ALU.mult, op1=ALU.add))
                    bsel = bpos
                    if dirn == 1:
                        K.op("dve", [g_t], [g_t], lambda h: h.tensor_tensor(out=bb[:, :n], in0=lsb[:, :n], in1=bpos[:, :n], op=ALU.subtract))
                        K.op("dve", [g_t], [g_t], lambda h: h.tensor_tensor(
                            out=bb[:, :n].rearrange("p (a b) -> p a b", b=128), in0=bb[:, :n].rearrange("p (a b) -> p a b", b=128),
                            in1=bpos[:, :n].rearrange("p (a b) -> p a b", b=128)[:, :, 127:128].to_broadcast([128, n // 128, 128]), op=ALU.add))
                        bsel = bb
                    K.op("act", [g_t], [g_t], lambda h, bsel=bsel: h.activation(out=eb[:, :n], in_=bsel[:, :n], func=AF.Exp, scale=-1.0 / 16))
                    K.op("act", [g_t], [g_t], lambda h, bsel=bsel: h.activation(out=enb[:, :n], in_=bsel[:, :n], func=AF.Exp, scale=1.0 / 16))
                    sel = 127 if dirn == 0 else 0
                    K.op("dve", [g_t], [dec_t], lambda h, hp=hp, sel=sel: h.tensor_copy(
                        out=dec[:, hp, :], in_=eb[:, :n].rearrange("p (a b) -> p a b", b=128)[:, :, sel]))
                    for which in range(2):
                        base = which * 512
                        lin_fm(None, wg, base + hp * 128, 128, hT, hT_t, bb5, c0, n, 5, w_t)
                        dst = qin if which == 0 else kin
                        ee = eb if which == 0 else enb
                        sc = QS if which == 0 else 1.0
                        if lat:
                            lin_fm(None, wg, base + 256 + hp * 128, 128, hT, hT_t, bb5, c0, n, 6, w_t)
                            K.op("dve", [pt[5], cs_t], [r_t], lambda h: h.tensor_tensor(out=t1[:, :n], in0=ps[5][:, :n], in1=cosb[:, :n], op=ALU.mult))
                            K.op("dve", [pt[6], cs_t, r_t], [r_t], lambda h: h.tensor_tensor(out=t2[:, :n], in0=ps[6][:, :n], in1=sinb[:, :n], op=ALU.mult))
                            K.op("dve", [r_t], [r_t], lambda h: h.tensor_tensor(out=t1[:, :n], in0=t1[:, :n], in1=t2[:, :n], op=ALU.add))
                            K.op("dve", [r_t, g_t], [qk_t], lambda h, dst=dst, ee=ee, sc=sc, hp=hp: h.scalar_tensor_tensor(
                                out=dst[:, hp, :n], in0=t1[:, :n], scalar=sc, in1=ee[:, :n], op0=ALU.mult, op1=ALU.mult))
                        else:
                            K.op("dve", [pt[5], g_t], [qk_t], lambda h, dst=dst, ee=ee, sc=sc, hp=hp: h.scalar_tensor_tensor(
                                out=dst[:, hp, :n], in0=ps[5][:, :n], scalar=sc, in1=ee[:, :n], op0=ALU.mult, op1=ALU.mult))
                    for tt in range(n // 128):
                        K.op("pe", [qk_t, t_const], [ptb], lambda h, hp=hp, tt=tt: h.transpose(
                            psb[:, 0:128], kin[:, hp, tt * 128:(tt + 1) * 128], identb[:]))
                        K.op("act", [ptb], [ktok_t], lambda h, hp=hp, tt=tt: h.activation(out=ktok[:, tt, hp, :], in_=psb[:, 0:128], func=AF.Copy))
                for tt in range(n // 128):
                    K.mm([w_t, hT_t[bb5]], [pt[3]], ps[3][:, :384],
                         [(hT[:, kc, c0 + tt * 128:c0 + (tt + 1) * 128], wgv[:, kc, :]) for kc in range(8)])
                    K.op("act", [pt[3]], [gv_t], lambda h, tt=tt: h.activation(out=gv[:, tt, :], in_=ps[3][:, :384], func=AF.Copy))
                return c0, n, bb5

            SS = int(os.environ.get("KSCANSTOP", "99"))

            def scan_tile(dirn, tt, c0, want_o, o_sink):
                tc0 = tt * 128
                if want_o:
                    for hd in (0, 2, 1, 3):
                        hp, r0 = hd // 2, (hd % 2) * 64
                        pa = 0 if hd % 2 == 0 else 6
                        K.mm([qk_t], [pt[pa]], ps[pa][:, (hd // 2) * 128:(hd // 2 + 1) * 128],
                             [(kin[r0:r0 + 48, hp, tc0:tc0 + 128], qin[r0:r0 + 48, hp, tc0:tc0 + 128])])
                    if SS <= 1:
                        return
                    for par, pa in ((0, 0), (1, 6)):
                        K.op("dve", [pt[pa], w_t], [AM_t], lambda h, par=par, pa=pa: h.tensor_tensor(
                            out=AM[:, par:4:2, :], in0=ps[pa][:, 0:256].rearrange("p (a b) -> p a b", b=128),
                            in1=msk[:, dirn, 0:2, :], op=ALU.mult))
                    if SS <= 2:
                        return
                    for hd in range(4):
                        hp, r0 = hd // 2, (hd % 2) * 64
                        K.mm([AM_t, gv_t, Sb_t, qk_t], [pt[1]], ps[1][:96, hd * 128:(hd + 1) * 128],
                             [(gv[:, tt, hd * 96:(hd + 1) * 96], AM[:, hd, :]),
                              (Sb[r0:r0 + 48, hp, (hd % 2) * 96:(hd % 2 + 1) * 96], qin[r0:r0 + 48, hp, tc0:tc0 + 128])])
                    if SS <= 3:
                        return
                    o_sink(tt)
                if SS <= 4:
                    return
                for hp in range(2):
                    K.mm([ktok_t, gv_t], [pt[2]], ps[2][:, hp * 192:(hp + 1) * 192],
                         [(ktok[:, tt, hp, :], gv[:, tt, hp * 192:(hp + 1) * 192])])
                if SS <= 5:
                    return
                for hp in range(2):
                    K.op("act", [S_t, dec_t], [S_t], lambda h, hp=hp: h.activation(
                        out=S[:, hp, :], in_=S[:, hp, :], func=AF.Identity, scale=dec[:, hp, tt:tt + 1]))
                    K.op("dve", [pt[2], S_t, dec_t], [S_t], lambda h, hp=hp: h.scalar_tensor_tensor(
                        out=S[:, hp, :], in0=ps[2][:, hp * 192:(hp + 1) * 192], scalar=dec[:, hp, tt:tt + 1], in1=S[:, hp, :],
                        op0=ALU.mult, op1=ALU.add))
                K.op("act", [S_t], [Sb_t], lambda h: h.activation(out=Sb[:], in_=S[:], func=AF.Copy))

            GS = int(os.environ.get("KGLASTOP", "99"))
            if GS <= 1:
                K.barrier()
                return
            K.op("dve", [], [S_t], lambda h: h.memset(S[:], 0.0))
            K.op("dve", [], [Sb_t], lambda h: h.memset(Sb[:], 0.0))
            for b in [8] + list(range(8)):
                c0, n, bb5 = block_prep(b, 0)
                if GS <= 2:
                    K.barrier()
                    return
                want_o = (b < 8) or need_ctx

                def sink_f(tt, c0=c0):
                    K.op("act", [pt[1]], [oF_t], lambda h: h.activation(
                        out=oF[:, :, c0 + tt * 128:c0 + (tt + 1) * 128],
                        in_=ps[1][:96, :].rearrange("p (a b) -> p a b", b=128), func=AF.Copy))
                for tt in range(n // 128):
                    scan_tile(0, tt, c0, want_o, sink_f)
                if GS <= 3:
                    K.barrier()
                    return
            if GS <= 4:
                K.barrier()
                return
            with contextlib.ExitStack() as p2:
                wgg = K.sb(p2, "gl_wgg", [128, 8, 384], BF16)
                wo = K.sb(p2, "gl_wo", [96, 4, D], BF16)
                w2_t = Tok()
                K.dma("pool", wgg[:], win_d[l][:, :, 2176:2560], [], [w2_t], "glw2")
                K.dma("pool", wo[:], woutg_d[l], [], [w2_t], "glw2")
                sg = K.sb(p2, "gl_sg", [96, 4, NB], BF16)
                sg_t = Tok()
                osum = K.sb(p2, "gl_osum", [96, 4, NB], F32)
                os_t = Tok()
                yg = K.sb(p2, "gl_yg", [96, 4, NB], BF16)
                yg_t = Tok()
                sq = K.sb(p2, "gl_sq", [96, NB], BF16)
                rt = K.sb(p2, "gl_rt", [96, NB], F32)
                rs = K.sb(p2, "gl_rs", [96, NB], F32)
                ty = K.sb(p2, "gl_ty", [96, NB], F32)
                n_t = Tok()
                K.op("dve", [], [S_t], lambda h: h.memset(S[:], 0.0))
                K.op("dve", [], [Sb_t], lambda h: h.memset(Sb[:], 0.0))
                for b in [8] + list(range(7, -1, -1)):
                    c0, n, bb5 = block_prep(b, 1)
                    want_o = (b < 8) or need_ctx
                    r = 0 if b < 8 else 1
                    if want_o:
                        for hd in range(4):
                            lin_fm(None, wgg, hd * 96, 96, hT, hT_t, bb5, c0, n, 3 + hd % 2, w2_t)
                            K.op("act", [pt[3 + hd % 2]], [sg_t], lambda h, hd=hd: h.activation(
                                out=sg[:, hd, :n], in_=ps[3 + hd % 2][:96, :n], func=AF.Silu))

                    def sink_b(tt, c0=c0):
                        K.op("dve", [pt[1], oF_t], [os_t], lambda h: h.tensor_tensor(
                            out=osum[:, :, tt * 128:(tt + 1) * 128],
                            in0=ps[1][:96, :].rearrange("p (a b) -> p a b", b=128),
                            in1=oF[:, :, c0 + tt * 128:c0 + (tt + 1) * 128], op=ALU.add))
                    for tt in range(n // 128 - 1, -1, -1):
                        scan_tile(1, tt, c0, want_o, sink_b)
                    if not want_o:
                        continue
                    for hd in range(4):
                        K.op("act", [os_t], [n_t], lambda h, hd=hd: h.activation(out=sq[:, :n], in_=osum[:, hd, :n], func=AF.Square))
                        K.mm([n_t, t_const], [pt[5]], ps[5][:96, :n], [(onesb[:96, :96], sq[:, :n])])
                        K.op("act", [pt[5], w_t], [n_t], lambda h: h.activation(
                            out=rt[:, :n], in_=ps[5][:96, :n], func=AF.Sqrt, bias=epsb[:96, :], scale=1.0 / 96))
                        K.op("dve", [n_t], [n_t], lambda h: h.reciprocal(out=rs[:, :n], in_=rt[:, :n]))
                        K.op("dve", [os_t, n_t, w_t], [n_t], lambda h, hd=hd: h.scalar_tensor_tensor(
                            out=ty[:, :n], in0=osum[:, hd, :n], scalar=gon[:, 0:1], in1=rs[:, :n], op0=ALU.mult, op1=ALU.mult))
                        K.op("dve", [n_t, sg_t], [yg_t], lambda h, hd=hd: h.tensor_tensor(
                            out=yg[:, hd, :n], in0=ty[:, :n], in1=sg[:, hd, :n], op=ALU.mult))
                    out_proj_update(lambda dt: [(wo[:, hd, dt * 128:(dt + 1) * 128], yg[:, hd, :n]) for hd in range(4)],
                                    [yg_t, w2_t], bb5, c0, n, r, [5, 6])
                K.barrier()
            K.barrier()

    def mixers(l, hT, hT_t, need_ctx):
        if "nofn" not in debug:
            fnet(l, hT, hT_t, need_ctx)
        if "nona" not in debug:
            natt(l, hT, hT_t, need_ctx)
        if "nogla" not in debug:
            gla(l, hT, hT_t, need_ctx)

    for l in range(2):
        need_ctx = (l == 0)
        with contextlib.ExitStack() as ph:
            K.dma("sp", adab[:], adab_d[l], [], [t_small], "s")
            K.dma("sp", nmix[:], nmix_d[l], [], [t_small], "s")
            K.dma("sp", nffn[:], nffn_d[l], [], [t_small], "s")
            wsl = [K.sb(ph, "adaw%d" % i, [128, 8, 1024], BF16) for i in range(2)]
            wsl_t = [Tok("adaw%d" % i) for i in range(2)]
            for sec in range(6):
                s = sec % 2
                K.dma("pool", wsl[s][:], adaw_d[l][:, :, sec * 1024:(sec + 1) * 1024], [], [wsl_t[s]], "adaw")
                pb = sec % 2
                for mt in range(8):
                    j = sec * 8 + mt
                    K.mm([wsl_t[s], t_small], [pt[pb]], ps[pb][:, mt * 2:mt * 2 + 2],
                         [(wsl[s][:, kc, mt * 128:(mt + 1) * 128], scb[:, kc, :]) for kc in range(8)])
                K.op("dve", [pt[pb], t_small], [t_mod],
                     lambda h, sec=sec, pb=pb: h.tensor_tensor(
                         out=modT[:, sec * 8:(sec + 1) * 8, :],
                         in0=ps[pb][:, 0:16].rearrange("p (a b) -> p a b", b=2),
                         in1=adab[:, sec * 8:(sec + 1) * 8].unsqueeze(2).to_broadcast([128, 8, 2]),
                         op=ALU.add))
            for (dst, gain, sec) in ((amix, nmix, 1), (affn, nffn, 4)):
                K.op("dve", [t_mod, t_small], [t_mod],
                     lambda h, dst=dst, gain=gain, sec=sec: h.scalar_tensor_tensor(
                         out=dst[:], in0=modT[:, sec * 8:(sec + 1) * 8, :], scalar=1.0,
                         in1=gain[:].unsqueeze(2).to_broadcast([128, 8, 2]),
                         op0=ALU.add, op1=ALU.mult))
            K.barrier()
        if "mod" in debug and l == 0:
            dbg["modT"] = (modT, [128, 48, 2], F32)
            break
        with contextlib.ExitStack() as lay:
            hT = K.sb(lay, "hT", [128, 8, NT], BF16)
            hT_t = [Tok("hT%d" % i) for i in range(5)]
            if "ffn" not in debug:
                norm_mod(l, "mix", hT, hT_t, range(5))
                mixers(l, hT, hT_t, need_ctx)
            blocks = range(5) if need_ctx else range(4)
            if "noffn" not in debug:
                norm_mod(l, "ffn", hT, hT_t, blocks)
                ffn(l, hT, hT_t, blocks)
            K.barrier()
        if "ffn" in debug or "l0" in debug:
            break

    t_out = Tok("out")
    for name, (buf, shape, dt) in dbg.items():
        dd = K.dram("dbg_" + name, shape, dt, kind="ExternalOutput")
        K.barrier()
        K.dma("sp", dd, buf[:], [], [t_out], "out")
        K.outs["dbg_" + name] = shape

    with contextlib.ExitStack() as ph:
        xo = [K.sb(ph, "xo%d" % i, [128, D], F32) for i in range(3)]
        xo_t = [Tok("xo%d" % i) for i in range(3)]
        for ti in range(16):
            s = ti % 3
            b = ti // 4
            for half in range(2):
                pb = (ti * 2 + half) % 7
                for j in range(4):
                    kc = half * 4 + j
                    K.op("pe", [xT_t[b], t_const], [pt[pb]],
                         lambda h, kc=kc, j=j, pb=pb, ti=ti: h.transpose(
                             ps[pb][:, j * 128:(j + 1) * 128], xT[:, kc, ti * 128:(ti + 1) * 128], identf[:]))
                dst = xo[s][:, half * 512:(half + 1) * 512]
                if half == 0:
                    K.op("dve", [pt[pb]], [xo_t[s]], lambda h, dst=dst, pb=pb: h.tensor_copy(out=dst, in_=ps[pb][:, :]))
                else:
                    K.op("act", [pt[pb]], [xo_t[s]], lambda h, dst=dst, pb=pb: h.activation(out=dst, in_=ps[pb][:, :], func=AF.Copy))
            K.dma("sp", out_d[ti * 128:(ti + 1) * 128, :], xo[s][:], [xo_t[s]], [t_out], "out")
        K.barrier()
    S = t_out.dsem["sp"]
    nc.sync.wait_ge(S.sem, S.count * 16)
    es.close()
    return K


_CONSTS = {}


def _consts():
    if _CONSTS:
        return _CONSTS
    bf = ml_dtypes.bfloat16
    t = np.arange(T, dtype=np.int64)
    ang = 2.0 * np.pi * ((t[:, None] * t[None, :]) % T).astype(np.float64) / T
    _CONSTS["c_ct"] = np.ascontiguousarray(np.cos(ang).reshape(16, 128, T).transpose(1, 0, 2)).astype(bf)
    _CONSTS["c_st"] = np.ascontiguousarray(np.sin(ang).reshape(16, 128, T).transpose(1, 0, 2)).astype(bf)
    t2 = np.arange(NCTX, dtype=np.int64)
    ang2 = 2.0 * np.pi * ((t2[:, None] * t2[None, :]) % NCTX).astype(np.float64) / NCTX
    _CONSTS["c_c256"] = np.ascontiguousarray(np.cos(ang2).reshape(2, 128, NCTX).transpose(1, 0, 2)).astype(bf)
    _CONSTS["c_s256"] = np.ascontiguousarray(np.sin(ang2).reshape(2, 128, NCTX).transpose(1, 0, 2)).astype(bf)
    g = np.arange(64)
    a64 = 2.0 * np.pi * ((g[:, None] * g[None, :]) % 64) / 64.0
    c64 = np.zeros((128, 4, 128))
    for blk in range(2):
        sl = slice(64 * blk, 64 * blk + 64)
        c64[sl, 0, sl] = np.cos(a64) / np.sqrt(T * 64.0)
        c64[sl, 1, sl] = -np.sin(a64) / np.sqrt(T * 64.0)
        c64[sl, 2, sl] = np.cos(a64) / np.sqrt(NCTX * 64.0)
        c64[sl, 3, sl] = -np.sin(a64) / np.sqrt(NCTX * 64.0)
    _CONSTS["c_c64"] = c64.astype(bf)
    cosT = np.zeros((128, T), np.float32)
    sinT = np.zeros((128, T), np.float32)
    row = (t // 64).astype(np.float64)
    col = (t % 64).astype(np.float64)
    for p in range(128):
        d = p % 64
        if d >= 48:
            continue
        within = d % 24
        j = within % 12
        inv = 10000.0 ** (-j / 12.0)
        pos = row if d < 24 else col
        cosT[p] = np.cos(pos * inv)
        sinT[p] = np.sin(pos * inv) * (-1.0 if within < 12 else 1.0)
    _CONSTS["c_cos"] = cosT
    _CONSTS["c_sin"] = sinT
    s_ = np.arange(128)
    gm = np.zeros((128, 2, 4, 128), np.float32)
    gm[:, 0, :, :] = (s_[:, None] <= s_[None, :])[:, None, :]
    gm[:, 1, :, :] = (s_[:, None] >= s_[None, :])[:, None, :]
    _CONSTS["c_gmask"] = gm
    ms = np.ones((128, 256), np.float32)
    ms[:, 0] = 0.0
    ms[:, 128] = 0.0
    _CONSTS["c_mscan"] = ms
    return _CONSTS


def _host_prep(inputs):
    f = lambda a: np.ascontiguousarray(np.asarray(a, dtype=np.float32))
    x, c, ctx, c_ctx = f(inputs["x"]), f(inputs["c"]), f(inputs["ctx"]), f(inputs["c_ctx"])
    shared = {}
    shared["ada_w"] = f(f(inputs["ada_w"]).reshape(2, 8, 128, 6144).transpose(0, 2, 1, 3))
    shared["ada_b"] = f(f(inputs["ada_b"]).reshape(2, 48, 128).transpose(0, 2, 1))
    shared["norm_mix"] = f(f(inputs["norm_mix"]).reshape(2, 8, 128).transpose(0, 2, 1))
    shared["norm_ffn"] = f(f(inputs["norm_ffn"]).reshape(2, 8, 128).transpose(0, 2, 1))
    shared["ident"] = np.eye(128, dtype=np.float32)
    shared["ffn_w1"] = f(f(inputs["ffn_w1"]).reshape(2, 8, 128, DFF).transpose(0, 2, 1, 3))
    shared["ffn_w3"] = f(f(inputs["ffn_w3"]).reshape(2, 8, 128, DFF).transpose(0, 2, 1, 3))
    shared["ffn_w2"] = f(f(inputs["ffn_w2"]).reshape(2, 22, 128, D).transpose(0, 2, 1, 3))
    w_in = f(inputs["w_in"])
    shared["w_in"] = f(w_in.reshape(2, 8, 128, NIN).transpose(0, 2, 1, 3))
    wg = np.zeros((2, D, 1024), np.float32)
    dd = np.arange(48)
    partner = np.where((dd % 24) < 12, dd + 12, dd - 12)
    for hh in range(4):
        wg[:, :, 64 * hh:64 * hh + 48] = w_in[:, :, 1408 + 48 * hh + dd]
        wg[:, :, 256 + 64 * hh:256 + 64 * hh + 48] = w_in[:, :, 1408 + 48 * hh + partner]
        wg[:, :, 512 + 64 * hh:512 + 64 * hh + 48] = w_in[:, :, 1600 + 48 * hh + dd]
        wg[:, :, 768 + 64 * hh:768 + 64 * hh + 48] = w_in[:, :, 1600 + 48 * hh + partner]
    shared["w_g"] = f(wg.reshape(2, 8, 128, 1024).transpose(0, 2, 1, 3))
    shared["fnet_w"] = f(f(inputs["fnet_w"]).reshape(2, 2, 128, 256).transpose(0, 2, 1, 3))
    w_out = f(inputs["w_out"])
    shared["w_out"] = f(w_out.reshape(2, 8, 128, D).transpose(0, 2, 1, 3))
    shared["w_out_g"] = f(w_out[:, 640:, :].reshape(2, 4, 96, D).transpose(0, 2, 1, 3))
    shared["na_q"] = f(np.tile(f(inputs["na_q_norm"]), (1, 2))[:, :, None])
    shared["na_k"] = f(np.tile(f(inputs["na_k_norm"]), (1, 2))[:, :, None])
    rpb = f(inputs["na_rpb"])
    pk = np.arange(128)
    a_, kc_ = pk // 64, pk % 64
    b_, c_ = pk // 64, pk % 64
    dcm = np.clip(kc_[:, None] - c_[None, :], -15, 15) + 15
    cq0 = np.clip(c_ - 8, 0, 48)
    col_ok = (kc_[:, None] >= cq0[None, :]) & (kc_[:, None] < cq0[None, :] + 16)
    bt = np.full((2, 128, 54, 128), NEG, np.float32)
    for v in range(9):
        dl = (v - 3) if v < 7 else (-2 if v == 7 else 2)
        drm = 2 * dl + a_[:, None] - b_[None, :] + 7
        ok = col_ok & (drm >= 0) & (drm <= 14)
        if v == 7:
            ok = ok & ((2 * dl + a_[:, None]) >= (-4 + b_[None, :]))
        if v == 8:
            ok = ok & ((2 * dl + a_[:, None]) <= (3 + b_[None, :]))
        drc = np.clip(drm, 0, 14)
        for hh in range(6):
            vals = rpb[:, hh][:, drc, dcm]
            bt[:, :, hh * 9 + v, :] = np.where(ok[None], vals, NEG)
    shared["na_bt"] = bt
    aw = np.zeros((2, 32, 2, 256), np.float32)
    ab = np.zeros((2, 128, 2, 2), np.float32)
    gaw = f(inputs["gla_alpha_w"]); gab = f(inputs["gla_alpha_b"])
    for dr_ in range(2):
        for hh in range(4):
            aw[:, 16 * dr_:16 * dr_ + 16, dr_, 64 * hh:64 * hh + 48] = gaw[:, dr_, :, 48 * hh:48 * hh + 48]
            ab[:, 64 * (hh % 2):64 * (hh % 2) + 48, dr_, hh // 2] = gab[:, dr_, 48 * hh:48 * hh + 48]
    shared["gla_aw"] = aw
    shared["gla_ab"] = ab
    shared["gla_gon"] = f(f(inputs["gla_o_norm"])[:, :, None])
    shared.update(_consts())
    per_core = []
    for b in range(8):
        m = dict(shared)
        m["x"] = x[b]
        m["ctx"] = ctx[b]
        ccv = np.stack([c[b], c_ctx], axis=-1)
        m["cc"] = f(ccv.reshape(8, 128, 2).transpose(1, 0, 2))
        per_core.append(m)
    return per_core


_DEBUG = tuple(x for x in os.environ.get("KDEBUG", "").split(",") if x)
LAST = {}


def kernel(**inputs):
    in_maps = _host_prep(inputs)
    K = build_program(debug=_DEBUG)
    ncores = int(os.environ.get("KCORES", "8"))
    res = run_bass_kernel_spmd(K.nc, in_maps[:ncores], core_ids=list(range(ncores)))
    LAST["res"] = res
    out = np.stack([np.asarray(r["out"]) for r in res.results], axis=0).astype(np.float32)
    return out
```

```python
import contextlib
import os
import numpy as np
import ml_dtypes
import concourse.bass as bass
import concourse.mybir as mybir
from concourse.bass_utils import run_bass_kernel_spmd

F32 = mybir.dt.float32
BF16 = mybir.dt.bfloat16
AF = mybir.ActivationFunctionType
ALU = mybir.AluOpType

D = 1024
T = 2048
NCTX = 256
NT = T + NCTX
DFF = 2816
NIN = 2592
EPS = 1e-6
NEG = -30000.0
SAME_ENGINE_RAW = True


class Tok:
    __slots__ = ("w", "r", "dsem", "name")

    def __init__(self, name=""):
        self.w = []
        self.r = {}
        self.dsem = None
        self.name = name


class Src:
    def __init__(self, name, sem, unit, h=None):
        self.name = name
        self.sem = sem
        self.unit = unit
        self.h = h
        self.count = 0
        self.seen = {}


class Ctx:
    def __init__(self):
        self.nc = bass.Bass("TRN2", target_bir_lowering=False)
        nc = self.nc
        self.es = contextlib.ExitStack()
        self.engs = {}
        for nm, h in (("pe", nc.tensor), ("act", nc.scalar), ("dve", nc.vector),
                      ("pool", nc.gpsimd), ("sp", nc.sync)):
            sem = self.es.enter_context(nc.semaphore("sem_" + nm))
            self.engs[nm] = Src(nm, sem, 1, h)
        self.dsems = []
        self.outs = {}
        self.n_inst = 0

    def dram(self, name, shape, dt, kind="ExternalInput"):
        return self.nc.dram_tensor(name, list(shape), dt, kind=kind).ap()

    def sb(self, stack, name, shape, dt):
        self.n_sb = getattr(self, "n_sb", 0) + 1
        return stack.enter_context(self.nc.sbuf_tensor("sb%d_%s" % (self.n_sb, name), list(shape), dt))

    def new_dsem(self, name):
        sem = self.es.enter_context(self.nc.semaphore("d_" + name + str(len(self.dsems))))
        s = Src("dma_" + name, sem, 16)
        self.dsems.append(s)
        return s

    def _wait_deps(self, E, reads, writes):
        deps = {}
        for t in reads:
            for (s, n) in t.w:
                deps[s] = max(deps.get(s, 0), n)
        for t in writes:
            for (s, n) in t.w:
                deps[s] = max(deps.get(s, 0), n)
            for s, n in t.r.items():
                deps[s] = max(deps.get(s, 0), n)
        for s, n in deps.items():
            if s is E and (E.name == "pe" or not SAME_ENGINE_RAW):
                continue
            if E.seen.get(s, 0) < n:
                E.h.wait_ge(s.sem, n * s.unit)
                E.seen[s] = n

    def op(self, eng, reads, writes, emit):
        E = self.engs[eng]
        self._wait_deps(E, reads, writes)
        ins = emit(E.h)
        E.count += 1
        ins.then_inc(E.sem, 1)
        self.n_inst += 1
        for t in reads:
            t.r[E] = E.count
        for t in writes:
            t.w = [(E, E.count)]
            t.r = {}
        return ins

    def mm(self, reads, writes, out, pairs, first=True, last=True):
        E = self.engs["pe"]
        self._wait_deps(E, reads, writes)
        n = len(pairs)
        ins = None
        for i, (l, r) in enumerate(pairs):
            ins = E.h.matmul(out, l, r, start=(first and i == 0), stop=(last and i == n - 1))
            self.n_inst += 1
        E.count += 1
        ins.then_inc(E.sem, 1)
        for t in reads:
            t.r[E] = E.count
        for t in writes:
            t.w = [(E, E.count)]
            t.r = {}

    def transpose(self, reads, writes, out, in_, ident):
        return self.op("pe", reads, writes, lambda h: h.transpose(out, in_, ident))

    def dma(self, q, out, in_, reads, writes, name="x"):
        E = self.engs[q]
        self._wait_deps(E, reads, writes)
        wt = writes[0]
        if wt.dsem is None:
            wt.dsem = {}
        if q not in wt.dsem:
            wt.dsem[q] = self.new_dsem(name + q)
        S = wt.dsem[q]
        ins = E.h.dma_start(out=out, in_=in_)
        S.count += 1
        ins.then_inc(S.sem, 16)
        self.n_inst += 1
        for t in reads:
            t.r[S] = S.count
        for t in writes:
            t.w = [(s_, n_) for (s_, n_) in t.w if (s_.unit == 16 and s_ is not S)] + [(S, S.count)]
            t.r = {}

    def barrier(self):
        allsrc = list(self.engs.values()) + self.dsems
        for E in self.engs.values():
            for s in allsrc:
                if s is E or s.count == 0:
                    continue
                if E.seen.get(s, 0) < s.count:
                    E.h.wait_ge(s.sem, s.count * s.unit)
                    E.seen[s] = s.count


def build_program(debug=()):
    K = Ctx()
    nc = K.nc
    es = K.es
    dbg = {}

    x_d = K.dram("x", [T, D], F32)
    ctx_d = K.dram("ctx", [NCTX, D], F32)
    cc_d = K.dram("cc", [128, 8, 2], F32)
    adaw_d = K.dram("ada_w", [2, 128, 8, 6144], F32)
    adab_d = K.dram("ada_b", [2, 128, 48], F32)
    nmix_d = K.dram("norm_mix", [2, 128, 8], F32)
    nffn_d = K.dram("norm_ffn", [2, 128, 8], F32)
    out_d = K.dram("out", [T, D], F32, kind="ExternalOutput")
    ident_d = K.dram("ident", [128, 128], F32)
    w1_d = K.dram("ffn_w1", [2, 128, 8, DFF], F32)
    win_d = K.dram("w_in", [2, 128, 8, NIN], F32)
    wg_d = K.dram("w_g", [2, 128, 8, 1024], F32)
    fnetw_d = K.dram("fnet_w", [2, 128, 2, 256], F32)
    wout_d = K.dram("w_out", [2, 128, 8, D], F32)
    woutg_d = K.dram("w_out_g", [2, 96, 4, D], F32)
    naq_d = K.dram("na_q", [2, 128, 1], F32)
    nak_d = K.dram("na_k", [2, 128, 1], F32)
    bt_d = K.dram("na_bt", [2, 128, 54, 128], F32)
    aw_d = K.dram("gla_aw", [2, 32, 2, 256], F32)
    ab_d = K.dram("gla_ab", [2, 128, 2, 2], F32)
    gon_d = K.dram("gla_gon", [2, 96, 1], F32)
    ct_d = K.dram("c_ct", [128, 16, T], BF16)
    st_d = K.dram("c_st", [128, 16, T], BF16)
    c256_d = K.dram("c_c256", [128, 2, 256], BF16)
    s256_d = K.dram("c_s256", [128, 2, 256], BF16)
    c64_d = K.dram("c_c64", [128, 4, 128], BF16)
    cos_d = K.dram("c_cos", [128, T], F32)
    sin_d = K.dram("c_sin", [128, T], F32)
    gmask_d = K.dram("c_gmask", [128, 2, 2, 128], BF16)
    mscan_d = K.dram("c_mscan", [128, 256], BF16)
    w3_d = K.dram("ffn_w3", [2, 128, 8, DFF], F32)
    w2_d = K.dram("ffn_w2", [2, 128, 22, D], F32)

    top = es
    xT = K.sb(top, "xT", [128, 8, NT], F32)
    xT_t = [Tok("xT%d" % i) for i in range(5)]
    identf = K.sb(top, "identf", [128, 128], F32)
    identb = K.sb(top, "identb", [128, 128], BF16)
    onesb = K.sb(top, "onesb", [128, 128], BF16)
    cc = K.sb(top, "cc", [128, 8, 2], F32)
    scb = K.sb(top, "scb", [128, 8, 2], BF16)
    modT = K.sb(top, "modT", [128, 48, 2], F32)
    adab = K.sb(top, "adab", [128, 48], F32)
    nmix = K.sb(top, "nmix", [128, 8], F32)
    nffn = K.sb(top, "nffn", [128, 8], F32)
    amix = K.sb(top, "amix", [128, 8, 2], F32)
    affn = K.sb(top, "affn", [128, 8, 2], F32)
    t_const = Tok("const")
    t_mod = Tok("mod")
    t_small = Tok("small")

    ps = [es.enter_context(nc.psum_tensor("ps%d" % i, [128, 512], F32)) for i in range(7)]
    pt = [Tok("ps%d" % i) for i in range(7)]
    psb = es.enter_context(nc.psum_tensor("psb", [128, 1024], BF16))
    ptb = Tok("psb")

    def blk_cols(b):
        return (b * 512, 512) if b < 4 else (T, NCTX)

    K.dma("sp", identf[:], ident_d, [], [t_const], "c")
    K.op("dve", [t_const], [t_const], lambda h: h.tensor_copy(out=identb[:], in_=identf[:]))
    K.op("dve", [], [t_const], lambda h: h.memset(onesb[:], 1.0))
    K.dma("sp", cc[:], cc_d, [], [t_small], "s")
    K.op("act", [t_small], [t_small], lambda h: h.activation(out=scb[:], in_=cc[:], func=AF.Silu))

    with contextlib.ExitStack() as ph:
        xin = [K.sb(ph, "xin%d" % i, [128, D], F32) for i in range(3)]
        xin_t = [Tok("xin%d" % i) for i in range(3)]
        for ti in range(18):
            s = ti % 3
            src = x_d[ti * 128:(ti + 1) * 128, :] if ti < 16 else ctx_d[(ti - 16) * 128:(ti - 15) * 128, :]
            K.dma("sp", xin[s][:], src, [], [xin_t[s]], "xin")
            b = min(ti // 4, 4)
            for half in range(2):
                pb = (ti * 2 + half) % 7
                for j in range(4):
                    kc = half * 4 + j
                    K.op("pe", [xin_t[s], t_const], [pt[pb]],
                         lambda h, kc=kc, j=j, pb=pb, s=s: h.transpose(
                             ps[pb][:, j * 128:(j + 1) * 128], xin[s][:, kc * 128:(kc + 1) * 128], identf[:]))
                eng = "dve" if half == 0 else "act"
                dst = xT[:, half * 4:(half + 1) * 4, ti * 128:(ti + 1) * 128]
                srcp = ps[pb][:, :].rearrange("p (a b) -> p a b", a=4)
                if eng == "dve":
                    K.op("dve", [pt[pb]], [xT_t[b]], lambda h, dst=dst, srcp=srcp: h.tensor_copy(out=dst, in_=srcp))
                else:
                    K.op("act", [pt[pb]], [xT_t[b]], lambda h, dst=dst, srcp=srcp: h.activation(out=dst, in_=srcp, func=AF.Copy))
        K.barrier()


    rot = {"ps": 0}

    def norm_mod(l, which, hT, hT_t, blocks):
        a = amix if which == "mix" else affn
        sec = 0 if which == "mix" else 3
        with contextlib.ExitStack() as ph:
            sq = [K.sb(ph, "nm_sq%d" % i, [128, 8, 512], BF16) for i in range(2)]
            sq_t = [Tok() for i in range(2)]
            rt = [K.sb(ph, "nm_rt%d" % i, [128, 512], F32) for i in range(2)]
            rs = [K.sb(ph, "nm_rs%d" % i, [128, 512], F32) for i in range(2)]
            rs_t = [Tok() for i in range(2)]
            tmp = [K.sb(ph, "nm_tmp%d" % i, [128, 512], F32) for i in range(3)]
            tmp_t = [Tok() for i in range(3)]
            epsb = K.sb(ph, "nm_eps", [128, 1], F32)
            eps_t = Tok()
            K.op("dve", [], [eps_t], lambda h: h.memset(epsb[:], EPS))
            ti = 0
            for bi, b in enumerate(blocks):
                c0, n = blk_cols(b)
                r = 0 if b < 4 else 1
                s = bi % 2
                pb = bi % 2
                K.op("act", [xT_t[b]], [sq_t[s]], lambda h, s=s, c0=c0, n=n: h.activation(
                    out=sq[s][:, :, :n], in_=xT[:, :, c0:c0 + n], func=AF.Square))
                K.mm([sq_t[s], t_const], [pt[pb]], ps[pb][:, :n],
                     [(onesb[:], sq[s][:, kc, :n]) for kc in range(8)])
                K.op("act", [pt[pb], eps_t], [rs_t[s]], lambda h, s=s, pb=pb, n=n: h.activation(
                    out=rt[s][:, :n], in_=ps[pb][:, :n], func=AF.Ln, bias=epsb[:], scale=1.0 / D))
                K.op("act", [rs_t[s]], [rs_t[s]], lambda h, s=s, n=n: h.activation(
                    out=rs[s][:, :n], in_=rt[s][:, :n], func=AF.Exp, scale=-0.5))
                for kc in range(8):
                    u = ti % 3
                    ti += 1
                    K.op("dve", [xT_t[b], rs_t[s], t_mod], [tmp_t[u]], lambda h, u=u, kc=kc, c0=c0, n=n, r=r, s=s: h.scalar_tensor_tensor(
                        out=tmp[u][:, :n], in0=xT[:, kc, c0:c0 + n], scalar=a[:, kc, r:r + 1], in1=rs[s][:, :n],
                        op0=ALU.mult, op1=ALU.mult))
                    K.op("act", [tmp_t[u], t_mod], [hT_t[b]], lambda h, u=u, kc=kc, c0=c0, n=n, r=r: h.activation(
                        out=hT[:, kc, c0:c0 + n], in_=tmp[u][:, :n], func=AF.Identity,
                        bias=modT[:, sec * 8 + kc, r:r + 1], scale=1.0))
            K.barrier()

    def ffn(l, hT, hT_t, blocks):
        with contextlib.ExitStack() as ph:
            w1s = [K.sb(ph, "w1s%d" % i, [128, 8, 512], BF16) for i in range(2)]
            w3s = [K.sb(ph, "w3s%d" % i, [128, 8, 512], BF16) for i in range(2)]
            w2s = [K.sb(ph, "w2s%d" % i, [128, 4, D], BF16) for i in range(2)]
            w_t = [Tok() for i in range(2)]
            s1 = [K.sb(ph, "ff_s1%d" % i, [128, 512], F32) for i in range(2)]
            s1_t = [Tok() for i in range(2)]
            g = [K.sb(ph, "ff_g%d" % i, [128, 4, 512], BF16) for i in range(2)]
            g_t = [Tok() for i in range(2)]
            chunks = [(i * 512, 512) for i in range(5)] + [(2560, 256)]

            def load(ch):
                f0, nf = chunks[ch]
                s = ch % 2
                K.dma("pool", w1s[s][:, :, :nf], w1_d[l][:, :, f0:f0 + nf], [], [w_t[s]], "ffw")
                K.dma("pool", w3s[s][:, :, :nf], w3_d[l][:, :, f0:f0 + nf], [], [w_t[s]], "ffw")
                K.dma("pool", w2s[s][:, :nf // 128, :], w2_d[l][:, f0 // 128:(f0 + nf) // 128, :], [], [w_t[s]], "ffw")
            load(0)
            cnt = 0
            gi = 0
            oi = 0
            for ch in range(6):
                if ch + 1 < 6:
                    load(ch + 1)
                f0, nf = chunks[ch]
                s = ch % 2
                nft = nf // 128
                for b in blocks:
                    c0, n = blk_cols(b)
                    r = 0 if b < 4 else 1
                    gs = gi % 2
                    gi += 1
                    for ft in range(nft):
                        pa = cnt % 2
                        pbk = 2 + cnt % 2
                        u = cnt % 2
                        cnt += 1
                        K.mm([w_t[s], hT_t[b]], [pt[pa]], ps[pa][:, :n],
                             [(w1s[s][:, kc, ft * 128:(ft + 1) * 128], hT[:, kc, c0:c0 + n]) for kc in range(8)])
                        K.mm([w_t[s], hT_t[b]], [pt[pbk]], ps[pbk][:, :n],
                             [(w3s[s][:, kc, ft * 128:(ft + 1) * 128], hT[:, kc, c0:c0 + n]) for kc in range(8)])
                        K.op("act", [pt[pa]], [s1_t[u]], lambda h, u=u, pa=pa, n=n: h.activation(
                            out=s1[u][:, :n], in_=ps[pa][:, :n], func=AF.Silu))
                        K.op("dve", [s1_t[u], pt[pbk]], [g_t[gs]], lambda h, u=u, pbk=pbk, n=n, gs=gs, ft=ft: h.tensor_tensor(
                            out=g[gs][:, ft, :n], in0=s1[u][:, :n], in1=ps[pbk][:, :n], op=ALU.mult))
                    for dt in range(8):
                        pc = 4 + oi % 3
                        oi += 1
                        K.mm([w_t[s], g_t[gs]], [pt[pc]], ps[pc][:, :n],
                             [(w2s[s][:, ft, dt * 128:(dt + 1) * 128], g[gs][:, ft, :n]) for ft in range(nft)])
                        K.op("dve", [pt[pc], t_mod, xT_t[b]], [xT_t[b]], lambda h, pc=pc, dt=dt, c0=c0, n=n, r=r: h.scalar_tensor_tensor(
                            out=xT[:, dt, c0:c0 + n], in0=ps[pc][:, :n], scalar=modT[:, 40 + dt, r:r + 1],
                            in1=xT[:, dt, c0:c0 + n], op0=ALU.mult, op1=ALU.add))
            K.barrier()

    def lin_fm(wt, w, col0, M, hT, hT_t, b, c0, n, pb, wtok):
        K.mm([wtok, hT_t[b]], [pt[pb]], ps[pb][:M, :n],
             [(w[:, kc, col0:col0 + M], hT[:, kc, c0:c0 + n]) for kc in range(8)])

    def out_proj_update(pairs_fn, reads, b, c0, n, r, pbs):
        for dt in range(8):
            pc = pbs[dt % len(pbs)]
            K.mm(reads, [pt[pc]], ps[pc][:, :n], pairs_fn(dt))
            K.op("dve", [pt[pc], t_mod, xT_t[b]], [xT_t[b]], lambda h, pc=pc, dt=dt: h.scalar_tensor_tensor(
                out=xT[:, dt, c0:c0 + n], in0=ps[pc][:, :n], scalar=modT[:, 16 + dt, r:r + 1],
                in1=xT[:, dt, c0:c0 + n], op0=ALU.mult, op1=ALU.add))

    def rstd_from_ps(pb, P, n, inv, rt, rs, rs_tok, eps_ap, eps_t):
        K.op("act", [pt[pb], eps_t], [rs_tok], lambda h: h.activation(
            out=rt[:P, :n], in_=ps[pb][:P, :n], func=AF.Sqrt, bias=eps_ap[:P, :], scale=inv))
        K.op("dve", [rs_tok], [rs_tok], lambda h: h.reciprocal(out=rs[:P, :n], in_=rt[:P, :n]))

    def fnet(l, hT, hT_t, need_ctx):
        with contextlib.ExitStack() as ph:
            Z = K.sb(ph, "fn_Z", [128, 18, 256], BF16)
            Z_t = Tok()
            wfn = K.sb(ph, "fn_w", [128, 8, 256], BF16)
            fnw = K.sb(ph, "fn_fw", [128, 2, 256], BF16)
            wo = K.sb(ph, "fn_wo", [128, 2, D], BF16)
            c64 = K.sb(ph, "fn_c64", [128, 4, 128], BF16)
            w_t = Tok()
            K.dma("pool", wfn[:], win_d[l][:, :, 0:256], [], [w_t], "fnw")
            K.dma("pool", fnw[:], fnetw_d[l], [], [w_t], "fnw")
            K.dma("pool", wo[:], wout_d[l][:, 0:2, :], [], [w_t], "fnw")
            K.dma("sp", c64[:], c64_d, [], [w_t], "fnw")
            cts = [K.sb(ph, "fn_ct%d" % i, [128, 2, 16, 512], BF16) for i in range(2)]
            ct_t = [Tok() for i in range(2)]
            Psb = [K.sb(ph, "fn_P%d" % i, [128, 2, 512], BF16) for i in range(2)]
            P_t = [Tok() for i in range(2)]
            Ysb = K.sb(ph, "fn_Y", [128, 2, 512], BF16)
            Y_t = Tok()
            yf = K.sb(ph, "fn_yf", [128, 2, 512], BF16)
            yf_t = Tok()
            for ti in range(18):
                pb = ti % 2
                b = min(ti // 4, 4)
                K.mm([w_t, hT_t[b]], [pt[pb]], ps[pb][:, :256],
                     [(hT[:, kc, ti * 128:(ti + 1) * 128], wfn[:, kc, :]) for kc in range(8)])
                if ti % 2 == 0:
                    K.op("dve", [pt[pb]], [Z_t], lambda h, ti=ti, pb=pb: h.tensor_copy(out=Z[:, ti, :], in_=ps[pb][:, :256]))
                else:
                    K.op("act", [pt[pb]], [Z_t], lambda h, ti=ti, pb=pb: h.activation(out=Z[:, ti, :], in_=ps[pb][:, :256], func=AF.Copy))
            jobs = [(kb, 0) for kb in range(4)] + ([(0, 1)] if need_ctx else [])
            for ji, (kb, isctx) in enumerate(jobs):
                s = ji % 2
                if not isctx:
                    ntt, n, tt0, c0, b, r = 16, 512, 0, kb * 512, kb, 0
                    K.dma("sp", cts[s][:, 0, :, :], ct_d[:, :, kb * 512:(kb + 1) * 512], [], [ct_t[s]], "ct")
                    K.dma("sp", cts[s][:, 1, :, :], st_d[:, :, kb * 512:(kb + 1) * 512], [], [ct_t[s]], "ct")
                else:
                    ntt, n, tt0, c0, b, r = 2, 256, 16, T, 4, 1
                    K.dma("sp", cts[s][:, 0, 0:2, 0:256], c256_d, [], [ct_t[s]], "ct")
                    K.dma("sp", cts[s][:, 1, 0:2, 0:256], s256_d, [], [ct_t[s]], "ct")
                for chc in range(2):
                    u = chc
                    for cs in range(2):
                        pb = cs
                        K.mm([Z_t, ct_t[s]], [pt[pb]], ps[pb][:, :n],
                             [(Z[:, tt0 + tt, chc * 128:(chc + 1) * 128], cts[s][:, cs, tt, :n]) for tt in range(ntt)])
                        if cs == 0:
                            K.op("dve", [pt[pb]], [P_t[u]], lambda h, u=u, pb=pb, cs=cs: h.tensor_copy(out=Psb[u][:, cs, :n], in_=ps[pb][:, :n]))
                        else:
                            K.op("act", [pt[pb]], [P_t[u]], lambda h, u=u, pb=pb, cs=cs: h.activation(out=Psb[u][:, cs, :n], in_=ps[pb][:, :n], func=AF.Copy))
                    pb = 2 + chc
                    K.mm([P_t[u], w_t], [pt[pb]], ps[pb][:, :n],
                         [(c64[:, 2 * isctx + 0, :], Psb[u][:, 0, :n]), (c64[:, 2 * isctx + 1, :], Psb[u][:, 1, :n])])
                    K.op("dve", [pt[pb]], [Y_t], lambda h, pb=pb, chc=chc: h.tensor_copy(out=Ysb[:, chc, :n], in_=ps[pb][:, :n]))
                for c2 in range(2):
                    pb = 4 + c2
                    K.mm([Y_t, w_t], [pt[pb]], ps[pb][:, :n],
                         [(fnw[:, c1, c2 * 128:(c2 + 1) * 128], Ysb[:, c1, :n]) for c1 in range(2)])
                    K.op("act", [pt[pb]], [yf_t], lambda h, pb=pb, c2=c2: h.activation(out=yf[:, c2, :n], in_=ps[pb][:, :n], func=AF.Copy))
                out_proj_update(lambda dt: [(wo[:, c2, dt * 128:(dt + 1) * 128], yf[:, c2, :n]) for c2 in range(2)],
                                [yf_t, w_t], b, c0, n, r, [0, 1, 2, 3, 4, 5, 6])
            K.barrier()

    def natt(l, hT, hT_t, need_ctx):
        with contextlib.ExitStack() as ph:
            kT = K.sb(ph, "na_kT", [128, 3, NT], BF16)
            kT_t = Tok()
            V = K.sb(ph, "na_V", [128, 18, 6, 65], BF16)
            V_t = Tok()
            gq = K.sb(ph, "na_gq", [128, 1], F32)
            gk = K.sb(ph, "na_gk", [128, 1], F32)
            epsb = K.sb(ph, "na_eps", [128, 1], F32)
            bones = K.sb(ph, "na_bones", [128, 128], BF16)
            g_t = Tok()
            K.dma("sp", gq[:], naq_d[l], [], [g_t], "nag")
            K.dma("sp", gk[:], nak_d[l], [], [g_t], "nag")
            K.op("dve", [g_t], [g_t], lambda h: h.tensor_scalar(out=gq[:], in0=gq[:], scalar1=0.125, scalar2=None, op0=ALU.mult))
            K.op("dve", [], [g_t], lambda h: h.memset(epsb[:], EPS))
            K.op("dve", [], [g_t], lambda h: h.memset(bones[:], 0.0))
            K.op("dve", [g_t], [g_t], lambda h: h.memset(bones[0:64, 0:64], 1.0))
            K.op("dve", [g_t], [g_t], lambda h: h.memset(bones[64:128, 64:128], 1.0))
            K.op("dve", [], [V_t], lambda h: h.memset(V[:, :, :, 64:65], 1.0))
            sq = [K.sb(ph, "na_sq%d" % i, [128, 512], BF16) for i in range(2)]
            sq_t = [Tok() for i in range(2)]
            rtl = [K.sb(ph, "na_rt%d" % i, [128, 512], F32) for i in range(2)]
            rs = [K.sb(ph, "na_rs%d" % i, [128, 512], F32) for i in range(2)]
            rs_t = [Tok() for i in range(2)]

            def qk_norm(w, wtok, gain, dst, dst_t, dcol, b, c0, n, cnt):
                for hp in range(3):
                    u = (cnt[0]) % 2
                    cnt[0] += 1
                    pa, pbk = u, 2 + u
                    lin_fm(None, w, hp * 128, 128, hT, hT_t, b, c0, n, pa, wtok)
                    K.op("act", [pt[pa]], [sq_t[u]], lambda h, u=u, pa=pa: h.activation(out=sq[u][:, :n], in_=ps[pa][:, :n], func=AF.Square))
                    K.mm([sq_t[u], g_t], [pt[pbk]], ps[pbk][:, :n], [(bones[:], sq[u][:, :n])])
                    K.op("act", [pt[pbk], g_t], [rs_t[u]], lambda h, u=u, pbk=pbk: h.activation(
                        out=rtl[u][:, :n], in_=ps[pbk][:, :n], func=AF.Ln, bias=epsb[:], scale=1.0 / 64))
                    K.op("act", [rs_t[u]], [rs_t[u]], lambda h, u=u: h.activation(
                        out=rs[u][:, :n], in_=rtl[u][:, :n], func=AF.Exp, scale=-0.5))
                    K.op("dve", [pt[pa], rs_t[u], g_t], [dst_t], lambda h, u=u, pa=pa, hp=hp: h.scalar_tensor_tensor(
                        out=dst[:, hp, dcol:dcol + n], in0=ps[pa][:, :n], scalar=gain[:, 0:1], in1=rs[u][:, :n],
                        op0=ALU.mult, op1=ALU.mult))

            cnt = [0]
            with contextlib.ExitStack() as p1:
                wk = K.sb(p1, "na_wk", [128, 8, 384], BF16)
                wv = K.sb(p1, "na_wv", [128, 8, 384], BF16)
                wk_t = Tok()
                K.dma("pool", wk[:], win_d[l][:, :, 640:1024], [], [wk_t], "naw")
                K.dma("pool", wv[:], win_d[l][:, :, 1024:1408], [], [wk_t], "naw")
                for b in range(5):
                    c0, n = blk_cols(b)
                    qk_norm(wk, wk_t, gk, kT, kT_t, c0, b, c0, n, cnt)
                for ti in range(18):
                    pb = 4 + ti % 2
                    b = min(ti // 4, 4)
                    K.mm([wk_t, hT_t[b]], [pt[pb]], ps[pb][:, :384],
                         [(hT[:, kc, ti * 128:(ti + 1) * 128], wv[:, kc, :]) for kc in range(8)])
                    K.op("act" if ti % 2 else "dve", [pt[pb]], [V_t],
                         (lambda h, ti=ti, pb=pb: h.activation(out=V[:, ti, :, 0:64], in_=ps[pb][:, :384].rearrange("p (a b) -> p a b", b=64), func=AF.Copy))
                         if ti % 2 else
                         (lambda h, ti=ti, pb=pb: h.tensor_copy(out=V[:, ti, :, 0:64], in_=ps[pb][:, :384].rearrange("p (a b) -> p a b", b=64))))
                K.barrier()
            wq = K.sb(ph, "na_wq", [128, 8, 384], BF16)
            wo = K.sb(ph, "na_wo", [128, 3, D], BF16)
            BT = K.sb(ph, "na_BT", [128, 54, 128], BF16)
            w_t = Tok()
            K.dma("pool", wq[:], win_d[l][:, :, 256:640], [], [w_t], "naw2")
            K.dma("pool", wo[:], wout_d[l][:, 2:5, :], [], [w_t], "naw2")
            K.dma("pool", BT[:, 0:27, :], bt_d[l][:, 0:27, :], [], [w_t], "naw2")
            K.dma("pool", BT[:, 27:54, :], bt_d[l][:, 27:54, :], [], [w_t], "naw2")
            qT = K.sb(ph, "na_qT", [128, 3, 512], BF16)
            qT_t = Tok()
            PT = [K.sb(ph, "na_PT%d" % i, [128, 7, 128], BF16) for i in range(2)]
            PT_t = [Tok() for i in range(2)]
            rec = [K.sb(ph, "na_rec%d" % i, [128, 6], F32) for i in range(2)]
            rec_t = [Tok(), Tok()]
            Osb = [K.sb(ph, "na_O%d" % i, [128, 6, 64], BF16) for i in range(2)]
            O_t = [Tok(), Tok()]
            yna = K.sb(ph, "na_y", [128, 3, 512], BF16)
            yna_t = Tok()
            hcount = 0
            for b in (range(5) if need_ctx else range(4)):
                c0, n = blk_cols(b)
                r = 0 if b < 4 else 1
                qk_norm(wq, w_t, gq, qT, qT_t, 0, b, c0, n, cnt)
                tile_chunks = []
                for qi in range(n // 128):
                    if b < 4:
                        i = b * 4 + qi
                        if i <= 1:
                            js = list(range(0, 4))
                        elif i >= 14:
                            js = list(range(12, 16))
                        else:
                            js = list(range(i - 2, i + 3))
                        chunks = []
                        for j in js:
                            dl = j - i
                            if 2 <= i <= 13 and abs(dl) == 2:
                                v = 7 if dl < 0 else 8
                            else:
                                v = dl + 3
                            chunks.append((j, v))
                        chunks += [(16, None), (17, None)]
                    else:
                        chunks = [(16, None), (17, None)]
                    tile_chunks.append(chunks)
                items = [(qi, hd) for qi in range(n // 128) for hd in range(6)]

                def stage1(k):
                    qi, hd = items[k]
                    chunks = tile_chunks[qi]
                    nch = len(chunks)
                    hp, r0 = hd // 2, (hd % 2) * 64
                    u = (hbase + k) % 2
                    pS = (0 + 2 * u, 1 + 2 * u)
                    for c, (kt, v) in enumerate(chunks):
                        pb = pS[c // 4]
                        pairs = [(kT[r0:r0 + 64, hp, kt * 128:(kt + 1) * 128], qT[r0:r0 + 64, hp, qi * 128:(qi + 1) * 128])]
                        if v is not None:
                            pairs.append((identb[:], BT[:, hd * 9 + v, :]))
                        K.mm([kT_t, qT_t, w_t, t_const], [pt[pb]], ps[pb][:, (c % 4) * 128:(c % 4 + 1) * 128], pairs)
                    n0 = min(nch, 4)
                    K.op("act", [pt[pS[0]]], [PT_t[u]], lambda h: h.activation(
                        out=PT[u][:, 0:n0, :], in_=ps[pS[0]][:, 0:n0 * 128].rearrange("p (a b) -> p a b", b=128), func=AF.Exp))
                    if nch > 4:
                        n1 = nch - 4
                        K.op("act", [pt[pS[1]]], [PT_t[u]], lambda h: h.activation(
                            out=PT[u][:, 4:4 + n1, :], in_=ps[pS[1]][:, 0:n1 * 128].rearrange("p (a b) -> p a b", b=128), func=AF.Exp))

                def stage2(k):
                    qi, hd = items[k]
                    chunks = tile_chunks[qi]
                    u = (hbase + k) % 2
                    po = 4 + qi % 2
                    K.mm([PT_t[u], V_t], [pt[po]], ps[po][:, hd * 65:(hd + 1) * 65],
                         [(PT[u][:, c, :], V[:, kt, hd, :]) for c, (kt, v) in enumerate(chunks)])

                def fin_dve(qi):
                    po = 4 + qi % 2
                    oi = qi % 2
                    ov = ps[po][:, 0:390].rearrange("p (a b) -> p a b", b=65)
                    K.op("dve", [pt[po]], [rec_t[oi]], lambda h: h.reciprocal(out=rec[oi][:], in_=ov[:, :, 64]))
                    K.op("dve", [pt[po], rec_t[oi]], [O_t[oi]], lambda h: h.tensor_tensor(
                        out=Osb[oi][:], in0=ov[:, :, 0:64], in1=rec[oi][:].unsqueeze(2).to_broadcast([128, 6, 64]), op=ALU.mult))

                def fin_pe(qi):
                    oi = qi % 2
                    Of = Osb[oi][:].rearrange("p a b -> p (a b)")
                    for hp in range(3):
                        K.op("pe", [O_t[oi], t_const], [ptb], lambda h, hp=hp: h.transpose(
                            psb[:, hp * 128:(hp + 1) * 128], Of[:, hp * 128:(hp + 1) * 128], identb[:]))
                    K.op("act", [ptb], [yna_t], lambda h: h.activation(
                        out=yna[:, :, qi * 128:(qi + 1) * 128], in_=psb[:, 0:384].rearrange("p (a b) -> p a b", b=128), func=AF.Copy))

                hbase = hcount
                nit = len(items)
                for k in range(nit + 2):
                    if k < nit:
                        stage1(k)
                    if 1 <= k <= nit:
                        stage2(k - 1)
                        if items[k - 1][1] == 5:
                            fin_dve(items[k - 1][0])
                    if 2 <= k <= nit + 1 and items[k - 2][1] == 5:
                        fin_pe(items[k - 2][0])
                hcount += nit
                out_proj_update(lambda dt: [(wo[:, hp, dt * 128:(dt + 1) * 128], yna[:, hp, :n]) for hp in range(3)],
                                [yna_t, w_t], b, c0, n, r, [6])
            K.barrier()

    def gla(l, hT, hT_t, need_ctx):
        NB = 256
        QS = 48 ** -0.5
        with contextlib.ExitStack() as ph:
            qr = K.sb(ph, "gl_qr", [128, 2, NT], BF16)
            kr = K.sb(ph, "gl_kr", [128, 2, NT], BF16)
            aT = K.sb(ph, "gl_aT", [32, NT], BF16)
            st_t = [Tok() for i in range(5)]
            aw = K.sb(ph, "gl_aw", [32, 2, 256], BF16)
            nab = K.sb(ph, "gl_nab", [128, 2, 2], F32)
            gon = K.sb(ph, "gl_gon", [96, 1], F32)
            msk = K.sb(ph, "gl_msk", [128, 2, 2, 128], BF16)
            mscan = K.sb(ph, "gl_mscan", [128, NB], BF16)
            epsb = K.sb(ph, "gl_eps", [128, 1], F32)
            oneb = K.sb(ph, "gl_one", [128, 1], F32)
            w_t = Tok()
            K.dma("pool", aw[:], aw_d[l], [], [w_t], "glw")
            K.dma("sp", nab[:], ab_d[l], [], [w_t], "glw")
            K.dma("sp", gon[:], gon_d[l], [], [w_t], "glw")
            K.dma("sp", msk[:], gmask_d, [], [w_t], "glw")
            K.dma("sp", mscan[:], mscan_d, [], [w_t], "glw")
            K.op("dve", [w_t], [w_t], lambda h: h.tensor_scalar(out=nab[:], in0=nab[:], scalar1=-1.0, scalar2=None, op0=ALU.mult))
            K.op("dve", [], [w_t], lambda h: h.memset(epsb[:], EPS))
            K.op("dve", [], [w_t], lambda h: h.memset(oneb[:], 1.0))
            with contextlib.ExitStack() as p0:
                wg = K.sb(p0, "gl_wg", [128, 8, 1024], BF16)
                wga = K.sb(p0, "gl_wga", [128, 8, 32], BF16)
                w0_t = Tok()
                K.dma("pool", wg[:], wg_d[l], [], [w0_t], "glw0")
                K.dma("pool", wga[:], win_d[l][:, :, 2560:2592], [], [w0_t], "glw0")
                cosb = K.sb(p0, "gl_cos", [128, 512], F32)
                sinb = K.sb(p0, "gl_sin", [128, 512], F32)
                cs_t = Tok()
                t1 = [K.sb(p0, "gl_t1%d" % i, [128, 512], F32) for i in range(2)]
                t2 = [K.sb(p0, "gl_t2%d" % i, [128, 512], F32) for i in range(2)]
                r_t = [Tok(), Tok()]
                cnt = 0
                for b5 in range(5):
                    c0, n = blk_cols(b5)
                    lat = b5 < 4
                    if lat:
                        K.dma("sp", cosb[:, :n], cos_d[:, c0:c0 + n], [], [cs_t], "cs")
                        K.dma("sp", sinb[:, :n], sin_d[:, c0:c0 + n], [], [cs_t], "cs")
                    lin_fm(None, wga, 0, 32, hT, hT_t, b5, c0, n, 6, w0_t)
                    K.op("act", [pt[6]], [st_t[b5]], lambda h: h.activation(out=aT[:, c0:c0 + n], in_=ps[6][:32, :n], func=AF.Copy))
                    for hp in range(2):
                        for which in range(2):
                            base = which * 512
                            dst = qr if which == 0 else kr
                            u = cnt % 2
                            cnt += 1
                            pa, pbk = (0, 1) if u == 0 else (2, 3)
                            lin_fm(None, wg, base + hp * 128, 128, hT, hT_t, b5, c0, n, pa, w0_t)
                            if lat:
                                lin_fm(None, wg, base + 256 + hp * 128, 128, hT, hT_t, b5, c0, n, pbk, w0_t)
                                K.op("dve", [pt[pa], cs_t], [r_t[u]], lambda h: h.tensor_tensor(out=t1[u][:, :n], in0=ps[pa][:, :n], in1=cosb[:, :n], op=ALU.mult))
                                K.op("dve", [pt[pbk], cs_t, r_t[u]], [r_t[u]], lambda h: h.tensor_tensor(out=t2[u][:, :n], in0=ps[pbk][:, :n], in1=sinb[:, :n], op=ALU.mult))
                                K.op("pool", [r_t[u]], [st_t[b5]], lambda h: h.tensor_tensor(out=dst[:, hp, c0:c0 + n], in0=t1[u][:, :n], in1=t2[u][:, :n], op=ALU.add))
                            else:
                                K.op("act", [pt[pa]], [st_t[b5]], lambda h: h.activation(out=dst[:, hp, c0:c0 + n], in_=ps[pa][:, :n], func=AF.Copy))
                K.barrier()
            wgv = K.sb(ph, "gl_wgv", [128, 8, 384], BF16)
            wgg = K.sb(ph, "gl_wgg", [128, 8, 384], BF16)
            wo = K.sb(ph, "gl_wo", [96, 4, D], BF16)
            w2_t = Tok()
            K.dma("pool", wgv[:], win_d[l][:, :, 1792:2176], [], [w2_t], "glw2")
            K.dma("pool", wgg[:], win_d[l][:, :, 2176:2560], [], [w2_t], "glw2")
            K.dma("pool", wo[:], woutg_d[l], [], [w2_t], "glw2")
            oF = K.sb(ph, "gl_oF", [96, 4, NT], BF16)
            oF_t = [Tok() for i in range(9)]
            sg = K.sb(ph, "gl_sg", [96, 4, NB], BF16)
            sg_t = Tok()
            yg = sg
            yg_t = sg_t
            sq = K.sb(ph, "gl_sq", [96, NB], BF16)
            rt = K.sb(ph, "gl_rt", [96, NB], F32)
            rs = rt
            ty = K.sb(ph, "gl_ty", [96, NB], F32)
            n_t = Tok()
            R = []
            for d in range(2):
                r_ = dict(
                    e1=K.sb(ph, "gl_e1_%d" % d, [128, NB], F32),
                    bpos=K.sb(ph, "gl_bp_%d" % d, [128, NB], F32), eb=K.sb(ph, "gl_eb_%d" % d, [128, NB], F32),
                    qin=K.sb(ph, "gl_qin_%d" % d, [128, 2, NB], BF16), kin=K.sb(ph, "gl_kin_%d" % d, [128, 2, NB], BF16),
                    ktok=K.sb(ph, "gl_ktok_%d" % d, [128, 2, 2, 128], BF16), gv=K.sb(ph, "gl_gv_%d" % d, [128, 2, 384], BF16),
                    dec=K.sb(ph, "gl_dec_%d" % d, [128, 2, 2], F32), AM=K.sb(ph, "gl_AM_%d" % d, [128, 4, 128], BF16),
                    S=K.sb(ph, "gl_S_%d" % d, [128, 2, 192], F32), Sb=K.sb(ph, "gl_Sb_%d" % d, [128, 2, 192], BF16),
                    osum=K.sb(ph, "gl_osum_%d" % d, [96, 4, NB], F32),
                    g_t=Tok(), qk_t=Tok(), ktok_t=Tok(), gv_t=Tok(), dec_t=Tok(), AM_t=Tok(), S_t=Tok(), Sb_t=Tok(), os_t=Tok(),
                    BA=3 * d, BB=3 * d + 1, BO=3 * d + 2)
                R.append(r_)

            def block_prep(d, b):
                r_ = R[d]
                c0 = b * NB if b < 8 else T
                n = NB
                b5 = min(c0 // 512, 4)
                BB = r_["BB"]
                e1, bpos, eb = r_["e1"], r_["bpos"], r_["eb"]
                lsb = e1
                enb = eb
                gt = r_["g_t"]
                for hp in range(2):
                    K.mm([st_t[b5], w_t], [pt[BB]], ps[BB][:, :n], [(aw[:, d, hp * 128:(hp + 1) * 128], aT[:, c0:c0 + n])])
                    yield
                    K.op("act", [pt[BB], w_t], [gt], lambda h: h.activation(
                        out=e1[:, :n], in_=ps[BB][:, :n], func=AF.Exp, bias=nab[:, d, hp:hp + 1], scale=-1.0))
                    yield
                    K.op("act", [gt, w_t], [gt], lambda h: h.activation(out=lsb[:, :n], in_=e1[:, :n], func=AF.Ln, bias=oneb[:], scale=1.0))
                    yield
                    K.op("dve", [gt, w_t], [gt], lambda h: h.tensor_tensor_scan(
                        out=bpos[:, :n], data0=mscan[:, :n], data1=lsb[:, :n], initial=0.0, op0=ALU.mult, op1=ALU.add))
                    yield
                    bsel = bpos
                    if d == 1:
                        K.op("dve", [gt], [gt], lambda h: h.tensor_tensor(out=e1[:, :n], in0=lsb[:, :n], in1=bpos[:, :n], op=ALU.subtract))
                        yield
                        K.op("dve", [gt], [gt], lambda h: h.tensor_tensor(
                            out=e1[:, :n].rearrange("p (a b) -> p a b", b=128), in0=e1[:, :n].rearrange("p (a b) -> p a b", b=128),
                            in1=bpos[:, :n].rearrange("p (a b) -> p a b", b=128)[:, :, 127:128].to_broadcast([128, n // 128, 128]), op=ALU.add))
                        yield
                        bsel = e1
                    K.op("act", [gt], [gt], lambda h, bsel=bsel: h.activation(out=eb[:, :n], in_=bsel[:, :n], func=AF.Exp, scale=-1.0 / 16))
                    yield
                    sel = 127 if d == 0 else 0
                    K.op("dve", [gt], [r_["dec_t"]], lambda h: h.tensor_copy(
                        out=r_["dec"][:, hp, :], in_=eb[:, :n].rearrange("p (a b) -> p a b", b=128)[:, :, sel]))
                    yield
                    K.op("dve", [gt, st_t[b5]], [r_["qk_t"]], lambda h: h.scalar_tensor_tensor(
                        out=r_["qin"][:, hp, :n], in0=qr[:, hp, c0:c0 + n], scalar=QS, in1=eb[:, :n], op0=ALU.mult, op1=ALU.mult))
                    yield
                    K.op("act", [gt, r_["dec_t"], r_["qk_t"]], [gt], lambda h, bsel=bsel: h.activation(out=enb[:, :n], in_=bsel[:, :n], func=AF.Exp, scale=1.0 / 16))
                    yield
                    K.op("dve", [gt, st_t[b5]], [r_["qk_t"]], lambda h: h.tensor_tensor(
                        out=r_["kin"][:, hp, :n], in0=kr[:, hp, c0:c0 + n], in1=enb[:, :n], op=ALU.mult))
                    yield
                    for tt in range(2):
                        K.op("pe", [r_["qk_t"], t_const], [ptb], lambda h, tt=tt: h.transpose(
                            psb[:, (d * 2 + tt) * 128:(d * 2 + tt + 1) * 128], r_["kin"][:, hp, tt * 128:(tt + 1) * 128], identb[:]))
                        yield
                    K.op("act", [ptb], [r_["ktok_t"]], lambda h: h.activation(
                        out=r_["ktok"][:, :, hp, :], in_=psb[:, d * 256:(d + 1) * 256].rearrange("p (t f) -> p t f", t=2), func=AF.Copy))
                    yield
                for tt in range(2):
                    K.mm([w2_t, hT_t[b5]], [pt[BB]], ps[BB][:, :384],
                         [(hT[:, kc, c0 + tt * 128:c0 + (tt + 1) * 128], wgv[:, kc, :]) for kc in range(8)])
                    yield
                    K.op("act", [pt[BB]], [r_["gv_t"]], lambda h, tt=tt: h.activation(out=r_["gv"][:, tt, :], in_=ps[BB][:, :384], func=AF.Copy))
                    yield

            def scan_tile(d, b, tt, want_o, second):
                r_ = R[d]
                c0 = b * NB if b < 8 else T
                tc0 = tt * 128
                BA, BB, BO = r_["BA"], r_["BB"], r_["BO"]
                qin, kin, gv, Sb, S, AM, dec, ktok = r_["qin"], r_["kin"], r_["gv"], r_["Sb"], r_["S"], r_["AM"], r_["dec"], r_["ktok"]
                if want_o:
                    for hd in (0, 2, 1, 3):
                        hp, r0 = hd // 2, (hd % 2) * 64
                        pa = BA if hd % 2 == 0 else BB
                        K.mm([r_["qk_t"]], [pt[pa]], ps[pa][:, (hd // 2) * 128:(hd // 2 + 1) * 128],
                             [(kin[r0:r0 + 48, hp, tc0:tc0 + 128], qin[r0:r0 + 48, hp, tc0:tc0 + 128])])
                        yield
                    for par, pa in ((0, BA), (1, BB)):
                        K.op("dve", [pt[pa], w_t], [r_["AM_t"]], lambda h, par=par, pa=pa: h.tensor_tensor(
                            out=AM[:, par:4:2, :], in0=ps[pa][:, 0:256].rearrange("p (a b) -> p a b", b=128),
                            in1=msk[:, d, :, :], op=ALU.mult))
                        yield
                    for hd in range(4):
                        hp, r0 = hd // 2, (hd % 2) * 64
                        K.mm([r_["AM_t"], r_["gv_t"], r_["Sb_t"], r_["qk_t"]], [pt[BO]], ps[BO][:96, hd * 128:(hd + 1) * 128],
                             [(gv[:, tt, hd * 96:(hd + 1) * 96], AM[:, hd, :]),
                              (Sb[r0:r0 + 48, hp, (hd % 2) * 96:(hd % 2 + 1) * 96], qin[r0:r0 + 48, hp, tc0:tc0 + 128])])
                        yield
                    ov = ps[BO][:96, :].rearrange("p (a b) -> p a b", b=128)
                    if not second:
                        K.op("act", [pt[BO]], [oF_t[b]], lambda h: h.activation(
                            out=oF[:, :, c0 + tc0:c0 + tc0 + 128], in_=ov, func=AF.Copy))
                    else:
                        K.op("dve", [pt[BO], oF_t[b]], [r_["os_t"]], lambda h: h.tensor_tensor(
                            out=r_["osum"][:, :, tc0:tc0 + 128], in0=ov, in1=oF[:, :, c0 + tc0:c0 + tc0 + 128], op=ALU.add))
                    yield
                for hp in range(2):
                    K.mm([r_["ktok_t"], r_["gv_t"]], [pt[BA]], ps[BA][:, hp * 192:(hp + 1) * 192],
                         [(ktok[:, tt, hp, :], gv[:, tt, hp * 192:(hp + 1) * 192])])
                    yield
                for hp in range(2):
                    K.op("act", [r_["S_t"], r_["dec_t"]], [r_["S_t"]], lambda h, hp=hp: h.activation(
                        out=S[:, hp, :], in_=S[:, hp, :], func=AF.Identity, scale=dec[:, hp, tt:tt + 1]))
                    yield
                    K.op("dve", [pt[BA], r_["S_t"], r_["dec_t"]], [r_["S_t"]], lambda h, hp=hp: h.scalar_tensor_tensor(
                        out=S[:, hp, :], in0=ps[BA][:, hp * 192:(hp + 1) * 192], scalar=dec[:, hp, tt:tt + 1], in1=S[:, hp, :],
                        op0=ALU.mult, op1=ALU.add))
                    yield
                K.op("act", [r_["S_t"]], [r_["Sb_t"]], lambda h: h.activation(out=Sb[:], in_=S[:], func=AF.Copy))
                yield

            def finalize(d, b):
                r_ = R[d]
                if os.environ.get("KTRACE"):
                    print("finalize", d, b, "n_inst", K.n_inst)
                c0 = b * NB if b < 8 else T
                n = NB
                b5 = min(c0 // 512, 4)
                r = 0 if b < 8 else 1
                osum = r_["osum"]
                for hd in range(4):
                    lin_fm(None, wgg, hd * 96, 96, hT, hT_t, b5, c0, n, 6, w2_t)
                    K.op("act", [pt[6]], [sg_t], lambda h, hd=hd: h.activation(out=sg[:, hd, :n], in_=ps[6][:96, :n], func=AF.Silu))
                for hd in range(4):
                    K.op("act", [r_["os_t"]], [n_t], lambda h, hd=hd: h.activation(out=sq[:, :n], in_=osum[:, hd, :n], func=AF.Square))
                    K.mm([n_t, t_const], [pt[6]], ps[6][:96, :n], [(onesb[:96, :96], sq[:, :n])])
                    K.op("act", [pt[6], w_t], [n_t], lambda h: h.activation(
                        out=rt[:, :n], in_=ps[6][:96, :n], func=AF.Ln, bias=epsb[:96, :], scale=1.0 / 96))
                    K.op("act", [n_t], [n_t], lambda h: h.activation(out=rs[:, :n], in_=rt[:, :n], func=AF.Exp, scale=-0.5))
                    K.op("dve", [r_["os_t"], n_t, w_t], [n_t], lambda h, hd=hd: h.scalar_tensor_tensor(
                        out=ty[:, :n], in0=osum[:, hd, :n], scalar=gon[:, 0:1], in1=rs[:, :n], op0=ALU.mult, op1=ALU.mult))
                    K.op("dve", [n_t, sg_t], [yg_t], lambda h, hd=hd: h.tensor_tensor(
                        out=yg[:, hd, :n], in0=ty[:, :n], in1=sg[:, hd, :n], op=ALU.mult))
                for dt in range(8):
                    K.mm([yg_t, w2_t], [pt[6]], ps[6][:, :n], [(wo[:, hd, dt * 128:(dt + 1) * 128], yg[:, hd, :n]) for hd in range(4)])
                    K.op("dve", [pt[6], t_mod, xT_t[b5]], [xT_t[b5]], lambda h, dt=dt: h.scalar_tensor_tensor(
                        out=xT[:, dt, c0:c0 + n], in0=ps[6][:, :n], scalar=modT[:, 16 + dt, r:r + 1],
                        in1=xT[:, dt, c0:c0 + n], op0=ALU.mult, op1=ALU.add))

            def dir_gen(d):
                r_ = R[d]
                K.op("dve", [], [r_["S_t"]], lambda h: h.memset(r_["S"][:], 0.0))
                K.op("dve", [], [r_["Sb_t"]], lambda h: h.memset(r_["Sb"][:], 0.0))
                order = [8] + (list(range(8)) if d == 0 else list(range(7, -1, -1)))
                for b in order:
                    want_o = (b < 8) or need_ctx
                    if b == 8:
                        second = (d == 1)
                    else:
                        second = (b >= 4) if d == 0 else (b <= 3)
                    yield from block_prep(d, b)
                    for tt in ((0, 1) if d == 0 else (1, 0)):
                        yield from scan_tile(d, b, tt, want_o, second)
                    if want_o and second:
                        finalize(d, b)
                        yield
                    yield "STEP"

            def run_step(gens):
                active = list(gens)
                while active:
                    for g in list(active):
                        if next(g) == "STEP":
                            active.remove(g)
            gF, gB = dir_gen(0), dir_gen(1)
            run_step([gF])
            run_step([gB])
            for s_ in range(8):
                run_step([gF, gB])
            K.barrier()

    def mixers(l, hT, hT_t, need_ctx):
        if "nofn" not in debug:
            fnet(l, hT, hT_t, need_ctx)
        if "nona" not in debug:
            natt(l, hT, hT_t, need_ctx)
        if "nogla" not in debug:
            gla(l, hT, hT_t, need_ctx)

    for l in range(2):
        need_ctx = (l == 0)
        with contextlib.ExitStack() as ph:
            K.dma("sp", adab[:], adab_d[l], [], [t_small], "s")
            K.dma("sp", nmix[:], nmix_d[l], [], [t_small], "s")
            K.dma("sp", nffn[:], nffn_d[l], [], [t_small], "s")
            wsl = [K.sb(ph, "adaw%d" % i, [128, 8, 1024], BF16) for i in range(2)]
            wsl_t = [Tok("adaw%d" % i) for i in range(2)]
            for sec in range(6):
                s = sec % 2
                K.dma("pool", wsl[s][:], adaw_d[l][:, :, sec * 1024:(sec + 1) * 1024], [], [wsl_t[s]], "adaw")
                pb = sec % 2
                for mt in range(8):
                    j = sec * 8 + mt
                    K.mm([wsl_t[s], t_small], [pt[pb]], ps[pb][:, mt * 2:mt * 2 + 2],
                         [(wsl[s][:, kc, mt * 128:(mt + 1) * 128], scb[:, kc, :]) for kc in range(8)])
                K.op("dve", [pt[pb], t_small], [t_mod],
                     lambda h, sec=sec, pb=pb: h.tensor_tensor(
                         out=modT[:, sec * 8:(sec + 1) * 8, :],
                         in0=ps[pb][:, 0:16].rearrange("p (a b) -> p a b", b=2),
                         in1=adab[:, sec * 8:(sec + 1) * 8].unsqueeze(2).to_broadcast([128, 8, 2]),
                         op=ALU.add))
            for (dst, gain, sec) in ((amix, nmix, 1), (affn, nffn, 4)):
                K.op("dve", [t_mod, t_small], [t_mod],
                     lambda h, dst=dst, gain=gain, sec=sec: h.scalar_tensor_tensor(
                         out=dst[:], in0=modT[:, sec * 8:(sec + 1) * 8, :], scalar=1.0,
                         in1=gain[:].unsqueeze(2).to_broadcast([128, 8, 2]),
                         op0=ALU.add, op1=ALU.mult))
            K.barrier()
        if "mod" in debug and l == 0:
            dbg["modT"] = (modT, [128, 48, 2], F32)
            break
        with contextlib.ExitStack() as lay:
            hT = K.sb(lay, "hT", [128, 8, NT], BF16)
            hT_t = [Tok("hT%d" % i) for i in range(5)]
            if "ffn" not in debug:
                norm_mod(l, "mix", hT, hT_t, range(5))
                mixers(l, hT, hT_t, need_ctx)
            blocks = range(5) if need_ctx else range(4)
            if "noffn" not in debug:
                norm_mod(l, "ffn", hT, hT_t, blocks)
                ffn(l, hT, hT_t, blocks)
            K.barrier()
        if "ffn" in debug or "l0" in debug:
            break

    t_out = Tok("out")
    for name, (buf, shape, dt) in dbg.items():
        dd = K.dram("dbg_" + name, shape, dt, kind="ExternalOutput")
        K.barrier()
        K.dma("sp", dd, buf[:], [], [t_out], "out")
        K.outs["dbg_" + name] = shape

    with contextlib.ExitStack() as ph:
        xo = [K.sb(ph, "xo%d" % i, [128, D], F32) for i in range(3)]
        xo_t = [Tok("xo%d" % i) for i in range(3)]
        for ti in range(16):
            s = ti % 3
            b = ti // 4
            for half in range(2):
                pb = (ti * 2 + half) % 7
                for j in range(4):
                    kc = half * 4 + j
                    K.op("pe", [xT_t[b], t_const], [pt[pb]],
                         lambda h, kc=kc, j=j, pb=pb, ti=ti: h.transpose(
                             ps[pb][:, j * 128:(j + 1) * 128], xT[:, kc, ti * 128:(ti + 1) * 128], identf[:]))
                dst = xo[s][:, half * 512:(half + 1) * 512]
                if half == 0:
                    K.op("dve", [pt[pb]], [xo_t[s]], lambda h, dst=dst, pb=pb: h.tensor_copy(out=dst, in_=ps[pb][:, :]))
                else:
                    K.op("act", [pt[pb]], [xo_t[s]], lambda h, dst=dst, pb=pb: h.activation(out=dst, in_=ps[pb][:, :], func=AF.Copy))
            K.dma("sp", out_d[ti * 128:(ti + 1) * 128, :], xo[s][:], [xo_t[s]], [t_out], "out")
        K.barrier()
    S = t_out.dsem["sp"]
    nc.sync.wait_ge(S.sem, S.count * 16)
    es.close()
    return K


_CONSTS = {}


def _consts():
    if _CONSTS:
        return _CONSTS
    bf = ml_dtypes.bfloat16
    t = np.arange(T, dtype=np.int64)
    ang = 2.0 * np.pi * ((t[:, None] * t[None, :]) % T).astype(np.float64) / T
    _CONSTS["c_ct"] = np.ascontiguousarray(np.cos(ang).reshape(16, 128, T).transpose(1, 0, 2)).astype(bf)
    _CONSTS["c_st"] = np.ascontiguousarray(np.sin(ang).reshape(16, 128, T).transpose(1, 0, 2)).astype(bf)
    t2 = np.arange(NCTX, dtype=np.int64)
    ang2 = 2.0 * np.pi * ((t2[:, None] * t2[None, :]) % NCTX).astype(np.float64) / NCTX
    _CONSTS["c_c256"] = np.ascontiguousarray(np.cos(ang2).reshape(2, 128, NCTX).transpose(1, 0, 2)).astype(bf)
    _CONSTS["c_s256"] = np.ascontiguousarray(np.sin(ang2).reshape(2, 128, NCTX).transpose(1, 0, 2)).astype(bf)
    g = np.arange(64)
    a64 = 2.0 * np.pi * ((g[:, None] * g[None, :]) % 64) / 64.0
    c64 = np.zeros((128, 4, 128))
    for blk in range(2):
        sl = slice(64 * blk, 64 * blk + 64)
        c64[sl, 0, sl] = np.cos(a64) / np.sqrt(T * 64.0)
        c64[sl, 1, sl] = -np.sin(a64) / np.sqrt(T * 64.0)
        c64[sl, 2, sl] = np.cos(a64) / np.sqrt(NCTX * 64.0)
        c64[sl, 3, sl] = -np.sin(a64) / np.sqrt(NCTX * 64.0)
    _CONSTS["c_c64"] = c64.astype(bf)
    cosT = np.zeros((128, T), np.float32)
    sinT = np.zeros((128, T), np.float32)
    row = (t // 64).astype(np.float64)
    col = (t % 64).astype(np.float64)
    for p in range(128):
        d = p % 64
        if d >= 48:
            continue
        within = d % 24
        j = within % 12
        inv = 10000.0 ** (-j / 12.0)
        pos = row if d < 24 else col
        cosT[p] = np.cos(pos * inv)
        sinT[p] = np.sin(pos * inv) * (-1.0 if within < 12 else 1.0)
    _CONSTS["c_cos"] = cosT
    _CONSTS["c_sin"] = sinT
    s_ = np.arange(128)
    gm = np.zeros((128, 2, 2, 128), np.float32)
    gm[:, 0, :, :] = (s_[:, None] <= s_[None, :])[:, None, :]
    gm[:, 1, :, :] = (s_[:, None] >= s_[None, :])[:, None, :]
    _CONSTS["c_gmask"] = gm.astype(bf)
    ms = np.ones((128, 256), np.float32)
    ms[:, 0] = 0.0
    ms[:, 128] = 0.0
    _CONSTS["c_mscan"] = ms.astype(bf)
    return _CONSTS


def _host_prep(inputs):
    f = lambda a: np.ascontiguousarray(np.asarray(a, dtype=np.float32))
    x, c, ctx, c_ctx = f(inputs["x"]), f(inputs["c"]), f(inputs["ctx"]), f(inputs["c_ctx"])
    shared = {}
    shared["ada_w"] = f(f(inputs["ada_w"]).reshape(2, 8, 128, 6144).transpose(0, 2, 1, 3))
    shared["ada_b"] = f(f(inputs["ada_b"]).reshape(2, 48, 128).transpose(0, 2, 1))
    shared["norm_mix"] = f(f(inputs["norm_mix"]).reshape(2, 8, 128).transpose(0, 2, 1))
    shared["norm_ffn"] = f(f(inputs["norm_ffn"]).reshape(2, 8, 128).transpose(0, 2, 1))
    shared["ident"] = np.eye(128, dtype=np.float32)
    shared["ffn_w1"] = f(f(inputs["ffn_w1"]).reshape(2, 8, 128, DFF).transpose(0, 2, 1, 3))
    shared["ffn_w3"] = f(f(inputs["ffn_w3"]).reshape(2, 8, 128, DFF).transpose(0, 2, 1, 3))
    shared["ffn_w2"] = f(f(inputs["ffn_w2"]).reshape(2, 22, 128, D).transpose(0, 2, 1, 3))
    w_in = f(inputs["w_in"])
    shared["w_in"] = f(w_in.reshape(2, 8, 128, NIN).transpose(0, 2, 1, 3))
    wg = np.zeros((2, D, 1024), np.float32)
    dd = np.arange(48)
    partner = np.where((dd % 24) < 12, dd + 12, dd - 12)
    for hh in range(4):
        wg[:, :, 64 * hh:64 * hh + 48] = w_in[:, :, 1408 + 48 * hh + dd]
        wg[:, :, 256 + 64 * hh:256 + 64 * hh + 48] = w_in[:, :, 1408 + 48 * hh + partner]
        wg[:, :, 512 + 64 * hh:512 + 64 * hh + 48] = w_in[:, :, 1600 + 48 * hh + dd]
        wg[:, :, 768 + 64 * hh:768 + 64 * hh + 48] = w_in[:, :, 1600 + 48 * hh + partner]
    shared["w_g"] = f(wg.reshape(2, 8, 128, 1024).transpose(0, 2, 1, 3))
    shared["fnet_w"] = f(f(inputs["fnet_w"]).reshape(2, 2, 128, 256).transpose(0, 2, 1, 3))
    w_out = f(inputs["w_out"])
    shared["w_out"] = f(w_out.reshape(2, 8, 128, D).transpose(0, 2, 1, 3))
    shared["w_out_g"] = f(w_out[:, 640:, :].reshape(2, 4, 96, D).transpose(0, 2, 1, 3))
    shared["na_q"] = f(np.tile(f(inputs["na_q_norm"]), (1, 2))[:, :, None])
    shared["na_k"] = f(np.tile(f(inputs["na_k_norm"]), (1, 2))[:, :, None])
    rpb = f(inputs["na_rpb"])
    pk = np.arange(128)
    a_, kc_ = pk // 64, pk % 64
    b_, c_ = pk // 64, pk % 64
    dcm = np.clip(kc_[:, None] - c_[None, :], -15, 15) + 15
    cq0 = np.clip(c_ - 8, 0, 48)
    col_ok = (kc_[:, None] >= cq0[None, :]) & (kc_[:, None] < cq0[None, :] + 16)
    bt = np.full((2, 128, 54, 128), NEG, np.float32)
    for v in range(9):
        dl = (v - 3) if v < 7 else (-2 if v == 7 else 2)
        drm = 2 * dl + a_[:, None] - b_[None, :] + 7
        ok = col_ok & (drm >= 0) & (drm <= 14)
        if v == 7:
            ok = ok & ((2 * dl + a_[:, None]) >= (-4 + b_[None, :]))
        if v == 8:
            ok = ok & ((2 * dl + a_[:, None]) <= (3 + b_[None, :]))
        drc = np.clip(drm, 0, 14)
        for hh in range(6):
            vals = rpb[:, hh][:, drc, dcm]
            bt[:, :, hh * 9 + v, :] = np.where(ok[None], vals, NEG)
    shared["na_bt"] = bt
    aw = np.zeros((2, 32, 2, 256), np.float32)
    ab = np.zeros((2, 128, 2, 2), np.float32)
    gaw = f(inputs["gla_alpha_w"]); gab = f(inputs["gla_alpha_b"])
    for dr_ in range(2):
        for hh in range(4):
            aw[:, 16 * dr_:16 * dr_ + 16, dr_, 64 * hh:64 * hh + 48] = gaw[:, dr_, :, 48 * hh:48 * hh + 48]
            ab[:, 64 * (hh % 2):64 * (hh % 2) + 48, dr_, hh // 2] = gab[:, dr_, 48 * hh:48 * hh + 48]
    shared["gla_aw"] = aw
    shared["gla_ab"] = ab
    shared["gla_gon"] = f(f(inputs["gla_o_norm"])[:, :, None])
    shared.update(_consts())
    per_core = []
    for b in range(8):
        m = dict(shared)
        m["x"] = x[b]
        m["ctx"] = ctx[b]
        ccv = np.stack([c[b], c_ctx], axis=-1)
        m["cc"] = f(ccv.reshape(8, 128, 2).transpose(1, 0, 2))
        per_core.append(m)
    return per_core


_DEBUG = tuple(x for x in os.environ.get("KDEBUG", "").split(",") if x)
LAST = {}


def kernel(**inputs):
    in_maps = _host_prep(inputs)
    K = build_program(debug=_DEBUG)
    ncores = int(os.environ.get("KCORES", "8"))
    res = run_bass_kernel_spmd(K.nc, in_maps[:ncores], core_ids=list(range(ncores)))
    LAST["res"] = res
    out = np.stack([np.asarray(r["out"]) for r in res.results], axis=0).astype(np.float32)
    return out
```

```python
import contextlib
import os
import numpy as np
import ml_dtypes
import concourse.bass as bass
import concourse.mybir as mybir
from concourse.bass_utils import run_bass_kernel_spmd

F32 = mybir.dt.float32
BF16 = mybir.dt.bfloat16
AF = mybir.ActivationFunctionType
ALU = mybir.AluOpType

D = 1024
T = 2048
NCTX = 256
NT = T + NCTX
DFF = 2816
NIN = 2592
EPS = 1e-6
NEG = -30000.0
SAME_ENGINE_RAW = True


class Tok:
    __slots__ = ("w", "r", "dsem", "name")

    def __init__(self, name=""):
        self.w = []
        self.r = {}
        self.dsem = None
        self.name = name


class Src:
    def __init__(self, name, sem, unit, h=None):
        self.name = name
        self.sem = sem
        self.unit = unit
        self.h = h
        self.count = 0
        self.seen = {}


class Ctx:
    def __init__(self):
        self.nc = bass.Bass("TRN2", target_bir_lowering=False)
        nc = self.nc
        self.es = contextlib.ExitStack()
        self.engs = {}
        for nm, h in (("pe", nc.tensor), ("act", nc.scalar), ("dve", nc.vector),
                      ("pool", nc.gpsimd), ("sp", nc.sync)):
            sem = self.es.enter_context(nc.semaphore("sem_" + nm))
            self.engs[nm] = Src(nm, sem, 1, h)
        self.dsems = []
        self.outs = {}
        self.n_inst = 0

    def dram(self, name, shape, dt, kind="ExternalInput"):
        return self.nc.dram_tensor(name, list(shape), dt, kind=kind).ap()

    def sb(self, stack, name, shape, dt):
        self.n_sb = getattr(self, "n_sb", 0) + 1
        return stack.enter_context(self.nc.sbuf_tensor("sb%d_%s" % (self.n_sb, name), list(shape), dt))

    def new_dsem(self, name):
        sem = self.es.enter_context(self.nc.semaphore("d_" + name + str(len(self.dsems))))
        s = Src("dma_" + name, sem, 16)
        self.dsems.append(s)
        return s

    def _wait_deps(self, E, reads, writes):
        deps = {}
        for t in reads:
            for (s, n) in t.w:
                deps[s] = max(deps.get(s, 0), n)
        for t in writes:
            for (s, n) in t.w:
                deps[s] = max(deps.get(s, 0), n)
            for s, n in t.r.items():
                deps[s] = max(deps.get(s, 0), n)
        for s, n in deps.items():
            if s is E and (E.name == "pe" or not SAME_ENGINE_RAW):
                continue
            if E.seen.get(s, 0) < n:
                E.h.wait_ge(s.sem, n * s.unit)
                E.seen[s] = n

    def op(self, eng, reads, writes, emit):
        E = self.engs[eng]
        self._wait_deps(E, reads, writes)
        ins = emit(E.h)
        E.count += 1
        ins.then_inc(E.sem, 1)
        self.n_inst += 1
        for t in reads:
            t.r[E] = E.count
        for t in writes:
            t.w = [(E, E.count)]
            t.r = {}
        return ins

    def mm(self, reads, writes, out, pairs, first=True, last=True):
        E = self.engs["pe"]
        self._wait_deps(E, reads, writes)
        n = len(pairs)
        ins = None
        for i, (l, r) in enumerate(pairs):
            ins = E.h.matmul(out, l, r, start=(first and i == 0), stop=(last and i == n - 1))
            self.n_inst += 1
        E.count += 1
        ins.then_inc(E.sem, 1)
        for t in reads:
            t.r[E] = E.count
        for t in writes:
            t.w = [(E, E.count)]
            t.r = {}

    def transpose(self, reads, writes, out, in_, ident):
        return self.op("pe", reads, writes, lambda h: h.transpose(out, in_, ident))

    def dma(self, q, out, in_, reads, writes, name="x"):
        E = self.engs[q]
        self._wait_deps(E, reads, writes)
        wt = writes[0]
        if wt.dsem is None:
            wt.dsem = {}
        if q not in wt.dsem:
            wt.dsem[q] = self.new_dsem(name + q)
        S = wt.dsem[q]
        ins = E.h.dma_start(out=out, in_=in_)
        S.count += 1
        ins.then_inc(S.sem, 16)
        self.n_inst += 1
        for t in reads:
            t.r[S] = S.count
        for t in writes:
            t.w = [(s_, n_) for (s_, n_) in t.w if (s_.unit == 16 and s_ is not S)] + [(S, S.count)]
            t.r = {}

    def barrier(self):
        allsrc = list(self.engs.values()) + self.dsems
        for E in self.engs.values():
            for s in allsrc:
                if s is E or s.count == 0:
                    continue
                if E.seen.get(s, 0) < s.count:
                    E.h.wait_ge(s.sem, s.count * s.unit)
                    E.seen[s] = s.count


def build_program(debug=()):
    K = Ctx()
    nc = K.nc
    es = K.es
    dbg = {}

    x_d = K.dram("x", [T, D], F32)
    ctx_d = K.dram("ctx", [NCTX, D], F32)
    cc_d = K.dram("cc", [128, 8, 2], F32)
    adaw_d = K.dram("ada_w", [2, 128, 8, 6144], F32)
    adab_d = K.dram("ada_b", [2, 128, 48], F32)
    nmix_d = K.dram("norm_mix", [2, 128, 8], F32)
    nffn_d = K.dram("norm_ffn", [2, 128, 8], F32)
    out_d = K.dram("out", [T, D], F32, kind="ExternalOutput")
    ident_d = K.dram("ident", [128, 128], F32)
    w1_d = K.dram("ffn_w1", [2, 128, 8, DFF], F32)
    win_d = K.dram("w_in", [2, 128, 8, NIN], F32)
    wg_d = K.dram("w_g", [2, 128, 8, 1024], F32)
    fnetw_d = K.dram("fnet_w", [2, 128, 2, 256], F32)
    wout_d = K.dram("w_out", [2, 128, 8, D], F32)
    woutg_d = K.dram("w_out_g", [2, 96, 4, D], F32)
    naq_d = K.dram("na_q", [2, 128, 1], F32)
    nak_d = K.dram("na_k", [2, 128, 1], F32)
    bt_d = K.dram("na_bt", [2, 128, 72, 128], F32)
    aw_d = K.dram("gla_aw", [2, 32, 2, 256], F32)
    ab_d = K.dram("gla_ab", [2, 128, 2, 2], F32)
    gon_d = K.dram("gla_gon", [2, 96, 1], F32)
    ct_d = K.dram("c_ct", [128, 16, T], BF16)
    st_d = K.dram("c_st", [128, 16, T], BF16)
    c256_d = K.dram("c_c256", [128, 2, 256], BF16)
    s256_d = K.dram("c_s256", [128, 2, 256], BF16)
    c64_d = K.dram("c_c64", [128, 4, 128], BF16)
    cos_d = K.dram("c_cos", [128, T], F32)
    sin_d = K.dram("c_sin", [128, T], F32)
    gmask_d = K.dram("c_gmask", [128, 2, 2, 128], BF16)
    mscan_d = K.dram("c_mscan", [128, 256], BF16)
    w3_d = K.dram("ffn_w3", [2, 128, 8, DFF], F32)
    w2_d = K.dram("ffn_w2", [2, 128, 22, D], F32)

    top = es
    xT = K.sb(top, "xT", [128, 8, NT], F32)
    xT_t = [Tok("xT%d" % i) for i in range(5)]
    identf = K.sb(top, "identf", [128, 128], F32)
    identb = K.sb(top, "identb", [128, 128], BF16)
    onesb = K.sb(top, "onesb", [128, 128], BF16)
    cc = K.sb(top, "cc", [128, 8, 2], F32)
    scb = K.sb(top, "scb", [128, 8, 2], BF16)
    modT = K.sb(top, "modT", [128, 48, 2], F32)
    adab = K.sb(top, "adab", [128, 48], F32)
    nmix = K.sb(top, "nmix", [128, 8], F32)
    nffn = K.sb(top, "nffn", [128, 8], F32)
    amix = K.sb(top, "amix", [128, 8, 2], F32)
    affn = K.sb(top, "affn", [128, 8, 2], F32)
    t_const = Tok("const")
    t_mod = Tok("mod")
    t_small = Tok("small")

    ps = [es.enter_context(nc.psum_tensor("ps%d" % i, [128, 512], F32)) for i in range(7)]
    pt = [Tok("ps%d" % i) for i in range(7)]
    psb = es.enter_context(nc.psum_tensor("psb", [128, 1024], BF16))
    ptb = Tok("psb")

    def blk_cols(b):
        return (b * 512, 512) if b < 4 else (T, NCTX)

    K.dma("sp", identf[:], ident_d, [], [t_const], "c")
    K.op("dve", [t_const], [t_const], lambda h: h.tensor_copy(out=identb[:], in_=identf[:]))
    K.op("dve", [], [t_const], lambda h: h.memset(onesb[:], 1.0))
    K.dma("sp", cc[:], cc_d, [], [t_small], "s")
    K.op("act", [t_small], [t_small], lambda h: h.activation(out=scb[:], in_=cc[:], func=AF.Silu))

    with contextlib.ExitStack() as ph:
        xin = [K.sb(ph, "xin%d" % i, [128, D], F32) for i in range(3)]
        xin_t = [Tok("xin%d" % i) for i in range(3)]
        for ti in range(18):
            s = ti % 3
            src = x_d[ti * 128:(ti + 1) * 128, :] if ti < 16 else ctx_d[(ti - 16) * 128:(ti - 15) * 128, :]
            K.dma("sp", xin[s][:], src, [], [xin_t[s]], "xin")
            b = min(ti // 4, 4)
            for half in range(2):
                pb = (ti * 2 + half) % 7
                for j in range(4):
                    kc = half * 4 + j
                    K.op("pe", [xin_t[s], t_const], [pt[pb]],
                         lambda h, kc=kc, j=j, pb=pb, s=s: h.transpose(
                             ps[pb][:, j * 128:(j + 1) * 128], xin[s][:, kc * 128:(kc + 1) * 128], identf[:]))
                eng = "dve" if half == 0 else "act"
                dst = xT[:, half * 4:(half + 1) * 4, ti * 128:(ti + 1) * 128]
                srcp = ps[pb][:, :].rearrange("p (a b) -> p a b", a=4)
                if eng == "dve":
                    K.op("dve", [pt[pb]], [xT_t[b]], lambda h, dst=dst, srcp=srcp: h.tensor_copy(out=dst, in_=srcp))
                else:
                    K.op("act", [pt[pb]], [xT_t[b]], lambda h, dst=dst, srcp=srcp: h.activation(out=dst, in_=srcp, func=AF.Copy))
        K.barrier()


    rot = {"ps": 0}

    def norm_mod(l, which, hT, hT_t, blocks):
        a = amix if which == "mix" else affn
        sec = 0 if which == "mix" else 3
        with contextlib.ExitStack() as ph:
            sq = [K.sb(ph, "nm_sq%d" % i, [128, 8, 512], BF16) for i in range(2)]
            sq_t = [Tok() for i in range(2)]
            rt = [K.sb(ph, "nm_rt%d" % i, [128, 512], F32) for i in range(2)]
            rs = [K.sb(ph, "nm_rs%d" % i, [128, 512], F32) for i in range(2)]
            rs_t = [Tok() for i in range(2)]
            tmp = [K.sb(ph, "nm_tmp%d" % i, [128, 512], F32) for i in range(3)]
            tmp_t = [Tok() for i in range(3)]
            epsb = K.sb(ph, "nm_eps", [128, 1], F32)
            eps_t = Tok()
            K.op("dve", [], [eps_t], lambda h: h.memset(epsb[:], EPS))
            ti = 0
            for bi, b in enumerate(blocks):
                c0, n = blk_cols(b)
                r = 0 if b < 4 else 1
                s = bi % 2
                pb = bi % 2
                K.op("act", [xT_t[b]], [sq_t[s]], lambda h, s=s, c0=c0, n=n: h.activation(
                    out=sq[s][:, :, :n], in_=xT[:, :, c0:c0 + n], func=AF.Square))
                K.mm([sq_t[s], t_const], [pt[pb]], ps[pb][:, :n],
                     [(onesb[:], sq[s][:, kc, :n]) for kc in range(8)])
                K.op("act", [pt[pb], eps_t], [rs_t[s]], lambda h, s=s, pb=pb, n=n: h.activation(
                    out=rt[s][:, :n], in_=ps[pb][:, :n], func=AF.Ln, bias=epsb[:], scale=1.0 / D))
                K.op("act", [rs_t[s]], [rs_t[s]], lambda h, s=s, n=n: h.activation(
                    out=rs[s][:, :n], in_=rt[s][:, :n], func=AF.Exp, scale=-0.5))
                for kc in range(8):
                    u = ti % 3
                    ti += 1
                    K.op("dve", [xT_t[b], rs_t[s], t_mod], [tmp_t[u]], lambda h, u=u, kc=kc, c0=c0, n=n, r=r, s=s: h.scalar_tensor_tensor(
                        out=tmp[u][:, :n], in0=xT[:, kc, c0:c0 + n], scalar=a[:, kc, r:r + 1], in1=rs[s][:, :n],
                        op0=ALU.mult, op1=ALU.mult))
                    K.op("act", [tmp_t[u], t_mod], [hT_t[b]], lambda h, u=u, kc=kc, c0=c0, n=n, r=r: h.activation(
                        out=hT[:, kc, c0:c0 + n], in_=tmp[u][:, :n], func=AF.Identity,
                        bias=modT[:, sec * 8 + kc, r:r + 1], scale=1.0))
            K.barrier()

    def ffn(l, hT, hT_t, blocks):
        with contextlib.ExitStack() as ph:
            w1s = [K.sb(ph, "w1s%d" % i, [128, 8, 512], BF16) for i in range(2)]
            w3s = [K.sb(ph, "w3s%d" % i, [128, 8, 512], BF16) for i in range(2)]
            w2s = [K.sb(ph, "w2s%d" % i, [128, 4, D], BF16) for i in range(2)]
            w_t = [Tok() for i in range(2)]
            s1 = [K.sb(ph, "ff_s1%d" % i, [128, 512], F32) for i in range(2)]
            s1_t = [Tok() for i in range(2)]
            g = [K.sb(ph, "ff_g%d" % i, [128, 4, 512], BF16) for i in range(2)]
            g_t = [Tok() for i in range(2)]
            chunks = [(i * 512, 512) for i in range(5)] + [(2560, 256)]

            def load(ch):
                f0, nf = chunks[ch]
                s = ch % 2
                K.dma("pool", w1s[s][:, :, :nf], w1_d[l][:, :, f0:f0 + nf], [], [w_t[s]], "ffw")
                K.dma("pool", w3s[s][:, :, :nf], w3_d[l][:, :, f0:f0 + nf], [], [w_t[s]], "ffw")
                K.dma("pool", w2s[s][:, :nf // 128, :], w2_d[l][:, f0 // 128:(f0 + nf) // 128, :], [], [w_t[s]], "ffw")
            load(0)
            cnt = 0
            gi = 0
            oi = 0
            for ch in range(6):
                if ch + 1 < 6:
                    load(ch + 1)
                f0, nf = chunks[ch]
                s = ch % 2
                nft = nf // 128
                for b in blocks:
                    c0, n = blk_cols(b)
                    r = 0 if b < 4 else 1
                    gs = gi % 2
                    gi += 1
                    for ft in range(nft):
                        pa = cnt % 2
                        pbk = 2 + cnt % 2
                        u = cnt % 2
                        cnt += 1
                        K.mm([w_t[s], hT_t[b]], [pt[pa]], ps[pa][:, :n],
                             [(w1s[s][:, kc, ft * 128:(ft + 1) * 128], hT[:, kc, c0:c0 + n]) for kc in range(8)])
                        K.mm([w_t[s], hT_t[b]], [pt[pbk]], ps[pbk][:, :n],
                             [(w3s[s][:, kc, ft * 128:(ft + 1) * 128], hT[:, kc, c0:c0 + n]) for kc in range(8)])
                        K.op("act", [pt[pa]], [s1_t[u]], lambda h, u=u, pa=pa, n=n: h.activation(
                            out=s1[u][:, :n], in_=ps[pa][:, :n], func=AF.Silu))
                        K.op("dve", [s1_t[u], pt[pbk]], [g_t[gs]], lambda h, u=u, pbk=pbk, n=n, gs=gs, ft=ft: h.tensor_tensor(
                            out=g[gs][:, ft, :n], in0=s1[u][:, :n], in1=ps[pbk][:, :n], op=ALU.mult))
                    for dt in range(8):
                        pc = 4 + oi % 3
                        oi += 1
                        K.mm([w_t[s], g_t[gs]], [pt[pc]], ps[pc][:, :n],
                             [(w2s[s][:, ft, dt * 128:(dt + 1) * 128], g[gs][:, ft, :n]) for ft in range(nft)])
                        K.op("dve", [pt[pc], t_mod, xT_t[b]], [xT_t[b]], lambda h, pc=pc, dt=dt, c0=c0, n=n, r=r: h.scalar_tensor_tensor(
                            out=xT[:, dt, c0:c0 + n], in0=ps[pc][:, :n], scalar=modT[:, 40 + dt, r:r + 1],
                            in1=xT[:, dt, c0:c0 + n], op0=ALU.mult, op1=ALU.add))
            K.barrier()

    def lin_fm(wt, w, col0, M, hT, hT_t, b, c0, n, pb, wtok):
        K.mm([wtok, hT_t[b]], [pt[pb]], ps[pb][:M, :n],
             [(w[:, kc, col0:col0 + M], hT[:, kc, c0:c0 + n]) for kc in range(8)])

    def out_proj_update(pairs_fn, reads, b, c0, n, r, pbs):
        for dt in range(8):
            pc = pbs[dt % len(pbs)]
            K.mm(reads, [pt[pc]], ps[pc][:, :n], pairs_fn(dt))
            K.op("dve", [pt[pc], t_mod, xT_t[b]], [xT_t[b]], lambda h, pc=pc, dt=dt: h.scalar_tensor_tensor(
                out=xT[:, dt, c0:c0 + n], in0=ps[pc][:, :n], scalar=modT[:, 16 + dt, r:r + 1],
                in1=xT[:, dt, c0:c0 + n], op0=ALU.mult, op1=ALU.add))

    def rstd_from_ps(pb, P, n, inv, rt, rs, rs_tok, eps_ap, eps_t):
        K.op("act", [pt[pb], eps_t], [rs_tok], lambda h: h.activation(
            out=rt[:P, :n], in_=ps[pb][:P, :n], func=AF.Sqrt, bias=eps_ap[:P, :], scale=inv))
        K.op("dve", [rs_tok], [rs_tok], lambda h: h.reciprocal(out=rs[:P, :n], in_=rt[:P, :n]))

    def fnet(l, hT, hT_t, need_ctx):
        with contextlib.ExitStack() as ph:
            Z = K.sb(ph, "fn_Z", [128, 18, 256], BF16)
            Z_t = Tok()
            wfn = K.sb(ph, "fn_w", [128, 8, 256], BF16)
            fnw = K.sb(ph, "fn_fw", [128, 2, 256], BF16)
            wo = K.sb(ph, "fn_wo", [128, 2, D], BF16)
            c64 = K.sb(ph, "fn_c64", [128, 4, 128], BF16)
            w_t = Tok()
            K.dma("pool", wfn[:], win_d[l][:, :, 0:256], [], [w_t], "fnw")
            K.dma("pool", fnw[:], fnetw_d[l], [], [w_t], "fnw")
            K.dma("pool", wo[:], wout_d[l][:, 0:2, :], [], [w_t], "fnw")
            K.dma("sp", c64[:], c64_d, [], [w_t], "fnw")
            cts = [K.sb(ph, "fn_ct%d" % i, [128, 2, 16, 512], BF16) for i in range(2)]
            ct_t = [Tok() for i in range(2)]
            Psb = [K.sb(ph, "fn_P%d" % i, [128, 2, 512], BF16) for i in range(2)]
            P_t = [Tok() for i in range(2)]
            Ysb = K.sb(ph, "fn_Y", [128, 2, 512], BF16)
            Y_t = Tok()
            yf = K.sb(ph, "fn_yf", [128, 2, 512], BF16)
            yf_t = Tok()
            for ti in range(18):
                pb = ti % 2
                b = min(ti // 4, 4)
                K.mm([w_t, hT_t[b]], [pt[pb]], ps[pb][:, :256],
                     [(hT[:, kc, ti * 128:(ti + 1) * 128], wfn[:, kc, :]) for kc in range(8)])
                if ti % 2 == 0:
                    K.op("dve", [pt[pb]], [Z_t], lambda h, ti=ti, pb=pb: h.tensor_copy(out=Z[:, ti, :], in_=ps[pb][:, :256]))
                else:
                    K.op("act", [pt[pb]], [Z_t], lambda h, ti=ti, pb=pb: h.activation(out=Z[:, ti, :], in_=ps[pb][:, :256], func=AF.Copy))
            jobs = [(kb, 0) for kb in range(4)] + ([(0, 1)] if need_ctx else [])
            for ji, (kb, isctx) in enumerate(jobs):
                s = ji % 2
                if not isctx:
                    ntt, n, tt0, c0, b, r = 16, 512, 0, kb * 512, kb, 0
                    K.dma("sp", cts[s][:, 0, :, :], ct_d[:, :, kb * 512:(kb + 1) * 512], [], [ct_t[s]], "ct")
                    K.dma("sp", cts[s][:, 1, :, :], st_d[:, :, kb * 512:(kb + 1) * 512], [], [ct_t[s]], "ct")
                else:
                    ntt, n, tt0, c0, b, r = 2, 256, 16, T, 4, 1
                    K.dma("sp", cts[s][:, 0, 0:2, 0:256], c256_d, [], [ct_t[s]], "ct")
                    K.dma("sp", cts[s][:, 1, 0:2, 0:256], s256_d, [], [ct_t[s]], "ct")
                for chc in range(2):
                    u = chc
                    for cs in range(2):
                        pb = cs
                        K.mm([Z_t, ct_t[s]], [pt[pb]], ps[pb][:, :n],
                             [(Z[:, tt0 + tt, chc * 128:(chc + 1) * 128], cts[s][:, cs, tt, :n]) for tt in range(ntt)])
                        if cs == 0:
                            K.op("dve", [pt[pb]], [P_t[u]], lambda h, u=u, pb=pb, cs=cs: h.tensor_copy(out=Psb[u][:, cs, :n], in_=ps[pb][:, :n]))
                        else:
                            K.op("act", [pt[pb]], [P_t[u]], lambda h, u=u, pb=pb, cs=cs: h.activation(out=Psb[u][:, cs, :n], in_=ps[pb][:, :n], func=AF.Copy))
                    pb = 2 + chc
                    K.mm([P_t[u], w_t], [pt[pb]], ps[pb][:, :n],
                         [(c64[:, 2 * isctx + 0, :], Psb[u][:, 0, :n]), (c64[:, 2 * isctx + 1, :], Psb[u][:, 1, :n])])
                    K.op("dve", [pt[pb]], [Y_t], lambda h, pb=pb, chc=chc: h.tensor_copy(out=Ysb[:, chc, :n], in_=ps[pb][:, :n]))
                for c2 in range(2):
                    pb = 4 + c2
                    K.mm([Y_t, w_t], [pt[pb]], ps[pb][:, :n],
                         [(fnw[:, c1, c2 * 128:(c2 + 1) * 128], Ysb[:, c1, :n]) for c1 in range(2)])
                    K.op("act", [pt[pb]], [yf_t], lambda h, pb=pb, c2=c2: h.activation(out=yf[:, c2, :n], in_=ps[pb][:, :n], func=AF.Copy))
                out_proj_update(lambda dt: [(wo[:, c2, dt * 128:(dt + 1) * 128], yf[:, c2, :n]) for c2 in range(2)],
                                [yf_t, w_t], b, c0, n, r, [0, 1, 2, 3, 4, 5, 6])
            K.barrier()

    def natt(l, hT, hT_t, need_ctx):
        with contextlib.ExitStack() as ph:
            kT = K.sb(ph, "na_kT", [128, 3, NT], BF16)
            kT_t = Tok()
            V = K.sb(ph, "na_V", [128, 18, 6, 65], BF16)
            V_t = Tok()
            gq = K.sb(ph, "na_gq", [128, 1], F32)
            gk = K.sb(ph, "na_gk", [128, 1], F32)
            epsb = K.sb(ph, "na_eps", [128, 1], F32)
            bones = K.sb(ph, "na_bones", [128, 128], BF16)
            g_t = Tok()
            K.dma("sp", gq[:], naq_d[l], [], [g_t], "nag")
            K.dma("sp", gk[:], nak_d[l], [], [g_t], "nag")
            K.op("dve", [g_t], [g_t], lambda h: h.tensor_scalar(out=gq[:], in0=gq[:], scalar1=0.125, scalar2=None, op0=ALU.mult))
            K.op("dve", [], [g_t], lambda h: h.memset(epsb[:], EPS))
            K.op("dve", [], [g_t], lambda h: h.memset(bones[:], 0.0))
            K.op("dve", [g_t], [g_t], lambda h: h.memset(bones[0:64, 0:64], 1.0))
            K.op("dve", [g_t], [g_t], lambda h: h.memset(bones[64:128, 64:128], 1.0))
            K.op("dve", [], [V_t], lambda h: h.memset(V[:, :, :, 64:65], 1.0))
            sq = [K.sb(ph, "na_sq%d" % i, [128, 512], BF16) for i in range(2)]
            sq_t = [Tok() for i in range(2)]
            rtl = [K.sb(ph, "na_rt%d" % i, [128, 512], F32) for i in range(2)]
            rs = [K.sb(ph, "na_rs%d" % i, [128, 512], F32) for i in range(2)]
            rs_t = [Tok() for i in range(2)]

            def qk_norm(w, wtok, gain, dst, dst_t, dcol, b, c0, n, cnt):
                for hp in range(3):
                    u = (cnt[0]) % 2
                    cnt[0] += 1
                    pa, pbk = u, 2 + u
                    lin_fm(None, w, hp * 128, 128, hT, hT_t, b, c0, n, pa, wtok)
                    K.op("act", [pt[pa]], [sq_t[u]], lambda h, u=u, pa=pa: h.activation(out=sq[u][:, :n], in_=ps[pa][:, :n], func=AF.Square))
                    K.mm([sq_t[u], g_t], [pt[pbk]], ps[pbk][:, :n], [(bones[:], sq[u][:, :n])])
                    K.op("act", [pt[pbk], g_t], [rs_t[u]], lambda h, u=u, pbk=pbk: h.activation(
                        out=rtl[u][:, :n], in_=ps[pbk][:, :n], func=AF.Ln, bias=epsb[:], scale=1.0 / 64))
                    K.op("act", [rs_t[u]], [rs_t[u]], lambda h, u=u: h.activation(
                        out=rs[u][:, :n], in_=rtl[u][:, :n], func=AF.Exp, scale=-0.5))
                    K.op("dve", [pt[pa], rs_t[u], g_t], [dst_t], lambda h, u=u, pa=pa, hp=hp: h.scalar_tensor_tensor(
                        out=dst[:, hp, dcol:dcol + n], in0=ps[pa][:, :n], scalar=gain[:, 0:1], in1=rs[u][:, :n],
                        op0=ALU.mult, op1=ALU.mult))

            cnt = [0]
            with contextlib.ExitStack() as p1:
                wk = K.sb(p1, "na_wk", [128, 8, 384], BF16)
                wv = K.sb(p1, "na_wv", [128, 8, 384], BF16)
                wk_t = Tok()
                K.dma("pool", wk[:], win_d[l][:, :, 640:1024], [], [wk_t], "naw")
                K.dma("pool", wv[:], win_d[l][:, :, 1024:1408], [], [wk_t], "naw")
                for b in range(5):
                    c0, n = blk_cols(b)
                    qk_norm(wk, wk_t, gk, kT, kT_t, c0, b, c0, n, cnt)
                for ti in range(18):
                    pb = 4 + ti % 2
                    b = min(ti // 4, 4)
                    K.mm([wk_t, hT_t[b]], [pt[pb]], ps[pb][:, :384],
                         [(hT[:, kc, ti * 128:(ti + 1) * 128], wv[:, kc, :]) for kc in range(8)])
                    K.op("act" if ti % 2 else "dve", [pt[pb]], [V_t],
                         (lambda h, ti=ti, pb=pb: h.activation(out=V[:, ti, :, 0:64], in_=ps[pb][:, :384].rearrange("p (a b) -> p a b", b=64), func=AF.Copy))
                         if ti % 2 else
                         (lambda h, ti=ti, pb=pb: h.tensor_copy(out=V[:, ti, :, 0:64], in_=ps[pb][:, :384].rearrange("p (a b) -> p a b", b=64))))
                K.barrier()
            wq = K.sb(ph, "na_wq", [128, 8, 384], BF16)
            wo = K.sb(ph, "na_wo", [128, 3, D], BF16)
            BT = K.sb(ph, "na_BT", [128, 72, 128], BF16)
            bt_t = Tok()
            w_t = Tok()
            K.dma("pool", wq[:], win_d[l][:, :, 256:640], [], [w_t], "naw2")
            K.dma("pool", wo[:], wout_d[l][:, 2:5, :], [], [w_t], "naw2")
            for i3 in range(3):
                K.dma("pool", BT[:, 24 * i3:24 * (i3 + 1), :], bt_d[l][:, 24 * i3:24 * (i3 + 1), :], [], [bt_t], "nabt")
            for i3 in range(3):
                K.op("act", [bt_t], [bt_t], lambda h, i3=i3: h.activation(
                    out=BT[:, 24 * i3:24 * (i3 + 1), :], in_=BT[:, 24 * i3:24 * (i3 + 1), :], func=AF.Exp))
            qT = K.sb(ph, "na_qT", [128, 3, 512], BF16)
            qT_t = Tok()
            PT = [K.sb(ph, "na_PT%d" % i, [128, 7, 128], BF16) for i in range(2)]
            PT_t = [Tok() for i in range(2)]
            rec = [K.sb(ph, "na_rec%d" % i, [128, 6], F32) for i in range(2)]
            rec_t = [Tok(), Tok()]
            Osb = [K.sb(ph, "na_O%d" % i, [128, 6, 64], BF16) for i in range(2)]
            O_t = [Tok(), Tok()]
            yna = K.sb(ph, "na_y", [128, 3, 512], BF16)
            yna_t = Tok()
            hcount = 0
            for b in (range(5) if need_ctx else range(4)):
                c0, n = blk_cols(b)
                r = 0 if b < 4 else 1
                qk_norm(wq, w_t, gq, qT, qT_t, 0, b, c0, n, cnt)
                tile_chunks = []
                for qi in range(n // 128):
                    if b < 4:
                        i = b * 4 + qi
                        if i <= 1:
                            js = list(range(0, 4))
                        elif i >= 14:
                            js = list(range(12, 16))
                        else:
                            js = list(range(i - 2, i + 3))
                        chunks = []
                        for j in js:
                            dl = j - i
                            v = (dl + 9) if 2 <= i <= 13 else (dl + 3)
                            chunks.append((j, v))
                        chunks += [(16, None), (17, None)]
                    else:
                        chunks = [(16, None), (17, None)]
                    tile_chunks.append(chunks)
                items = [(qi, hd) for qi in range(n // 128) for hd in range(6)]

                def stage1(k):
                    qi, hd = items[k]
                    chunks = tile_chunks[qi]
                    nch = len(chunks)
                    hp, r0 = hd // 2, (hd % 2) * 64
                    u = (hbase + k) % 2
                    pS = (0 + 2 * u, 1 + 2 * u)
                    for c, (kt, v) in enumerate(chunks):
                        pb = pS[c // 4]
                        pairs = [(kT[r0:r0 + 64, hp, kt * 128:(kt + 1) * 128], qT[r0:r0 + 64, hp, qi * 128:(qi + 1) * 128])]
                        K.mm([kT_t, qT_t, w_t, t_const], [pt[pb]], ps[pb][:, (c % 4) * 128:(c % 4 + 1) * 128], pairs)
                    n0 = min(nch, 4)
                    K.op("act", [pt[pS[0]]], [PT_t[u]], lambda h: h.activation(
                        out=PT[u][:, 0:n0, :], in_=ps[pS[0]][:, 0:n0 * 128].rearrange("p (a b) -> p a b", b=128), func=AF.Exp))
                    if nch > 4:
                        n1 = nch - 4
                        K.op("act", [pt[pS[1]]], [PT_t[u]], lambda h: h.activation(
                            out=PT[u][:, 4:4 + n1, :], in_=ps[pS[1]][:, 0:n1 * 128].rearrange("p (a b) -> p a b", b=128), func=AF.Exp))
                    loc = [v for (kt, v) in chunks if v is not None]
                    if loc:
                        nl, v0 = len(loc), loc[0]
                        K.op("dve", [PT_t[u], bt_t], [PT_t[u]], lambda h: h.tensor_tensor(
                            out=PT[u][:, 0:nl, :], in0=PT[u][:, 0:nl, :], in1=BT[:, hd * 12 + v0:hd * 12 + v0 + nl, :], op=ALU.mult))

                def stage2(k):
                    qi, hd = items[k]
                    chunks = tile_chunks[qi]
                    u = (hbase + k) % 2
                    po = 4 + qi % 2
                    K.mm([PT_t[u], V_t], [pt[po]], ps[po][:, hd * 65:(hd + 1) * 65],
                         [(PT[u][:, c, :], V[:, kt, hd, :]) for c, (kt, v) in enumerate(chunks)])

                def fin_dve(qi):
                    po = 4 + qi % 2
                    oi = qi % 2
                    ov = ps[po][:, 0:390].rearrange("p (a b) -> p a b", b=65)
                    K.op("dve", [pt[po]], [rec_t[oi]], lambda h: h.reciprocal(out=rec[oi][:], in_=ov[:, :, 64]))
                    K.op("dve", [pt[po], rec_t[oi]], [O_t[oi]], lambda h: h.tensor_tensor(
                        out=Osb[oi][:], in0=ov[:, :, 0:64], in1=rec[oi][:].unsqueeze(2).to_broadcast([128, 6, 64]), op=ALU.mult))

                def fin_pe(qi):
                    oi = qi % 2
                    Of = Osb[oi][:].rearrange("p a b -> p (a b)")
                    for hp in range(3):
                        K.op("pe", [O_t[oi], t_const], [ptb], lambda h, hp=hp: h.transpose(
                            psb[:, hp * 128:(hp + 1) * 128], Of[:, hp * 128:(hp + 1) * 128], identb[:]))
                    K.op("act", [ptb], [yna_t], lambda h: h.activation(
                        out=yna[:, :, qi * 128:(qi + 1) * 128], in_=psb[:, 0:384].rearrange("p (a b) -> p a b", b=128), func=AF.Copy))

                hbase = hcount
                nit = len(items)
                for k in range(nit + 2):
                    if k < nit:
                        stage1(k)
                    if 1 <= k <= nit:
                        stage2(k - 1)
                        if items[k - 1][1] == 5:
                            fin_dve(items[k - 1][0])
                    if 2 <= k <= nit + 1 and items[k - 2][1] == 5:
                        fin_pe(items[k - 2][0])
                hcount += nit
                out_proj_update(lambda dt: [(wo[:, hp, dt * 128:(dt + 1) * 128], yna[:, hp, :n]) for hp in range(3)],
                                [yna_t, w_t], b, c0, n, r, [6])
            K.barrier()

    def gla(l, hT, hT_t, need_ctx):
        NB = 256
        QS = 48 ** -0.5
        with contextlib.ExitStack() as ph:
            qr = K.sb(ph, "gl_qr", [128, 2, NT], BF16)
            kr = K.sb(ph, "gl_kr", [128, 2, NT], BF16)
            aT = K.sb(ph, "gl_aT", [32, NT], BF16)
            st_t = [Tok() for i in range(5)]
            aw = K.sb(ph, "gl_aw", [32, 2, 256], BF16)
            nab = K.sb(ph, "gl_nab", [128, 2, 2], F32)
            gon = K.sb(ph, "gl_gon", [96, 1], F32)
            msk = K.sb(ph, "gl_msk", [128, 2, 2, 128], BF16)
            mscan = K.sb(ph, "gl_mscan", [128, NB], BF16)
            epsb = K.sb(ph, "gl_eps", [128, 1], F32)
            oneb = K.sb(ph, "gl_one", [128, 1], F32)
            w_t = Tok()
            K.dma("pool", aw[:], aw_d[l], [], [w_t], "glw")
            K.dma("sp", nab[:], ab_d[l], [], [w_t], "glw")
            K.dma("sp", gon[:], gon_d[l], [], [w_t], "glw")
            K.dma("sp", msk[:], gmask_d, [], [w_t], "glw")
            K.dma("sp", mscan[:], mscan_d, [], [w_t], "glw")
            K.op("dve", [w_t], [w_t], lambda h: h.tensor_scalar(out=nab[:], in0=nab[:], scalar1=-1.0, scalar2=None, op0=ALU.mult))
            K.op("dve", [], [w_t], lambda h: h.memset(epsb[:], EPS))
            K.op("dve", [], [w_t], lambda h: h.memset(oneb[:], 1.0))
            with contextlib.ExitStack() as p0:
                wg = K.sb(p0, "gl_wg", [128, 8, 1024], BF16)
                wga = K.sb(p0, "gl_wga", [128, 8, 32], BF16)
                w0_t = Tok()
                K.dma("pool", wg[:], wg_d[l], [], [w0_t], "glw0")
                K.dma("pool", wga[:], win_d[l][:, :, 2560:2592], [], [w0_t], "glw0")
                cosb = K.sb(p0, "gl_cos", [128, 512], F32)
                sinb = K.sb(p0, "gl_sin", [128, 512], F32)
                cs_t = Tok()
                t1 = [K.sb(p0, "gl_t1%d" % i, [128, 512], F32) for i in range(2)]
                t2 = [K.sb(p0, "gl_t2%d" % i, [128, 512], F32) for i in range(2)]
                r_t = [Tok(), Tok()]
                cnt = 0
                for b5 in range(5):
                    c0, n = blk_cols(b5)
                    lat = b5 < 4
                    if lat:
                        K.dma("sp", cosb[:, :n], cos_d[:, c0:c0 + n], [], [cs_t], "cs")
                        K.dma("sp", sinb[:, :n], sin_d[:, c0:c0 + n], [], [cs_t], "cs")
                    lin_fm(None, wga, 0, 32, hT, hT_t, b5, c0, n, 6, w0_t)
                    K.op("act", [pt[6]], [st_t[b5]], lambda h: h.activation(out=aT[:, c0:c0 + n], in_=ps[6][:32, :n], func=AF.Copy))
                    for hp in range(2):
                        for which in range(2):
                            base = which * 512
                            dst = qr if which == 0 else kr
                            u = cnt % 2
                            cnt += 1
                            pa, pbk = (0, 1) if u == 0 else (2, 3)
                            lin_fm(None, wg, base + hp * 128, 128, hT, hT_t, b5, c0, n, pa, w0_t)
                            if lat:
                                lin_fm(None, wg, base + 256 + hp * 128, 128, hT, hT_t, b5, c0, n, pbk, w0_t)
                                K.op("dve", [pt[pa], cs_t], [r_t[u]], lambda h: h.tensor_tensor(out=t1[u][:, :n], in0=ps[pa][:, :n], in1=cosb[:, :n], op=ALU.mult))
                                K.op("dve", [pt[pbk], cs_t, r_t[u]], [r_t[u]], lambda h: h.tensor_tensor(out=t2[u][:, :n], in0=ps[pbk][:, :n], in1=sinb[:, :n], op=ALU.mult))
                                K.op("pool", [r_t[u]], [st_t[b5]], lambda h: h.tensor_tensor(out=dst[:, hp, c0:c0 + n], in0=t1[u][:, :n], in1=t2[u][:, :n], op=ALU.add))
                            else:
                                K.op("act", [pt[pa]], [st_t[b5]], lambda h: h.activation(out=dst[:, hp, c0:c0 + n], in_=ps[pa][:, :n], func=AF.Copy))
                K.barrier()
            wgv = K.sb(ph, "gl_wgv", [128, 8, 384], BF16)
            wgg = K.sb(ph, "gl_wgg", [128, 8, 384], BF16)
            wo = K.sb(ph, "gl_wo", [96, 4, D], BF16)
            w2_t = Tok()
            K.dma("pool", wgv[:], win_d[l][:, :, 1792:2176], [], [w2_t], "glw2")
            K.dma("pool", wgg[:], win_d[l][:, :, 2176:2560], [], [w2_t], "glw2")
            K.dma("pool", wo[:], woutg_d[l], [], [w2_t], "glw2")
            oF = K.sb(ph, "gl_oF", [96, 4, NT], BF16)
            oF_t = [Tok() for i in range(9)]
            sg = K.sb(ph, "gl_sg", [96, 4, NB], BF16)
            sg_t = Tok()
            yg = sg
            yg_t = sg_t
            sq = K.sb(ph, "gl_sq", [96, NB], BF16)
            rt = K.sb(ph, "gl_rt", [96, NB], F32)
            rs = rt
            ty = K.sb(ph, "gl_ty", [96, NB], F32)
            n_t = Tok()
            R = []
            for d in range(2):
                r_ = dict(
                    e1=K.sb(ph, "gl_e1_%d" % d, [128, NB], F32),
                    bpos=K.sb(ph, "gl_bp_%d" % d, [128, NB], F32), eb=K.sb(ph, "gl_eb_%d" % d, [128, NB], F32),
                    qin=K.sb(ph, "gl_qin_%d" % d, [128, 2, NB], BF16), kin=K.sb(ph, "gl_kin_%d" % d, [128, 2, NB], BF16),
                    ktok=K.sb(ph, "gl_ktok_%d" % d, [128, 2, 2, 128], BF16), gv=K.sb(ph, "gl_gv_%d" % d, [128, 2, 384], BF16),
                    dec=K.sb(ph, "gl_dec_%d" % d, [128, 2, 2], F32), AM=K.sb(ph, "gl_AM_%d" % d, [128, 4, 128], BF16),
                    S=K.sb(ph, "gl_S_%d" % d, [128, 2, 192], F32), Sb=K.sb(ph, "gl_Sb_%d" % d, [128, 2, 192], BF16),
                    osum=K.sb(ph, "gl_osum_%d" % d, [96, 4, NB], F32),
                    g_t=Tok(), qk_t=Tok(), ktok_t=Tok(), gv_t=Tok(), dec_t=Tok(), AM_t=Tok(), S_t=Tok(), Sb_t=Tok(), os_t=Tok(),
                    BA=3 * d, BB=3 * d + 1, BO=3 * d + 2)
                R.append(r_)

            def block_prep(d, b):
                r_ = R[d]
                c0 = b * NB if b < 8 else T
                n = NB
                b5 = min(c0 // 512, 4)
                BB = r_["BB"]
                e1, bpos, eb = r_["e1"], r_["bpos"], r_["eb"]
                lsb = e1
                enb = eb
                gt = r_["g_t"]
                for hp in range(2):
                    K.mm([st_t[b5], w_t], [pt[BB]], ps[BB][:, :n], [(aw[:, d, hp * 128:(hp + 1) * 128], aT[:, c0:c0 + n])])
                    yield
                    K.op("act", [pt[BB], w_t], [gt], lambda h: h.activation(
                        out=e1[:, :n], in_=ps[BB][:, :n], func=AF.Exp, bias=nab[:, d, hp:hp + 1], scale=-1.0))
                    yield
                    K.op("act", [gt, w_t], [gt], lambda h: h.activation(out=lsb[:, :n], in_=e1[:, :n], func=AF.Ln, bias=oneb[:], scale=1.0))
                    yield
                    K.op("dve", [gt, w_t], [gt], lambda h: h.tensor_tensor_scan(
                        out=bpos[:, :n], data0=mscan[:, :n], data1=lsb[:, :n], initial=0.0, op0=ALU.mult, op1=ALU.add))
                    yield
                    bsel = bpos
                    if d == 1:
                        K.op("dve", [gt], [gt], lambda h: h.tensor_tensor(out=e1[:, :n], in0=lsb[:, :n], in1=bpos[:, :n], op=ALU.subtract))
                        yield
                        K.op("dve", [gt], [gt], lambda h: h.tensor_tensor(
                            out=e1[:, :n].rearrange("p (a b) -> p a b", b=128), in0=e1[:, :n].rearrange("p (a b) -> p a b", b=128),
                            in1=bpos[:, :n].rearrange("p (a b) -> p a b", b=128)[:, :, 127:128].to_broadcast([128, n // 128, 128]), op=ALU.add))
                        yield
                        bsel = e1
                    K.op("act", [gt], [gt], lambda h, bsel=bsel: h.activation(out=eb[:, :n], in_=bsel[:, :n], func=AF.Exp, scale=-1.0 / 16))
                    yield
                    sel = 127 if d == 0 else 0
                    K.op("dve", [gt], [r_["dec_t"]], lambda h: h.tensor_copy(
                        out=r_["dec"][:, hp, :], in_=eb[:, :n].rearrange("p (a b) -> p a b", b=128)[:, :, sel]))
                    yield
                    K.op("dve", [gt, st_t[b5]], [r_["qk_t"]], lambda h: h.scalar_tensor_tensor(
                        out=r_["qin"][:, hp, :n], in0=qr[:, hp, c0:c0 + n], scalar=QS, in1=eb[:, :n], op0=ALU.mult, op1=ALU.mult))
                    yield
                    K.op("act", [gt, r_["dec_t"], r_["qk_t"]], [gt], lambda h, bsel=bsel: h.activation(out=enb[:, :n], in_=bsel[:, :n], func=AF.Exp, scale=1.0 / 16))
                    yield
                    K.op("dve", [gt, st_t[b5]], [r_["qk_t"]], lambda h: h.tensor_tensor(
                        out=r_["kin"][:, hp, :n], in0=kr[:, hp, c0:c0 + n], in1=enb[:, :n], op=ALU.mult))
                    yield
                    for tt in range(2):
                        K.op("pe", [r_["qk_t"], t_const], [ptb], lambda h, tt=tt: h.transpose(
                            psb[:, (d * 2 + tt) * 128:(d * 2 + tt + 1) * 128], r_["kin"][:, hp, tt * 128:(tt + 1) * 128], identb[:]))
                        yield
                    K.op("act", [ptb], [r_["ktok_t"]], lambda h: h.activation(
                        out=r_["ktok"][:, :, hp, :], in_=psb[:, d * 256:(d + 1) * 256].rearrange("p (t f) -> p t f", t=2), func=AF.Copy))
                    yield
                for tt in range(2):
                    K.mm([w2_t, hT_t[b5]], [pt[BB]], ps[BB][:, :384],
                         [(hT[:, kc, c0 + tt * 128:c0 + (tt + 1) * 128], wgv[:, kc, :]) for kc in range(8)])
                    yield
                    K.op("act", [pt[BB]], [r_["gv_t"]], lambda h, tt=tt: h.activation(out=r_["gv"][:, tt, :], in_=ps[BB][:, :384], func=AF.Copy))
                    yield

            def scan_tile(d, b, tt, want_o, second):
                r_ = R[d]
                c0 = b * NB if b < 8 else T
                tc0 = tt * 128
                BA, BB, BO = r_["BA"], r_["BB"], r_["BO"]
                qin, kin, gv, Sb, S, AM, dec, ktok = r_["qin"], r_["kin"], r_["gv"], r_["Sb"], r_["S"], r_["AM"], r_["dec"], r_["ktok"]
                if want_o:
                    for hd in (0, 2, 1, 3):
                        hp, r0 = hd // 2, (hd % 2) * 64
                        pa = BA if hd % 2 == 0 else BB
                        K.mm([r_["qk_t"]], [pt[pa]], ps[pa][:, (hd // 2) * 128:(hd // 2 + 1) * 128],
                             [(kin[r0:r0 + 48, hp, tc0:tc0 + 128], qin[r0:r0 + 48, hp, tc0:tc0 + 128])])
                        yield
                    for par, pa in ((0, BA), (1, BB)):
                        K.op("dve", [pt[pa], w_t], [r_["AM_t"]], lambda h, par=par, pa=pa: h.tensor_tensor(
                            out=AM[:, par:4:2, :], in0=ps[pa][:, 0:256].rearrange("p (a b) -> p a b", b=128),
                            in1=msk[:, d, :, :], op=ALU.mult))
                        yield
                    for hd in range(4):
                        hp, r0 = hd // 2, (hd % 2) * 64
                        K.mm([r_["AM_t"], r_["gv_t"], r_["Sb_t"], r_["qk_t"]], [pt[BO]], ps[BO][:96, hd * 128:(hd + 1) * 128],
                             [(gv[:, tt, hd * 96:(hd + 1) * 96], AM[:, hd, :]),
                              (Sb[r0:r0 + 48, hp, (hd % 2) * 96:(hd % 2 + 1) * 96], qin[r0:r0 + 48, hp, tc0:tc0 + 128])])
                        yield
                    ov = ps[BO][:96, :].rearrange("p (a b) -> p a b", b=128)
                    if not second:
                        K.op("act", [pt[BO]], [oF_t[b]], lambda h: h.activation(
                            out=oF[:, :, c0 + tc0:c0 + tc0 + 128], in_=ov, func=AF.Copy))
                    else:
                        K.op("dve", [pt[BO], oF_t[b]], [r_["os_t"]], lambda h: h.tensor_tensor(
                            out=r_["osum"][:, :, tc0:tc0 + 128], in0=ov, in1=oF[:, :, c0 + tc0:c0 + tc0 + 128], op=ALU.add))
                    yield
                for hp in range(2):
                    K.mm([r_["ktok_t"], r_["gv_t"]], [pt[BA]], ps[BA][:, hp * 192:(hp + 1) * 192],
                         [(ktok[:, tt, hp, :], gv[:, tt, hp * 192:(hp + 1) * 192])])
                    yield
                for hp in range(2):
                    K.op("act", [r_["S_t"], r_["dec_t"]], [r_["S_t"]], lambda h, hp=hp: h.activation(
                        out=S[:, hp, :], in_=S[:, hp, :], func=AF.Identity, scale=dec[:, hp, tt:tt + 1]))
                    yield
                    K.op("dve", [pt[BA], r_["S_t"], r_["dec_t"]], [r_["S_t"]], lambda h, hp=hp: h.scalar_tensor_tensor(
                        out=S[:, hp, :], in0=ps[BA][:, hp * 192:(hp + 1) * 192], scalar=dec[:, hp, tt:tt + 1], in1=S[:, hp, :],
                        op0=ALU.mult, op1=ALU.add))
                    yield
                K.op("act", [r_["S_t"]], [r_["Sb_t"]], lambda h: h.activation(out=Sb[:], in_=S[:], func=AF.Copy))
                yield

            def finalize(d, b):
                r_ = R[d]
                if os.environ.get("KTRACE"):
                    print("finalize", d, b, "n_inst", K.n_inst)
                c0 = b * NB if b < 8 else T
                n = NB
                b5 = min(c0 // 512, 4)
                r = 0 if b < 8 else 1
                osum = r_["osum"]
                for hd in range(4):
                    lin_fm(None, wgg, hd * 96, 96, hT, hT_t, b5, c0, n, 6, w2_t)
                    K.op("act", [pt[6]], [sg_t], lambda h, hd=hd: h.activation(out=sg[:, hd, :n], in_=ps[6][:96, :n], func=AF.Silu))
                for hd in range(4):
                    K.op("act", [r_["os_t"]], [n_t], lambda h, hd=hd: h.activation(out=sq[:, :n], in_=osum[:, hd, :n], func=AF.Square))
                    K.mm([n_t, t_const], [pt[6]], ps[6][:96, :n], [(onesb[:96, :96], sq[:, :n])])
                    K.op("act", [pt[6], w_t], [n_t], lambda h: h.activation(
                        out=rt[:, :n], in_=ps[6][:96, :n], func=AF.Ln, bias=epsb[:96, :], scale=1.0 / 96))
                    K.op("act", [n_t], [n_t], lambda h: h.activation(out=rs[:, :n], in_=rt[:, :n], func=AF.Exp, scale=-0.5))
                    K.op("dve", [r_["os_t"], n_t, w_t], [n_t], lambda h, hd=hd: h.scalar_tensor_tensor(
                        out=ty[:, :n], in0=osum[:, hd, :n], scalar=gon[:, 0:1], in1=rs[:, :n], op0=ALU.mult, op1=ALU.mult))
                    K.op("dve", [n_t, sg_t], [yg_t], lambda h, hd=hd: h.tensor_tensor(
                        out=yg[:, hd, :n], in0=ty[:, :n], in1=sg[:, hd, :n], op=ALU.mult))
                for dt in range(8):
                    K.mm([yg_t, w2_t], [pt[6]], ps[6][:, :n], [(wo[:, hd, dt * 128:(dt + 1) * 128], yg[:, hd, :n]) for hd in range(4)])
                    K.op("dve", [pt[6], t_mod, xT_t[b5]], [xT_t[b5]], lambda h, dt=dt: h.scalar_tensor_tensor(
                        out=xT[:, dt, c0:c0 + n], in0=ps[6][:, :n], scalar=modT[:, 16 + dt, r:r + 1],
                        in1=xT[:, dt, c0:c0 + n], op0=ALU.mult, op1=ALU.add))

            def dir_gen(d):
                r_ = R[d]
                K.op("dve", [], [r_["S_t"]], lambda h: h.memset(r_["S"][:], 0.0))
                K.op("dve", [], [r_["Sb_t"]], lambda h: h.memset(r_["Sb"][:], 0.0))
                order = [8] + (list(range(8)) if d == 0 else list(range(7, -1, -1)))
                for b in order:
                    want_o = (b < 8) or need_ctx
                    if b == 8:
                        second = (d == 1)
                    else:
                        second = (b >= 4) if d == 0 else (b <= 3)
                    yield from block_prep(d, b)
                    for tt in ((0, 1) if d == 0 else (1, 0)):
                        yield from scan_tile(d, b, tt, want_o, second)
                    if want_o and second:
                        finalize(d, b)
                        yield
                    yield "STEP"

            def run_step(gens):
                active = list(gens)
                while active:
                    for g in list(active):
                        if next(g) == "STEP":
                            active.remove(g)
            gF, gB = dir_gen(0), dir_gen(1)
            run_step([gF])
            run_step([gB])
            for s_ in range(8):
                run_step([gF, gB])
            K.barrier()

    def mixers(l, hT, hT_t, need_ctx):
        if "nofn" not in debug:
            fnet(l, hT, hT_t, need_ctx)
        if "nona" not in debug:
            natt(l, hT, hT_t, need_ctx)
        if "nogla" not in debug:
            gla(l, hT, hT_t, need_ctx)

    for l in range(2):
        need_ctx = (l == 0)
        with contextlib.ExitStack() as ph:
            K.dma("sp", adab[:], adab_d[l], [], [t_small], "s")
            K.dma("sp", nmix[:], nmix_d[l], [], [t_small], "s")
            K.dma("sp", nffn[:], nffn_d[l], [], [t_small], "s")
            wsl = [K.sb(ph, "adaw%d" % i, [128, 8, 1024], BF16) for i in range(2)]
            wsl_t = [Tok("adaw%d" % i) for i in range(2)]
            for sec in range(6):
                s = sec % 2
                K.dma("pool", wsl[s][:], adaw_d[l][:, :, sec * 1024:(sec + 1) * 1024], [], [wsl_t[s]], "adaw")
                pb = sec % 2
                for mt in range(8):
                    j = sec * 8 + mt
                    K.mm([wsl_t[s], t_small], [pt[pb]], ps[pb][:, mt * 2:mt * 2 + 2],
                         [(wsl[s][:, kc, mt * 128:(mt + 1) * 128], scb[:, kc, :]) for kc in range(8)])
                K.op("dve", [pt[pb], t_small], [t_mod],
                     lambda h, sec=sec, pb=pb: h.tensor_tensor(
                         out=modT[:, sec * 8:(sec + 1) * 8, :],
                         in0=ps[pb][:, 0:16].rearrange("p (a b) -> p a b", b=2),
                         in1=adab[:, sec * 8:(sec + 1) * 8].unsqueeze(2).to_broadcast([128, 8, 2]),
                         op=ALU.add))
            for (dst, gain, sec) in ((amix, nmix, 1), (affn, nffn, 4)):
                K.op("dve", [t_mod, t_small], [t_mod],
                     lambda h, dst=dst, gain=gain, sec=sec: h.scalar_tensor_tensor(
                         out=dst[:], in0=modT[:, sec * 8:(sec + 1) * 8, :], scalar=1.0,
                         in1=gain[:].unsqueeze(2).to_broadcast([128, 8, 2]),
                         op0=ALU.add, op1=ALU.mult))
            K.barrier()
        if "mod" in debug and l == 0:
            dbg["modT"] = (modT, [128, 48, 2], F32)
            break
        with contextlib.ExitStack() as lay:
            hT = K.sb(lay, "hT", [128, 8, NT], BF16)
            hT_t = [Tok("hT%d" % i) for i in range(5)]
            if "ffn" not in debug:
                norm_mod(l, "mix", hT, hT_t, range(5))
                mixers(l, hT, hT_t, need_ctx)
            blocks = range(5) if need_ctx else range(4)
            if "noffn" not in debug:
                norm_mod(l, "ffn", hT, hT_t, blocks)
                ffn(l, hT, hT_t, blocks)
            K.barrier()
        if "ffn" in debug or "l0" in debug:
            break

    t_out = Tok("out")
    for name, (buf, shape, dt) in dbg.items():
        dd = K.dram("dbg_" + name, shape, dt, kind="ExternalOutput")
        K.barrier()
        K.dma("sp", dd, buf[:], [], [t_out], "out")
        K.outs["dbg_" + name] = shape

    with contextlib.ExitStack() as ph:
        xo = [K.sb(ph, "xo%d" % i, [128, D], F32) for i in range(3)]
        xo_t = [Tok("xo%d" % i) for i in range(3)]
        for ti in range(16):
            s = ti % 3
            b = ti // 4
            for half in range(2):
                pb = (ti * 2 + half) % 7
                for j in range(4):
                    kc = half * 4 + j
                    K.op("pe", [xT_t[b], t_const], [pt[pb]],
                         lambda h, kc=kc, j=j, pb=pb, ti=ti: h.transpose(
                             ps[pb][:, j * 128:(j + 1) * 128], xT[:, kc, ti * 128:(ti + 1) * 128], identf[:]))
                dst = xo[s][:, half * 512:(half + 1) * 512]
                if half == 0:
                    K.op("dve", [pt[pb]], [xo_t[s]], lambda h, dst=dst, pb=pb: h.tensor_copy(out=dst, in_=ps[pb][:, :]))
                else:
                    K.op("act", [pt[pb]], [xo_t[s]], lambda h, dst=dst, pb=pb: h.activation(out=dst, in_=ps[pb][:, :], func=AF.Copy))
            K.dma("sp", out_d[ti * 128:(ti + 1) * 128, :], xo[s][:], [xo_t[s]], [t_out], "out")
        K.barrier()
    S = t_out.dsem["sp"]
    nc.sync.wait_ge(S.sem, S.count * 16)
    es.close()
    return K


_CONSTS = {}


def _consts():
    if _CONSTS:
        return _CONSTS
    bf = ml_dtypes.bfloat16
    t = np.arange(T, dtype=np.int64)
    ang = 2.0 * np.pi * ((t[:, None] * t[None, :]) % T).astype(np.float64) / T
    _CONSTS["c_ct"] = np.ascontiguousarray(np.cos(ang).reshape(16, 128, T).transpose(1, 0, 2)).astype(bf)
    _CONSTS["c_st"] = np.ascontiguousarray(np.sin(ang).reshape(16, 128, T).transpose(1, 0, 2)).astype(bf)
    t2 = np.arange(NCTX, dtype=np.int64)
    ang2 = 2.0 * np.pi * ((t2[:, None] * t2[None, :]) % NCTX).astype(np.float64) / NCTX
    _CONSTS["c_c256"] = np.ascontiguousarray(np.cos(ang2).reshape(2, 128, NCTX).transpose(1, 0, 2)).astype(bf)
    _CONSTS["c_s256"] = np.ascontiguousarray(np.sin(ang2).reshape(2, 128, NCTX).transpose(1, 0, 2)).astype(bf)
    g = np.arange(64)
    a64 = 2.0 * np.pi * ((g[:, None] * g[None, :]) % 64) / 64.0
    c64 = np.zeros((128, 4, 128))
    for blk in range(2):
        sl = slice(64 * blk, 64 * blk + 64)
        c64[sl, 0, sl] = np.cos(a64) / np.sqrt(T * 64.0)
        c64[sl, 1, sl] = -np.sin(a64) / np.sqrt(T * 64.0)
        c64[sl, 2, sl] = np.cos(a64) / np.sqrt(NCTX * 64.0)
        c64[sl, 3, sl] = -np.sin(a64) / np.sqrt(NCTX * 64.0)
    _CONSTS["c_c64"] = c64.astype(bf)
    cosT = np.zeros((128, T), np.float32)
    sinT = np.zeros((128, T), np.float32)
    row = (t // 64).astype(np.float64)
    col = (t % 64).astype(np.float64)
    for p in range(128):
        d = p % 64
        if d >= 48:
            continue
        within = d % 24
        j = within % 12
        inv = 10000.0 ** (-j / 12.0)
        pos = row if d < 24 else col
        cosT[p] = np.cos(pos * inv)
        sinT[p] = np.sin(pos * inv) * (-1.0 if within < 12 else 1.0)
    _CONSTS["c_cos"] = cosT
    _CONSTS["c_sin"] = sinT
    s_ = np.arange(128)
    gm = np.zeros((128, 2, 2, 128), np.float32)
    gm[:, 0, :, :] = (s_[:, None] <= s_[None, :])[:, None, :]
    gm[:, 1, :, :] = (s_[:, None] >= s_[None, :])[:, None, :]
    _CONSTS["c_gmask"] = gm.astype(bf)
    ms = np.ones((128, 256), np.float32)
    ms[:, 0] = 0.0
    ms[:, 128] = 0.0
    _CONSTS["c_mscan"] = ms.astype(bf)
    return _CONSTS


def _host_prep(inputs):
    f = lambda a: np.ascontiguousarray(np.asarray(a, dtype=np.float32))
    x, c, ctx, c_ctx = f(inputs["x"]), f(inputs["c"]), f(inputs["ctx"]), f(inputs["c_ctx"])
    shared = {}
    shared["ada_w"] = f(f(inputs["ada_w"]).reshape(2, 8, 128, 6144).transpose(0, 2, 1, 3))
    shared["ada_b"] = f(f(inputs["ada_b"]).reshape(2, 48, 128).transpose(0, 2, 1))
    shared["norm_mix"] = f(f(inputs["norm_mix"]).reshape(2, 8, 128).transpose(0, 2, 1))
    shared["norm_ffn"] = f(f(inputs["norm_ffn"]).reshape(2, 8, 128).transpose(0, 2, 1))
    shared["ident"] = np.eye(128, dtype=np.float32)
    shared["ffn_w1"] = f(f(inputs["ffn_w1"]).reshape(2, 8, 128, DFF).transpose(0, 2, 1, 3))
    shared["ffn_w3"] = f(f(inputs["ffn_w3"]).reshape(2, 8, 128, DFF).transpose(0, 2, 1, 3))
    shared["ffn_w2"] = f(f(inputs["ffn_w2"]).reshape(2, 22, 128, D).transpose(0, 2, 1, 3))
    w_in = f(inputs["w_in"])
    shared["w_in"] = f(w_in.reshape(2, 8, 128, NIN).transpose(0, 2, 1, 3))
    wg = np.zeros((2, D, 1024), np.float32)
    dd = np.arange(48)
    partner = np.where((dd % 24) < 12, dd + 12, dd - 12)
    for hh in range(4):
        wg[:, :, 64 * hh:64 * hh + 48] = w_in[:, :, 1408 + 48 * hh + dd]
        wg[:, :, 256 + 64 * hh:256 + 64 * hh + 48] = w_in[:, :, 1408 + 48 * hh + partner]
        wg[:, :, 512 + 64 * hh:512 + 64 * hh + 48] = w_in[:, :, 1600 + 48 * hh + dd]
        wg[:, :, 768 + 64 * hh:768 + 64 * hh + 48] = w_in[:, :, 1600 + 48 * hh + partner]
    shared["w_g"] = f(wg.reshape(2, 8, 128, 1024).transpose(0, 2, 1, 3))
    shared["fnet_w"] = f(f(inputs["fnet_w"]).reshape(2, 2, 128, 256).transpose(0, 2, 1, 3))
    w_out = f(inputs["w_out"])
    shared["w_out"] = f(w_out.reshape(2, 8, 128, D).transpose(0, 2, 1, 3))
    shared["w_out_g"] = f(w_out[:, 640:, :].reshape(2, 4, 96, D).transpose(0, 2, 1, 3))
    shared["na_q"] = f(np.tile(f(inputs["na_q_norm"]), (1, 2))[:, :, None])
    shared["na_k"] = f(np.tile(f(inputs["na_k_norm"]), (1, 2))[:, :, None])
    rpb = f(inputs["na_rpb"])
    pk = np.arange(128)
    a_, kc_ = pk // 64, pk % 64
    b_, c_ = pk // 64, pk % 64
    dcm = np.clip(kc_[:, None] - c_[None, :], -15, 15) + 15
    cq0 = np.clip(c_ - 8, 0, 48)
    col_ok = (kc_[:, None] >= cq0[None, :]) & (kc_[:, None] < cq0[None, :] + 16)
    bt = np.full((2, 128, 72, 128), NEG, np.float32)
    ents = [(dl, 0) for dl in range(-3, 4)] + [(-2, 1), (-1, 0), (0, 0), (1, 0), (2, 1)]
    for e, (dl, msk_) in enumerate(ents):
        drm = 2 * dl + a_[:, None] - b_[None, :] + 7
        ok = col_ok & (drm >= 0) & (drm <= 14)
        if msk_ and dl == -2:
            ok = ok & ((2 * dl + a_[:, None]) >= (-4 + b_[None, :]))
        if msk_ and dl == 2:
            ok = ok & ((2 * dl + a_[:, None]) <= (3 + b_[None, :]))
        drc = np.clip(drm, 0, 14)
        for hh in range(6):
            vals = rpb[:, hh][:, drc, dcm]
            bt[:, :, hh * 12 + e, :] = np.where(ok[None], vals, NEG)
    shared["na_bt"] = bt
    aw = np.zeros((2, 32, 2, 256), np.float32)
    ab = np.zeros((2, 128, 2, 2), np.float32)
    gaw = f(inputs["gla_alpha_w"]); gab = f(inputs["gla_alpha_b"])
    for dr_ in range(2):
        for hh in range(4):
            aw[:, 16 * dr_:16 * dr_ + 16, dr_, 64 * hh:64 * hh + 48] = gaw[:, dr_, :, 48 * hh:48 * hh + 48]
            ab[:, 64 * (hh % 2):64 * (hh % 2) + 48, dr_, hh // 2] = gab[:, dr_, 48 * hh:48 * hh + 48]
    shared["gla_aw"] = aw
    shared["gla_ab"] = ab
    shared["gla_gon"] = f(f(inputs["gla_o_norm"])[:, :, None])
    shared.update(_consts())
    per_core = []
    for b in range(8):
        m = dict(shared)
        m["x"] = x[b]
        m["ctx"] = ctx[b]
        ccv = np.stack([c[b], c_ctx], axis=-1)
        m["cc"] = f(ccv.reshape(8, 128, 2).transpose(1, 0, 2))
        per_core.append(m)
    return per_core


_DEBUG = tuple(x for x in os.environ.get("KDEBUG", "").split(",") if x)
LAST = {}


def kernel(**inputs):
    in_maps = _host_prep(inputs)
    K = build_program(debug=_DEBUG)
    ncores = int(os.environ.get("KCORES", "8"))
    res = run_bass_kernel_spmd(K.nc, in_maps[:ncores], core_ids=list(range(ncores)))
    LAST["res"] = res
    out = np.stack([np.asarray(r["out"]) for r in res.results], axis=0).astype(np.float32)
    return out
```

```python
import contextlib
import os
import numpy as np
import ml_dtypes
import concourse.bass as bass
import concourse.mybir as mybir
from concourse.bass_utils import run_bass_kernel_spmd

F32 = mybir.dt.float32
BF16 = mybir.dt.bfloat16
AF = mybir.ActivationFunctionType
ALU = mybir.AluOpType

D = 1024
T = 2048
NCTX = 256
NT = T + NCTX
DFF = 2816
NIN = 2592
EPS = 1e-6
NEG = -30000.0
SAME_ENGINE_RAW = True


class Tok:
    __slots__ = ("w", "r", "dsem", "name")

    def __init__(self, name=""):
        self.w = []
        self.r = {}
        self.dsem = None
        self.name = name


class Src:
    def __init__(self, name, sem, unit, h=None):
        self.name = name
        self.sem = sem
        self.unit = unit
        self.h = h
        self.count = 0
        self.seen = {}


class Ctx:
    def __init__(self):
        self.nc = bass.Bass("TRN2", target_bir_lowering=False)
        nc = self.nc
        self.es = contextlib.ExitStack()
        self.engs = {}
        for nm, h in (("pe", nc.tensor), ("act", nc.scalar), ("dve", nc.vector),
                      ("pool", nc.gpsimd), ("sp", nc.sync)):
            sem = self.es.enter_context(nc.semaphore("sem_" + nm))
            self.engs[nm] = Src(nm, sem, 1, h)
        self.dsems = []
        self.outs = {}
        self.n_inst = 0

    def dram(self, name, shape, dt, kind="ExternalInput"):
        return self.nc.dram_tensor(name, list(shape), dt, kind=kind).ap()

    def sb(self, stack, name, shape, dt):
        self.n_sb = getattr(self, "n_sb", 0) + 1
        return stack.enter_context(self.nc.sbuf_tensor("sb%d_%s" % (self.n_sb, name), list(shape), dt))

    def new_dsem(self, name):
        sem = self.es.enter_context(self.nc.semaphore("d_" + name + str(len(self.dsems))))
        s = Src("dma_" + name, sem, 16)
        self.dsems.append(s)
        return s

    def _wait_deps(self, E, reads, writes):
        deps = {}
        for t in reads:
            for (s, n) in t.w:
                deps[s] = max(deps.get(s, 0), n)
        for t in writes:
            for (s, n) in t.w:
                deps[s] = max(deps.get(s, 0), n)
            for s, n in t.r.items():
                deps[s] = max(deps.get(s, 0), n)
        for s, n in deps.items():
            if s is E and (E.name == "pe" or not SAME_ENGINE_RAW):
                continue
            if E.seen.get(s, 0) < n:
                E.h.wait_ge(s.sem, n * s.unit)
                E.seen[s] = n

    def op(self, eng, reads, writes, emit):
        E = self.engs[eng]
        self._wait_deps(E, reads, writes)
        ins = emit(E.h)
        E.count += 1
        ins.then_inc(E.sem, 1)
        self.n_inst += 1
        for t in reads:
            t.r[E] = E.count
        for t in writes:
            t.w = [(E, E.count)]
            t.r = {}
        return ins

    def mm(self, reads, writes, out, pairs, first=True, last=True):
        E = self.engs["pe"]
        self._wait_deps(E, reads, writes)
        n = len(pairs)
        ins = None
        for i, (l, r) in enumerate(pairs):
            ins = E.h.matmul(out, l, r, start=(first and i == 0), stop=(last and i == n - 1))
            self.n_inst += 1
        E.count += 1
        ins.then_inc(E.sem, 1)
        for t in reads:
            t.r[E] = E.count
        for t in writes:
            t.w = [(E, E.count)]
            t.r = {}

    def transpose(self, reads, writes, out, in_, ident):
        return self.op("pe", reads, writes, lambda h: h.transpose(out, in_, ident))

    def dma(self, q, out, in_, reads, writes, name="x"):
        E = self.engs[q]
        self._wait_deps(E, reads, writes)
        wt = writes[0]
        if wt.dsem is None:
            wt.dsem = {}
        if q not in wt.dsem:
            wt.dsem[q] = self.new_dsem(name + q)
        S = wt.dsem[q]
        ins = E.h.dma_start(out=out, in_=in_)
        S.count += 1
        ins.then_inc(S.sem, 16)
        self.n_inst += 1
        for t in reads:
            t.r[S] = S.count
        for t in writes:
            t.w = [(s_, n_) for (s_, n_) in t.w if (s_.unit == 16 and s_ is not S)] + [(S, S.count)]
            t.r = {}

    def barrier(self):
        allsrc = list(self.engs.values()) + self.dsems
        for E in self.engs.values():
            for s in allsrc:
                if s is E or s.count == 0:
                    continue
                if E.seen.get(s, 0) < s.count:
                    E.h.wait_ge(s.sem, s.count * s.unit)
                    E.seen[s] = s.count


def build_program(debug=()):
    K = Ctx()
    nc = K.nc
    es = K.es
    dbg = {}

    x_d = K.dram("x", [T, D], F32)
    ctx_d = K.dram("ctx", [NCTX, D], F32)
    cc_d = K.dram("cc", [128, 8, 2], F32)
    adaw_d = K.dram("ada_w", [2, 128, 8, 6144], F32)
    adab_d = K.dram("ada_b", [2, 128, 48], F32)
    nmix_d = K.dram("norm_mix", [2, 128, 8], F32)
    nffn_d = K.dram("norm_ffn", [2, 128, 8], F32)
    out_d = K.dram("out", [T, D], F32, kind="ExternalOutput")
    ident_d = K.dram("ident", [128, 128], F32)
    w1_d = K.dram("ffn_w1", [2, 128, 8, DFF], F32)
    win_d = K.dram("w_in", [2, 128, 8, NIN], F32)
    wg_d = K.dram("w_g", [2, 128, 8, 1024], F32)
    fnetw_d = K.dram("fnet_w", [2, 128, 2, 256], F32)
    wout_d = K.dram("w_out", [2, 128, 8, D], F32)
    woutg_d = K.dram("w_out_g", [2, 96, 4, D], F32)
    naq_d = K.dram("na_q", [2, 128, 1], F32)
    nak_d = K.dram("na_k", [2, 128, 1], F32)
    bt_d = K.dram("na_bt", [2, 128, 72, 128], F32)
    aw_d = K.dram("gla_aw", [2, 32, 2, 256], F32)
    ab_d = K.dram("gla_ab", [2, 128, 2, 2], F32)
    gon_d = K.dram("gla_gon", [2, 96, 1], F32)
    ct_d = K.dram("c_ct", [128, 16, T], BF16)
    st_d = K.dram("c_st", [128, 16, T], BF16)
    c256_d = K.dram("c_c256", [128, 2, 256], BF16)
    s256_d = K.dram("c_s256", [128, 2, 256], BF16)
    c64_d = K.dram("c_c64", [128, 4, 128], BF16)
    cos_d = K.dram("c_cos", [128, T], F32)
    sin_d = K.dram("c_sin", [128, T], F32)
    gmask_d = K.dram("c_gmask", [128, 2, 2, 128], BF16)
    mscan_d = K.dram("c_mscan", [128, 256], BF16)
    w3_d = K.dram("ffn_w3", [2, 128, 8, DFF], F32)
    w2_d = K.dram("ffn_w2", [2, 128, 22, D], F32)

    top = es
    xT = K.sb(top, "xT", [128, 8, NT], F32)
    xT_t = [Tok("xT%d" % i) for i in range(5)]
    identf = K.sb(top, "identf", [128, 128], F32)
    identb = K.sb(top, "identb", [128, 128], BF16)
    onesb = K.sb(top, "onesb", [128, 128], BF16)
    cc = K.sb(top, "cc", [128, 8, 2], F32)
    scb = K.sb(top, "scb", [128, 8, 2], BF16)
    modT = K.sb(top, "modT", [128, 48, 2], F32)
    adab = K.sb(top, "adab", [128, 48], F32)
    nmix = K.sb(top, "nmix", [128, 8], F32)
    nffn = K.sb(top, "nffn", [128, 8], F32)
    amix = K.sb(top, "amix", [128, 8, 2], F32)
    affn = K.sb(top, "affn", [128, 8, 2], F32)
    t_const = Tok("const")
    t_mod = Tok("mod")
    t_small = Tok("small")

    ps = [es.enter_context(nc.psum_tensor("ps%d" % i, [128, 512], F32)) for i in range(7)]
    pt = [Tok("ps%d" % i) for i in range(7)]
    psb = es.enter_context(nc.psum_tensor("psb", [128, 1024], BF16))
    ptb = Tok("psb")

    def blk_cols(b):
        return (b * 512, 512) if b < 4 else (T, NCTX)

    K.dma("sp", identf[:], ident_d, [], [t_const], "c")
    K.op("dve", [t_const], [t_const], lambda h: h.tensor_copy(out=identb[:], in_=identf[:]))
    K.op("dve", [], [t_const], lambda h: h.memset(onesb[:], 1.0))
    K.dma("sp", cc[:], cc_d, [], [t_small], "s")
    K.op("act", [t_small], [t_small], lambda h: h.activation(out=scb[:], in_=cc[:], func=AF.Silu))

    with contextlib.ExitStack() as ph:
        xin = [K.sb(ph, "xin%d" % i, [128, D], F32) for i in range(3)]
        xin_t = [Tok("xin%d" % i) for i in range(3)]
        for ti in range(18):
            s = ti % 3
            src = x_d[ti * 128:(ti + 1) * 128, :] if ti < 16 else ctx_d[(ti - 16) * 128:(ti - 15) * 128, :]
            K.dma("sp", xin[s][:], src, [], [xin_t[s]], "xin")
            b = min(ti // 4, 4)
            for half in range(2):
                pb = (ti * 2 + half) % 7
                for j in range(4):
                    kc = half * 4 + j
                    K.op("pe", [xin_t[s], t_const], [pt[pb]],
                         lambda h, kc=kc, j=j, pb=pb, s=s: h.transpose(
                             ps[pb][:, j * 128:(j + 1) * 128], xin[s][:, kc * 128:(kc + 1) * 128], identf[:]))
                eng = "dve" if half == 0 else "act"
                dst = xT[:, half * 4:(half + 1) * 4, ti * 128:(ti + 1) * 128]
                srcp = ps[pb][:, :].rearrange("p (a b) -> p a b", a=4)
                if eng == "dve":
                    K.op("dve", [pt[pb]], [xT_t[b]], lambda h, dst=dst, srcp=srcp: h.tensor_copy(out=dst, in_=srcp))
                else:
                    K.op("act", [pt[pb]], [xT_t[b]], lambda h, dst=dst, srcp=srcp: h.activation(out=dst, in_=srcp, func=AF.Copy))
        K.barrier()


    rot = {"ps": 0}

    def norm_mod(l, which, hT, hT_t, blocks):
        a = amix if which == "mix" else affn
        sec = 0 if which == "mix" else 3
        with contextlib.ExitStack() as ph:
            sq = [K.sb(ph, "nm_sq%d" % i, [128, 8, 512], BF16) for i in range(2)]
            sq_t = [Tok() for i in range(2)]
            rt = [K.sb(ph, "nm_rt%d" % i, [128, 512], F32) for i in range(2)]
            rs = [K.sb(ph, "nm_rs%d" % i, [128, 512], F32) for i in range(2)]
            rs_t = [Tok() for i in range(2)]
            tmp = [K.sb(ph, "nm_tmp%d" % i, [128, 512], F32) for i in range(3)]
            tmp_t = [Tok() for i in range(3)]
            epsb = K.sb(ph, "nm_eps", [128, 1], F32)
            eps_t = Tok()
            K.op("dve", [], [eps_t], lambda h: h.memset(epsb[:], EPS))
            ti = 0
            for bi, b in enumerate(blocks):
                c0, n = blk_cols(b)
                r = 0 if b < 4 else 1
                s = bi % 2
                pb = bi % 2
                K.op("act", [xT_t[b]], [sq_t[s]], lambda h, s=s, c0=c0, n=n: h.activation(
                    out=sq[s][:, :, :n], in_=xT[:, :, c0:c0 + n], func=AF.Square))
                K.mm([sq_t[s], t_const], [pt[pb]], ps[pb][:, :n],
                     [(onesb[:], sq[s][:, kc, :n]) for kc in range(8)])
                K.op("act", [pt[pb], eps_t], [rs_t[s]], lambda h, s=s, pb=pb, n=n: h.activation(
                    out=rt[s][:, :n], in_=ps[pb][:, :n], func=AF.Ln, bias=epsb[:], scale=1.0 / D))
                K.op("act", [rs_t[s]], [rs_t[s]], lambda h, s=s, n=n: h.activation(
                    out=rs[s][:, :n], in_=rt[s][:, :n], func=AF.Exp, scale=-0.5))
                for kc in range(8):
                    u = ti % 3
                    ti += 1
                    K.op("dve", [xT_t[b], rs_t[s], t_mod], [tmp_t[u]], lambda h, u=u, kc=kc, c0=c0, n=n, r=r, s=s: h.scalar_tensor_tensor(
                        out=tmp[u][:, :n], in0=xT[:, kc, c0:c0 + n], scalar=a[:, kc, r:r + 1], in1=rs[s][:, :n],
                        op0=ALU.mult, op1=ALU.mult))
                    K.op("act", [tmp_t[u], t_mod], [hT_t[b]], lambda h, u=u, kc=kc, c0=c0, n=n, r=r: h.activation(
                        out=hT[:, kc, c0:c0 + n], in_=tmp[u][:, :n], func=AF.Identity,
                        bias=modT[:, sec * 8 + kc, r:r + 1], scale=1.0))
            K.barrier()

    def ffn(l, hT, hT_t, blocks):
        with contextlib.ExitStack() as ph:
            w1s = [K.sb(ph, "w1s%d" % i, [128, 8, 512], BF16) for i in range(2)]
            w3s = [K.sb(ph, "w3s%d" % i, [128, 8, 512], BF16) for i in range(2)]
            w2s = [K.sb(ph, "w2s%d" % i, [128, 4, D], BF16) for i in range(2)]
            w_t = [Tok() for i in range(2)]
            s1 = [K.sb(ph, "ff_s1%d" % i, [128, 512], F32) for i in range(2)]
            s1_t = [Tok() for i in range(2)]
            g = [K.sb(ph, "ff_g%d" % i, [128, 4, 512], BF16) for i in range(2)]
            g_t = [Tok() for i in range(2)]
            chunks = [(i * 512, 512) for i in range(5)] + [(2560, 256)]

            def load(ch):
                f0, nf = chunks[ch]
                s = ch % 2
                K.dma("pool", w1s[s][:, :, :nf], w1_d[l][:, :, f0:f0 + nf], [], [w_t[s]], "ffw")
                K.dma("pool", w3s[s][:, :, :nf], w3_d[l][:, :, f0:f0 + nf], [], [w_t[s]], "ffw")
                K.dma("pool", w2s[s][:, :nf // 128, :], w2_d[l][:, f0 // 128:(f0 + nf) // 128, :], [], [w_t[s]], "ffw")
            load(0)
            cnt = 0
            gi = 0
            oi = 0
            for ch in range(6):
                if ch + 1 < 6:
                    load(ch + 1)
                f0, nf = chunks[ch]
                s = ch % 2
                nft = nf // 128
                for b in blocks:
                    c0, n = blk_cols(b)
                    r = 0 if b < 4 else 1
                    gs = gi % 2
                    gi += 1
                    for ft in range(nft):
                        pa = cnt % 2
                        pbk = 2 + cnt % 2
                        u = cnt % 2
                        cnt += 1
                        K.mm([w_t[s], hT_t[b]], [pt[pa]], ps[pa][:, :n],
                             [(w1s[s][:, kc, ft * 128:(ft + 1) * 128], hT[:, kc, c0:c0 + n]) for kc in range(8)])
                        K.mm([w_t[s], hT_t[b]], [pt[pbk]], ps[pbk][:, :n],
                             [(w3s[s][:, kc, ft * 128:(ft + 1) * 128], hT[:, kc, c0:c0 + n]) for kc in range(8)])
                        K.op("act", [pt[pa]], [s1_t[u]], lambda h, u=u, pa=pa, n=n: h.activation(
                            out=s1[u][:, :n], in_=ps[pa][:, :n], func=AF.Silu))
                        K.op("dve", [s1_t[u], pt[pbk]], [g_t[gs]], lambda h, u=u, pbk=pbk, n=n, gs=gs, ft=ft: h.tensor_tensor(
                            out=g[gs][:, ft, :n], in0=s1[u][:, :n], in1=ps[pbk][:, :n], op=ALU.mult))
                    for dt in range(8):
                        pc = 4 + oi % 3
                        oi += 1
                        K.mm([w_t[s], g_t[gs]], [pt[pc]], ps[pc][:, :n],
                             [(w2s[s][:, ft, dt * 128:(dt + 1) * 128], g[gs][:, ft, :n]) for ft in range(nft)])
                        K.op("dve", [pt[pc], t_mod, xT_t[b]], [xT_t[b]], lambda h, pc=pc, dt=dt, c0=c0, n=n, r=r: h.scalar_tensor_tensor(
                            out=xT[:, dt, c0:c0 + n], in0=ps[pc][:, :n], scalar=modT[:, 40 + dt, r:r + 1],
                            in1=xT[:, dt, c0:c0 + n], op0=ALU.mult, op1=ALU.add))
            K.barrier()

    def lin_fm(wt, w, col0, M, hT, hT_t, b, c0, n, pb, wtok):
        K.mm([wtok, hT_t[b]], [pt[pb]], ps[pb][:M, :n],
             [(w[:, kc, col0:col0 + M], hT[:, kc, c0:c0 + n]) for kc in range(8)])

    def out_proj_update(pairs_fn, reads, b, c0, n, r, pbs):
        for dt in range(8):
            pc = pbs[dt % len(pbs)]
            K.mm(reads, [pt[pc]], ps[pc][:, :n], pairs_fn(dt))
            K.op("dve", [pt[pc], t_mod, xT_t[b]], [xT_t[b]], lambda h, pc=pc, dt=dt: h.scalar_tensor_tensor(
                out=xT[:, dt, c0:c0 + n], in0=ps[pc][:, :n], scalar=modT[:, 16 + dt, r:r + 1],
                in1=xT[:, dt, c0:c0 + n], op0=ALU.mult, op1=ALU.add))

    def rstd_from_ps(pb, P, n, inv, rt, rs, rs_tok, eps_ap, eps_t):
        K.op("act", [pt[pb], eps_t], [rs_tok], lambda h: h.activation(
            out=rt[:P, :n], in_=ps[pb][:P, :n], func=AF.Sqrt, bias=eps_ap[:P, :], scale=inv))
        K.op("dve", [rs_tok], [rs_tok], lambda h: h.reciprocal(out=rs[:P, :n], in_=rt[:P, :n]))

    def fnet(l, hT, hT_t, need_ctx):
        with contextlib.ExitStack() as ph:
            Z = K.sb(ph, "fn_Z", [128, 18, 256], BF16)
            Z_t = Tok()
            wfn = K.sb(ph, "fn_w", [128, 8, 256], BF16)
            fnw = K.sb(ph, "fn_fw", [128, 2, 256], BF16)
            wo = K.sb(ph, "fn_wo", [128, 2, D], BF16)
            c64 = K.sb(ph, "fn_c64", [128, 4, 128], BF16)
            w_t = Tok()
            K.dma("pool", wfn[:], win_d[l][:, :, 0:256], [], [w_t], "fnw")
            K.dma("pool", fnw[:], fnetw_d[l], [], [w_t], "fnw")
            K.dma("pool", wo[:], wout_d[l][:, 0:2, :], [], [w_t], "fnw")
            K.dma("sp", c64[:], c64_d, [], [w_t], "fnw")
            cts = [K.sb(ph, "fn_ct%d" % i, [128, 2, 16, 512], BF16) for i in range(2)]
            ct_t = [Tok() for i in range(2)]
            Psb = [K.sb(ph, "fn_P%d" % i, [128, 2, 512], BF16) for i in range(2)]
            P_t = [Tok() for i in range(2)]
            Ysb = K.sb(ph, "fn_Y", [128, 2, 512], BF16)
            Y_t = Tok()
            yf = K.sb(ph, "fn_yf", [128, 2, 512], BF16)
            yf_t = Tok()
            for ti in range(18):
                pb = ti % 2
                b = min(ti // 4, 4)
                K.mm([w_t, hT_t[b]], [pt[pb]], ps[pb][:, :256],
                     [(hT[:, kc, ti * 128:(ti + 1) * 128], wfn[:, kc, :]) for kc in range(8)])
                if ti % 2 == 0:
                    K.op("dve", [pt[pb]], [Z_t], lambda h, ti=ti, pb=pb: h.tensor_copy(out=Z[:, ti, :], in_=ps[pb][:, :256]))
                else:
                    K.op("act", [pt[pb]], [Z_t], lambda h, ti=ti, pb=pb: h.activation(out=Z[:, ti, :], in_=ps[pb][:, :256], func=AF.Copy))
            jobs = [(kb, 0) for kb in range(4)] + ([(0, 1)] if need_ctx else [])
            for ji, (kb, isctx) in enumerate(jobs):
                s = ji % 2
                if not isctx:
                    ntt, n, tt0, c0, b, r = 16, 512, 0, kb * 512, kb, 0
                    K.dma("sp", cts[s][:, 0, :, :], ct_d[:, :, kb * 512:(kb + 1) * 512], [], [ct_t[s]], "ct")
                    K.dma("sp", cts[s][:, 1, :, :], st_d[:, :, kb * 512:(kb + 1) * 512], [], [ct_t[s]], "ct")
                else:
                    ntt, n, tt0, c0, b, r = 2, 256, 16, T, 4, 1
                    K.dma("sp", cts[s][:, 0, 0:2, 0:256], c256_d, [], [ct_t[s]], "ct")
                    K.dma("sp", cts[s][:, 1, 0:2, 0:256], s256_d, [], [ct_t[s]], "ct")
                for chc in range(2):
                    u = chc
                    for cs in range(2):
                        pb = cs
                        K.mm([Z_t, ct_t[s]], [pt[pb]], ps[pb][:, :n],
                             [(Z[:, tt0 + tt, chc * 128:(chc + 1) * 128], cts[s][:, cs, tt, :n]) for tt in range(ntt)])
                        if cs == 0:
                            K.op("dve", [pt[pb]], [P_t[u]], lambda h, u=u, pb=pb, cs=cs: h.tensor_copy(out=Psb[u][:, cs, :n], in_=ps[pb][:, :n]))
                        else:
                            K.op("act", [pt[pb]], [P_t[u]], lambda h, u=u, pb=pb, cs=cs: h.activation(out=Psb[u][:, cs, :n], in_=ps[pb][:, :n], func=AF.Copy))
                    pb = 2 + chc
                    K.mm([P_t[u], w_t], [pt[pb]], ps[pb][:, :n],
                         [(c64[:, 2 * isctx + 0, :], Psb[u][:, 0, :n]), (c64[:, 2 * isctx + 1, :], Psb[u][:, 1, :n])])
                    K.op("dve", [pt[pb]], [Y_t], lambda h, pb=pb, chc=chc: h.tensor_copy(out=Ysb[:, chc, :n], in_=ps[pb][:, :n]))
                for c2 in range(2):
                    pb = 4 + c2
                    K.mm([Y_t, w_t], [pt[pb]], ps[pb][:, :n],
                         [(fnw[:, c1, c2 * 128:(c2 + 1) * 128], Ysb[:, c1, :n]) for c1 in range(2)])
                    K.op("act", [pt[pb]], [yf_t], lambda h, pb=pb, c2=c2: h.activation(out=yf[:, c2, :n], in_=ps[pb][:, :n], func=AF.Copy))
                out_proj_update(lambda dt: [(wo[:, c2, dt * 128:(dt + 1) * 128], yf[:, c2, :n]) for c2 in range(2)],
                                [yf_t, w_t], b, c0, n, r, [0, 1, 2, 3, 4, 5, 6])
            K.barrier()

    def natt(l, hT, hT_t, need_ctx):
        with contextlib.ExitStack() as ph:
            kT = K.sb(ph, "na_kT", [128, 3, NT], BF16)
            kT_t = Tok()
            V = K.sb(ph, "na_V", [128, 18, 6, 65], BF16)
            V_t = Tok()
            gq = K.sb(ph, "na_gq", [128, 1], F32)
            gk = K.sb(ph, "na_gk", [128, 1], F32)
            epsb = K.sb(ph, "na_eps", [128, 1], F32)
            bones = K.sb(ph, "na_bones", [128, 128], BF16)
            g_t = Tok()
            K.dma("sp", gq[:], naq_d[l], [], [g_t], "nag")
            K.dma("sp", gk[:], nak_d[l], [], [g_t], "nag")
            K.op("dve", [g_t], [g_t], lambda h: h.tensor_scalar(out=gq[:], in0=gq[:], scalar1=0.125, scalar2=None, op0=ALU.mult))
            K.op("dve", [], [g_t], lambda h: h.memset(epsb[:], EPS))
            K.op("dve", [], [g_t], lambda h: h.memset(bones[:], 0.0))
            K.op("dve", [g_t], [g_t], lambda h: h.memset(bones[0:64, 0:64], 1.0))
            K.op("dve", [g_t], [g_t], lambda h: h.memset(bones[64:128, 64:128], 1.0))
            K.op("dve", [], [V_t], lambda h: h.memset(V[:, :, :, 64:65], 1.0))
            sq = [K.sb(ph, "na_sq%d" % i, [128, 512], BF16) for i in range(2)]
            sq_t = [Tok() for i in range(2)]
            rtl = [K.sb(ph, "na_rt%d" % i, [128, 512], F32) for i in range(2)]
            rs = [K.sb(ph, "na_rs%d" % i, [128, 512], F32) for i in range(2)]
            rs_t = [Tok() for i in range(2)]

            def qk_norm(w, wtok, gain, dst, dst_t, dcol, b, c0, n, cnt):
                for hp in range(3):
                    u = (cnt[0]) % 2
                    cnt[0] += 1
                    pa, pbk = u, 2 + u
                    lin_fm(None, w, hp * 128, 128, hT, hT_t, b, c0, n, pa, wtok)
                    K.op("act", [pt[pa]], [sq_t[u]], lambda h, u=u, pa=pa: h.activation(out=sq[u][:, :n], in_=ps[pa][:, :n], func=AF.Square))
                    K.mm([sq_t[u], g_t], [pt[pbk]], ps[pbk][:, :n], [(bones[:], sq[u][:, :n])])
                    K.op("act", [pt[pbk], g_t], [rs_t[u]], lambda h, u=u, pbk=pbk: h.activation(
                        out=rtl[u][:, :n], in_=ps[pbk][:, :n], func=AF.Ln, bias=epsb[:], scale=1.0 / 64))
                    K.op("act", [rs_t[u]], [rs_t[u]], lambda h, u=u: h.activation(
                        out=rs[u][:, :n], in_=rtl[u][:, :n], func=AF.Exp, scale=-0.5))
                    K.op("dve", [pt[pa], rs_t[u], g_t], [dst_t], lambda h, u=u, pa=pa, hp=hp: h.scalar_tensor_tensor(
                        out=dst[:, hp, dcol:dcol + n], in0=ps[pa][:, :n], scalar=gain[:, 0:1], in1=rs[u][:, :n],
                        op0=ALU.mult, op1=ALU.mult))

            cnt = [0]
            with contextlib.ExitStack() as p1:
                wk = K.sb(p1, "na_wk", [128, 8, 384], BF16)
                wv = K.sb(p1, "na_wv", [128, 8, 384], BF16)
                wk_t = Tok()
                K.dma("pool", wk[:], win_d[l][:, :, 640:1024], [], [wk_t], "naw")
                K.dma("pool", wv[:], win_d[l][:, :, 1024:1408], [], [wk_t], "naw")
                for b in range(5):
                    c0, n = blk_cols(b)
                    qk_norm(wk, wk_t, gk, kT, kT_t, c0, b, c0, n, cnt)
                for ti in range(18):
                    pb = 4 + ti % 2
                    b = min(ti // 4, 4)
                    K.mm([wk_t, hT_t[b]], [pt[pb]], ps[pb][:, :384],
                         [(hT[:, kc, ti * 128:(ti + 1) * 128], wv[:, kc, :]) for kc in range(8)])
                    K.op("act" if ti % 2 else "dve", [pt[pb]], [V_t],
                         (lambda h, ti=ti, pb=pb: h.activation(out=V[:, ti, :, 0:64], in_=ps[pb][:, :384].rearrange("p (a b) -> p a b", b=64), func=AF.Copy))
                         if ti % 2 else
                         (lambda h, ti=ti, pb=pb: h.tensor_copy(out=V[:, ti, :, 0:64], in_=ps[pb][:, :384].rearrange("p (a b) -> p a b", b=64))))
                K.barrier()
            wq = K.sb(ph, "na_wq", [128, 8, 384], BF16)
            wo = K.sb(ph, "na_wo", [128, 3, D], BF16)
            BT = K.sb(ph, "na_BT", [128, 72, 128], BF16)
            bt_t = Tok()
            w_t = Tok()
            K.dma("pool", wq[:], win_d[l][:, :, 256:640], [], [w_t], "naw2")
            K.dma("pool", wo[:], wout_d[l][:, 2:5, :], [], [w_t], "naw2")
            for i3 in range(3):
                K.dma("pool", BT[:, 24 * i3:24 * (i3 + 1), :], bt_d[l][:, 24 * i3:24 * (i3 + 1), :], [], [bt_t], "nabt")
            for i3 in range(3):
                K.op("act", [bt_t], [bt_t], lambda h, i3=i3: h.activation(
                    out=BT[:, 24 * i3:24 * (i3 + 1), :], in_=BT[:, 24 * i3:24 * (i3 + 1), :], func=AF.Exp))
            qT = K.sb(ph, "na_qT", [128, 3, 512], BF16)
            qT_t = Tok()
            PT = [K.sb(ph, "na_PT%d" % i, [128, 7, 128], BF16) for i in range(2)]
            PT_t = [Tok() for i in range(2)]
            rec = [K.sb(ph, "na_rec%d" % i, [128, 6], F32) for i in range(2)]
            rec_t = [Tok(), Tok()]
            Osb = [K.sb(ph, "na_O%d" % i, [128, 6, 64], BF16) for i in range(2)]
            O_t = [Tok(), Tok()]
            yna = K.sb(ph, "na_y", [128, 3, 512], BF16)
            yna_t = Tok()
            hcount = 0
            for b in (range(5) if need_ctx else range(4)):
                c0, n = blk_cols(b)
                r = 0 if b < 4 else 1
                qk_norm(wq, w_t, gq, qT, qT_t, 0, b, c0, n, cnt)
                tile_chunks = []
                for qi in range(n // 128):
                    if b < 4:
                        i = b * 4 + qi
                        if i <= 1:
                            js = list(range(0, 4))
                        elif i >= 14:
                            js = list(range(12, 16))
                        else:
                            js = list(range(i - 2, i + 3))
                        chunks = []
                        for j in js:
                            dl = j - i
                            v = (dl + 9) if 2 <= i <= 13 else (dl + 3)
                            chunks.append((j, v))
                        chunks += [(16, None), (17, None)]
                    else:
                        chunks = [(16, None), (17, None)]
                    tile_chunks.append(chunks)
                items = [(qi, hd) for qi in range(n // 128) for hd in range(6)]

                def stage1(k):
                    qi, hd = items[k]
                    chunks = tile_chunks[qi]
                    nch = len(chunks)
                    hp, r0 = hd // 2, (hd % 2) * 64
                    u = (hbase + k) % 2
                    pS = (0 + 2 * u, 1 + 2 * u)
                    for c, (kt, v) in enumerate(chunks):
                        pb = pS[c // 4]
                        pairs = [(kT[r0:r0 + 64, hp, kt * 128:(kt + 1) * 128], qT[r0:r0 + 64, hp, qi * 128:(qi + 1) * 128])]
                        K.mm([kT_t, qT_t, w_t, t_const], [pt[pb]], ps[pb][:, (c % 4) * 128:(c % 4 + 1) * 128], pairs)
                    n0 = min(nch, 4)
                    K.op("act", [pt[pS[0]]], [PT_t[u]], lambda h: h.activation(
                        out=PT[u][:, 0:n0, :], in_=ps[pS[0]][:, 0:n0 * 128].rearrange("p (a b) -> p a b", b=128), func=AF.Exp))
                    if nch > 4:
                        n1 = nch - 4
                        K.op("act", [pt[pS[1]]], [PT_t[u]], lambda h: h.activation(
                            out=PT[u][:, 4:4 + n1, :], in_=ps[pS[1]][:, 0:n1 * 128].rearrange("p (a b) -> p a b", b=128), func=AF.Exp))
                    loc = [v for (kt, v) in chunks if v is not None]
                    if loc:
                        nl, v0 = len(loc), loc[0]
                        K.op("dve", [PT_t[u], bt_t], [PT_t[u]], lambda h: h.tensor_tensor(
                            out=PT[u][:, 0:nl, :], in0=PT[u][:, 0:nl, :], in1=BT[:, hd * 12 + v0:hd * 12 + v0 + nl, :], op=ALU.mult))

                def stage2(k):
                    qi, hd = items[k]
                    chunks = tile_chunks[qi]
                    u = (hbase + k) % 2
                    po = 4 + qi % 2
                    K.mm([PT_t[u], V_t], [pt[po]], ps[po][:, hd * 65:(hd + 1) * 65],
                         [(PT[u][:, c, :], V[:, kt, hd, :]) for c, (kt, v) in enumerate(chunks)])

                def fin_dve(qi):
                    po = 4 + qi % 2
                    oi = qi % 2
                    ov = ps[po][:, 0:390].rearrange("p (a b) -> p a b", b=65)
                    K.op("dve", [pt[po]], [rec_t[oi]], lambda h: h.reciprocal(out=rec[oi][:], in_=ov[:, :, 64]))
                    K.op("dve", [pt[po], rec_t[oi]], [O_t[oi]], lambda h: h.tensor_tensor(
                        out=Osb[oi][:], in0=ov[:, :, 0:64], in1=rec[oi][:].unsqueeze(2).to_broadcast([128, 6, 64]), op=ALU.mult))

                def fin_pe(qi):
                    oi = qi % 2
                    Of = Osb[oi][:].rearrange("p a b -> p (a b)")
                    for hp in range(3):
                        K.op("pe", [O_t[oi], t_const], [ptb], lambda h, hp=hp: h.transpose(
                            psb[:, hp * 128:(hp + 1) * 128], Of[:, hp * 128:(hp + 1) * 128], identb[:]))
                    K.op("act", [ptb], [yna_t], lambda h: h.activation(
                        out=yna[:, :, qi * 128:(qi + 1) * 128], in_=psb[:, 0:384].rearrange("p (a b) -> p a b", b=128), func=AF.Copy))

                hbase = hcount
                nit = len(items)
                for k in range(nit + 2):
                    if k < nit:
                        stage1(k)
                    if 1 <= k <= nit:
                        stage2(k - 1)
                        if items[k - 1][1] == 5:
                            fin_dve(items[k - 1][0])
                    if 2 <= k <= nit + 1 and items[k - 2][1] == 5:
                        fin_pe(items[k - 2][0])
                hcount += nit
                out_proj_update(lambda dt: [(wo[:, hp, dt * 128:(dt + 1) * 128], yna[:, hp, :n]) for hp in range(3)],
                                [yna_t, w_t], b, c0, n, r, [6])
            K.barrier()

    def gla(l, hT, hT_t, need_ctx):
        NB = 256
        QS = 48 ** -0.5
        with contextlib.ExitStack() as ph:
            qr = K.sb(ph, "gl_qr", [128, 2, NT], BF16)
            kr = K.sb(ph, "gl_kr", [128, 2, NT], BF16)
            aT = K.sb(ph, "gl_aT", [32, NT], BF16)
            st_t = [Tok() for i in range(5)]
            aw = K.sb(ph, "gl_aw", [32, 2, 256], BF16)
            nab = K.sb(ph, "gl_nab", [128, 2, 2], F32)
            gon = K.sb(ph, "gl_gon", [96, 1], F32)
            msk = K.sb(ph, "gl_msk", [128, 2, 2, 128], BF16)
            mscan = K.sb(ph, "gl_mscan", [128, NB], BF16)
            epsb = K.sb(ph, "gl_eps", [128, 1], F32)
            oneb = K.sb(ph, "gl_one", [128, 1], F32)
            w_t = Tok()
            K.dma("pool", aw[:], aw_d[l], [], [w_t], "glw")
            K.dma("sp", nab[:], ab_d[l], [], [w_t], "glw")
            K.dma("sp", gon[:], gon_d[l], [], [w_t], "glw")
            K.dma("sp", msk[:], gmask_d, [], [w_t], "glw")
            K.dma("sp", mscan[:], mscan_d, [], [w_t], "glw")
            K.op("dve", [w_t], [w_t], lambda h: h.tensor_scalar(out=nab[:], in0=nab[:], scalar1=-1.0, scalar2=None, op0=ALU.mult))
            K.op("dve", [], [w_t], lambda h: h.memset(epsb[:], EPS))
            K.op("dve", [], [w_t], lambda h: h.memset(oneb[:], 1.0))
            with contextlib.ExitStack() as p0:
                wg = K.sb(p0, "gl_wg", [128, 8, 1024], BF16)
                wga = K.sb(p0, "gl_wga", [128, 8, 32], BF16)
                w0_t = Tok()
                K.dma("pool", wg[:], wg_d[l], [], [w0_t], "glw0")
                K.dma("pool", wga[:], win_d[l][:, :, 2560:2592], [], [w0_t], "glw0")
                cosb = K.sb(p0, "gl_cos", [128, 512], F32)
                sinb = K.sb(p0, "gl_sin", [128, 512], F32)
                cs_t = Tok()
                t1 = [K.sb(p0, "gl_t1%d" % i, [128, 512], F32) for i in range(2)]
                t2 = [K.sb(p0, "gl_t2%d" % i, [128, 512], F32) for i in range(2)]
                r_t = [Tok(), Tok()]
                cnt = 0
                for b5 in range(5):
                    c0, n = blk_cols(b5)
                    lat = b5 < 4
                    if lat:
                        K.dma("sp", cosb[:, :n], cos_d[:, c0:c0 + n], [], [cs_t], "cs")
                        K.dma("sp", sinb[:, :n], sin_d[:, c0:c0 + n], [], [cs_t], "cs")
                    lin_fm(None, wga, 0, 32, hT, hT_t, b5, c0, n, 6, w0_t)
                    K.op("act", [pt[6]], [st_t[b5]], lambda h: h.activation(out=aT[:, c0:c0 + n], in_=ps[6][:32, :n], func=AF.Copy))
                    for hp in range(2):
                        for which in range(2):
                            base = which * 512
                            dst = qr if which == 0 else kr
                            u = cnt % 2
                            cnt += 1
                            pa, pbk = (0, 1) if u == 0 else (2, 3)
                            lin_fm(None, wg, base + hp * 128, 128, hT, hT_t, b5, c0, n, pa, w0_t)
                            if lat:
                                lin_fm(None, wg, base + 256 + hp * 128, 128, hT, hT_t, b5, c0, n, pbk, w0_t)
                                K.op("dve", [pt[pa], cs_t], [r_t[u]], lambda h: h.tensor_tensor(out=t1[u][:, :n], in0=ps[pa][:, :n], in1=cosb[:, :n], op=ALU.mult))
                                K.op("dve", [pt[pbk], cs_t, r_t[u]], [r_t[u]], lambda h: h.tensor_tensor(out=t2[u][:, :n], in0=ps[pbk][:, :n], in1=sinb[:, :n], op=ALU.mult))
                                K.op("pool", [r_t[u]], [st_t[b5]], lambda h: h.tensor_tensor(out=dst[:, hp, c0:c0 + n], in0=t1[u][:, :n], in1=t2[u][:, :n], op=ALU.add))
                            else:
                                K.op("act", [pt[pa]], [st_t[b5]], lambda h: h.activation(out=dst[:, hp, c0:c0 + n], in_=ps[pa][:, :n], func=AF.Copy))
                K.barrier()
            wgv = K.sb(ph, "gl_wgv", [128, 8, 384], BF16)
            wgg = K.sb(ph, "gl_wgg", [128, 8, 384], BF16)
            wo = K.sb(ph, "gl_wo", [96, 4, D], BF16)
            w2_t = Tok()
            K.dma("pool", wgv[:], win_d[l][:, :, 1792:2176], [], [w2_t], "glw2")
            K.dma("pool", wgg[:], win_d[l][:, :, 2176:2560], [], [w2_t], "glw2")
            K.dma("pool", wo[:], woutg_d[l], [], [w2_t], "glw2")
            oF = K.sb(ph, "gl_oF", [96, 4, NT], BF16)
            oF_t = [Tok() for i in range(9)]
            R = []
            for d in range(2):
                r_ = dict(
                    e1=K.sb(ph, "gl_e1_%d" % d, [128, NB], F32),
                    bpos=K.sb(ph, "gl_bp_%d" % d, [128, NB], F32), eb=K.sb(ph, "gl_eb_%d" % d, [128, NB], F32),
                    qin=K.sb(ph, "gl_qin_%d" % d, [128, 2, NB], BF16), kin=K.sb(ph, "gl_kin_%d" % d, [128, 2, NB], BF16),
                    ktok=K.sb(ph, "gl_ktok_%d" % d, [128, 2, 2, 128], BF16), gv=K.sb(ph, "gl_gv_%d" % d, [128, 2, 384], BF16),
                    dec=K.sb(ph, "gl_dec_%d" % d, [128, 2, 2], F32), AM=K.sb(ph, "gl_AM_%d" % d, [128, 4, 128], BF16),
                    S=K.sb(ph, "gl_S_%d" % d, [128, 2, 192], F32), Sb=K.sb(ph, "gl_Sb_%d" % d, [128, 2, 192], BF16),
                    osum=K.sb(ph, "gl_osum_%d" % d, [96, 4, NB], F32),
                    sg=K.sb(ph, "gl_sg_%d" % d, [96, 4, NB], BF16), sq=K.sb(ph, "gl_sq_%d" % d, [96, NB], BF16),
                    sg_t=Tok(), n_t=Tok(),
                    g_t=Tok(), qk_t=Tok(), ktok_t=Tok(), gv_t=Tok(), dec_t=Tok(), AM_t=Tok(), S_t=Tok(), Sb_t=Tok(), os_t=Tok(),
                    BA=3 * d, BB=3 * d + 1, BO=3 * d + 2)
                R.append(r_)

            def block_prep(d, b):
                r_ = R[d]
                c0 = b * NB if b < 8 else T
                n = NB
                b5 = min(c0 // 512, 4)
                BB = r_["BB"]
                e1, bpos, eb = r_["e1"], r_["bpos"], r_["eb"]
                lsb = e1
                enb = eb
                gt = r_["g_t"]
                for hp in range(2):
                    K.mm([st_t[b5], w_t], [pt[BB]], ps[BB][:, :n], [(aw[:, d, hp * 128:(hp + 1) * 128], aT[:, c0:c0 + n])])
                    yield
                    K.op("act", [pt[BB], w_t], [gt], lambda h: h.activation(
                        out=e1[:, :n], in_=ps[BB][:, :n], func=AF.Exp, bias=nab[:, d, hp:hp + 1], scale=-1.0))
                    yield
                    K.op("act", [gt, w_t], [gt], lambda h: h.activation(out=lsb[:, :n], in_=e1[:, :n], func=AF.Ln, bias=oneb[:], scale=1.0))
                    yield
                    K.op("dve", [gt, w_t], [gt], lambda h: h.tensor_tensor_scan(
                        out=bpos[:, :n], data0=mscan[:, :n], data1=lsb[:, :n], initial=0.0, op0=ALU.mult, op1=ALU.add))
                    yield
                    bsel = bpos
                    if d == 1:
                        K.op("dve", [gt], [gt], lambda h: h.tensor_tensor(out=e1[:, :n], in0=lsb[:, :n], in1=bpos[:, :n], op=ALU.subtract))
                        yield
                        K.op("dve", [gt], [gt], lambda h: h.tensor_tensor(
                            out=e1[:, :n].rearrange("p (a b) -> p a b", b=128), in0=e1[:, :n].rearrange("p (a b) -> p a b", b=128),
                            in1=bpos[:, :n].rearrange("p (a b) -> p a b", b=128)[:, :, 127:128].to_broadcast([128, n // 128, 128]), op=ALU.add))
                        yield
                        bsel = e1
                    K.op("act", [gt], [gt], lambda h, bsel=bsel: h.activation(out=eb[:, :n], in_=bsel[:, :n], func=AF.Exp, scale=-1.0 / 16))
                    yield
                    sel = 127 if d == 0 else 0
                    K.op("dve", [gt], [r_["dec_t"]], lambda h: h.tensor_copy(
                        out=r_["dec"][:, hp, :], in_=eb[:, :n].rearrange("p (a b) -> p a b", b=128)[:, :, sel]))
                    yield
                    K.op("dve", [gt, st_t[b5]], [r_["qk_t"]], lambda h: h.scalar_tensor_tensor(
                        out=r_["qin"][:, hp, :n], in0=qr[:, hp, c0:c0 + n], scalar=QS, in1=eb[:, :n], op0=ALU.mult, op1=ALU.mult))
                    yield
                    K.op("act", [gt, r_["dec_t"], r_["qk_t"]], [gt], lambda h, bsel=bsel: h.activation(out=enb[:, :n], in_=bsel[:, :n], func=AF.Exp, scale=1.0 / 16))
                    yield
                    K.op("dve", [gt, st_t[b5]], [r_["qk_t"]], lambda h: h.tensor_tensor(
                        out=r_["kin"][:, hp, :n], in0=kr[:, hp, c0:c0 + n], in1=enb[:, :n], op=ALU.mult))
                    yield
                    for tt in range(2):
                        K.op("pe", [r_["qk_t"], t_const], [ptb], lambda h, tt=tt: h.transpose(
                            psb[:, (d * 2 + tt) * 128:(d * 2 + tt + 1) * 128], r_["kin"][:, hp, tt * 128:(tt + 1) * 128], identb[:]))
                        yield
                    K.op("act", [ptb], [r_["ktok_t"]], lambda h: h.activation(
                        out=r_["ktok"][:, :, hp, :], in_=psb[:, d * 256:(d + 1) * 256].rearrange("p (t f) -> p t f", t=2), func=AF.Copy))
                    yield
                for tt in range(2):
                    K.mm([w2_t, hT_t[b5]], [pt[BB]], ps[BB][:, :384],
                         [(hT[:, kc, c0 + tt * 128:c0 + (tt + 1) * 128], wgv[:, kc, :]) for kc in range(8)])
                    yield
                    K.op("act", [pt[BB]], [r_["gv_t"]], lambda h, tt=tt: h.activation(out=r_["gv"][:, tt, :], in_=ps[BB][:, :384], func=AF.Copy))
                    yield

            def scan_tile(d, b, tt, want_o, second):
                r_ = R[d]
                c0 = b * NB if b < 8 else T
                tc0 = tt * 128
                BA, BB, BO = r_["BA"], r_["BB"], r_["BO"]
                qin, kin, gv, Sb, S, AM, dec, ktok = r_["qin"], r_["kin"], r_["gv"], r_["Sb"], r_["S"], r_["AM"], r_["dec"], r_["ktok"]
                if want_o:
                    for hd in (0, 2, 1, 3):
                        hp, r0 = hd // 2, (hd % 2) * 64
                        pa = BA if hd % 2 == 0 else BB
                        K.mm([r_["qk_t"]], [pt[pa]], ps[pa][:, (hd // 2) * 128:(hd // 2 + 1) * 128],
                             [(kin[r0:r0 + 48, hp, tc0:tc0 + 128], qin[r0:r0 + 48, hp, tc0:tc0 + 128])])
                        yield
                    for par, pa in ((0, BA), (1, BB)):
                        K.op("dve", [pt[pa], w_t], [r_["AM_t"]], lambda h, par=par, pa=pa: h.tensor_tensor(
                            out=AM[:, par:4:2, :], in0=ps[pa][:, 0:256].rearrange("p (a b) -> p a b", b=128),
                            in1=msk[:, d, :, :], op=ALU.mult))
                        yield
                    for hd in range(4):
                        hp, r0 = hd // 2, (hd % 2) * 64
                        K.mm([r_["AM_t"], r_["gv_t"], r_["Sb_t"], r_["qk_t"]], [pt[BO]], ps[BO][:96, hd * 128:(hd + 1) * 128],
                             [(gv[:, tt, hd * 96:(hd + 1) * 96], AM[:, hd, :]),
                              (Sb[r0:r0 + 48, hp, (hd % 2) * 96:(hd % 2 + 1) * 96], qin[r0:r0 + 48, hp, tc0:tc0 + 128])])
                        yield
                    ov = ps[BO][:96, :].rearrange("p (a b) -> p a b", b=128)
                    if not second:
                        K.op("act", [pt[BO]], [oF_t[b]], lambda h: h.activation(
                            out=oF[:, :, c0 + tc0:c0 + tc0 + 128], in_=ov, func=AF.Copy))
                    else:
                        K.op("dve", [pt[BO], oF_t[b]], [r_["os_t"]], lambda h: h.tensor_tensor(
                            out=r_["osum"][:, :, tc0:tc0 + 128], in0=ov, in1=oF[:, :, c0 + tc0:c0 + tc0 + 128], op=ALU.add))
                    yield
                for hp in range(2):
                    K.mm([r_["ktok_t"], r_["gv_t"]], [pt[BA]], ps[BA][:, hp * 192:(hp + 1) * 192],
                         [(ktok[:, tt, hp, :], gv[:, tt, hp * 192:(hp + 1) * 192])])
                    yield
                for hp in range(2):
                    K.op("act", [r_["S_t"], r_["dec_t"]], [r_["S_t"]], lambda h, hp=hp: h.activation(
                        out=S[:, hp, :], in_=S[:, hp, :], func=AF.Identity, scale=dec[:, hp, tt:tt + 1]))
                    yield
                    K.op("dve", [pt[BA], r_["S_t"], r_["dec_t"]], [r_["S_t"]], lambda h, hp=hp: h.scalar_tensor_tensor(
                        out=S[:, hp, :], in0=ps[BA][:, hp * 192:(hp + 1) * 192], scalar=dec[:, hp, tt:tt + 1], in1=S[:, hp, :],
                        op0=ALU.mult, op1=ALU.add))
                    yield
                K.op("act", [r_["S_t"]], [r_["Sb_t"]], lambda h: h.activation(out=Sb[:], in_=S[:], func=AF.Copy))
                yield

            def finalize(d, b):
                r_ = R[d]
                c0 = b * NB if b < 8 else T
                n = NB
                b5 = min(c0 // 512, 4)
                r = 0 if b < 8 else 1
                osum = r_["osum"]
                BA, BB, BO = r_["BA"], r_["BB"], r_["BO"]
                sg, sq = r_["sg"], r_["sq"]
                rt = r_["e1"]
                sg_t, n_t, gt = r_["sg_t"], r_["n_t"], r_["g_t"]
                for hd in range(4):
                    lin_fm(None, wgg, hd * 96, 96, hT, hT_t, b5, c0, n, BB, w2_t)
                    yield
                    K.op("act", [pt[BB]], [sg_t], lambda h, hd=hd: h.activation(out=sg[:, hd, :n], in_=ps[BB][:96, :n], func=AF.Silu))
                    yield
                for hd in range(4):
                    K.op("act", [r_["os_t"]], [n_t], lambda h, hd=hd: h.activation(out=sq[:, :n], in_=osum[:, hd, :n], func=AF.Square))
                    yield
                    K.mm([n_t, t_const], [pt[BB]], ps[BB][:96, :n], [(onesb[:96, :96], sq[:, :n])])
                    yield
                    K.op("act", [pt[BB], w_t], [n_t, gt], lambda h: h.activation(
                        out=rt[:96, :n], in_=ps[BB][:96, :n], func=AF.Ln, bias=epsb[:96, :], scale=1.0 / 96))
                    yield
                    K.op("act", [n_t], [n_t, gt], lambda h: h.activation(out=rt[:96, :n], in_=rt[:96, :n], func=AF.Exp, scale=-0.5))
                    yield
                    K.op("dve", [n_t, gt, w_t], [r_["os_t"]], lambda h, hd=hd: h.scalar_tensor_tensor(
                        out=osum[:, hd, :n], in0=osum[:, hd, :n], scalar=gon[:, 0:1], in1=rt[:96, :n], op0=ALU.mult, op1=ALU.mult))
                    yield
                    K.op("dve", [r_["os_t"], sg_t], [sg_t], lambda h, hd=hd: h.tensor_tensor(
                        out=sg[:, hd, :n], in0=osum[:, hd, :n], in1=sg[:, hd, :n], op=ALU.mult))
                    yield
                for dt in range(8):
                    pc = BA if dt % 2 == 0 else BO
                    K.mm([sg_t, w2_t], [pt[pc]], ps[pc][:, :n], [(wo[:, hd, dt * 128:(dt + 1) * 128], sg[:, hd, :n]) for hd in range(4)])
                    yield
                    K.op("dve", [pt[pc], t_mod, xT_t[b5]], [xT_t[b5]], lambda h, dt=dt, pc=pc: h.scalar_tensor_tensor(
                        out=xT[:, dt, c0:c0 + n], in0=ps[pc][:, :n], scalar=modT[:, 16 + dt, r:r + 1],
                        in1=xT[:, dt, c0:c0 + n], op0=ALU.mult, op1=ALU.add))
                    yield

            def dir_gen(d):
                r_ = R[d]
                K.op("dve", [], [r_["S_t"]], lambda h: h.memset(r_["S"][:], 0.0))
                K.op("dve", [], [r_["Sb_t"]], lambda h: h.memset(r_["Sb"][:], 0.0))
                order = [8] + (list(range(8)) if d == 0 else list(range(7, -1, -1)))
                for b in order:
                    want_o = (b < 8) or need_ctx
                    if b == 8:
                        second = (d == 1)
                    else:
                        second = (b >= 4) if d == 0 else (b <= 3)
                    yield from block_prep(d, b)
                    for tt in ((0, 1) if d == 0 else (1, 0)):
                        yield from scan_tile(d, b, tt, want_o, second)
                    if want_o and second:
                        yield from finalize(d, b)
                    yield "STEP"

            def run_step(gens):
                active = list(gens)
                while active:
                    for g in list(active):
                        if next(g) == "STEP":
                            active.remove(g)
            gF, gB = dir_gen(0), dir_gen(1)
            run_step([gF])
            run_step([gB])
            for s_ in range(8):
                run_step([gF, gB])
            K.barrier()

    def mixers(l, hT, hT_t, need_ctx):
        if "nofn" not in debug:
            fnet(l, hT, hT_t, need_ctx)
        if "nona" not in debug:
            natt(l, hT, hT_t, need_ctx)
        if "nogla" not in debug:
            gla(l, hT, hT_t, need_ctx)

    for l in range(2):
        need_ctx = (l == 0)
        with contextlib.ExitStack() as ph:
            K.dma("sp", adab[:], adab_d[l], [], [t_small], "s")
            K.dma("sp", nmix[:], nmix_d[l], [], [t_small], "s")
            K.dma("sp", nffn[:], nffn_d[l], [], [t_small], "s")
            wsl = [K.sb(ph, "adaw%d" % i, [128, 8, 1024], BF16) for i in range(2)]
            wsl_t = [Tok("adaw%d" % i) for i in range(2)]
            for sec in range(6):
                s = sec % 2
                K.dma("pool", wsl[s][:], adaw_d[l][:, :, sec * 1024:(sec + 1) * 1024], [], [wsl_t[s]], "adaw")
                pb = sec % 2
                for mt in range(8):
                    j = sec * 8 + mt
                    K.mm([wsl_t[s], t_small], [pt[pb]], ps[pb][:, mt * 2:mt * 2 + 2],
                         [(wsl[s][:, kc, mt * 128:(mt + 1) * 128], scb[:, kc, :]) for kc in range(8)])
                K.op("dve", [pt[pb], t_small], [t_mod],
                     lambda h, sec=sec, pb=pb: h.tensor_tensor(
                         out=modT[:, sec * 8:(sec + 1) * 8, :],
                         in0=ps[pb][:, 0:16].rearrange("p (a b) -> p a b", b=2),
                         in1=adab[:, sec * 8:(sec + 1) * 8].unsqueeze(2).to_broadcast([128, 8, 2]),
                         op=ALU.add))
            for (dst, gain, sec) in ((amix, nmix, 1), (affn, nffn, 4)):
                K.op("dve", [t_mod, t_small], [t_mod],
                     lambda h, dst=dst, gain=gain, sec=sec: h.scalar_tensor_tensor(
                         out=dst[:], in0=modT[:, sec * 8:(sec + 1) * 8, :], scalar=1.0,
                         in1=gain[:].unsqueeze(2).to_broadcast([128, 8, 2]),
                         op0=ALU.add, op1=ALU.mult))
            K.barrier()
        if "mod" in debug and l == 0:
            dbg["modT"] = (modT, [128, 48, 2], F32)
            break
        with contextlib.ExitStack() as lay:
            hT = K.sb(lay, "hT", [128, 8, NT], BF16)
            hT_t = [Tok("hT%d" % i) for i in range(5)]
            if "ffn" not in debug:
                norm_mod(l, "mix", hT, hT_t, range(5))
                mixers(l, hT, hT_t, need_ctx)
            blocks = range(5) if need_ctx else range(4)
            if "noffn" not in debug:
                norm_mod(l, "ffn", hT, hT_t, blocks)
                ffn(l, hT, hT_t, blocks)
            K.barrier()
        if "ffn" in debug or "l0" in debug:
            break

    t_out = Tok("out")
    for name, (buf, shape, dt) in dbg.items():
        dd = K.dram("dbg_" + name, shape, dt, kind="ExternalOutput")
        K.barrier()
        K.dma("sp", dd, buf[:], [], [t_out], "out")
        K.outs["dbg_" + name] = shape

    with contextlib.ExitStack() as ph:
        xo = [K.sb(ph, "xo%d" % i, [128, D], F32) for i in range(3)]
        xo_t = [Tok("xo%d" % i) for i in range(3)]
        for ti in range(16):
            s = ti % 3
            b = ti // 4
            for half in range(2):
                pb = (ti * 2 + half) % 7
                for j in range(4):
                    kc = half * 4 + j
                    K.op("pe", [xT_t[b], t_const], [pt[pb]],
                         lambda h, kc=kc, j=j, pb=pb, ti=ti: h.transpose(
                             ps[pb][:, j * 128:(j + 1) * 128], xT[:, kc, ti * 128:(ti + 1) * 128], identf[:]))
                dst = xo[s][:, half * 512:(half + 1) * 512]
                if half == 0:
                    K.op("dve", [pt[pb]], [xo_t[s]], lambda h, dst=dst, pb=pb: h.tensor_copy(out=dst, in_=ps[pb][:, :]))
                else:
                    K.op("act", [pt[pb]], [xo_t[s]], lambda h, dst=dst, pb=pb: h.activation(out=dst, in_=ps[pb][:, :], func=AF.Copy))
            K.dma("sp", out_d[ti * 128:(ti + 1) * 128, :], xo[s][:], [xo_t[s]], [t_out], "out")
        K.barrier()
    S = t_out.dsem["sp"]
    nc.sync.wait_ge(S.sem, S.count * 16)
    es.close()
    return K


_CONSTS = {}


def _consts():
    if _CONSTS:
        return _CONSTS
    bf = ml_dtypes.bfloat16
    t = np.arange(T, dtype=np.int64)
    ang = 2.0 * np.pi * ((t[:, None] * t[None, :]) % T).astype(np.float64) / T
    _CONSTS["c_ct"] = np.ascontiguousarray(np.cos(ang).reshape(16, 128, T).transpose(1, 0, 2)).astype(bf)
    _CONSTS["c_st"] = np.ascontiguousarray(np.sin(ang).reshape(16, 128, T).transpose(1, 0, 2)).astype(bf)
    t2 = np.arange(NCTX, dtype=np.int64)
    ang2 = 2.0 * np.pi * ((t2[:, None] * t2[None, :]) % NCTX).astype(np.float64) / NCTX
    _CONSTS["c_c256"] = np.ascontiguousarray(np.cos(ang2).reshape(2, 128, NCTX).transpose(1, 0, 2)).astype(bf)
    _CONSTS["c_s256"] = np.ascontiguousarray(np.sin(ang2).reshape(2, 128, NCTX).transpose(1, 0, 2)).astype(bf)
    g = np.arange(64)
    a64 = 2.0 * np.pi * ((g[:, None] * g[None, :]) % 64) / 64.0
    c64 = np.zeros((128, 4, 128))
    for blk in range(2):
        sl = slice(64 * blk, 64 * blk + 64)
        c64[sl, 0, sl] = np.cos(a64) / np.sqrt(T * 64.0)
        c64[sl, 1, sl] = -np.sin(a64) / np.sqrt(T * 64.0)
        c64[sl, 2, sl] = np.cos(a64) / np.sqrt(NCTX * 64.0)
        c64[sl, 3, sl] = -np.sin(a64) / np.sqrt(NCTX * 64.0)
    _CONSTS["c_c64"] = c64.astype(bf)
    cosT = np.zeros((128, T), np.float32)
    sinT = np.zeros((128, T), np.float32)
    row = (t // 64).astype(np.float64)
    col = (t % 64).astype(np.float64)
    for p in range(128):
        d = p % 64
        if d >= 48:
            continue
        within = d % 24
        j = within % 12
        inv = 10000.0 ** (-j / 12.0)
        pos = row if d < 24 else col
        cosT[p] = np.cos(pos * inv)
        sinT[p] = np.sin(pos * inv) * (-1.0 if within < 12 else 1.0)
    _CONSTS["c_cos"] = cosT
    _CONSTS["c_sin"] = sinT
    s_ = np.arange(128)
    gm = np.zeros((128, 2, 2, 128), np.float32)
    gm[:, 0, :, :] = (s_[:, None] <= s_[None, :])[:, None, :]
    gm[:, 1, :, :] = (s_[:, None] >= s_[None, :])[:, None, :]
    _CONSTS["c_gmask"] = gm.astype(bf)
    ms = np.ones((128, 256), np.float32)
    ms[:, 0] = 0.0
    ms[:, 128] = 0.0
    _CONSTS["c_mscan"] = ms.astype(bf)
    return _CONSTS


def _host_prep(inputs):
    f = lambda a: np.ascontiguousarray(np.asarray(a, dtype=np.float32))
    x, c, ctx, c_ctx = f(inputs["x"]), f(inputs["c"]), f(inputs["ctx"]), f(inputs["c_ctx"])
    shared = {}
    shared["ada_w"] = f(f(inputs["ada_w"]).reshape(2, 8, 128, 6144).transpose(0, 2, 1, 3))
    shared["ada_b"] = f(f(inputs["ada_b"]).reshape(2, 48, 128).transpose(0, 2, 1))
    shared["norm_mix"] = f(f(inputs["norm_mix"]).reshape(2, 8, 128).transpose(0, 2, 1))
    shared["norm_ffn"] = f(f(inputs["norm_ffn"]).reshape(2, 8, 128).transpose(0, 2, 1))
    shared["ident"] = np.eye(128, dtype=np.float32)
    shared["ffn_w1"] = f(f(inputs["ffn_w1"]).reshape(2, 8, 128, DFF).transpose(0, 2, 1, 3))
    shared["ffn_w3"] = f(f(inputs["ffn_w3"]).reshape(2, 8, 128, DFF).transpose(0, 2, 1, 3))
    shared["ffn_w2"] = f(f(inputs["ffn_w2"]).reshape(2, 22, 128, D).transpose(0, 2, 1, 3))
    w_in = f(inputs["w_in"])
    shared["w_in"] = f(w_in.reshape(2, 8, 128, NIN).transpose(0, 2, 1, 3))
    wg = np.zeros((2, D, 1024), np.float32)
    dd = np.arange(48)
    partner = np.where((dd % 24) < 12, dd + 12, dd - 12)
    for hh in range(4):
        wg[:, :, 64 * hh:64 * hh + 48] = w_in[:, :, 1408 + 48 * hh + dd]
        wg[:, :, 256 + 64 * hh:256 + 64 * hh + 48] = w_in[:, :, 1408 + 48 * hh + partner]
        wg[:, :, 512 + 64 * hh:512 + 64 * hh + 48] = w_in[:, :, 1600 + 48 * hh + dd]
        wg[:, :, 768 + 64 * hh:768 + 64 * hh + 48] = w_in[:, :, 1600 + 48 * hh + partner]
    shared["w_g"] = f(wg.reshape(2, 8, 128, 1024).transpose(0, 2, 1, 3))
    shared["fnet_w"] = f(f(inputs["fnet_w"]).reshape(2, 2, 128, 256).transpose(0, 2, 1, 3))
    w_out = f(inputs["w_out"])
    shared["w_out"] = f(w_out.reshape(2, 8, 128, D).transpose(0, 2, 1, 3))
    shared["w_out_g"] = f(w_out[:, 640:, :].reshape(2, 4, 96, D).transpose(0, 2, 1, 3))
    shared["na_q"] = f(np.tile(f(inputs["na_q_norm"]), (1, 2))[:, :, None])
    shared["na_k"] = f(np.tile(f(inputs["na_k_norm"]), (1, 2))[:, :, None])
    rpb = f(inputs["na_rpb"])
    pk = np.arange(128)
    a_, kc_ = pk // 64, pk % 64
    b_, c_ = pk // 64, pk % 64
    dcm = np.clip(kc_[:, None] - c_[None, :], -15, 15) + 15
    cq0 = np.clip(c_ - 8, 0, 48)
    col_ok = (kc_[:, None] >= cq0[None, :]) & (kc_[:, None] < cq0[None, :] + 16)
    bt = np.full((2, 128, 72, 128), NEG, np.float32)
    ents = [(dl, 0) for dl in range(-3, 4)] + [(-2, 1), (-1, 0), (0, 0), (1, 0), (2, 1)]
    for e, (dl, msk_) in enumerate(ents):
        drm = 2 * dl + a_[:, None] - b_[None, :] + 7
        ok = col_ok & (drm >= 0) & (drm <= 14)
        if msk_ and dl == -2:
            ok = ok & ((2 * dl + a_[:, None]) >= (-4 + b_[None, :]))
        if msk_ and dl == 2:
            ok = ok & ((2 * dl + a_[:, None]) <= (3 + b_[None, :]))
        drc = np.clip(drm, 0, 14)
        for hh in range(6):
            vals = rpb[:, hh][:, drc, dcm]
            bt[:, :, hh * 12 + e, :] = np.where(ok[None], vals, NEG)
    shared["na_bt"] = bt
    aw = np.zeros((2, 32, 2, 256), np.float32)
    ab = np.zeros((2, 128, 2, 2), np.float32)
    gaw = f(inputs["gla_alpha_w"]); gab = f(inputs["gla_alpha_b"])
    for dr_ in range(2):
        for hh in range(4):
            aw[:, 16 * dr_:16 * dr_ + 16, dr_, 64 * hh:64 * hh + 48] = gaw[:, dr_, :, 48 * hh:48 * hh + 48]
            ab[:, 64 * (hh % 2):64 * (hh % 2) + 48, dr_, hh // 2] = gab[:, dr_, 48 * hh:48 * hh + 48]
    shared["gla_aw"] = aw
    shared["gla_ab"] = ab
    shared["gla_gon"] = f(f(inputs["gla_o_norm"])[:, :, None])
    shared.update(_consts())
    per_core = []
    for b in range(8):
        m = dict(shared)
        m["x"] = x[b]
        m["ctx"] = ctx[b]
        ccv = np.stack([c[b], c_ctx], axis=-1)
        m["cc"] = f(ccv.reshape(8, 128, 2).transpose(1, 0, 2))
        per_core.append(m)
    return per_core


_DEBUG = tuple(x for x in os.environ.get("KDEBUG", "").split(",") if x)
LAST = {}


def kernel(**inputs):
    in_maps = _host_prep(inputs)
    K = build_program(debug=_DEBUG)
    ncores = int(os.environ.get("KCORES", "8"))
    res = run_bass_kernel_spmd(K.nc, in_maps[:ncores], core_ids=list(range(ncores)))
    LAST["res"] = res
    out = np.stack([np.asarray(r["out"]) for r in res.results], axis=0).astype(np.float32)
    return out
```

```python
import contextlib
import os
import numpy as np
import ml_dtypes
import concourse.bass as bass
import concourse.mybir as mybir
from concourse.bass_utils import run_bass_kernel_spmd

F32 = mybir.dt.float32
BF16 = mybir.dt.bfloat16
AF = mybir.ActivationFunctionType
ALU = mybir.AluOpType

D = 1024
T = 2048
NCTX = 256
NT = T + NCTX
DFF = 2816
NIN = 2592
EPS = 1e-6
NEG = -30000.0
SAME_ENGINE_RAW = True


class Tok:
    __slots__ = ("w", "r", "dsem", "name")

    def __init__(self, name=""):
        self.w = []
        self.r = {}
        self.dsem = None
        self.name = name


class Src:
    def __init__(self, name, sem, unit, h=None):
        self.name = name
        self.sem = sem
        self.unit = unit
        self.h = h
        self.count = 0
        self.seen = {}


class Ctx:
    def __init__(self):
        self.nc = bass.Bass("TRN2", target_bir_lowering=False)
        nc = self.nc
        self.es = contextlib.ExitStack()
        self.engs = {}
        for nm, h in (("pe", nc.tensor), ("act", nc.scalar), ("dve", nc.vector),
                      ("pool", nc.gpsimd), ("sp", nc.sync)):
            sem = self.es.enter_context(nc.semaphore("sem_" + nm))
            self.engs[nm] = Src(nm, sem, 1, h)
        self.dsems = []
        self.outs = {}
        self.n_inst = 0

    def dram(self, name, shape, dt, kind="ExternalInput"):
        return self.nc.dram_tensor(name, list(shape), dt, kind=kind).ap()

    def sb(self, stack, name, shape, dt):
        self.n_sb = getattr(self, "n_sb", 0) + 1
        return stack.enter_context(self.nc.sbuf_tensor("sb%d_%s" % (self.n_sb, name), list(shape), dt))

    def new_dsem(self, name):
        sem = self.es.enter_context(self.nc.semaphore("d_" + name + str(len(self.dsems))))
        s = Src("dma_" + name, sem, 16)
        self.dsems.append(s)
        return s

    def _wait_deps(self, E, reads, writes):
        deps = {}
        for t in reads:
            for (s, n) in t.w:
                deps[s] = max(deps.get(s, 0), n)
        for t in writes:
            for (s, n) in t.w:
                deps[s] = max(deps.get(s, 0), n)
            for s, n in t.r.items():
                deps[s] = max(deps.get(s, 0), n)
        for s, n in deps.items():
            if s is E and (E.name == "pe" or not SAME_ENGINE_RAW):
                continue
            if E.seen.get(s, 0) < n:
                E.h.wait_ge(s.sem, n * s.unit)
                E.seen[s] = n

    def op(self, eng, reads, writes, emit):
        E = self.engs[eng]
        self._wait_deps(E, reads, writes)
        ins = emit(E.h)
        E.count += 1
        ins.then_inc(E.sem, 1)
        self.n_inst += 1
        for t in reads:
            t.r[E] = E.count
        for t in writes:
            t.w = [(E, E.count)]
            t.r = {}
        return ins

    def mm(self, reads, writes, out, pairs, first=True, last=True):
        E = self.engs["pe"]
        self._wait_deps(E, reads, writes)
        n = len(pairs)
        ins = None
        for i, (l, r) in enumerate(pairs):
            ins = E.h.matmul(out, l, r, start=(first and i == 0), stop=(last and i == n - 1))
            self.n_inst += 1
        E.count += 1
        ins.then_inc(E.sem, 1)
        for t in reads:
            t.r[E] = E.count
        for t in writes:
            t.w = [(E, E.count)]
            t.r = {}

    def transpose(self, reads, writes, out, in_, ident):
        return self.op("pe", reads, writes, lambda h: h.transpose(out, in_, ident))

    def dma(self, q, out, in_, reads, writes, name="x"):
        E = self.engs[q]
        self._wait_deps(E, reads, writes)
        wt = writes[0]
        if wt.dsem is None:
            wt.dsem = {}
        if q not in wt.dsem:
            wt.dsem[q] = self.new_dsem(name + q)
        S = wt.dsem[q]
        ins = E.h.dma_start(out=out, in_=in_)
        S.count += 1
        ins.then_inc(S.sem, 16)
        self.n_inst += 1
        for t in reads:
            t.r[S] = S.count
        for t in writes:
            t.w = [(s_, n_) for (s_, n_) in t.w if (s_.unit == 16 and s_ is not S)] + [(S, S.count)]
            t.r = {}

    def barrier(self):
        allsrc = list(self.engs.values()) + self.dsems
        for E in self.engs.values():
            for s in allsrc:
                if s is E or s.count == 0:
                    continue
                if E.seen.get(s, 0) < s.count:
                    E.h.wait_ge(s.sem, s.count * s.unit)
                    E.seen[s] = s.count


def build_program(debug=()):
    K = Ctx()
    nc = K.nc
    es = K.es
    dbg = {}

    x_d = K.dram("x", [T, D], F32)
    ctx_d = K.dram("ctx", [NCTX, D], F32)
    cc_d = K.dram("cc", [128, 8, 2], F32)
    adaw_d = K.dram("ada_w", [2, 128, 8, 6144], F32)
    adab_d = K.dram("ada_b", [2, 128, 48], F32)
    nmix_d = K.dram("norm_mix", [2, 128, 8], F32)
    nffn_d = K.dram("norm_ffn", [2, 128, 8], F32)
    out_d = K.dram("out", [T, D], F32, kind="ExternalOutput")
    ident_d = K.dram("ident", [128, 128], F32)
    w1_d = K.dram("ffn_w1", [2, 128, 8, DFF], F32)
    win_d = K.dram("w_in", [2, 128, 8, NIN], F32)
    wg_d = K.dram("w_g", [2, 128, 8, 1024], F32)
    fnetw_d = K.dram("fnet_w", [2, 128, 2, 256], F32)
    wout_d = K.dram("w_out", [2, 128, 8, D], F32)
    woutg_d = K.dram("w_out_g", [2, 96, 4, D], F32)
    naq_d = K.dram("na_q", [2, 128, 1], F32)
    nak_d = K.dram("na_k", [2, 128, 1], F32)
    bt_d = K.dram("na_bt", [2, 128, 72, 128], F32)
    aw_d = K.dram("gla_aw", [2, 32, 2, 256], F32)
    ab_d = K.dram("gla_ab", [2, 128, 2, 2], F32)
    gon_d = K.dram("gla_gon", [2, 96, 1], F32)
    ct_d = K.dram("c_ct", [128, 16, T], BF16)
    st_d = K.dram("c_st", [128, 16, T], BF16)
    c256_d = K.dram("c_c256", [128, 2, 256], BF16)
    s256_d = K.dram("c_s256", [128, 2, 256], BF16)
    c64_d = K.dram("c_c64", [128, 4, 128], BF16)
    cos_d = K.dram("c_cos", [128, T], F32)
    sin_d = K.dram("c_sin", [128, T], F32)
    gmask_d = K.dram("c_gmask", [128, 2, 2, 128], BF16)
    mscan_d = K.dram("c_mscan", [128, 256], BF16)
    w3_d = K.dram("ffn_w3", [2, 128, 8, DFF], F32)
    w2_d = K.dram("ffn_w2", [2, 128, 22, D], F32)

    top = es
    xT = K.sb(top, "xT", [128, 8, NT], F32)
    xT_t = [Tok("xT%d" % i) for i in range(5)]
    identf = K.sb(top, "identf", [128, 128], F32)
    identb = K.sb(top, "identb", [128, 128], BF16)
    onesb = K.sb(top, "onesb", [128, 128], BF16)
    cc = K.sb(top, "cc", [128, 8, 2], F32)
    scb = K.sb(top, "scb", [128, 8, 2], BF16)
    class Cur:
        def __init__(self):
            self.t = None

        def __getitem__(self, k):
            return self.t[k]
    modT = Cur()
    modTs = [K.sb(top, "modT%d" % i, [128, 48, 2], F32) for i in range(2)]
    amix = Cur()
    affn = Cur()
    amixs = [K.sb(top, "amix%d" % i, [128, 8, 2], F32) for i in range(2)]
    affns = [K.sb(top, "affn%d" % i, [128, 8, 2], F32) for i in range(2)]
    adabs = [K.sb(top, "adab%d" % i, [128, 48], F32) for i in range(2)]
    nmixs = [K.sb(top, "nmix%d" % i, [128, 8], F32) for i in range(2)]
    nffns = [K.sb(top, "nffn%d" % i, [128, 8], F32) for i in range(2)]
    t_const = Tok("const")
    t_mod = Tok("mod")
    t_small = Tok("small")

    ps = [es.enter_context(nc.psum_tensor("ps%d" % i, [128, 512], F32)) for i in range(7)]
    pt = [Tok("ps%d" % i) for i in range(7)]
    psb = es.enter_context(nc.psum_tensor("psb", [128, 1024], BF16))
    ptb = Tok("psb")

    def blk_cols(b):
        return (b * 512, 512) if b < 4 else (T, NCTX)

    K.dma("sp", identf[:], ident_d, [], [t_const], "c")
    K.op("dve", [t_const], [t_const], lambda h: h.tensor_copy(out=identb[:], in_=identf[:]))
    K.op("dve", [], [t_const], lambda h: h.memset(onesb[:], 1.0))
    K.dma("sp", cc[:], cc_d, [], [t_small], "s")
    K.op("act", [t_small], [t_small], lambda h: h.activation(out=scb[:], in_=cc[:], func=AF.Silu))

    def make_ada(l, stack):
        t_sm = Tok("adasmall")
        K.dma("sp", adabs[l][:], adab_d[l], [], [t_sm], "s")
        K.dma("sp", nmixs[l][:], nmix_d[l], [], [t_sm], "s")
        K.dma("sp", nffns[l][:], nffn_d[l], [], [t_sm], "s")
        wsl = [K.sb(stack, "adaw%d" % i, [128, 8, 512], BF16) for i in range(2)]
        wsl_t = [Tok("adaw%d" % i) for i in range(2)]

        def dma(hs):
            K.dma("pool", wsl[hs % 2][:], adaw_d[l][:, :, hs * 512:(hs + 1) * 512], [], [wsl_t[hs % 2]], "adaw")

        def sec(hs):
            s = hs % 2
            if hs == 0:
                dma(0)
            if hs + 1 < 12:
                dma(hs + 1)
            for mt in range(4):
                K.mm([wsl_t[s], t_small], [pt[6]], ps[6][:, mt * 2:mt * 2 + 2],
                     [(wsl[s][:, kc, mt * 128:(mt + 1) * 128], scb[:, kc, :]) for kc in range(8)])
            K.op("dve", [pt[6], t_sm], [t_mod], lambda h: h.tensor_tensor(
                out=modTs[l][:, hs * 4:(hs + 1) * 4, :],
                in0=ps[6][:, 0:8].rearrange("p (a b) -> p a b", b=2),
                in1=adabs[l][:, hs * 4:(hs + 1) * 4].unsqueeze(2).to_broadcast([128, 4, 2]),
                op=ALU.add))

        def tail():
            for (dst, gain, sc_) in ((amixs[l], nmixs[l], 1), (affns[l], nffns[l], 4)):
                K.op("dve", [t_mod, t_sm], [t_mod], lambda h, dst=dst, gain=gain, sc_=sc_: h.scalar_tensor_tensor(
                    out=dst[:], in0=modTs[l][:, sc_ * 8:(sc_ + 1) * 8, :], scalar=1.0,
                    in1=gain[:].unsqueeze(2).to_broadcast([128, 8, 2]),
                    op0=ALU.add, op1=ALU.mult))
        return [lambda hs=hs: sec(hs) for hs in range(12)] + [tail]

    with contextlib.ExitStack() as ph:
        xin = [K.sb(ph, "xin%d" % i, [128, D], F32) for i in range(3)]
        xin_t = [Tok("xin%d" % i) for i in range(3)]
        ada0 = make_ada(0, ph)
        for ti in range(18):
            s = ti % 3
            src = x_d[ti * 128:(ti + 1) * 128, :] if ti < 16 else ctx_d[(ti - 16) * 128:(ti - 15) * 128, :]
            K.dma("sp", xin[s][:], src, [], [xin_t[s]], "xin")
            b = min(ti // 4, 4)
            for half in range(2):
                pb = (ti * 2 + half) % 7
                for j in range(4):
                    kc = half * 4 + j
                    K.op("pe", [xin_t[s], t_const], [pt[pb]],
                         lambda h, kc=kc, j=j, pb=pb, s=s: h.transpose(
                             ps[pb][:, j * 128:(j + 1) * 128], xin[s][:, kc * 128:(kc + 1) * 128], identf[:]))
                eng = "dve" if half == 0 else "act"
                dst = xT[:, half * 4:(half + 1) * 4, ti * 128:(ti + 1) * 128]
                srcp = ps[pb][:, :].rearrange("p (a b) -> p a b", a=4)
                if eng == "dve":
                    K.op("dve", [pt[pb]], [xT_t[b]], lambda h, dst=dst, srcp=srcp: h.tensor_copy(out=dst, in_=srcp))
                else:
                    K.op("act", [pt[pb]], [xT_t[b]], lambda h, dst=dst, srcp=srcp: h.activation(out=dst, in_=srcp, func=AF.Copy))
            if 2 <= ti < 14:
                ada0[ti - 2]()
        ada0[12]()
        K.barrier()


    rot = {"ps": 0}

    def norm_mod(l, which, hT, hT_t, blocks):
        a = amix if which == "mix" else affn
        sec = 0 if which == "mix" else 3
        blocks = list(blocks)
        with contextlib.ExitStack() as ph:
            sq1 = K.sb(ph, "nm_sq", [128, 8, 512], BF16)
            sq = [sq1, sq1]
            sq1_t = Tok()
            sq_t = [sq1_t, sq1_t]
            rs = [K.sb(ph, "nm_rs%d" % i, [128, 512], F32) for i in range(2)]
            rs_t = [Tok() for i in range(2)]
            tmp = [K.sb(ph, "nm_tmp%d" % i, [128, 512], F32) for i in range(4)]
            tmp_t = [Tok() for i in range(4)]
            epsb = K.sb(ph, "nm_eps", [128, 1], F32)
            eps_t = Tok()
            K.op("dve", [], [eps_t], lambda h: h.memset(epsb[:], EPS))

            def stats(bi):
                b = blocks[bi]
                c0, n = blk_cols(b)
                s = bi % 2
                pb = bi % 2
                K.op("dve", [xT_t[b]], [sq_t[s]], lambda h: h.tensor_tensor(
                    out=sq[s][:, :, :n], in0=xT[:, :, c0:c0 + n], in1=xT[:, :, c0:c0 + n], op=ALU.mult))
                K.mm([sq_t[s], t_const], [pt[pb]], ps[pb][:, :n],
                     [(onesb[:], sq[s][:, kc, :n]) for kc in range(8)])
                K.op("act", [pt[pb], eps_t], [rs_t[s]], lambda h: h.activation(
                    out=rs[s][:, :n], in_=ps[pb][:, :n], func=AF.Ln, bias=epsb[:], scale=1.0 / D))
                K.op("act", [rs_t[s]], [rs_t[s]], lambda h: h.activation(
                    out=rs[s][:, :n], in_=rs[s][:, :n], func=AF.Exp, scale=-0.5))

            ti = 0
            stats(0)
            for bi, b in enumerate(blocks):
                if bi + 1 < len(blocks):
                    stats(bi + 1)
                c0, n = blk_cols(b)
                r = 0 if b < 4 else 1
                s = bi % 2
                for kc in range(8):
                    u = ti % 4
                    ti += 1
                    eng = "pool" if kc % 4 == 3 else "dve"
                    K.op(eng, [xT_t[b], rs_t[s]], [tmp_t[u]], lambda h, u=u, kc=kc: h.tensor_tensor(
                        out=tmp[u][:, :n], in0=xT[:, kc, c0:c0 + n], in1=rs[s][:, :n], op=ALU.mult))
                    K.op("act", [tmp_t[u], t_mod], [hT_t[b]], lambda h, u=u, kc=kc: h.activation(
                        out=hT[:, kc, c0:c0 + n], in_=tmp[u][:, :n], func=AF.Identity,
                        bias=modT[:, sec * 8 + kc, r:r + 1], scale=a[:, kc, r:r + 1]))
            K.barrier()

    def ffn(l, hT, hT_t, blocks):
        with contextlib.ExitStack() as ph:
            chunks = [(i * 512, 512) for i in range(5)] + [(2560, 256)]
            w1s = [K.sb(ph, "w1s%d" % i, [128, 8, 512], BF16) for i in range(2)]
            w3s = [K.sb(ph, "w3s%d" % i, [128, 8, 512], BF16) for i in range(2)]
            w2s = [K.sb(ph, "w2s%d" % i, [128, 4, D], BF16) for i in range(2)]
            w_t = [Tok() for i in range(2)]
            s1 = [K.sb(ph, "ff_s1%d" % i, [128, 512], F32) for i in range(2)]
            s1_t = [Tok() for i in range(2)]
            g = [K.sb(ph, "ff_g%d" % i, [128, 4, 512], BF16) for i in range(2)]
            g_t = [Tok() for i in range(2)]

            def load(ch):
                f0, nf = chunks[ch]
                s = ch % 2
                K.dma("pool", w1s[s][:, :, :nf], w1_d[l][:, :, f0:f0 + nf], [], [w_t[s]], "ffw")
                K.dma("pool", w3s[s][:, :, :nf], w3_d[l][:, :, f0:f0 + nf], [], [w_t[s]], "ffw")
                K.dma("pool", w2s[s][:, :nf // 128, :], w2_d[l][:, f0 // 128:(f0 + nf) // 128, :], [], [w_t[s]], "ffw")
            load(0)
            ada_next = make_ada(l + 1, ph) if l + 1 < 2 else None
            norm_mod(l, "ffn", hT, hT_t, blocks)
            cnt = 0
            gi = 0
            oi = 0
            for ch in range(6):
                if ch + 1 < 6:
                    load(ch + 1)
                f0, nf = chunks[ch]
                s = ch % 2
                nft = nf // 128
                for b in blocks:
                    c0, n = blk_cols(b)
                    r = 0 if b < 4 else 1
                    gs = gi % 2
                    gi += 1
                    for ft in range(nft):
                        pa = cnt % 2
                        pbk = 2 + cnt % 2
                        u = cnt % 2
                        cnt += 1
                        K.mm([w_t[s], hT_t[b]], [pt[pa]], ps[pa][:, :n],
                             [(w1s[s][:, kc, ft * 128:(ft + 1) * 128], hT[:, kc, c0:c0 + n]) for kc in range(8)])
                        K.mm([w_t[s], hT_t[b]], [pt[pbk]], ps[pbk][:, :n],
                             [(w3s[s][:, kc, ft * 128:(ft + 1) * 128], hT[:, kc, c0:c0 + n]) for kc in range(8)])
                        K.op("act", [pt[pa]], [s1_t[u]], lambda h, u=u, pa=pa, n=n: h.activation(
                            out=s1[u][:, :n], in_=ps[pa][:, :n], func=AF.Silu))
                        K.op("dve", [s1_t[u], pt[pbk]], [g_t[gs]], lambda h, u=u, pbk=pbk, n=n, gs=gs, ft=ft: h.tensor_tensor(
                            out=g[gs][:, ft, :n], in0=s1[u][:, :n], in1=ps[pbk][:, :n], op=ALU.mult))
                    for dt in range(8):
                        pc = 4 + oi % 3
                        oi += 1
                        K.mm([w_t[s], g_t[gs]], [pt[pc]], ps[pc][:, :n],
                             [(w2s[s][:, ft, dt * 128:(dt + 1) * 128], g[gs][:, ft, :n]) for ft in range(nft)])
                        K.op("dve", [pt[pc], t_mod, xT_t[b]], [xT_t[b]], lambda h, pc=pc, dt=dt, c0=c0, n=n, r=r: h.scalar_tensor_tensor(
                            out=xT[:, dt, c0:c0 + n], in0=ps[pc][:, :n], scalar=modT[:, 40 + dt, r:r + 1],
                            in1=xT[:, dt, c0:c0 + n], op0=ALU.mult, op1=ALU.add))
                if ada_next is not None:
                    ada_next[2 * ch]()
                    ada_next[2 * ch + 1]()
            if ada_next is not None:
                ada_next[12]()
            K.barrier()

    def lin_fm(wt, w, col0, M, hT, hT_t, b, c0, n, pb, wtok):
        K.mm([wtok, hT_t[b]], [pt[pb]], ps[pb][:M, :n],
             [(w[:, kc, col0:col0 + M], hT[:, kc, c0:c0 + n]) for kc in range(8)])

    def out_proj_update(pairs_fn, reads, b, c0, n, r, pbs):
        for dt in range(8):
            pc = pbs[dt % len(pbs)]
            K.mm(reads, [pt[pc]], ps[pc][:, :n], pairs_fn(dt))
            K.op("dve", [pt[pc], t_mod, xT_t[b]], [xT_t[b]], lambda h, pc=pc, dt=dt: h.scalar_tensor_tensor(
                out=xT[:, dt, c0:c0 + n], in0=ps[pc][:, :n], scalar=modT[:, 16 + dt, r:r + 1],
                in1=xT[:, dt, c0:c0 + n], op0=ALU.mult, op1=ALU.add))

    def rstd_from_ps(pb, P, n, inv, rt, rs, rs_tok, eps_ap, eps_t):
        K.op("act", [pt[pb], eps_t], [rs_tok], lambda h: h.activation(
            out=rt[:P, :n], in_=ps[pb][:P, :n], func=AF.Sqrt, bias=eps_ap[:P, :], scale=inv))
        K.op("dve", [rs_tok], [rs_tok], lambda h: h.reciprocal(out=rs[:P, :n], in_=rt[:P, :n]))

    def fnet(l, hT, hT_t, need_ctx):
        with contextlib.ExitStack() as ph:
            Z = K.sb(ph, "fn_Z", [128, 18, 256], BF16)
            Z_t = Tok()
            wfn = K.sb(ph, "fn_w", [128, 8, 256], BF16)
            fnw = K.sb(ph, "fn_fw", [128, 2, 256], BF16)
            wo = K.sb(ph, "fn_wo", [128, 2, D], BF16)
            c64 = K.sb(ph, "fn_c64", [128, 4, 128], BF16)
            w_t = Tok()
            K.dma("pool", wfn[:], win_d[l][:, :, 0:256], [], [w_t], "fnw")
            K.dma("pool", fnw[:], fnetw_d[l], [], [w_t], "fnw")
            K.dma("pool", wo[:], wout_d[l][:, 0:2, :], [], [w_t], "fnw")
            K.dma("sp", c64[:], c64_d, [], [w_t], "fnw")
            cts = [K.sb(ph, "fn_ct%d" % i, [128, 2, 16, 512], BF16) for i in range(2)]
            ct_t = [Tok() for i in range(2)]
            Psb = [K.sb(ph, "fn_P%d" % i, [128, 2, 512], BF16) for i in range(2)]
            P_t = [Tok() for i in range(2)]
            Ysb = K.sb(ph, "fn_Y", [128, 2, 512], BF16)
            Y_t = Tok()
            yf = K.sb(ph, "fn_yf", [128, 2, 512], BF16)
            yf_t = Tok()
            for ti in range(18):
                pb = ti % 2
                b = min(ti // 4, 4)
                K.mm([w_t, hT_t[b]], [pt[pb]], ps[pb][:, :256],
                     [(hT[:, kc, ti * 128:(ti + 1) * 128], wfn[:, kc, :]) for kc in range(8)])
                if ti % 2 == 0:
                    K.op("dve", [pt[pb]], [Z_t], lambda h, ti=ti, pb=pb: h.tensor_copy(out=Z[:, ti, :], in_=ps[pb][:, :256]))
                else:
                    K.op("act", [pt[pb]], [Z_t], lambda h, ti=ti, pb=pb: h.activation(out=Z[:, ti, :], in_=ps[pb][:, :256], func=AF.Copy))
            jobs = [(kb, 0) for kb in range(4)] + ([(0, 1)] if need_ctx else [])
            for ji, (kb, isctx) in enumerate(jobs):
                s = ji % 2
                if not isctx:
                    ntt, n, tt0, c0, b, r = 16, 512, 0, kb * 512, kb, 0
                    K.dma("sp", cts[s][:, 0, :, :], ct_d[:, :, kb * 512:(kb + 1) * 512], [], [ct_t[s]], "ct")
                    K.dma("sp", cts[s][:, 1, :, :], st_d[:, :, kb * 512:(kb + 1) * 512], [], [ct_t[s]], "ct")
                else:
                    ntt, n, tt0, c0, b, r = 2, 256, 16, T, 4, 1
                    K.dma("sp", cts[s][:, 0, 0:2, 0:256], c256_d, [], [ct_t[s]], "ct")
                    K.dma("sp", cts[s][:, 1, 0:2, 0:256], s256_d, [], [ct_t[s]], "ct")
                for chc in range(2):
                    u = chc
                    for cs in range(2):
                        pb = cs
                        K.mm([Z_t, ct_t[s]], [pt[pb]], ps[pb][:, :n],
                             [(Z[:, tt0 + tt, chc * 128:(chc + 1) * 128], cts[s][:, cs, tt, :n]) for tt in range(ntt)])
                        if cs == 0:
                            K.op("dve", [pt[pb]], [P_t[u]], lambda h, u=u, pb=pb, cs=cs: h.tensor_copy(out=Psb[u][:, cs, :n], in_=ps[pb][:, :n]))
                        else:
                            K.op("act", [pt[pb]], [P_t[u]], lambda h, u=u, pb=pb, cs=cs: h.activation(out=Psb[u][:, cs, :n], in_=ps[pb][:, :n], func=AF.Copy))
                    pb = 2 + chc
                    K.mm([P_t[u], w_t], [pt[pb]], ps[pb][:, :n],
                         [(c64[:, 2 * isctx + 0, :], Psb[u][:, 0, :n]), (c64[:, 2 * isctx + 1, :], Psb[u][:, 1, :n])])
                    K.op("dve", [pt[pb]], [Y_t], lambda h, pb=pb, chc=chc: h.tensor_copy(out=Ysb[:, chc, :n], in_=ps[pb][:, :n]))
                for c2 in range(2):
                    pb = 4 + c2
                    K.mm([Y_t, w_t], [pt[pb]], ps[pb][:, :n],
                         [(fnw[:, c1, c2 * 128:(c2 + 1) * 128], Ysb[:, c1, :n]) for c1 in range(2)])
                    K.op("act", [pt[pb]], [yf_t], lambda h, pb=pb, c2=c2: h.activation(out=yf[:, c2, :n], in_=ps[pb][:, :n], func=AF.Copy))
                out_proj_update(lambda dt: [(wo[:, c2, dt * 128:(dt + 1) * 128], yf[:, c2, :n]) for c2 in range(2)],
                                [yf_t, w_t], b, c0, n, r, [0, 1, 2, 3, 4, 5, 6])
            K.barrier()

    def natt(l, hT, hT_t, need_ctx):
        with contextlib.ExitStack() as ph:
            kT = K.sb(ph, "na_kT", [128, 3, NT], BF16)
            kT_t = Tok()
            V = K.sb(ph, "na_V", [128, 18, 6, 65], BF16)
            V_t = Tok()
            gq = K.sb(ph, "na_gq", [128, 1], F32)
            gk = K.sb(ph, "na_gk", [128, 1], F32)
            epsb = K.sb(ph, "na_eps", [128, 1], F32)
            bones = K.sb(ph, "na_bones", [128, 128], BF16)
            g_t = Tok()
            K.dma("sp", gq[:], naq_d[l], [], [g_t], "nag")
            K.dma("sp", gk[:], nak_d[l], [], [g_t], "nag")
            K.op("dve", [g_t], [g_t], lambda h: h.tensor_scalar(out=gq[:], in0=gq[:], scalar1=0.125, scalar2=None, op0=ALU.mult))
            K.op("dve", [], [g_t], lambda h: h.memset(epsb[:], EPS))
            K.op("dve", [], [g_t], lambda h: h.memset(bones[:], 0.0))
            K.op("dve", [g_t], [g_t], lambda h: h.memset(bones[0:64, 0:64], 1.0))
            K.op("dve", [g_t], [g_t], lambda h: h.memset(bones[64:128, 64:128], 1.0))
            K.op("dve", [], [V_t], lambda h: h.memset(V[:, :, :, 64:65], 1.0))
            sq = [K.sb(ph, "na_sq%d" % i, [128, 512], BF16) for i in range(2)]
            sq_t = [Tok() for i in range(2)]
            rtl = [K.sb(ph, "na_rt%d" % i, [128, 512], F32) for i in range(2)]
            rs = [K.sb(ph, "na_rs%d" % i, [128, 512], F32) for i in range(2)]
            rs_t = [Tok() for i in range(2)]

            def qk_norm(w, wtok, gain, dst, dst_t, dcol, b, c0, n, cnt):
                for hp in range(3):
                    u = (cnt[0]) % 2
                    cnt[0] += 1
                    pa, pbk = u, 2 + u
                    lin_fm(None, w, hp * 128, 128, hT, hT_t, b, c0, n, pa, wtok)
                    K.op("act", [pt[pa]], [sq_t[u]], lambda h, u=u, pa=pa: h.activation(out=sq[u][:, :n], in_=ps[pa][:, :n], func=AF.Square))
                    K.mm([sq_t[u], g_t], [pt[pbk]], ps[pbk][:, :n], [(bones[:], sq[u][:, :n])])
                    K.op("act", [pt[pbk], g_t], [rs_t[u]], lambda h, u=u, pbk=pbk: h.activation(
                        out=rtl[u][:, :n], in_=ps[pbk][:, :n], func=AF.Ln, bias=epsb[:], scale=1.0 / 64))
                    K.op("act", [rs_t[u]], [rs_t[u]], lambda h, u=u: h.activation(
                        out=rs[u][:, :n], in_=rtl[u][:, :n], func=AF.Exp, scale=-0.5))
                    K.op("dve", [pt[pa], rs_t[u], g_t], [dst_t], lambda h, u=u, pa=pa, hp=hp: h.scalar_tensor_tensor(
                        out=dst[:, hp, dcol:dcol + n], in0=ps[pa][:, :n], scalar=gain[:, 0:1], in1=rs[u][:, :n],
                        op0=ALU.mult, op1=ALU.mult))

            cnt = [0]
            with contextlib.ExitStack() as p1:
                wk = K.sb(p1, "na_wk", [128, 8, 384], BF16)
                wv = K.sb(p1, "na_wv", [128, 8, 384], BF16)
                wk_t = Tok()
                K.dma("pool", wk[:], win_d[l][:, :, 640:1024], [], [wk_t], "naw")
                K.dma("pool", wv[:], win_d[l][:, :, 1024:1408], [], [wk_t], "naw")
                for b in range(5):
                    c0, n = blk_cols(b)
                    qk_norm(wk, wk_t, gk, kT, kT_t, c0, b, c0, n, cnt)
                for ti in range(18):
                    pb = 4 + ti % 2
                    b = min(ti // 4, 4)
                    K.mm([wk_t, hT_t[b]], [pt[pb]], ps[pb][:, :384],
                         [(hT[:, kc, ti * 128:(ti + 1) * 128], wv[:, kc, :]) for kc in range(8)])
                    K.op("act" if ti % 2 else "dve", [pt[pb]], [V_t],
                         (lambda h, ti=ti, pb=pb: h.activation(out=V[:, ti, :, 0:64], in_=ps[pb][:, :384].rearrange("p (a b) -> p a b", b=64), func=AF.Copy))
                         if ti % 2 else
                         (lambda h, ti=ti, pb=pb: h.tensor_copy(out=V[:, ti, :, 0:64], in_=ps[pb][:, :384].rearrange("p (a b) -> p a b", b=64))))
                K.barrier()
            wq = K.sb(ph, "na_wq", [128, 8, 384], BF16)
            wo = K.sb(ph, "na_wo", [128, 3, D], BF16)
            BT = K.sb(ph, "na_BT", [128, 72, 128], BF16)
            bt_t = Tok()
            w_t = Tok()
            K.dma("pool", wq[:], win_d[l][:, :, 256:640], [], [w_t], "naw2")
            K.dma("pool", wo[:], wout_d[l][:, 2:5, :], [], [w_t], "naw2")
            for i3 in range(3):
                K.dma("pool", BT[:, 24 * i3:24 * (i3 + 1), :], bt_d[l][:, 24 * i3:24 * (i3 + 1), :], [], [bt_t], "nabt")
            for i3 in range(3):
                K.op("act", [bt_t], [bt_t], lambda h, i3=i3: h.activation(
                    out=BT[:, 24 * i3:24 * (i3 + 1), :], in_=BT[:, 24 * i3:24 * (i3 + 1), :], func=AF.Exp))
            qT2 = [K.sb(ph, "na_qT%d" % i, [128, 3, 512], BF16) for i in range(2)]
            qT2_t = [Tok(), Tok()]
            PT = [K.sb(ph, "na_PT%d" % i, [128, 7, 128], BF16) for i in range(2)]
            PT_t = [Tok() for i in range(2)]
            rec = [K.sb(ph, "na_rec%d" % i, [128, 6], F32) for i in range(2)]
            rec_t = [Tok(), Tok()]
            Osb = [K.sb(ph, "na_O%d" % i, [128, 6, 64], BF16) for i in range(2)]
            O_t = [Tok(), Tok()]
            yna2 = [K.sb(ph, "na_y%d" % i, [128, 3, 512], BF16) for i in range(2)]
            yna2_t = [Tok(), Tok()]
            hcount = 0
            nblocks = list(range(5) if need_ctx else range(4))
            c0_, n_ = blk_cols(nblocks[0])
            qk_norm(wq, w_t, gq, qT2[0], qT2_t[0], 0, nblocks[0], c0_, n_, cnt)
            for bi, b in enumerate(nblocks):
                c0, n = blk_cols(b)
                r = 0 if b < 4 else 1
                qT, qT_t = qT2[bi % 2], qT2_t[bi % 2]
                yna, yna_t = yna2[bi % 2], yna2_t[bi % 2]
                tile_chunks = []
                for qi in range(n // 128):
                    if b < 4:
                        i = b * 4 + qi
                        if i <= 1:
                            js = list(range(0, 4))
                        elif i >= 14:
                            js = list(range(12, 16))
                        else:
                            js = list(range(i - 2, i + 3))
                        chunks = []
                        for j in js:
                            dl = j - i
                            v = (dl + 9) if 2 <= i <= 13 else (dl + 3)
                            chunks.append((j, v))
                        chunks += [(16, None), (17, None)]
                    else:
                        chunks = [(16, None), (17, None)]
                    tile_chunks.append(chunks)
                items = [(qi, hd) for qi in range(n // 128) for hd in range(6)]

                def stage1(k):
                    qi, hd = items[k]
                    chunks = tile_chunks[qi]
                    nch = len(chunks)
                    hp, r0 = hd // 2, (hd % 2) * 64
                    u = (hbase + k) % 2
                    pS = (0 + 2 * u, 1 + 2 * u)
                    for c, (kt, v) in enumerate(chunks):
                        pb = pS[c // 4]
                        pairs = [(kT[r0:r0 + 64, hp, kt * 128:(kt + 1) * 128], qT[r0:r0 + 64, hp, qi * 128:(qi + 1) * 128])]
                        K.mm([kT_t, qT_t, w_t, t_const], [pt[pb]], ps[pb][:, (c % 4) * 128:(c % 4 + 1) * 128], pairs)
                    n0 = min(nch, 4)
                    K.op("act", [pt[pS[0]]], [PT_t[u]], lambda h: h.activation(
                        out=PT[u][:, 0:n0, :], in_=ps[pS[0]][:, 0:n0 * 128].rearrange("p (a b) -> p a b", b=128), func=AF.Exp))
                    if nch > 4:
                        n1 = nch - 4
                        K.op("act", [pt[pS[1]]], [PT_t[u]], lambda h: h.activation(
                            out=PT[u][:, 4:4 + n1, :], in_=ps[pS[1]][:, 0:n1 * 128].rearrange("p (a b) -> p a b", b=128), func=AF.Exp))
                    loc = [v for (kt, v) in chunks if v is not None]
                    if loc:
                        nl, v0 = len(loc), loc[0]
                        K.op("dve", [PT_t[u], bt_t], [PT_t[u]], lambda h: h.tensor_tensor(
                            out=PT[u][:, 0:nl, :], in0=PT[u][:, 0:nl, :], in1=BT[:, hd * 12 + v0:hd * 12 + v0 + nl, :], op=ALU.mult))

                def stage2(k):
                    qi, hd = items[k]
                    chunks = tile_chunks[qi]
                    u = (hbase + k) % 2
                    po = 4 + qi % 2
                    K.mm([PT_t[u], V_t], [pt[po]], ps[po][:, hd * 65:(hd + 1) * 65],
                         [(PT[u][:, c, :], V[:, kt, hd, :]) for c, (kt, v) in enumerate(chunks)])

                def fin_dve(qi):
                    po = 4 + qi % 2
                    oi = qi % 2
                    ov = ps[po][:, 0:390].rearrange("p (a b) -> p a b", b=65)
                    K.op("dve", [pt[po]], [rec_t[oi]], lambda h: h.reciprocal(out=rec[oi][:], in_=ov[:, :, 64]))
                    K.op("dve", [pt[po], rec_t[oi]], [O_t[oi]], lambda h: h.tensor_tensor(
                        out=Osb[oi][:], in0=ov[:, :, 0:64], in1=rec[oi][:].unsqueeze(2).to_broadcast([128, 6, 64]), op=ALU.mult))

                def fin_pe(qi):
                    oi = qi % 2
                    Of = Osb[oi][:].rearrange("p a b -> p (a b)")
                    for hp in range(3):
                        K.op("pe", [O_t[oi], t_const], [ptb], lambda h, hp=hp: h.transpose(
                            psb[:, hp * 128:(hp + 1) * 128], Of[:, hp * 128:(hp + 1) * 128], identb[:]))
                    K.op("act", [ptb], [yna_t], lambda h: h.activation(
                        out=yna[:, :, qi * 128:(qi + 1) * 128], in_=psb[:, 0:384].rearrange("p (a b) -> p a b", b=128), func=AF.Copy))

                hbase = hcount
                nit = len(items)
                for k in range(nit + 2):
                    if k < nit:
                        stage1(k)
                    if 1 <= k <= nit:
                        stage2(k - 1)
                        if items[k - 1][1] == 5:
                            fin_dve(items[k - 1][0])
                    if 2 <= k <= nit + 1 and items[k - 2][1] == 5:
                        fin_pe(items[k - 2][0])
                hcount += nit
                if bi + 1 < len(nblocks):
                    c0_, n_ = blk_cols(nblocks[bi + 1])
                    qk_norm(wq, w_t, gq, qT2[(bi + 1) % 2], qT2_t[(bi + 1) % 2], 0, nblocks[bi + 1], c0_, n_, cnt)
                out_proj_update(lambda dt: [(wo[:, hp, dt * 128:(dt + 1) * 128], yna[:, hp, :n]) for hp in range(3)],
                                [yna_t, w_t], b, c0, n, r, [6])
            K.barrier()

    def gla(l, hT, hT_t, need_ctx):
        NB = 256
        QS = 48 ** -0.5
        with contextlib.ExitStack() as ph:
            qr = K.sb(ph, "gl_qr", [128, 2, NT], BF16)
            kr = K.sb(ph, "gl_kr", [128, 2, NT], BF16)
            aT = K.sb(ph, "gl_aT", [32, NT], BF16)
            st_t = [Tok() for i in range(5)]
            aw = K.sb(ph, "gl_aw", [32, 2, 256], BF16)
            nab = K.sb(ph, "gl_nab", [128, 2, 2], F32)
            gon = K.sb(ph, "gl_gon", [96, 1], F32)
            msk = K.sb(ph, "gl_msk", [128, 2, 2, 128], BF16)
            mscan = K.sb(ph, "gl_mscan", [128, NB], BF16)
            epsb = K.sb(ph, "gl_eps", [128, 1], F32)
            oneb = K.sb(ph, "gl_one", [128, 1], F32)
            w_t = Tok()
            K.dma("pool", aw[:], aw_d[l], [], [w_t], "glw")
            K.dma("sp", nab[:], ab_d[l], [], [w_t], "glw")
            K.dma("sp", gon[:], gon_d[l], [], [w_t], "glw")
            K.dma("sp", msk[:], gmask_d, [], [w_t], "glw")
            K.dma("sp", mscan[:], mscan_d, [], [w_t], "glw")
            K.op("dve", [w_t], [w_t], lambda h: h.tensor_scalar(out=nab[:], in0=nab[:], scalar1=-1.0, scalar2=None, op0=ALU.mult))
            K.op("dve", [], [w_t], lambda h: h.memset(epsb[:], EPS))
            K.op("dve", [], [w_t], lambda h: h.memset(oneb[:], 1.0))
            with contextlib.ExitStack() as p0:
                wg = K.sb(p0, "gl_wg", [128, 8, 1024], BF16)
                wga = K.sb(p0, "gl_wga", [128, 8, 32], BF16)
                w0_t = Tok()
                K.dma("pool", wg[:], wg_d[l], [], [w0_t], "glw0")
                K.dma("pool", wga[:], win_d[l][:, :, 2560:2592], [], [w0_t], "glw0")
                cosb = K.sb(p0, "gl_cos", [128, 512], F32)
                sinb = K.sb(p0, "gl_sin", [128, 512], F32)
                cs_t = Tok()
                t1 = [K.sb(p0, "gl_t1%d" % i, [128, 512], F32) for i in range(2)]
                t2 = [K.sb(p0, "gl_t2%d" % i, [128, 512], F32) for i in range(2)]
                r_t = [Tok(), Tok()]
                cnt = 0
                for b5 in range(5):
                    c0, n = blk_cols(b5)
                    lat = b5 < 4
                    if lat:
                        K.dma("sp", cosb[:, :n], cos_d[:, c0:c0 + n], [], [cs_t], "cs")
                        K.dma("sp", sinb[:, :n], sin_d[:, c0:c0 + n], [], [cs_t], "cs")
                    lin_fm(None, wga, 0, 32, hT, hT_t, b5, c0, n, 6, w0_t)
                    K.op("act", [pt[6]], [st_t[b5]], lambda h: h.activation(out=aT[:, c0:c0 + n], in_=ps[6][:32, :n], func=AF.Copy))
                    for hp in range(2):
                        for which in range(2):
                            base = which * 512
                            dst = qr if which == 0 else kr
                            u = cnt % 2
                            cnt += 1
                            pa, pbk = (0, 1) if u == 0 else (2, 3)
                            lin_fm(None, wg, base + hp * 128, 128, hT, hT_t, b5, c0, n, pa, w0_t)
                            if lat:
                                lin_fm(None, wg, base + 256 + hp * 128, 128, hT, hT_t, b5, c0, n, pbk, w0_t)
                                K.op("dve", [pt[pa], cs_t], [r_t[u]], lambda h: h.tensor_tensor(out=t1[u][:, :n], in0=ps[pa][:, :n], in1=cosb[:, :n], op=ALU.mult))
                                K.op("dve", [pt[pbk], cs_t, r_t[u]], [r_t[u]], lambda h: h.tensor_tensor(out=t2[u][:, :n], in0=ps[pbk][:, :n], in1=sinb[:, :n], op=ALU.mult))
                                K.op("pool", [r_t[u]], [st_t[b5]], lambda h: h.tensor_tensor(out=dst[:, hp, c0:c0 + n], in0=t1[u][:, :n], in1=t2[u][:, :n], op=ALU.add))
                            else:
                                K.op("act", [pt[pa]], [st_t[b5]], lambda h: h.activation(out=dst[:, hp, c0:c0 + n], in_=ps[pa][:, :n], func=AF.Copy))
                K.barrier()
            wgv = K.sb(ph, "gl_wgv", [128, 8, 384], BF16)
            wgg = K.sb(ph, "gl_wgg", [128, 8, 384], BF16)
            wo = K.sb(ph, "gl_wo", [96, 4, D], BF16)
            w2_t = Tok()
            K.dma("pool", wgv[:], win_d[l][:, :, 1792:2176], [], [w2_t], "glw2")
            K.dma("pool", wgg[:], win_d[l][:, :, 2176:2560], [], [w2_t], "glw2")
            K.dma("pool", wo[:], woutg_d[l], [], [w2_t], "glw2")
            oF = K.sb(ph, "gl_oF", [96, 4, NT], BF16)
            oF_t = [Tok() for i in range(9)]
            R = []
            for d in range(2):
                r_ = dict(
                    e1=K.sb(ph, "gl_e1_%d" % d, [128, NB], F32),
                    bpos=K.sb(ph, "gl_bp_%d" % d, [128, NB], F32), eb=K.sb(ph, "gl_eb_%d" % d, [128, NB], F32),
                    qin=K.sb(ph, "gl_qin_%d" % d, [128, 2, NB], BF16), kin=K.sb(ph, "gl_kin_%d" % d, [128, 2, NB], BF16),
                    ktok=K.sb(ph, "gl_ktok_%d" % d, [128, 2, 2, 128], BF16), gv=K.sb(ph, "gl_gv_%d" % d, [128, 2, 384], BF16),
                    dec=K.sb(ph, "gl_dec_%d" % d, [128, 2, 2], F32), AM=K.sb(ph, "gl_AM_%d" % d, [128, 4, 128], BF16),
                    S=K.sb(ph, "gl_S_%d" % d, [128, 2, 192], F32), Sb=K.sb(ph, "gl_Sb_%d" % d, [128, 2, 192], BF16),
                    osum=K.sb(ph, "gl_osum_%d" % d, [96, 4, NB], F32),
                    sg=K.sb(ph, "gl_sg_%d" % d, [96, 4, NB], BF16),
                    sg_t=Tok(), n_t=Tok(),
                    g_t=Tok(), qk_t=Tok(), ktok_t=Tok(), gv_t=Tok(), dec_t=Tok(), AM_t=Tok(), S_t=Tok(), Sb_t=Tok(), os_t=Tok(),
                    BA=3 * d, BB=3 * d + 1, BO=3 * d + 2)
                R.append(r_)

            def block_prep(d, b):
                r_ = R[d]
                c0 = b * NB if b < 8 else T
                n = NB
                b5 = min(c0 // 512, 4)
                BB = r_["BB"]
                e1, bpos, eb = r_["e1"], r_["bpos"], r_["eb"]
                lsb = e1
                enb = eb
                gt = r_["g_t"]
                for hp in range(2):
                    K.mm([st_t[b5], w_t], [pt[BB]], ps[BB][:, :n], [(aw[:, d, hp * 128:(hp + 1) * 128], aT[:, c0:c0 + n])])
                    yield
                    K.op("act", [pt[BB], w_t], [gt], lambda h: h.activation(
                        out=e1[:, :n], in_=ps[BB][:, :n], func=AF.Exp, bias=nab[:, d, hp:hp + 1], scale=-1.0))
                    yield
                    K.op("act", [gt, w_t], [gt], lambda h: h.activation(out=lsb[:, :n], in_=e1[:, :n], func=AF.Ln, bias=oneb[:], scale=1.0))
                    yield
                    K.op("dve", [gt, w_t], [gt], lambda h: h.tensor_tensor_scan(
                        out=bpos[:, :n], data0=mscan[:, :n], data1=lsb[:, :n], initial=0.0, op0=ALU.mult, op1=ALU.add))
                    yield
                    bsel = bpos
                    if d == 1:
                        K.op("dve", [gt], [gt], lambda h: h.tensor_tensor(out=e1[:, :n], in0=lsb[:, :n], in1=bpos[:, :n], op=ALU.subtract))
                        yield
                        K.op("dve", [gt], [gt], lambda h: h.tensor_tensor(
                            out=e1[:, :n].rearrange("p (a b) -> p a b", b=128), in0=e1[:, :n].rearrange("p (a b) -> p a b", b=128),
                            in1=bpos[:, :n].rearrange("p (a b) -> p a b", b=128)[:, :, 127:128].to_broadcast([128, n // 128, 128]), op=ALU.add))
                        yield
                        bsel = e1
                    K.op("act", [gt], [gt], lambda h, bsel=bsel: h.activation(out=eb[:, :n], in_=bsel[:, :n], func=AF.Exp, scale=-1.0 / 16))
                    yield
                    sel = 127 if d == 0 else 0
                    K.op("dve", [gt], [r_["dec_t"]], lambda h: h.tensor_copy(
                        out=r_["dec"][:, hp, :], in_=eb[:, :n].rearrange("p (a b) -> p a b", b=128)[:, :, sel]))
                    yield
                    K.op("dve", [gt, st_t[b5]], [r_["qk_t"]], lambda h: h.scalar_tensor_tensor(
                        out=r_["qin"][:, hp, :n], in0=qr[:, hp, c0:c0 + n], scalar=QS, in1=eb[:, :n], op0=ALU.mult, op1=ALU.mult))
                    yield
                    K.op("act", [gt, r_["dec_t"], r_["qk_t"]], [gt], lambda h, bsel=bsel: h.activation(out=enb[:, :n], in_=bsel[:, :n], func=AF.Exp, scale=1.0 / 16))
                    yield
                    K.op("dve", [gt, st_t[b5]], [r_["qk_t"]], lambda h: h.tensor_tensor(
                        out=r_["kin"][:, hp, :n], in0=kr[:, hp, c0:c0 + n], in1=enb[:, :n], op=ALU.mult))
                    yield
                    for tt in range(2):
                        K.op("pe", [r_["qk_t"], t_const], [ptb], lambda h, tt=tt: h.transpose(
                            psb[:, (d * 2 + tt) * 128:(d * 2 + tt + 1) * 128], r_["kin"][:, hp, tt * 128:(tt + 1) * 128], identb[:]))
                        yield
                    K.op("act", [ptb], [r_["ktok_t"]], lambda h: h.activation(
                        out=r_["ktok"][:, :, hp, :], in_=psb[:, d * 256:(d + 1) * 256].rearrange("p (t f) -> p t f", t=2), func=AF.Copy))
                    yield
                for tt in range(2):
                    K.mm([w2_t, hT_t[b5]], [pt[BB]], ps[BB][:, :384],
                         [(hT[:, kc, c0 + tt * 128:c0 + (tt + 1) * 128], wgv[:, kc, :]) for kc in range(8)])
                    yield
                    K.op("act", [pt[BB]], [r_["gv_t"]], lambda h, tt=tt: h.activation(out=r_["gv"][:, tt, :], in_=ps[BB][:, :384], func=AF.Copy))
                    yield

            def scan_tile(d, b, tt, want_o, second):
                r_ = R[d]
                c0 = b * NB if b < 8 else T
                tc0 = tt * 128
                BA, BB, BO = r_["BA"], r_["BB"], r_["BO"]
                qin, kin, gv, Sb, S, AM, dec, ktok = r_["qin"], r_["kin"], r_["gv"], r_["Sb"], r_["S"], r_["AM"], r_["dec"], r_["ktok"]
                if want_o:
                    for hd in (0, 2, 1, 3):
                        hp, r0 = hd // 2, (hd % 2) * 64
                        pa = BA if hd % 2 == 0 else BB
                        K.mm([r_["qk_t"]], [pt[pa]], ps[pa][:, (hd // 2) * 128:(hd // 2 + 1) * 128],
                             [(kin[r0:r0 + 48, hp, tc0:tc0 + 128], qin[r0:r0 + 48, hp, tc0:tc0 + 128])])
                        yield
                    for par, pa in ((0, BA), (1, BB)):
                        K.op("dve", [pt[pa], w_t], [r_["AM_t"]], lambda h, par=par, pa=pa: h.tensor_tensor(
                            out=AM[:, par:4:2, :], in0=ps[pa][:, 0:256].rearrange("p (a b) -> p a b", b=128),
                            in1=msk[:, d, :, :], op=ALU.mult))
                        yield
                    for hd in range(4):
                        hp, r0 = hd // 2, (hd % 2) * 64
                        K.mm([r_["AM_t"], r_["gv_t"], r_["Sb_t"], r_["qk_t"]], [pt[BO]], ps[BO][:96, hd * 128:(hd + 1) * 128],
                             [(gv[:, tt, hd * 96:(hd + 1) * 96], AM[:, hd, :]),
                              (Sb[r0:r0 + 48, hp, (hd % 2) * 96:(hd % 2 + 1) * 96], qin[r0:r0 + 48, hp, tc0:tc0 + 128])])
                        yield
                    ov = ps[BO][:96, :].rearrange("p (a b) -> p a b", b=128)
                    if not second:
                        K.op("act", [pt[BO]], [oF_t[b]], lambda h: h.activation(
                            out=oF[:, :, c0 + tc0:c0 + tc0 + 128], in_=ov, func=AF.Copy))
                    else:
                        K.op("dve", [pt[BO], oF_t[b]], [r_["os_t"]], lambda h: h.tensor_tensor(
                            out=r_["osum"][:, :, tc0:tc0 + 128], in0=ov, in1=oF[:, :, c0 + tc0:c0 + tc0 + 128], op=ALU.add))
                    yield
                for hp in range(2):
                    K.mm([r_["ktok_t"], r_["gv_t"]], [pt[BA]], ps[BA][:, hp * 192:(hp + 1) * 192],
                         [(ktok[:, tt, hp, :], gv[:, tt, hp * 192:(hp + 1) * 192])])
                    yield
                for hp in range(2):
                    K.op("act", [r_["S_t"], r_["dec_t"]], [r_["S_t"]], lambda h, hp=hp: h.activation(
                        out=S[:, hp, :], in_=S[:, hp, :], func=AF.Identity, scale=dec[:, hp, tt:tt + 1]))
                    yield
                    K.op("dve", [pt[BA], r_["S_t"], r_["dec_t"]], [r_["S_t"]], lambda h, hp=hp: h.scalar_tensor_tensor(
                        out=S[:, hp, :], in0=ps[BA][:, hp * 192:(hp + 1) * 192], scalar=dec[:, hp, tt:tt + 1], in1=S[:, hp, :],
                        op0=ALU.mult, op1=ALU.add))
                    yield
                K.op("act", [r_["S_t"]], [r_["Sb_t"]], lambda h: h.activation(out=Sb[:], in_=S[:], func=AF.Copy))
                yield

            def finalize(d, b):
                r_ = R[d]
                c0 = b * NB if b < 8 else T
                n = NB
                b5 = min(c0 // 512, 4)
                r = 0 if b < 8 else 1
                osum = r_["osum"]
                BA, BB, BO = r_["BA"], r_["BB"], r_["BO"]
                sg = r_["sg"]
                sq = r_["AM"][:96, 0:2, :].rearrange("p a b -> p (a b)")
                am_t = r_["AM_t"]
                rt = r_["e1"]
                sg_t, n_t, gt = r_["sg_t"], r_["n_t"], r_["g_t"]
                for hd in range(4):
                    lin_fm(None, wgg, hd * 96, 96, hT, hT_t, b5, c0, n, BB, w2_t)
                    yield
                    K.op("act", [pt[BB]], [sg_t], lambda h, hd=hd: h.activation(out=sg[:, hd, :n], in_=ps[BB][:96, :n], func=AF.Silu))
                    yield
                for hd in range(4):
                    K.op("act", [r_["os_t"]], [n_t, am_t], lambda h, hd=hd: h.activation(out=sq[:, :n], in_=osum[:, hd, :n], func=AF.Square))
                    yield
                    K.mm([n_t, am_t, t_const], [pt[BB]], ps[BB][:96, :n], [(onesb[:96, :96], sq[:, :n])])
                    yield
                    K.op("act", [pt[BB], w_t], [n_t, gt], lambda h: h.activation(
                        out=rt[:96, :n], in_=ps[BB][:96, :n], func=AF.Ln, bias=epsb[:96, :], scale=1.0 / 96))
                    yield
                    K.op("act", [n_t], [n_t, gt], lambda h: h.activation(out=rt[:96, :n], in_=rt[:96, :n], func=AF.Exp, scale=-0.5))
                    yield
                    K.op("dve", [n_t, gt, w_t], [r_["os_t"]], lambda h, hd=hd: h.scalar_tensor_tensor(
                        out=osum[:, hd, :n], in0=osum[:, hd, :n], scalar=gon[:, 0:1], in1=rt[:96, :n], op0=ALU.mult, op1=ALU.mult))
                    yield
                    K.op("dve", [r_["os_t"], sg_t], [sg_t], lambda h, hd=hd: h.tensor_tensor(
                        out=sg[:, hd, :n], in0=osum[:, hd, :n], in1=sg[:, hd, :n], op=ALU.mult))
                    yield
                for dt in range(8):
                    pc = BA if dt % 2 == 0 else BO
                    K.mm([sg_t, w2_t], [pt[pc]], ps[pc][:, :n], [(wo[:, hd, dt * 128:(dt + 1) * 128], sg[:, hd, :n]) for hd in range(4)])
                    yield
                    K.op("dve", [pt[pc], t_mod, xT_t[b5]], [xT_t[b5]], lambda h, dt=dt, pc=pc: h.scalar_tensor_tensor(
                        out=xT[:, dt, c0:c0 + n], in0=ps[pc][:, :n], scalar=modT[:, 16 + dt, r:r + 1],
                        in1=xT[:, dt, c0:c0 + n], op0=ALU.mult, op1=ALU.add))
                    yield

            def dir_gen(d):
                r_ = R[d]
                K.op("dve", [], [r_["S_t"]], lambda h: h.memset(r_["S"][:], 0.0))
                K.op("dve", [], [r_["Sb_t"]], lambda h: h.memset(r_["Sb"][:], 0.0))
                order = [8] + (list(range(8)) if d == 0 else list(range(7, -1, -1)))
                for b in order:
                    want_o = (b < 8) or need_ctx
                    if b == 8:
                        second = (d == 1)
                    else:
                        second = (b >= 4) if d == 0 else (b <= 3)
                    yield from block_prep(d, b)
                    for tt in ((0, 1) if d == 0 else (1, 0)):
                        yield from scan_tile(d, b, tt, want_o, second)
                    if want_o and second:
                        yield from finalize(d, b)
                    yield "STEP"

            def run_step(gens):
                active = list(gens)
                while active:
                    for g in list(active):
                        if next(g) == "STEP":
                            active.remove(g)
            gF, gB = dir_gen(0), dir_gen(1)
            run_step([gF])
            run_step([gB])
            for s_ in range(8):
                run_step([gF, gB])
            K.barrier()

    def mixers(l, hT, hT_t, need_ctx):
        if "nofn" not in debug:
            fnet(l, hT, hT_t, need_ctx)
        if "nona" not in debug:
            natt(l, hT, hT_t, need_ctx)
        if "nogla" not in debug:
            gla(l, hT, hT_t, need_ctx)

    for l in range(2):
        need_ctx = (l == 0)
        modT.t, amix.t, affn.t = modTs[l], amixs[l], affns[l]
        with contextlib.ExitStack() as lay:
            hT = K.sb(lay, "hT", [128, 8, NT], BF16)
            hT_t = [Tok("hT%d" % i) for i in range(5)]
            if "ffn" not in debug:
                norm_mod(l, "mix", hT, hT_t, range(5))
                mixers(l, hT, hT_t, need_ctx)
            blocks = range(5) if need_ctx else range(4)
            if "noffn" not in debug:
                ffn(l, hT, hT_t, blocks)
            K.barrier()
        if "ffn" in debug or "l0" in debug:
            break

    t_out = Tok("out")
    for name, (buf, shape, dt) in dbg.items():
        dd = K.dram("dbg_" + name, shape, dt, kind="ExternalOutput")
        K.barrier()
        K.dma("sp", dd, buf[:], [], [t_out], "out")
        K.outs["dbg_" + name] = shape

    with contextlib.ExitStack() as ph:
        xo = [K.sb(ph, "xo%d" % i, [128, D], F32) for i in range(3)]
        xo_t = [Tok("xo%d" % i) for i in range(3)]
        for ti in range(16):
            s = ti % 3
            b = ti // 4
            for half in range(2):
                pb = (ti * 2 + half) % 7
                for j in range(4):
                    kc = half * 4 + j
                    K.op("pe", [xT_t[b], t_const], [pt[pb]],
                         lambda h, kc=kc, j=j, pb=pb, ti=ti: h.transpose(
                             ps[pb][:, j * 128:(j + 1) * 128], xT[:, kc, ti * 128:(ti + 1) * 128], identf[:]))
                dst = xo[s][:, half * 512:(half + 1) * 512]
                if half == 0:
                    K.op("dve", [pt[pb]], [xo_t[s]], lambda h, dst=dst, pb=pb: h.tensor_copy(out=dst, in_=ps[pb][:, :]))
                else:
                    K.op("act", [pt[pb]], [xo_t[s]], lambda h, dst=dst, pb=pb: h.activation(out=dst, in_=ps[pb][:, :], func=AF.Copy))
            K.dma("sp", out_d[ti * 128:(ti + 1) * 128, :], xo[s][:], [xo_t[s]], [t_out], "out")
        K.barrier()
    S = t_out.dsem["sp"]
    nc.sync.wait_ge(S.sem, S.count * 16)
    es.close()
    return K


_CONSTS = {}


def _consts():
    if _CONSTS:
        return _CONSTS
    bf = ml_dtypes.bfloat16
    t = np.arange(T, dtype=np.int64)
    ang = 2.0 * np.pi * ((t[:, None] * t[None, :]) % T).astype(np.float64) / T
    _CONSTS["c_ct"] = np.ascontiguousarray(np.cos(ang).reshape(16, 128, T).transpose(1, 0, 2)).astype(bf)
    _CONSTS["c_st"] = np.ascontiguousarray(np.sin(ang).reshape(16, 128, T).transpose(1, 0, 2)).astype(bf)
    t2 = np.arange(NCTX, dtype=np.int64)
    ang2 = 2.0 * np.pi * ((t2[:, None] * t2[None, :]) % NCTX).astype(np.float64) / NCTX
    _CONSTS["c_c256"] = np.ascontiguousarray(np.cos(ang2).reshape(2, 128, NCTX).transpose(1, 0, 2)).astype(bf)
    _CONSTS["c_s256"] = np.ascontiguousarray(np.sin(ang2).reshape(2, 128, NCTX).transpose(1, 0, 2)).astype(bf)
    g = np.arange(64)
    a64 = 2.0 * np.pi * ((g[:, None] * g[None, :]) % 64) / 64.0
    c64 = np.zeros((128, 4, 128))
    for blk in range(2):
        sl = slice(64 * blk, 64 * blk + 64)
        c64[sl, 0, sl] = np.cos(a64) / np.sqrt(T * 64.0)
        c64[sl, 1, sl] = -np.sin(a64) / np.sqrt(T * 64.0)
        c64[sl, 2, sl] = np.cos(a64) / np.sqrt(NCTX * 64.0)
        c64[sl, 3, sl] = -np.sin(a64) / np.sqrt(NCTX * 64.0)
    _CONSTS["c_c64"] = c64.astype(bf)
    cosT = np.zeros((128, T), np.float32)
    sinT = np.zeros((128, T), np.float32)
    row = (t // 64).astype(np.float64)
    col = (t % 64).astype(np.float64)
    for p in range(128):
        d = p % 64
        if d >= 48:
            continue
        within = d % 24
        j = within % 12
        inv = 10000.0 ** (-j / 12.0)
        pos = row if d < 24 else col
        cosT[p] = np.cos(pos * inv)
        sinT[p] = np.sin(pos * inv) * (-1.0 if within < 12 else 1.0)
    _CONSTS["c_cos"] = cosT
    _CONSTS["c_sin"] = sinT
    s_ = np.arange(128)
    gm = np.zeros((128, 2, 2, 128), np.float32)
    gm[:, 0, :, :] = (s_[:, None] <= s_[None, :])[:, None, :]
    gm[:, 1, :, :] = (s_[:, None] >= s_[None, :])[:, None, :]
    _CONSTS["c_gmask"] = gm.astype(bf)
    ms = np.ones((128, 256), np.float32)
    ms[:, 0] = 0.0
    ms[:, 128] = 0.0
    _CONSTS["c_mscan"] = ms.astype(bf)
    return _CONSTS


def _host_prep(inputs):
    f = lambda a: np.ascontiguousarray(np.asarray(a, dtype=np.float32))
    x, c, ctx, c_ctx = f(inputs["x"]), f(inputs["c"]), f(inputs["ctx"]), f(inputs["c_ctx"])
    shared = {}
    shared["ada_w"] = f(f(inputs["ada_w"]).reshape(2, 8, 128, 6144).transpose(0, 2, 1, 3))
    shared["ada_b"] = f(f(inputs["ada_b"]).reshape(2, 48, 128).transpose(0, 2, 1))
    shared["norm_mix"] = f(f(inputs["norm_mix"]).reshape(2, 8, 128).transpose(0, 2, 1))
    shared["norm_ffn"] = f(f(inputs["norm_ffn"]).reshape(2, 8, 128).transpose(0, 2, 1))
    shared["ident"] = np.eye(128, dtype=np.float32)
    shared["ffn_w1"] = f(f(inputs["ffn_w1"]).reshape(2, 8, 128, DFF).transpose(0, 2, 1, 3))
    shared["ffn_w3"] = f(f(inputs["ffn_w3"]).reshape(2, 8, 128, DFF).transpose(0, 2, 1, 3))
    shared["ffn_w2"] = f(f(inputs["ffn_w2"]).reshape(2, 22, 128, D).transpose(0, 2, 1, 3))
    w_in = f(inputs["w_in"])
    shared["w_in"] = f(w_in.reshape(2, 8, 128, NIN).transpose(0, 2, 1, 3))
    wg = np.zeros((2, D, 1024), np.float32)
    dd = np.arange(48)
    partner = np.where((dd % 24) < 12, dd + 12, dd - 12)
    for hh in range(4):
        wg[:, :, 64 * hh:64 * hh + 48] = w_in[:, :, 1408 + 48 * hh + dd]
        wg[:, :, 256 + 64 * hh:256 + 64 * hh + 48] = w_in[:, :, 1408 + 48 * hh + partner]
        wg[:, :, 512 + 64 * hh:512 + 64 * hh + 48] = w_in[:, :, 1600 + 48 * hh + dd]
        wg[:, :, 768 + 64 * hh:768 + 64 * hh + 48] = w_in[:, :, 1600 + 48 * hh + partner]
    shared["w_g"] = f(wg.reshape(2, 8, 128, 1024).transpose(0, 2, 1, 3))
    shared["fnet_w"] = f(f(inputs["fnet_w"]).reshape(2, 2, 128, 256).transpose(0, 2, 1, 3))
    w_out = f(inputs["w_out"])
    shared["w_out"] = f(w_out.reshape(2, 8, 128, D).transpose(0, 2, 1, 3))
    shared["w_out_g"] = f(w_out[:, 640:, :].reshape(2, 4, 96, D).transpose(0, 2, 1, 3))
    shared["na_q"] = f(np.tile(f(inputs["na_q_norm"]), (1, 2))[:, :, None])
    shared["na_k"] = f(np.tile(f(inputs["na_k_norm"]), (1, 2))[:, :, None])
    rpb = f(inputs["na_rpb"])
    pk = np.arange(128)
    a_, kc_ = pk // 64, pk % 64
    b_, c_ = pk // 64, pk % 64
    dcm = np.clip(kc_[:, None] - c_[None, :], -15, 15) + 15
    cq0 = np.clip(c_ - 8, 0, 48)
    col_ok = (kc_[:, None] >= cq0[None, :]) & (kc_[:, None] < cq0[None, :] + 16)
    bt = np.full((2, 128, 72, 128), NEG, np.float32)
    ents = [(dl, 0) for dl in range(-3, 4)] + [(-2, 1), (-1, 0), (0, 0), (1, 0), (2, 1)]
    for e, (dl, msk_) in enumerate(ents):
        drm = 2 * dl + a_[:, None] - b_[None, :] + 7
        ok = col_ok & (drm >= 0) & (drm <= 14)
        if msk_ and dl == -2:
            ok = ok & ((2 * dl + a_[:, None]) >= (-4 + b_[None, :]))
        if msk_ and dl == 2:
            ok = ok & ((2 * dl + a_[:, None]) <= (3 + b_[None, :]))
        drc = np.clip(drm, 0, 14)
        for hh in range(6):
            vals = rpb[:, hh][:, drc, dcm]
            bt[:, :, hh * 12 + e, :] = np.where(ok[None], vals, NEG)
    shared["na_bt"] = bt
    aw = np.zeros((2, 32, 2, 256), np.float32)
    ab = np.zeros((2, 128, 2, 2), np.float32)
    gaw = f(inputs["gla_alpha_w"]); gab = f(inputs["gla_alpha_b"])
    for dr_ in range(2):
        for hh in range(4):
            aw[:, 16 * dr_:16 * dr_ + 16, dr_, 64 * hh:64 * hh + 48] = gaw[:, dr_, :, 48 * hh:48 * hh + 48]
            ab[:, 64 * (hh % 2):64 * (hh % 2) + 48, dr_, hh // 2] = gab[:, dr_, 48 * hh:48 * hh + 48]
    shared["gla_aw"] = aw
    shared["gla_ab"] = ab
    shared["gla_gon"] = f(f(inputs["gla_o_norm"])[:, :, None])
    shared.update(_consts())
    per_core = []
    for b in range(8):
        m = dict(shared)
        m["x"] = x[b]
        m["ctx"] = ctx[b]
        ccv = np.stack([c[b], c_ctx], axis=-1)
        m["cc"] = f(ccv.reshape(8, 128, 2).transpose(1, 0, 2))
        per_core.append(m)
    return per_core


_DEBUG = tuple(x for x in os.environ.get("KDEBUG", "").split(",") if x)
LAST = {}


def kernel(**inputs):
    in_maps = _host_prep(inputs)
    K = build_program(debug=_DEBUG)
    ncores = int(os.environ.get("KCORES", "8"))
    res = run_bass_kernel_spmd(K.nc, in_maps[:ncores], core_ids=list(range(ncores)))
    LAST["res"] = res
    out = np.stack([np.asarray(r["out"]) for r in res.results], axis=0).astype(np.float32)
    return out
```

```python
import contextlib
import os
import numpy as np
import ml_dtypes
import concourse.bass as bass
import concourse.mybir as mybir
from concourse.bass_utils import run_bass_kernel_spmd

F32 = mybir.dt.float32
BF16 = mybir.dt.bfloat16
AF = mybir.ActivationFunctionType
ALU = mybir.AluOpType

D = 1024
T = 2048
NCTX = 256
NT = T + NCTX
DFF = 2816
NIN = 2592
EPS = 1e-6
NEG = -30000.0
SAME_ENGINE_RAW = True


class Tok:
    __slots__ = ("w", "r", "dsem", "name")

    def __init__(self, name=""):
        self.w = []
        self.r = {}
        self.dsem = None
        self.name = name


class Src:
    def __init__(self, name, sem, unit, h=None):
        self.name = name
        self.sem = sem
        self.unit = unit
        self.h = h
        self.count = 0
        self.seen = {}


class Ctx:
    def __init__(self):
        self.nc = bass.Bass("TRN2", target_bir_lowering=False)
        nc = self.nc
        self.es = contextlib.ExitStack()
        self.engs = {}
        for nm, h in (("pe", nc.tensor), ("act", nc.scalar), ("dve", nc.vector),
                      ("pool", nc.gpsimd), ("sp", nc.sync)):
            sem = self.es.enter_context(nc.semaphore("sem_" + nm))
            self.engs[nm] = Src(nm, sem, 1, h)
        self.dsems = []
        self.outs = {}
        self.n_inst = 0

    def dram(self, name, shape, dt, kind="ExternalInput"):
        return self.nc.dram_tensor(name, list(shape), dt, kind=kind).ap()

    def sb(self, stack, name, shape, dt):
        self.n_sb = getattr(self, "n_sb", 0) + 1
        return stack.enter_context(self.nc.sbuf_tensor("sb%d_%s" % (self.n_sb, name), list(shape), dt))

    def new_dsem(self, name):
        sem = self.es.enter_context(self.nc.semaphore("d_" + name + str(len(self.dsems))))
        s = Src("dma_" + name, sem, 16)
        self.dsems.append(s)
        return s

    def _wait_deps(self, E, reads, writes):
        deps = {}
        for t in reads:
            for (s, n) in t.w:
                deps[s] = max(deps.get(s, 0), n)
        for t in writes:
            for (s, n) in t.w:
                deps[s] = max(deps.get(s, 0), n)
            for s, n in t.r.items():
                deps[s] = max(deps.get(s, 0), n)
        for s, n in deps.items():
            if s is E and (E.name == "pe" or not SAME_ENGINE_RAW):
                continue
            if E.seen.get(s, 0) < n:
                E.h.wait_ge(s.sem, n * s.unit)
                E.seen[s] = n

    def op(self, eng, reads, writes, emit):
        E = self.engs[eng]
        self._wait_deps(E, reads, writes)
        ins = emit(E.h)
        E.count += 1
        ins.then_inc(E.sem, 1)
        self.n_inst += 1
        for t in reads:
            t.r[E] = E.count
        for t in writes:
            t.w = [(E, E.count)]
            t.r = {}
        return ins

    def mm(self, reads, writes, out, pairs, first=True, last=True):
        E = self.engs["pe"]
        self._wait_deps(E, reads, writes)
        n = len(pairs)
        ins = None
        for i, (l, r) in enumerate(pairs):
            ins = E.h.matmul(out, l, r, start=(first and i == 0), stop=(last and i == n - 1))
            self.n_inst += 1
        E.count += 1
        ins.then_inc(E.sem, 1)
        for t in reads:
            t.r[E] = E.count
        for t in writes:
            t.w = [(E, E.count)]
            t.r = {}

    def transpose(self, reads, writes, out, in_, ident):
        return self.op("pe", reads, writes, lambda h: h.transpose(out, in_, ident))

    def dma(self, q, out, in_, reads, writes, name="x"):
        E = self.engs[q]
        self._wait_deps(E, reads, writes)
        wt = writes[0]
        if wt.dsem is None:
            wt.dsem = {}
        if q not in wt.dsem:
            wt.dsem[q] = self.new_dsem(name + q)
        S = wt.dsem[q]
        ins = E.h.dma_start(out=out, in_=in_)
        S.count += 1
        ins.then_inc(S.sem, 16)
        self.n_inst += 1
        for t in reads:
            t.r[S] = S.count
        for t in writes:
            t.w = [(s_, n_) for (s_, n_) in t.w if (s_.unit == 16 and s_ is not S)] + [(S, S.count)]
            t.r = {}

    def barrier(self):
        allsrc = list(self.engs.values()) + self.dsems
        for E in self.engs.values():
            for s in allsrc:
                if s is E or s.count == 0:
                    continue
                if E.seen.get(s, 0) < s.count:
                    E.h.wait_ge(s.sem, s.count * s.unit)
                    E.seen[s] = s.count


def build_program(debug=()):
    K = Ctx()
    nc = K.nc
    es = K.es
    dbg = {}

    x_d = K.dram("x", [128, 8, T], F32)
    ctx_d = K.dram("ctx", [128, 8, NCTX], F32)
    cc_d = K.dram("cc", [128, 8, 2], F32)
    adaw_d = K.dram("ada_w", [2, 128, 8, 6144], F32)
    adab_d = K.dram("ada_b", [2, 128, 48], F32)
    nmix_d = K.dram("norm_mix", [2, 128, 8], F32)
    nffn_d = K.dram("norm_ffn", [2, 128, 8], F32)
    out_d = K.dram("out", [128, 8, T], F32, kind="ExternalOutput")
    ident_d = K.dram("ident", [128, 128], F32)
    w1_d = K.dram("ffn_w1", [2, 128, 8, DFF], F32)
    win_d = K.dram("w_in", [2, 128, 8, NIN], F32)
    wg_d = K.dram("w_g", [2, 128, 8, 1024], F32)
    fnetw_d = K.dram("fnet_w", [2, 128, 2, 256], F32)
    wout_d = K.dram("w_out", [2, 128, 8, D], F32)
    woutg_d = K.dram("w_out_g", [2, 96, 4, D], F32)
    naq_d = K.dram("na_q", [2, 128, 1], F32)
    nak_d = K.dram("na_k", [2, 128, 1], F32)
    bt_d = K.dram("na_bt", [2, 128, 72, 128], F32)
    aw_d = K.dram("gla_aw", [2, 32, 2, 256], F32)
    ab_d = K.dram("gla_ab", [2, 128, 2, 2], F32)
    gon_d = K.dram("gla_gon", [2, 96, 1], F32)
    ct_d = K.dram("c_ct", [128, 16, T], BF16)
    st_d = K.dram("c_st", [128, 16, T], BF16)
    c256_d = K.dram("c_c256", [128, 2, 256], BF16)
    s256_d = K.dram("c_s256", [128, 2, 256], BF16)
    c64_d = K.dram("c_c64", [128, 4, 128], BF16)
    cos_d = K.dram("c_cos", [128, T], F32)
    sin_d = K.dram("c_sin", [128, T], F32)
    gmask_d = K.dram("c_gmask", [128, 2, 2, 128], BF16)
    mscan_d = K.dram("c_mscan", [128, 256], BF16)
    w3_d = K.dram("ffn_w3", [2, 128, 8, DFF], F32)
    w2_d = K.dram("ffn_w2", [2, 128, 22, D], F32)

    top = es
    xT = K.sb(top, "xT", [128, 8, NT], F32)
    xT_t = [Tok("xT%d" % i) for i in range(5)]
    identf = K.sb(top, "identf", [128, 128], F32)
    identb = K.sb(top, "identb", [128, 128], BF16)
    onesb = K.sb(top, "onesb", [128, 128], BF16)
    cc = K.sb(top, "cc", [128, 8, 2], F32)
    scb = K.sb(top, "scb", [128, 8, 2], BF16)
    class Cur:
        def __init__(self):
            self.t = None

        def __getitem__(self, k):
            return self.t[k]
    modT = Cur()
    modTs = [K.sb(top, "modT%d" % i, [128, 48, 2], F32) for i in range(2)]
    amix = Cur()
    affn = Cur()
    amixs = [K.sb(top, "amix%d" % i, [128, 8, 2], F32) for i in range(2)]
    affns = [K.sb(top, "affn%d" % i, [128, 8, 2], F32) for i in range(2)]
    adabs = [K.sb(top, "adab%d" % i, [128, 48], F32) for i in range(2)]
    nmixs = [K.sb(top, "nmix%d" % i, [128, 8], F32) for i in range(2)]
    nffns = [K.sb(top, "nffn%d" % i, [128, 8], F32) for i in range(2)]
    t_const = Tok("const")
    t_mod = Tok("mod")
    t_small = Tok("small")

    ps = [es.enter_context(nc.psum_tensor("ps%d" % i, [128, 512], F32)) for i in range(7)]
    pt = [Tok("ps%d" % i) for i in range(7)]
    psb = es.enter_context(nc.psum_tensor("psb", [128, 1024], BF16))
    ptb = Tok("psb")

    def blk_cols(b):
        return (b * 512, 512) if b < 4 else (T, NCTX)

    K.dma("sp", identf[:], ident_d, [], [t_const], "c")
    K.op("dve", [t_const], [t_const], lambda h: h.tensor_copy(out=identb[:], in_=identf[:]))
    K.op("dve", [], [t_const], lambda h: h.memset(onesb[:], 1.0))
    K.dma("sp", cc[:], cc_d, [], [t_small], "s")
    K.op("act", [t_small], [t_small], lambda h: h.activation(out=scb[:], in_=cc[:], func=AF.Silu))

    def make_ada(l, stack):
        t_sm = Tok("adasmall")
        K.dma("sp", adabs[l][:], adab_d[l], [], [t_sm], "s")
        K.dma("sp", nmixs[l][:], nmix_d[l], [], [t_sm], "s")
        K.dma("sp", nffns[l][:], nffn_d[l], [], [t_sm], "s")
        wsl = [K.sb(stack, "adaw%d" % i, [128, 8, 512], BF16) for i in range(2)]
        wsl_t = [Tok("adaw%d" % i) for i in range(2)]

        def dma(hs):
            K.dma("pool", wsl[hs % 2][:], adaw_d[l][:, :, hs * 512:(hs + 1) * 512], [], [wsl_t[hs % 2]], "adaw")

        def sec(hs):
            s = hs % 2
            if hs == 0:
                dma(0)
            if hs + 1 < 12:
                dma(hs + 1)
            for mt in range(4):
                K.mm([wsl_t[s], t_small], [pt[6]], ps[6][:, mt * 2:mt * 2 + 2],
                     [(wsl[s][:, kc, mt * 128:(mt + 1) * 128], scb[:, kc, :]) for kc in range(8)])
            K.op("dve", [pt[6], t_sm], [t_mod], lambda h: h.tensor_tensor(
                out=modTs[l][:, hs * 4:(hs + 1) * 4, :],
                in0=ps[6][:, 0:8].rearrange("p (a b) -> p a b", b=2),
                in1=adabs[l][:, hs * 4:(hs + 1) * 4].unsqueeze(2).to_broadcast([128, 4, 2]),
                op=ALU.add))

        def tail():
            for (dst, gain, sc_) in ((amixs[l], nmixs[l], 1), (affns[l], nffns[l], 4)):
                K.op("dve", [t_mod, t_sm], [t_mod], lambda h, dst=dst, gain=gain, sc_=sc_: h.scalar_tensor_tensor(
                    out=dst[:], in0=modTs[l][:, sc_ * 8:(sc_ + 1) * 8, :], scalar=1.0,
                    in1=gain[:].unsqueeze(2).to_broadcast([128, 8, 2]),
                    op0=ALU.add, op1=ALU.mult))
        return [lambda hs=hs: sec(hs) for hs in range(12)] + [tail]

    with contextlib.ExitStack() as ph:
        ada0 = make_ada(0, ph)
        for b in range(5):
            c0, n = blk_cols(b)
            src = x_d[:, :, c0:c0 + n] if b < 4 else ctx_d[:, :, :]
            for hf in range(2):
                K.dma("sp", xT[:, hf * 4:(hf + 1) * 4, c0:c0 + n], src[:, hf * 4:(hf + 1) * 4, :], [], [xT_t[b]], "xin")
        for hs in range(12):
            ada0[hs]()
        ada0[12]()
        K.barrier()

    rot = {"ps": 0}

    def norm_mod(l, which, hT, hT_t, blocks):
        a = amix if which == "mix" else affn
        sec = 0 if which == "mix" else 3
        blocks = list(blocks)
        with contextlib.ExitStack() as ph:
            sq1 = K.sb(ph, "nm_sq", [128, 8, 512], BF16)
            sq = [sq1, sq1]
            sq1_t = Tok()
            sq_t = [sq1_t, sq1_t]
            rs = [K.sb(ph, "nm_rs%d" % i, [128, 512], F32) for i in range(2)]
            rs_t = [Tok() for i in range(2)]
            tmp = [K.sb(ph, "nm_tmp%d" % i, [128, 512], F32) for i in range(4)]
            tmp_t = [Tok() for i in range(4)]
            epsb = K.sb(ph, "nm_eps", [128, 1], F32)
            eps_t = Tok()
            K.op("dve", [], [eps_t], lambda h: h.memset(epsb[:], EPS))

            def stats(bi):
                b = blocks[bi]
                c0, n = blk_cols(b)
                s = bi % 2
                pb = bi % 2
                K.op("dve", [xT_t[b]], [sq_t[s]], lambda h: h.tensor_tensor(
                    out=sq[s][:, :, :n], in0=xT[:, :, c0:c0 + n], in1=xT[:, :, c0:c0 + n], op=ALU.mult))
                K.mm([sq_t[s], t_const], [pt[pb]], ps[pb][:, :n],
                     [(onesb[:], sq[s][:, kc, :n]) for kc in range(8)])
                K.op("act", [pt[pb], eps_t], [rs_t[s]], lambda h: h.activation(
                    out=rs[s][:, :n], in_=ps[pb][:, :n], func=AF.Ln, bias=epsb[:], scale=1.0 / D))
                K.op("act", [rs_t[s]], [rs_t[s]], lambda h: h.activation(
                    out=rs[s][:, :n], in_=rs[s][:, :n], func=AF.Exp, scale=-0.5))

            ti = 0
            stats(0)
            for bi, b in enumerate(blocks):
                if bi + 1 < len(blocks):
                    stats(bi + 1)
                c0, n = blk_cols(b)
                r = 0 if b < 4 else 1
                s = bi % 2
                for kc in range(8):
                    u = ti % 4
                    ti += 1
                    eng = "pool" if kc % 4 == 3 else "dve"
                    K.op(eng, [xT_t[b], rs_t[s]], [tmp_t[u]], lambda h, u=u, kc=kc: h.tensor_tensor(
                        out=tmp[u][:, :n], in0=xT[:, kc, c0:c0 + n], in1=rs[s][:, :n], op=ALU.mult))
                    K.op("act", [tmp_t[u], t_mod], [hT_t[b]], lambda h, u=u, kc=kc: h.activation(
                        out=hT[:, kc, c0:c0 + n], in_=tmp[u][:, :n], func=AF.Identity,
                        bias=modT[:, sec * 8 + kc, r:r + 1], scale=a[:, kc, r:r + 1]))
            K.barrier()

    def ffn(l, hT, hT_t, blocks):
        with contextlib.ExitStack() as ph:
            chunks = [(i * 512, 512) for i in range(5)] + [(2560, 256)]
            w1s = [K.sb(ph, "w1s%d" % i, [128, 8, 512], BF16) for i in range(2)]
            w3s = [K.sb(ph, "w3s%d" % i, [128, 8, 512], BF16) for i in range(2)]
            w2s = [K.sb(ph, "w2s%d" % i, [128, 4, D], BF16) for i in range(2)]
            w_t = [Tok() for i in range(2)]
            s1 = [K.sb(ph, "ff_s1%d" % i, [128, 512], F32) for i in range(2)]
            s1_t = [Tok() for i in range(2)]
            g = [K.sb(ph, "ff_g%d" % i, [128, 4, 512], BF16) for i in range(2)]
            g_t = [Tok() for i in range(2)]

            def load(ch):
                f0, nf = chunks[ch]
                s = ch % 2
                K.dma("pool", w1s[s][:, :, :nf], w1_d[l][:, :, f0:f0 + nf], [], [w_t[s]], "ffw")
                K.dma("pool", w3s[s][:, :, :nf], w3_d[l][:, :, f0:f0 + nf], [], [w_t[s]], "ffw")
                K.dma("pool", w2s[s][:, :nf // 128, :], w2_d[l][:, f0 // 128:(f0 + nf) // 128, :], [], [w_t[s]], "ffw")
            load(0)
            ada_next = make_ada(l + 1, ph) if l + 1 < 2 else None
            norm_mod(l, "ffn", hT, hT_t, blocks)
            cnt = 0
            gi = 0
            oi = 0
            for ch in range(6):
                if ch + 1 < 6:
                    load(ch + 1)
                f0, nf = chunks[ch]
                s = ch % 2
                nft = nf // 128
                for b in blocks:
                    c0, n = blk_cols(b)
                    r = 0 if b < 4 else 1
                    gs = gi % 2
                    gi += 1
                    for ft in range(nft):
                        pa = cnt % 2
                        pbk = 2 + cnt % 2
                        u = cnt % 2
                        cnt += 1
                        K.mm([w_t[s], hT_t[b]], [pt[pa]], ps[pa][:, :n],
                             [(w1s[s][:, kc, ft * 128:(ft + 1) * 128], hT[:, kc, c0:c0 + n]) for kc in range(8)])
                        K.mm([w_t[s], hT_t[b]], [pt[pbk]], ps[pbk][:, :n],
                             [(w3s[s][:, kc, ft * 128:(ft + 1) * 128], hT[:, kc, c0:c0 + n]) for kc in range(8)])
                        K.op("act", [pt[pa]], [s1_t[u]], lambda h, u=u, pa=pa, n=n: h.activation(
                            out=s1[u][:, :n], in_=ps[pa][:, :n], func=AF.Silu))
                        K.op("dve", [s1_t[u], pt[pbk]], [g_t[gs]], lambda h, u=u, pbk=pbk, n=n, gs=gs, ft=ft: h.tensor_tensor(
                            out=g[gs][:, ft, :n], in0=s1[u][:, :n], in1=ps[pbk][:, :n], op=ALU.mult))
                    for dt in range(8):
                        pc = 4 + oi % 3
                        oi += 1
                        K.mm([w_t[s], g_t[gs]], [pt[pc]], ps[pc][:, :n],
                             [(w2s[s][:, ft, dt * 128:(dt + 1) * 128], g[gs][:, ft, :n]) for ft in range(nft)])
                        K.op("dve", [pt[pc], t_mod, xT_t[b]], [xT_t[b]], lambda h, pc=pc, dt=dt, c0=c0, n=n, r=r: h.scalar_tensor_tensor(
                            out=xT[:, dt, c0:c0 + n], in0=ps[pc][:, :n], scalar=modT[:, 40 + dt, r:r + 1],
                            in1=xT[:, dt, c0:c0 + n], op0=ALU.mult, op1=ALU.add))
                if ada_next is not None:
                    ada_next[2 * ch]()
                    ada_next[2 * ch + 1]()
            if ada_next is not None:
                ada_next[12]()
            K.barrier()

    def lin_fm(wt, w, col0, M, hT, hT_t, b, c0, n, pb, wtok):
        K.mm([wtok, hT_t[b]], [pt[pb]], ps[pb][:M, :n],
             [(w[:, kc, col0:col0 + M], hT[:, kc, c0:c0 + n]) for kc in range(8)])

    def out_proj_update(pairs_fn, reads, b, c0, n, r, pbs):
        for dt in range(8):
            pc = pbs[dt % len(pbs)]
            K.mm(reads, [pt[pc]], ps[pc][:, :n], pairs_fn(dt))
            K.op("dve", [pt[pc], t_mod, xT_t[b]], [xT_t[b]], lambda h, pc=pc, dt=dt: h.scalar_tensor_tensor(
                out=xT[:, dt, c0:c0 + n], in0=ps[pc][:, :n], scalar=modT[:, 16 + dt, r:r + 1],
                in1=xT[:, dt, c0:c0 + n], op0=ALU.mult, op1=ALU.add))

    def rstd_from_ps(pb, P, n, inv, rt, rs, rs_tok, eps_ap, eps_t):
        K.op("act", [pt[pb], eps_t], [rs_tok], lambda h: h.activation(
            out=rt[:P, :n], in_=ps[pb][:P, :n], func=AF.Sqrt, bias=eps_ap[:P, :], scale=inv))
        K.op("dve", [rs_tok], [rs_tok], lambda h: h.reciprocal(out=rs[:P, :n], in_=rt[:P, :n]))

    def fnet(l, hT, hT_t, need_ctx):
        with contextlib.ExitStack() as ph:
            Z = K.sb(ph, "fn_Z", [128, 18, 256], BF16)
            Z_t = Tok()
            wfn = K.sb(ph, "fn_w", [128, 8, 256], BF16)
            fnw = K.sb(ph, "fn_fw", [128, 2, 256], BF16)
            wo = K.sb(ph, "fn_wo", [128, 2, D], BF16)
            c64 = K.sb(ph, "fn_c64", [128, 4, 128], BF16)
            w_t = Tok()
            K.dma("pool", wfn[:], win_d[l][:, :, 0:256], [], [w_t], "fnw")
            K.dma("pool", fnw[:], fnetw_d[l], [], [w_t], "fnw")
            K.dma("pool", wo[:], wout_d[l][:, 0:2, :], [], [w_t], "fnw")
            K.dma("sp", c64[:], c64_d, [], [w_t], "fnw")
            cts = [K.sb(ph, "fn_ct%d" % i, [128, 2, 16, 512], BF16) for i in range(2)]
            ct_t = [Tok() for i in range(2)]
            Psb = [K.sb(ph, "fn_P%d" % i, [128, 2, 512], BF16) for i in range(2)]
            P_t = [Tok() for i in range(2)]
            Ysb = K.sb(ph, "fn_Y", [128, 2, 512], BF16)
            Y_t = Tok()
            yf = K.sb(ph, "fn_yf", [128, 2, 512], BF16)
            yf_t = Tok()
            for ti in range(18):
                pb = ti % 2
                b = min(ti // 4, 4)
                K.mm([w_t, hT_t[b]], [pt[pb]], ps[pb][:, :256],
                     [(hT[:, kc, ti * 128:(ti + 1) * 128], wfn[:, kc, :]) for kc in range(8)])
                if ti % 2 == 0:
                    K.op("dve", [pt[pb]], [Z_t], lambda h, ti=ti, pb=pb: h.tensor_copy(out=Z[:, ti, :], in_=ps[pb][:, :256]))
                else:
                    K.op("act", [pt[pb]], [Z_t], lambda h, ti=ti, pb=pb: h.activation(out=Z[:, ti, :], in_=ps[pb][:, :256], func=AF.Copy))
            jobs = [(kb, 0) for kb in range(4)] + ([(0, 1)] if need_ctx else [])
            for ji, (kb, isctx) in enumerate(jobs):
                s = ji % 2
                if not isctx:
                    ntt, n, tt0, c0, b, r = 16, 512, 0, kb * 512, kb, 0
                    K.dma("sp", cts[s][:, 0, :, :], ct_d[:, :, kb * 512:(kb + 1) * 512], [], [ct_t[s]], "ct")
                    K.dma("sp", cts[s][:, 1, :, :], st_d[:, :, kb * 512:(kb + 1) * 512], [], [ct_t[s]], "ct")
                else:
                    ntt, n, tt0, c0, b, r = 2, 256, 16, T, 4, 1
                    K.dma("sp", cts[s][:, 0, 0:2, 0:256], c256_d, [], [ct_t[s]], "ct")
                    K.dma("sp", cts[s][:, 1, 0:2, 0:256], s256_d, [], [ct_t[s]], "ct")
                for chc in range(2):
                    u = chc
                    for cs in range(2):
                        pb = cs
                        K.mm([Z_t, ct_t[s]], [pt[pb]], ps[pb][:, :n],
                             [(Z[:, tt0 + tt, chc * 128:(chc + 1) * 128], cts[s][:, cs, tt, :n]) for tt in range(ntt)])
                        if cs == 0:
                            K.op("dve", [pt[pb]], [P_t[u]], lambda h, u=u, pb=pb, cs=cs: h.tensor_copy(out=Psb[u][:, cs, :n], in_=ps[pb][:, :n]))
                        else:
                            K.op("act", [pt[pb]], [P_t[u]], lambda h, u=u, pb=pb, cs=cs: h.activation(out=Psb[u][:, cs, :n], in_=ps[pb][:, :n], func=AF.Copy))
                    pb = 2 + chc
                    K.mm([P_t[u], w_t], [pt[pb]], ps[pb][:, :n],
                         [(c64[:, 2 * isctx + 0, :], Psb[u][:, 0, :n]), (c64[:, 2 * isctx + 1, :], Psb[u][:, 1, :n])])
                    K.op("dve", [pt[pb]], [Y_t], lambda h, pb=pb, chc=chc: h.tensor_copy(out=Ysb[:, chc, :n], in_=ps[pb][:, :n]))
                for c2 in range(2):
                    pb = 4 + c2
                    K.mm([Y_t, w_t], [pt[pb]], ps[pb][:, :n],
                         [(fnw[:, c1, c2 * 128:(c2 + 1) * 128], Ysb[:, c1, :n]) for c1 in range(2)])
                    K.op("act", [pt[pb]], [yf_t], lambda h, pb=pb, c2=c2: h.activation(out=yf[:, c2, :n], in_=ps[pb][:, :n], func=AF.Copy))
                out_proj_update(lambda dt: [(wo[:, c2, dt * 128:(dt + 1) * 128], yf[:, c2, :n]) for c2 in range(2)],
                                [yf_t, w_t], b, c0, n, r, [0, 1, 2, 3, 4, 5, 6])
            K.barrier()

    def natt(l, hT, hT_t, need_ctx):
        with contextlib.ExitStack() as ph:
            kT = K.sb(ph, "na_kT", [128, 3, NT], BF16)
            kT_t = Tok()
            V = K.sb(ph, "na_V", [128, 18, 6, 65], BF16)
            V_t = Tok()
            gq = K.sb(ph, "na_gq", [128, 1], F32)
            gk = K.sb(ph, "na_gk", [128, 1], F32)
            epsb = K.sb(ph, "na_eps", [128, 1], F32)
            bones = K.sb(ph, "na_bones", [128, 128], BF16)
            g_t = Tok()
            K.dma("sp", gq[:], naq_d[l], [], [g_t], "nag")
            K.dma("sp", gk[:], nak_d[l], [], [g_t], "nag")
            K.op("dve", [g_t], [g_t], lambda h: h.tensor_scalar(out=gq[:], in0=gq[:], scalar1=0.125, scalar2=None, op0=ALU.mult))
            K.op("dve", [], [g_t], lambda h: h.memset(epsb[:], EPS))
            K.op("dve", [], [g_t], lambda h: h.memset(bones[:], 0.0))
            K.op("dve", [g_t], [g_t], lambda h: h.memset(bones[0:64, 0:64], 1.0))
            K.op("dve", [g_t], [g_t], lambda h: h.memset(bones[64:128, 64:128], 1.0))
            K.op("dve", [], [V_t], lambda h: h.memset(V[:, :, :, 64:65], 1.0))
            sq = [K.sb(ph, "na_sq%d" % i, [128, 512], BF16) for i in range(2)]
            sq_t = [Tok() for i in range(2)]
            rtl = [K.sb(ph, "na_rt%d" % i, [128, 512], F32) for i in range(2)]
            rs = [K.sb(ph, "na_rs%d" % i, [128, 512], F32) for i in range(2)]
            rs_t = [Tok() for i in range(2)]

            def qk_norm(w, wtok, gain, dst, dst_t, dcol, b, c0, n, cnt):
                for hp in range(3):
                    u = (cnt[0]) % 2
                    cnt[0] += 1
                    pa, pbk = u, 2 + u
                    lin_fm(None, w, hp * 128, 128, hT, hT_t, b, c0, n, pa, wtok)
                    K.op("act", [pt[pa]], [sq_t[u]], lambda h, u=u, pa=pa: h.activation(out=sq[u][:, :n], in_=ps[pa][:, :n], func=AF.Square))
                    K.mm([sq_t[u], g_t], [pt[pbk]], ps[pbk][:, :n], [(bones[:], sq[u][:, :n])])
                    K.op("act", [pt[pbk], g_t], [rs_t[u]], lambda h, u=u, pbk=pbk: h.activation(
                        out=rtl[u][:, :n], in_=ps[pbk][:, :n], func=AF.Ln, bias=epsb[:], scale=1.0 / 64))
                    K.op("act", [rs_t[u]], [rs_t[u]], lambda h, u=u: h.activation(
                        out=rs[u][:, :n], in_=rtl[u][:, :n], func=AF.Exp, scale=-0.5))
                    K.op("dve", [pt[pa], rs_t[u], g_t], [dst_t], lambda h, u=u, pa=pa, hp=hp: h.scalar_tensor_tensor(
                        out=dst[:, hp, dcol:dcol + n], in0=ps[pa][:, :n], scalar=gain[:, 0:1], in1=rs[u][:, :n],
                        op0=ALU.mult, op1=ALU.mult))

            cnt = [0]
            BT = K.sb(ph, "na_BT", [128, 72, 128], BF16)
            bt_t = Tok()
            for i3 in range(3):
                K.dma("pool", BT[:, 24 * i3:24 * (i3 + 1), :], bt_d[l][:, 24 * i3:24 * (i3 + 1), :], [], [bt_t], "nabt")
            for i3 in range(3):
                K.op("act", [bt_t], [bt_t], lambda h, i3=i3: h.activation(
                    out=BT[:, 24 * i3:24 * (i3 + 1), :], in_=BT[:, 24 * i3:24 * (i3 + 1), :], func=AF.Exp))
            with contextlib.ExitStack() as p1:
                wk = K.sb(p1, "na_wk", [128, 8, 384], BF16)
                wv = K.sb(p1, "na_wv", [128, 8, 384], BF16)
                wk_t = Tok()
                K.dma("pool", wk[:], win_d[l][:, :, 640:1024], [], [wk_t], "naw")
                K.dma("pool", wv[:], win_d[l][:, :, 1024:1408], [], [wk_t], "naw")
                for b in range(5):
                    c0, n = blk_cols(b)
                    qk_norm(wk, wk_t, gk, kT, kT_t, c0, b, c0, n, cnt)
                for ti in range(18):
                    pb = 4 + ti % 2
                    b = min(ti // 4, 4)
                    K.mm([wk_t, hT_t[b]], [pt[pb]], ps[pb][:, :384],
                         [(hT[:, kc, ti * 128:(ti + 1) * 128], wv[:, kc, :]) for kc in range(8)])
                    K.op("act" if ti % 2 else "dve", [pt[pb]], [V_t],
                         (lambda h, ti=ti, pb=pb: h.activation(out=V[:, ti, :, 0:64], in_=ps[pb][:, :384].rearrange("p (a b) -> p a b", b=64), func=AF.Copy))
                         if ti % 2 else
                         (lambda h, ti=ti, pb=pb: h.tensor_copy(out=V[:, ti, :, 0:64], in_=ps[pb][:, :384].rearrange("p (a b) -> p a b", b=64))))
                K.barrier()
            wq = K.sb(ph, "na_wq", [128, 8, 384], BF16)
            wo = K.sb(ph, "na_wo", [128, 3, D], BF16)
            w_t = Tok()
            K.dma("pool", wq[:], win_d[l][:, :, 256:640], [], [w_t], "naw2")
            K.dma("pool", wo[:], wout_d[l][:, 2:5, :], [], [w_t], "naw2")
            qT2 = [K.sb(ph, "na_qT%d" % i, [128, 3, 512], BF16) for i in range(2)]
            qT2_t = [Tok(), Tok()]
            PT = [K.sb(ph, "na_PT%d" % i, [128, 7, 128], BF16) for i in range(2)]
            PT_t = [Tok() for i in range(2)]
            rec = [K.sb(ph, "na_rec%d" % i, [128, 6], F32) for i in range(2)]
            rec_t = [Tok(), Tok()]
            Osb = [K.sb(ph, "na_O%d" % i, [128, 6, 64], BF16) for i in range(2)]
            O_t = [Tok(), Tok()]
            yna2 = [K.sb(ph, "na_y%d" % i, [128, 3, 512], BF16) for i in range(2)]
            yna2_t = [Tok(), Tok()]
            hcount = 0
            nblocks = list(range(5) if need_ctx else range(4))
            pending = [None]
            c0_, n_ = blk_cols(nblocks[0])
            qk_norm(wq, w_t, gq, qT2[0], qT2_t[0], 0, nblocks[0], c0_, n_, cnt)
            for bi, b in enumerate(nblocks):
                c0, n = blk_cols(b)
                r = 0 if b < 4 else 1
                qT, qT_t = qT2[bi % 2], qT2_t[bi % 2]
                yna, yna_t = yna2[bi % 2], yna2_t[bi % 2]
                tile_chunks = []
                for qi in range(n // 128):
                    if b < 4:
                        i = b * 4 + qi
                        if i <= 1:
                            js = list(range(0, 4))
                        elif i >= 14:
                            js = list(range(12, 16))
                        else:
                            js = list(range(i - 2, i + 3))
                        chunks = []
                        for j in js:
                            dl = j - i
                            v = (dl + 9) if 2 <= i <= 13 else (dl + 3)
                            chunks.append((j, v))
                        chunks += [(16, None), (17, None)]
                    else:
                        chunks = [(16, None), (17, None)]
                    tile_chunks.append(chunks)
                items = [(qi, hd) for qi in range(n // 128) for hd in range(6)]

                def stage1(k):
                    qi, hd = items[k]
                    chunks = tile_chunks[qi]
                    nch = len(chunks)
                    hp, r0 = hd // 2, (hd % 2) * 64
                    u = (hbase + k) % 2
                    pS = (0 + 2 * u, 1 + 2 * u)
                    for c, (kt, v) in enumerate(chunks):
                        pb = pS[c // 4]
                        pairs = [(kT[r0:r0 + 64, hp, kt * 128:(kt + 1) * 128], qT[r0:r0 + 64, hp, qi * 128:(qi + 1) * 128])]
                        K.mm([kT_t, qT_t, w_t, t_const], [pt[pb]], ps[pb][:, (c % 4) * 128:(c % 4 + 1) * 128], pairs)
                    n0 = min(nch, 4)
                    K.op("act", [pt[pS[0]]], [PT_t[u]], lambda h: h.activation(
                        out=PT[u][:, 0:n0, :], in_=ps[pS[0]][:, 0:n0 * 128].rearrange("p (a b) -> p a b", b=128), func=AF.Exp))
                    if nch > 4:
                        n1 = nch - 4
                        K.op("act", [pt[pS[1]]], [PT_t[u]], lambda h: h.activation(
                            out=PT[u][:, 4:4 + n1, :], in_=ps[pS[1]][:, 0:n1 * 128].rearrange("p (a b) -> p a b", b=128), func=AF.Exp))
                    loc = [v for (kt, v) in chunks if v is not None]
                    if loc:
                        nl, v0 = len(loc), loc[0]
                        K.op("dve", [PT_t[u], bt_t], [PT_t[u]], lambda h: h.tensor_tensor(
                            out=PT[u][:, 0:nl, :], in0=PT[u][:, 0:nl, :], in1=BT[:, hd * 12 + v0:hd * 12 + v0 + nl, :], op=ALU.mult))

                def stage2(k):
                    qi, hd = items[k]
                    chunks = tile_chunks[qi]
                    u = (hbase + k) % 2
                    po = 4 + qi % 2
                    K.mm([PT_t[u], V_t], [pt[po]], ps[po][:, hd * 65:(hd + 1) * 65],
                         [(PT[u][:, c, :], V[:, kt, hd, :]) for c, (kt, v) in enumerate(chunks)])

                def fin_dve(qi):
                    po = 4 + qi % 2
                    oi = qi % 2
                    ov = ps[po][:, 0:390].rearrange("p (a b) -> p a b", b=65)
                    K.op("dve", [pt[po]], [rec_t[oi]], lambda h: h.reciprocal(out=rec[oi][:], in_=ov[:, :, 64]))
                    K.op("dve", [pt[po], rec_t[oi]], [O_t[oi]], lambda h: h.tensor_tensor(
                        out=Osb[oi][:], in0=ov[:, :, 0:64], in1=rec[oi][:].unsqueeze(2).to_broadcast([128, 6, 64]), op=ALU.mult))

                def fin_pe(qi):
                    oi = qi % 2
                    Of = Osb[oi][:].rearrange("p a b -> p (a b)")
                    for hp in range(3):
                        K.op("pe", [O_t[oi], t_const], [ptb], lambda h, hp=hp: h.transpose(
                            psb[:, hp * 128:(hp + 1) * 128], Of[:, hp * 128:(hp + 1) * 128], identb[:]))
                    K.op("act", [ptb], [yna_t], lambda h: h.activation(
                        out=yna[:, :, qi * 128:(qi + 1) * 128], in_=psb[:, 0:384].rearrange("p (a b) -> p a b", b=128), func=AF.Copy))

                hbase = hcount
                nit = len(items)
                for k in range(nit + 2):
                    if k == 6 and pending[0] is not None:
                        pending[0]()
                        pending[0] = None
                    if k < nit:
                        stage1(k)
                    if 1 <= k <= nit:
                        stage2(k - 1)
                        if items[k - 1][1] == 5:
                            fin_dve(items[k - 1][0])
                    if 2 <= k <= nit + 1 and items[k - 2][1] == 5:
                        fin_pe(items[k - 2][0])
                hcount += nit
                if bi + 1 < len(nblocks):
                    c0_, n_ = blk_cols(nblocks[bi + 1])
                    qk_norm(wq, w_t, gq, qT2[(bi + 1) % 2], qT2_t[(bi + 1) % 2], 0, nblocks[bi + 1], c0_, n_, cnt)
                def _op(yna=yna, yna_t=yna_t, b=b, c0=c0, n=n, r=r):
                    out_proj_update(lambda dt: [(wo[:, hp, dt * 128:(dt + 1) * 128], yna[:, hp, :n]) for hp in range(3)],
                                    [yna_t, w_t], b, c0, n, r, [6])
                if pending[0] is not None:
                    pending[0]()
                pending[0] = _op
            if pending[0] is not None:
                pending[0]()
                pending[0] = None
            K.barrier()

    def gla(l, hT, hT_t, need_ctx):
        NB = 256
        QS = 48 ** -0.5
        with contextlib.ExitStack() as ph:
            qr = K.sb(ph, "gl_qr", [128, 2, NT], BF16)
            kr = K.sb(ph, "gl_kr", [128, 2, NT], BF16)
            aT = K.sb(ph, "gl_aT", [32, NT], BF16)
            st_t = [Tok() for i in range(5)]
            gvc = K.sb(ph, "gl_gvc", [128, 2, 384], BF16)
            sgc = K.sb(ph, "gl_sgc", [96, 4, NCTX], BF16)
            gvc_t = Tok()
            aw = K.sb(ph, "gl_aw", [32, 2, 256], BF16)
            nab = K.sb(ph, "gl_nab", [128, 2, 2], F32)
            gon = K.sb(ph, "gl_gon", [96, 1], F32)
            msk = K.sb(ph, "gl_msk", [128, 2, 2, 128], BF16)
            mscan = K.sb(ph, "gl_mscan", [128, NB], BF16)
            epsb = K.sb(ph, "gl_eps", [128, 1], F32)
            oneb = K.sb(ph, "gl_one", [128, 1], F32)
            w_t = Tok()
            K.dma("pool", aw[:], aw_d[l], [], [w_t], "glw")
            K.dma("sp", nab[:], ab_d[l], [], [w_t], "glw")
            K.dma("sp", gon[:], gon_d[l], [], [w_t], "glw")
            K.dma("sp", msk[:], gmask_d, [], [w_t], "glw")
            K.dma("sp", mscan[:], mscan_d, [], [w_t], "glw")
            K.op("dve", [w_t], [w_t], lambda h: h.tensor_scalar(out=nab[:], in0=nab[:], scalar1=-1.0, scalar2=None, op0=ALU.mult))
            K.op("dve", [], [w_t], lambda h: h.memset(epsb[:], EPS))
            K.op("dve", [], [w_t], lambda h: h.memset(oneb[:], 1.0))
            with contextlib.ExitStack() as p0:
                wg = K.sb(p0, "gl_wg", [128, 8, 1024], BF16)
                wga = K.sb(p0, "gl_wga", [128, 8, 32], BF16)
                w0_t = Tok()
                K.dma("pool", wg[:], wg_d[l], [], [w0_t], "glw0")
                K.dma("pool", wga[:], win_d[l][:, :, 2560:2592], [], [w0_t], "glw0")
                wgv = K.sb(p0, "gl_wgv", [128, 8, 384], BF16)
                wgg = K.sb(p0, "gl_wgg", [128, 8, 384], BF16)
                K.dma("pool", wgv[:], win_d[l][:, :, 1792:2176], [], [w0_t], "glw0")
                K.dma("pool", wgg[:], win_d[l][:, :, 2176:2560], [], [w0_t], "glw0")
                gvs = K.sb(p0, "gl_gvs", [128, 4, 384], BF16)
                sgs = K.sb(p0, "gl_sgs", [96, 4, 512], BF16)
                stg_t = Tok()
                cosb = K.sb(p0, "gl_cos", [128, 512], F32)
                sinb = K.sb(p0, "gl_sin", [128, 512], F32)
                cs_t = Tok()
                t1 = [K.sb(p0, "gl_t1%d" % i, [128, 512], F32) for i in range(2)]
                t2 = [K.sb(p0, "gl_t2%d" % i, [128, 512], F32) for i in range(2)]
                r_t = [Tok(), Tok()]
                cnt = 0
                for b5 in range(5):
                    c0, n = blk_cols(b5)
                    lat = b5 < 4
                    if lat:
                        K.dma("sp", cosb[:, :n], cos_d[:, c0:c0 + n], [], [cs_t], "cs")
                        K.dma("sp", sinb[:, :n], sin_d[:, c0:c0 + n], [], [cs_t], "cs")
                    lin_fm(None, wga, 0, 32, hT, hT_t, b5, c0, n, 6, w0_t)
                    K.op("act", [pt[6]], [st_t[b5]], lambda h: h.activation(out=aT[:, c0:c0 + n], in_=ps[6][:32, :n], func=AF.Copy))
                    for hp in range(2):
                        for which in range(2):
                            base = which * 512
                            dst = qr if which == 0 else kr
                            u = cnt % 2
                            cnt += 1
                            pa, pbk = (0, 1) if u == 0 else (2, 3)
                            lin_fm(None, wg, base + hp * 128, 128, hT, hT_t, b5, c0, n, pa, w0_t)
                            if lat:
                                lin_fm(None, wg, base + 256 + hp * 128, 128, hT, hT_t, b5, c0, n, pbk, w0_t)
                                K.op("dve", [pt[pa], cs_t], [r_t[u]], lambda h: h.tensor_tensor(out=t1[u][:, :n], in0=ps[pa][:, :n], in1=cosb[:, :n], op=ALU.mult))
                                K.op("dve", [pt[pbk], cs_t, r_t[u]], [r_t[u]], lambda h: h.tensor_tensor(out=t2[u][:, :n], in0=ps[pbk][:, :n], in1=sinb[:, :n], op=ALU.mult))
                                K.op("pool", [r_t[u]], [st_t[b5]], lambda h: h.tensor_tensor(out=dst[:, hp, c0:c0 + n], in0=t1[u][:, :n], in1=t2[u][:, :n], op=ALU.add))
                            else:
                                K.op("act", [pt[pa]], [st_t[b5]], lambda h: h.activation(out=dst[:, hp, c0:c0 + n], in_=ps[pa][:, :n], func=AF.Copy))
                    for tt in range(n // 128):
                        pv = 4 + tt % 2
                        K.mm([w0_t, hT_t[b5]], [pt[pv]], ps[pv][:, :384],
                             [(hT[:, kc, c0 + tt * 128:c0 + (tt + 1) * 128], wgv[:, kc, :]) for kc in range(8)])
                        if lat:
                            K.op("act", [pt[pv]], [stg_t], lambda h, tt=tt, pv=pv: h.activation(out=gvs[:, tt, :], in_=ps[pv][:, :384], func=AF.Copy))
                        else:
                            K.op("act", [pt[pv]], [gvc_t], lambda h, tt=tt, pv=pv: h.activation(out=gvc[:, tt, :], in_=ps[pv][:, :384], func=AF.Copy))
                    if lat or need_ctx:
                        for hd in range(4):
                            pv = 4 + hd % 2
                            lin_fm(None, wgg, hd * 96, 96, hT, hT_t, b5, c0, n, pv, w0_t)
                            if lat:
                                K.op("act", [pt[pv]], [stg_t], lambda h, hd=hd, pv=pv: h.activation(out=sgs[:, hd, :n], in_=ps[pv][:96, :n], func=AF.Silu))
                            else:
                                K.op("act", [pt[pv]], [gvc_t], lambda h, hd=hd, pv=pv: h.activation(out=sgc[:, hd, :n], in_=ps[pv][:96, :n], func=AF.Silu))
                    if lat:
                        for tt in range(4):
                            K.op("pool", [stg_t], [hT_t[b5]], lambda h, tt=tt: h.tensor_copy(out=hT[:, tt, c0:c0 + 384], in_=gvs[:, tt, :]))
                        for hd in range(4):
                            K.op("pool", [stg_t], [hT_t[b5]], lambda h, hd=hd: h.tensor_copy(out=hT[:96, 4 + hd, c0:c0 + 512], in_=sgs[:, hd, :]))
                K.barrier()
            wo = K.sb(ph, "gl_wo", [96, 4, D], BF16)
            w2_t = Tok()
            K.dma("pool", wo[:], woutg_d[l], [], [w2_t], "glw2")
            oF = K.sb(ph, "gl_oF", [96, 4, NT], BF16)
            oF_t = [Tok() for i in range(9)]
            R = []
            for d in range(2):
                r_ = dict(
                    e1=K.sb(ph, "gl_e1_%d" % d, [128, NB], F32),
                    bpos=K.sb(ph, "gl_bp_%d" % d, [128, NB], F32), eb=K.sb(ph, "gl_eb_%d" % d, [128, NB], F32),
                    qin=K.sb(ph, "gl_qin_%d" % d, [128, 2, NB], BF16), kin=K.sb(ph, "gl_kin_%d" % d, [128, 2, NB], BF16),
                    ktok=K.sb(ph, "gl_ktok_%d" % d, [128, 2, 2, 128], BF16),
                    dec=K.sb(ph, "gl_dec_%d" % d, [128, 2, 2], F32), AM=K.sb(ph, "gl_AM_%d" % d, [128, 4, 128], BF16),
                    S=K.sb(ph, "gl_S_%d" % d, [128, 2, 192], F32), Sb=K.sb(ph, "gl_Sb_%d" % d, [128, 2, 192], BF16),
                    osum=K.sb(ph, "gl_osum_%d" % d, [96, 4, NB], F32),
                    sg_t=Tok(), n_t=Tok(),
                    g_t=Tok(), qk_t=Tok(), ktok_t=Tok(), gv_t=Tok(), dec_t=Tok(), AM_t=Tok(), S_t=Tok(), Sb_t=Tok(), os_t=Tok(),
                    BA=3 * d, BB=3 * d + 1, BO=3 * d + 2)
                R.append(r_)

            def block_prep(d, b):
                r_ = R[d]
                c0 = b * NB if b < 8 else T
                n = NB
                b5 = min(c0 // 512, 4)
                BB = r_["BB"]
                e1, bpos, eb = r_["e1"], r_["bpos"], r_["eb"]
                lsb = e1
                enb = eb
                gt = r_["g_t"]
                for hp in range(2):
                    K.mm([st_t[b5], w_t], [pt[BB]], ps[BB][:, :n], [(aw[:, d, hp * 128:(hp + 1) * 128], aT[:, c0:c0 + n])])
                    yield
                    K.op("act", [pt[BB], w_t], [gt], lambda h: h.activation(
                        out=e1[:, :n], in_=ps[BB][:, :n], func=AF.Exp, bias=nab[:, d, hp:hp + 1], scale=-1.0))
                    yield
                    K.op("act", [gt, w_t], [gt], lambda h: h.activation(out=lsb[:, :n], in_=e1[:, :n], func=AF.Ln, bias=oneb[:], scale=1.0))
                    yield
                    K.op("dve", [gt, w_t], [gt], lambda h: h.tensor_tensor_scan(
                        out=bpos[:, :n], data0=mscan[:, :n], data1=lsb[:, :n], initial=0.0, op0=ALU.mult, op1=ALU.add))
                    yield
                    bsel = bpos
                    if d == 1:
                        K.op("dve", [gt], [gt], lambda h: h.tensor_tensor(out=e1[:, :n], in0=lsb[:, :n], in1=bpos[:, :n], op=ALU.subtract))
                        yield
                        K.op("dve", [gt], [gt], lambda h: h.tensor_tensor(
                            out=e1[:, :n].rearrange("p (a b) -> p a b", b=128), in0=e1[:, :n].rearrange("p (a b) -> p a b", b=128),
                            in1=bpos[:, :n].rearrange("p (a b) -> p a b", b=128)[:, :, 127:128].to_broadcast([128, n // 128, 128]), op=ALU.add))
                        yield
                        bsel = e1
                    K.op("act", [gt], [gt], lambda h, bsel=bsel: h.activation(out=eb[:, :n], in_=bsel[:, :n], func=AF.Exp, scale=-1.0 / 16))
                    yield
                    sel = 127 if d == 0 else 0
                    K.op("dve", [gt], [r_["dec_t"]], lambda h: h.tensor_copy(
                        out=r_["dec"][:, hp, :], in_=eb[:, :n].rearrange("p (a b) -> p a b", b=128)[:, :, sel]))
                    yield
                    K.op("dve", [gt, st_t[b5]], [r_["qk_t"]], lambda h: h.scalar_tensor_tensor(
                        out=r_["qin"][:, hp, :n], in0=qr[:, hp, c0:c0 + n], scalar=QS, in1=eb[:, :n], op0=ALU.mult, op1=ALU.mult))
                    yield
                    K.op("act", [gt, r_["dec_t"], r_["qk_t"]], [gt], lambda h, bsel=bsel: h.activation(out=enb[:, :n], in_=bsel[:, :n], func=AF.Exp, scale=1.0 / 16))
                    yield
                    K.op("dve", [gt, st_t[b5]], [r_["qk_t"]], lambda h: h.tensor_tensor(
                        out=r_["kin"][:, hp, :n], in0=kr[:, hp, c0:c0 + n], in1=enb[:, :n], op=ALU.mult))
                    yield
                    for tt in range(2):
                        K.op("pe", [r_["qk_t"], t_const], [ptb], lambda h, tt=tt: h.transpose(
                            psb[:, (d * 2 + tt) * 128:(d * 2 + tt + 1) * 128], r_["kin"][:, hp, tt * 128:(tt + 1) * 128], identb[:]))
                        yield
                    K.op("act", [ptb], [r_["ktok_t"]], lambda h: h.activation(
                        out=r_["ktok"][:, :, hp, :], in_=psb[:, d * 256:(d + 1) * 256].rearrange("p (t f) -> p t f", t=2), func=AF.Copy))
                    yield

            def scan_tile(d, b, tt, want_o, second):
                r_ = R[d]
                c0 = b * NB if b < 8 else T
                tc0 = tt * 128
                BA, BB, BO = r_["BA"], r_["BB"], r_["BO"]
                qin, kin, Sb, S, AM, dec, ktok = r_["qin"], r_["kin"], r_["Sb"], r_["S"], r_["AM"], r_["dec"], r_["ktok"]
                if b < 8:
                    ti_ = b * 2 + tt
                    gv_tok = hT_t[ti_ // 4]

                    def gvf(col, w):
                        return hT[:, ti_ % 4, (ti_ // 4) * 512 + col:(ti_ // 4) * 512 + col + w]
                else:
                    gv_tok = gvc_t

                    def gvf(col, w):
                        return gvc[:, tt, col:col + w]
                if want_o:
                    for hd in (0, 2, 1, 3):
                        hp, r0 = hd // 2, (hd % 2) * 64
                        pa = BA if hd % 2 == 0 else BB
                        K.mm([r_["qk_t"]], [pt[pa]], ps[pa][:, (hd // 2) * 128:(hd // 2 + 1) * 128],
                             [(kin[r0:r0 + 48, hp, tc0:tc0 + 128], qin[r0:r0 + 48, hp, tc0:tc0 + 128])])
                        yield
                    for par, pa in ((0, BA), (1, BB)):
                        K.op("dve", [pt[pa], w_t], [r_["AM_t"]], lambda h, par=par, pa=pa: h.tensor_tensor(
                            out=AM[:, par:4:2, :], in0=ps[pa][:, 0:256].rearrange("p (a b) -> p a b", b=128),
                            in1=msk[:, d, :, :], op=ALU.mult))
                        yield
                    for hd in range(4):
                        hp, r0 = hd // 2, (hd % 2) * 64
                        K.mm([r_["AM_t"], gv_tok, r_["Sb_t"], r_["qk_t"]], [pt[BO]], ps[BO][:96, hd * 128:(hd + 1) * 128],
                             [(gvf(hd * 96, 96), AM[:, hd, :]),
                              (Sb[r0:r0 + 48, hp, (hd % 2) * 96:(hd % 2 + 1) * 96], qin[r0:r0 + 48, hp, tc0:tc0 + 128])])
                        yield
                    ov = ps[BO][:96, :].rearrange("p (a b) -> p a b", b=128)
                    if not second:
                        K.op("act", [pt[BO]], [oF_t[b]], lambda h: h.activation(
                            out=oF[:, :, c0 + tc0:c0 + tc0 + 128], in_=ov, func=AF.Copy))
                    else:
                        K.op("dve", [pt[BO], oF_t[b]], [r_["os_t"]], lambda h: h.tensor_tensor(
                            out=r_["osum"][:, :, tc0:tc0 + 128], in0=ov, in1=oF[:, :, c0 + tc0:c0 + tc0 + 128], op=ALU.add))
                    yield
                for hp in range(2):
                    K.mm([r_["ktok_t"], gv_tok], [pt[BA]], ps[BA][:, hp * 192:(hp + 1) * 192],
                         [(ktok[:, tt, hp, :], gvf(hp * 192, 192))])
                    yield
                for hp in range(2):
                    K.op("act", [r_["S_t"], r_["dec_t"]], [r_["S_t"]], lambda h, hp=hp: h.activation(
                        out=S[:, hp, :], in_=S[:, hp, :], func=AF.Identity, scale=dec[:, hp, tt:tt + 1]))
                    yield
                    K.op("dve", [pt[BA], r_["S_t"], r_["dec_t"]], [r_["S_t"]], lambda h, hp=hp: h.scalar_tensor_tensor(
                        out=S[:, hp, :], in0=ps[BA][:, hp * 192:(hp + 1) * 192], scalar=dec[:, hp, tt:tt + 1], in1=S[:, hp, :],
                        op0=ALU.mult, op1=ALU.add))
                    yield
                K.op("act", [r_["S_t"]], [r_["Sb_t"]], lambda h: h.activation(out=Sb[:], in_=S[:], func=AF.Copy))
                yield

            def finalize(d, b):
                r_ = R[d]
                c0 = b * NB if b < 8 else T
                n = NB
                b5 = min(c0 // 512, 4)
                r = 0 if b < 8 else 1
                osum = r_["osum"]
                BA, BB, BO = r_["BA"], r_["BB"], r_["BO"]
                if b < 8:
                    sg_t = hT_t[b5]

                    def sgf(hd):
                        return hT[:96, 4 + hd, c0:c0 + n]
                else:
                    sg_t = gvc_t

                    def sgf(hd):
                        return sgc[:, hd, :n]
                sq = r_["AM"][:96, 0:2, :].rearrange("p a b -> p (a b)")
                am_t = r_["AM_t"]
                rt = r_["e1"]
                n_t, gt = r_["n_t"], r_["g_t"]
                for hd in range(4):
                    K.op("act", [r_["os_t"]], [n_t, am_t], lambda h, hd=hd: h.activation(out=sq[:, :n], in_=osum[:, hd, :n], func=AF.Square))
                    yield
                    K.mm([n_t, am_t, t_const], [pt[BB]], ps[BB][:96, :n], [(onesb[:96, :96], sq[:, :n])])
                    yield
                    K.op("act", [pt[BB], w_t], [n_t, gt], lambda h: h.activation(
                        out=rt[:96, :n], in_=ps[BB][:96, :n], func=AF.Ln, bias=epsb[:96, :], scale=1.0 / 96))
                    yield
                    K.op("act", [n_t], [n_t, gt], lambda h: h.activation(out=rt[:96, :n], in_=rt[:96, :n], func=AF.Exp, scale=-0.5))
                    yield
                    K.op("dve", [n_t, gt, w_t], [r_["os_t"]], lambda h, hd=hd: h.scalar_tensor_tensor(
                        out=osum[:, hd, :n], in0=osum[:, hd, :n], scalar=gon[:, 0:1], in1=rt[:96, :n], op0=ALU.mult, op1=ALU.mult))
                    yield
                    K.op("dve", [r_["os_t"], sg_t], [sg_t], lambda h, hd=hd: h.tensor_tensor(
                        out=sgf(hd), in0=osum[:, hd, :n], in1=sgf(hd), op=ALU.mult))
                    yield
                for dt in range(8):
                    pc = BA if dt % 2 == 0 else BO
                    K.mm([sg_t, w2_t], [pt[pc]], ps[pc][:, :n], [(wo[:, hd, dt * 128:(dt + 1) * 128], sgf(hd)) for hd in range(4)])
                    yield
                    K.op("dve", [pt[pc], t_mod, xT_t[b5]], [xT_t[b5]], lambda h, dt=dt, pc=pc: h.scalar_tensor_tensor(
                        out=xT[:, dt, c0:c0 + n], in0=ps[pc][:, :n], scalar=modT[:, 16 + dt, r:r + 1],
                        in1=xT[:, dt, c0:c0 + n], op0=ALU.mult, op1=ALU.add))
                    yield

            def dir_gen(d):
                r_ = R[d]
                K.op("dve", [], [r_["S_t"]], lambda h: h.memset(r_["S"][:], 0.0))
                K.op("dve", [], [r_["Sb_t"]], lambda h: h.memset(r_["Sb"][:], 0.0))
                order = [8] + (list(range(8)) if d == 0 else list(range(7, -1, -1)))
                for b in order:
                    want_o = (b < 8) or need_ctx
                    if b == 8:
                        second = (d == 1)
                    else:
                        second = (b >= 4) if d == 0 else (b <= 3)
                    yield from block_prep(d, b)
                    for tt in ((0, 1) if d == 0 else (1, 0)):
                        yield from scan_tile(d, b, tt, want_o, second)
                    if want_o and second:
                        yield from finalize(d, b)
                    yield "STEP"

            def run_step(gens):
                active = list(gens)
                while active:
                    for g in list(active):
                        if next(g) == "STEP":
                            active.remove(g)
            gF, gB = dir_gen(0), dir_gen(1)
            run_step([gF])
            run_step([gB])
            for s_ in range(8):
                run_step([gF, gB])
            K.barrier()

    def mixers(l, hT, hT_t, need_ctx):
        if "nofn" not in debug:
            fnet(l, hT, hT_t, need_ctx)
        if "nona" not in debug:
            natt(l, hT, hT_t, need_ctx)
        if "nogla" not in debug:
            gla(l, hT, hT_t, need_ctx)

    for l in range(2):
        need_ctx = (l == 0)
        modT.t, amix.t, affn.t = modTs[l], amixs[l], affns[l]
        with contextlib.ExitStack() as lay:
            hT = K.sb(lay, "hT", [128, 8, NT], BF16)
            hT_t = [Tok("hT%d" % i) for i in range(5)]
            if "ffn" not in debug:
                norm_mod(l, "mix", hT, hT_t, range(5))
                mixers(l, hT, hT_t, need_ctx)
            blocks = range(5) if need_ctx else range(4)
            if "noffn" not in debug:
                ffn(l, hT, hT_t, blocks)
            K.barrier()
        if "ffn" in debug or "l0" in debug:
            break

    t_out = Tok("out")
    for name, (buf, shape, dt) in dbg.items():
        dd = K.dram("dbg_" + name, shape, dt, kind="ExternalOutput")
        K.barrier()
        K.dma("sp", dd, buf[:], [], [t_out], "out")
        K.outs["dbg_" + name] = shape

    for b in range(4):
        c0, n = blk_cols(b)
        for hf in range(2):
            K.dma("sp", out_d[:, hf * 4:(hf + 1) * 4, c0:c0 + n], xT[:, hf * 4:(hf + 1) * 4, c0:c0 + n], [xT_t[b]], [t_out], "out")
    S = t_out.dsem["sp"]
    nc.sync.wait_ge(S.sem, S.count * 16)
    es.close()
    return K


_CONSTS = {}


def _consts():
    if _CONSTS:
        return _CONSTS
    bf = ml_dtypes.bfloat16
    t = np.arange(T, dtype=np.int64)
    ang = 2.0 * np.pi * ((t[:, None] * t[None, :]) % T).astype(np.float64) / T
    _CONSTS["c_ct"] = np.ascontiguousarray(np.cos(ang).reshape(16, 128, T).transpose(1, 0, 2)).astype(bf)
    _CONSTS["c_st"] = np.ascontiguousarray(np.sin(ang).reshape(16, 128, T).transpose(1, 0, 2)).astype(bf)
    t2 = np.arange(NCTX, dtype=np.int64)
    ang2 = 2.0 * np.pi * ((t2[:, None] * t2[None, :]) % NCTX).astype(np.float64) / NCTX
    _CONSTS["c_c256"] = np.ascontiguousarray(np.cos(ang2).reshape(2, 128, NCTX).transpose(1, 0, 2)).astype(bf)
    _CONSTS["c_s256"] = np.ascontiguousarray(np.sin(ang2).reshape(2, 128, NCTX).transpose(1, 0, 2)).astype(bf)
    g = np.arange(64)
    a64 = 2.0 * np.pi * ((g[:, None] * g[None, :]) % 64) / 64.0
    c64 = np.zeros((128, 4, 128))
    for blk in range(2):
        sl = slice(64 * blk, 64 * blk + 64)
        c64[sl, 0, sl] = np.cos(a64) / np.sqrt(T * 64.0)
        c64[sl, 1, sl] = -np.sin(a64) / np.sqrt(T * 64.0)
        c64[sl, 2, sl] = np.cos(a64) / np.sqrt(NCTX * 64.0)
        c64[sl, 3, sl] = -np.sin(a64) / np.sqrt(NCTX * 64.0)
    _CONSTS["c_c64"] = c64.astype(bf)
    cosT = np.zeros((128, T), np.float32)
    sinT = np.zeros((128, T), np.float32)
    row = (t // 64).astype(np.float64)
    col = (t % 64).astype(np.float64)
    for p in range(128):
        d = p % 64
        if d >= 48:
            continue
        within = d % 24
        j = within % 12
        inv = 10000.0 ** (-j / 12.0)
        pos = row if d < 24 else col
        cosT[p] = np.cos(pos * inv)
        sinT[p] = np.sin(pos * inv) * (-1.0 if within < 12 else 1.0)
    _CONSTS["c_cos"] = cosT
    _CONSTS["c_sin"] = sinT
    s_ = np.arange(128)
    gm = np.zeros((128, 2, 2, 128), np.float32)
    gm[:, 0, :, :] = (s_[:, None] <= s_[None, :])[:, None, :]
    gm[:, 1, :, :] = (s_[:, None] >= s_[None, :])[:, None, :]
    _CONSTS["c_gmask"] = gm.astype(bf)
    ms = np.ones((128, 256), np.float32)
    ms[:, 0] = 0.0
    ms[:, 128] = 0.0
    _CONSTS["c_mscan"] = ms.astype(bf)
    return _CONSTS


def _host_prep(inputs):
    f = lambda a: np.ascontiguousarray(np.asarray(a, dtype=np.float32))
    x, c, ctx, c_ctx = f(inputs["x"]), f(inputs["c"]), f(inputs["ctx"]), f(inputs["c_ctx"])
    shared = {}
    shared["ada_w"] = f(f(inputs["ada_w"]).reshape(2, 8, 128, 6144).transpose(0, 2, 1, 3))
    shared["ada_b"] = f(f(inputs["ada_b"]).reshape(2, 48, 128).transpose(0, 2, 1))
    shared["norm_mix"] = f(f(inputs["norm_mix"]).reshape(2, 8, 128).transpose(0, 2, 1))
    shared["norm_ffn"] = f(f(inputs["norm_ffn"]).reshape(2, 8, 128).transpose(0, 2, 1))
    shared["ident"] = np.eye(128, dtype=np.float32)
    shared["ffn_w1"] = f(f(inputs["ffn_w1"]).reshape(2, 8, 128, DFF).transpose(0, 2, 1, 3))
    shared["ffn_w3"] = f(f(inputs["ffn_w3"]).reshape(2, 8, 128, DFF).transpose(0, 2, 1, 3))
    shared["ffn_w2"] = f(f(inputs["ffn_w2"]).reshape(2, 22, 128, D).transpose(0, 2, 1, 3))
    w_in = f(inputs["w_in"])
    shared["w_in"] = f(w_in.reshape(2, 8, 128, NIN).transpose(0, 2, 1, 3))
    wg = np.zeros((2, D, 1024), np.float32)
    dd = np.arange(48)
    partner = np.where((dd % 24) < 12, dd + 12, dd - 12)
    for hh in range(4):
        wg[:, :, 64 * hh:64 * hh + 48] = w_in[:, :, 1408 + 48 * hh + dd]
        wg[:, :, 256 + 64 * hh:256 + 64 * hh + 48] = w_in[:, :, 1408 + 48 * hh + partner]
        wg[:, :, 512 + 64 * hh:512 + 64 * hh + 48] = w_in[:, :, 1600 + 48 * hh + dd]
        wg[:, :, 768 + 64 * hh:768 + 64 * hh + 48] = w_in[:, :, 1600 + 48 * hh + partner]
    shared["w_g"] = f(wg.reshape(2, 8, 128, 1024).transpose(0, 2, 1, 3))
    shared["fnet_w"] = f(f(inputs["fnet_w"]).reshape(2, 2, 128, 256).transpose(0, 2, 1, 3))
    w_out = f(inputs["w_out"])
    shared["w_out"] = f(w_out.reshape(2, 8, 128, D).transpose(0, 2, 1, 3))
    shared["w_out_g"] = f(w_out[:, 640:, :].reshape(2, 4, 96, D).transpose(0, 2, 1, 3))
    shared["na_q"] = f(np.tile(f(inputs["na_q_norm"]), (1, 2))[:, :, None])
    shared["na_k"] = f(np.tile(f(inputs["na_k_norm"]), (1, 2))[:, :, None])
    rpb = f(inputs["na_rpb"])
    pk = np.arange(128)
    a_, kc_ = pk // 64, pk % 64
    b_, c_ = pk // 64, pk % 64
    dcm = np.clip(kc_[:, None] - c_[None, :], -15, 15) + 15
    cq0 = np.clip(c_ - 8, 0, 48)
    col_ok = (kc_[:, None] >= cq0[None, :]) & (kc_[:, None] < cq0[None, :] + 16)
    bt = np.full((2, 128, 72, 128), NEG, np.float32)
    ents = [(dl, 0) for dl in range(-3, 4)] + [(-2, 1), (-1, 0), (0, 0), (1, 0), (2, 1)]
    for e, (dl, msk_) in enumerate(ents):
        drm = 2 * dl + a_[:, None] - b_[None, :] + 7
        ok = col_ok & (drm >= 0) & (drm <= 14)
        if msk_ and dl == -2:
            ok = ok & ((2 * dl + a_[:, None]) >= (-4 + b_[None, :]))
        if msk_ and dl == 2:
            ok = ok & ((2 * dl + a_[:, None]) <= (3 + b_[None, :]))
        drc = np.clip(drm, 0, 14)
        for hh in range(6):
            vals = rpb[:, hh][:, drc, dcm]
            bt[:, :, hh * 12 + e, :] = np.where(ok[None], vals, NEG)
    shared["na_bt"] = bt
    aw = np.zeros((2, 32, 2, 256), np.float32)
    ab = np.zeros((2, 128, 2, 2), np.float32)
    gaw = f(inputs["gla_alpha_w"]); gab = f(inputs["gla_alpha_b"])
    for dr_ in range(2):
        for hh in range(4):
            aw[:, 16 * dr_:16 * dr_ + 16, dr_, 64 * hh:64 * hh + 48] = gaw[:, dr_, :, 48 * hh:48 * hh + 48]
            ab[:, 64 * (hh % 2):64 * (hh % 2) + 48, dr_, hh // 2] = gab[:, dr_, 48 * hh:48 * hh + 48]
    shared["gla_aw"] = aw
    shared["gla_ab"] = ab
    shared["gla_gon"] = f(f(inputs["gla_o_norm"])[:, :, None])
    shared.update(_consts())
    per_core = []
    for b in range(8):
        m = dict(shared)
        m["x"] = f(x[b].T.reshape(8, 128, T).transpose(1, 0, 2))
        m["ctx"] = f(ctx[b].T.reshape(8, 128, NCTX).transpose(1, 0, 2))
        ccv = np.stack([c[b], c_ctx], axis=-1)
        m["cc"] = f(ccv.reshape(8, 128, 2).transpose(1, 0, 2))
        per_core.append(m)
    return per_core


_DEBUG = tuple(x for x in os.environ.get("KDEBUG", "").split(",") if x)
LAST = {}


def kernel(**inputs):
    in_maps = _host_prep(inputs)
    K = build_program(debug=_DEBUG)
    ncores = int(os.environ.get("KCORES", "8"))
    res = run_bass_kernel_spmd(K.nc, in_maps[:ncores], core_ids=list(range(ncores)))
    LAST["res"] = res
    outs = []
    for r in res.results:
        o = np.asarray(r["out"])
        outs.append(o.transpose(1, 0, 2).reshape(D, T).T)
    return np.ascontiguousarray(np.stack(outs, axis=0)).astype(np.float32)
```

```python
import contextlib
import os
import numpy as np
import ml_dtypes
import concourse.bass as bass
import concourse.mybir as mybir
from concourse.bass_utils import run_bass_kernel_spmd

F32 = mybir.dt.float32
BF16 = mybir.dt.bfloat16
AF = mybir.ActivationFunctionType
ALU = mybir.AluOpType

D = 1024
T = 2048
NCTX = 256
NT = T + NCTX
DFF = 2816
NIN = 2592
EPS = 1e-6
NEG = -30000.0
SAME_ENGINE_RAW = True


class Tok:
    __slots__ = ("w", "r", "dsem", "name")

    def __init__(self, name=""):
        self.w = []
        self.r = {}
        self.dsem = None
        self.name = name


class Src:
    def __init__(self, name, sem, unit, h=None):
        self.name = name
        self.sem = sem
        self.unit = unit
        self.h = h
        self.count = 0
        self.seen = {}


class Ctx:
    def __init__(self):
        self.nc = bass.Bass("TRN2", target_bir_lowering=False)
        nc = self.nc
        self.es = contextlib.ExitStack()
        self.engs = {}
        for nm, h in (("pe", nc.tensor), ("act", nc.scalar), ("dve", nc.vector),
                      ("pool", nc.gpsimd), ("sp", nc.sync)):
            sem = self.es.enter_context(nc.semaphore("sem_" + nm))
            self.engs[nm] = Src(nm, sem, 1, h)
        self.dsems = []
        self.outs = {}
        self.n_inst = 0

    def dram(self, name, shape, dt, kind="ExternalInput"):
        return self.nc.dram_tensor(name, list(shape), dt, kind=kind).ap()

    def sb(self, stack, name, shape, dt):
        self.n_sb = getattr(self, "n_sb", 0) + 1
        return stack.enter_context(self.nc.sbuf_tensor("sb%d_%s" % (self.n_sb, name), list(shape), dt))

    def new_dsem(self, name):
        sem = self.es.enter_context(self.nc.semaphore("d_" + name + str(len(self.dsems))))
        s = Src("dma_" + name, sem, 16)
        self.dsems.append(s)
        return s

    def _wait_deps(self, E, reads, writes):
        deps = {}
        for t in reads:
            for (s, n) in t.w:
                deps[s] = max(deps.get(s, 0), n)
        for t in writes:
            for (s, n) in t.w:
                deps[s] = max(deps.get(s, 0), n)
            for s, n in t.r.items():
                deps[s] = max(deps.get(s, 0), n)
        for s, n in deps.items():
            if s is E and (E.name == "pe" or not SAME_ENGINE_RAW):
                continue
            if E.seen.get(s, 0) < n:
                E.h.wait_ge(s.sem, n * s.unit)
                E.seen[s] = n

    def op(self, eng, reads, writes, emit):
        E = self.engs[eng]
        self._wait_deps(E, reads, writes)
        ins = emit(E.h)
        E.count += 1
        ins.then_inc(E.sem, 1)
        self.n_inst += 1
        for t in reads:
            t.r[E] = E.count
        for t in writes:
            t.w = [(E, E.count)]
            t.r = {}
        return ins

    def mm(self, reads, writes, out, pairs, first=True, last=True):
        E = self.engs["pe"]
        self._wait_deps(E, reads, writes)
        n = len(pairs)
        ins = None
        for i, (l, r) in enumerate(pairs):
            ins = E.h.matmul(out, l, r, start=(first and i == 0), stop=(last and i == n - 1))
            self.n_inst += 1
        E.count += 1
        ins.then_inc(E.sem, 1)
        for t in reads:
            t.r[E] = E.count
        for t in writes:
            t.w = [(E, E.count)]
            t.r = {}

    def transpose(self, reads, writes, out, in_, ident):
        return self.op("pe", reads, writes, lambda h: h.transpose(out, in_, ident))

    def dma(self, q, out, in_, reads, writes, name="x"):
        E = self.engs[q]
        self._wait_deps(E, reads, writes)
        wt = writes[0]
        if wt.dsem is None:
            wt.dsem = {}
        if q not in wt.dsem:
            wt.dsem[q] = self.new_dsem(name + q)
        S = wt.dsem[q]
        ins = E.h.dma_start(out=out, in_=in_)
        S.count += 1
        ins.then_inc(S.sem, 16)
        self.n_inst += 1
        for t in reads:
            t.r[S] = S.count
        for t in writes:
            t.w = [(s_, n_) for (s_, n_) in t.w if (s_.unit == 16 and s_ is not S)] + [(S, S.count)]
            t.r = {}

    def barrier(self):
        allsrc = list(self.engs.values()) + self.dsems
        for E in self.engs.values():
            for s in allsrc:
                if s is E or s.count == 0:
                    continue
                if E.seen.get(s, 0) < s.count:
                    E.h.wait_ge(s.sem, s.count * s.unit)
                    E.seen[s] = s.count


def build_program(debug=()):
    K = Ctx()
    nc = K.nc
    es = K.es
    dbg = {}

    x_d = K.dram("x", [128, 8, T], F32)
    ctx_d = K.dram("ctx", [128, 8, NCTX], F32)
    cc_d = K.dram("cc", [128, 8, 2], F32)
    adaw_d = K.dram("ada_w", [2, 128, 8, 6144], F32)
    adab_d = K.dram("ada_b", [2, 128, 48], F32)
    nmix_d = K.dram("norm_mix", [2, 128, 8], F32)
    nffn_d = K.dram("norm_ffn", [2, 128, 8], F32)
    out_d = K.dram("out", [128, 8, T], F32, kind="ExternalOutput")
    ident_d = K.dram("ident", [128, 128], F32)
    w1_d = K.dram("ffn_w1", [2, 128, 8, DFF], F32)
    win_d = K.dram("w_in", [2, 128, 8, NIN], F32)
    wg_d = K.dram("w_g", [2, 128, 8, 1024], F32)
    fnetw_d = K.dram("fnet_w", [2, 128, 2, 256], F32)
    wout_d = K.dram("w_out", [2, 128, 8, D], F32)
    woutg_d = K.dram("w_out_g", [2, 96, 4, D], F32)
    naq_d = K.dram("na_q", [2, 128, 1], F32)
    nak_d = K.dram("na_k", [2, 128, 1], F32)
    bt_d = K.dram("na_bt", [2, 128, 72, 128], F32)
    aw_d = K.dram("gla_aw", [2, 32, 2, 256], F32)
    ab_d = K.dram("gla_ab", [2, 128, 2, 2], F32)
    gon_d = K.dram("gla_gon", [2, 96, 1], F32)
    ct_d = K.dram("c_ct", [128, 16, T], BF16)
    st_d = K.dram("c_st", [128, 16, T], BF16)
    c256_d = K.dram("c_c256", [128, 2, 256], BF16)
    s256_d = K.dram("c_s256", [128, 2, 256], BF16)
    c64_d = K.dram("c_c64", [128, 4, 128], BF16)
    cos_d = K.dram("c_cos", [128, T], F32)
    sin_d = K.dram("c_sin", [128, T], F32)
    gmask_d = K.dram("c_gmask", [128, 2, 2, 128], BF16)
    mscan_d = K.dram("c_mscan", [128, 512], BF16)
    w3_d = K.dram("ffn_w3", [2, 128, 8, DFF], F32)
    w2_d = K.dram("ffn_w2", [2, 128, 22, D], F32)

    top = es
    xT = K.sb(top, "xT", [128, 8, NT], F32)
    xT_t = [Tok("xT%d" % i) for i in range(5)]
    identf = K.sb(top, "identf", [128, 128], F32)
    identb = K.sb(top, "identb", [128, 128], BF16)
    onesb = K.sb(top, "onesb", [128, 128], BF16)
    cc = K.sb(top, "cc", [128, 8, 2], F32)
    scb = K.sb(top, "scb", [128, 8, 2], BF16)
    class Cur:
        def __init__(self):
            self.t = None

        def __getitem__(self, k):
            return self.t[k]
    modT = Cur()
    modTs = [K.sb(top, "modT%d" % i, [128, 48, 2], F32) for i in range(2)]
    amix = Cur()
    affn = Cur()
    amixs = [K.sb(top, "amix%d" % i, [128, 8, 2], F32) for i in range(2)]
    affns = [K.sb(top, "affn%d" % i, [128, 8, 2], F32) for i in range(2)]
    adabs = [K.sb(top, "adab%d" % i, [128, 48], F32) for i in range(2)]
    nmixs = [K.sb(top, "nmix%d" % i, [128, 8], F32) for i in range(2)]
    nffns = [K.sb(top, "nffn%d" % i, [128, 8], F32) for i in range(2)]
    t_const = Tok("const")
    t_mod = Tok("mod")
    t_small = Tok("small")

    ps = [es.enter_context(nc.psum_tensor("ps%d" % i, [128, 512], F32)) for i in range(7)]
    pt = [Tok("ps%d" % i) for i in range(7)]
    psb = es.enter_context(nc.psum_tensor("psb", [128, 1024], BF16))
    ptb = Tok("psb")

    def blk_cols(b):
        return (b * 512, 512) if b < 4 else (T, NCTX)

    K.dma("sp", identf[:], ident_d, [], [t_const], "c")
    K.op("dve", [t_const], [t_const], lambda h: h.tensor_copy(out=identb[:], in_=identf[:]))
    K.op("dve", [], [t_const], lambda h: h.memset(onesb[:], 1.0))
    K.dma("sp", cc[:], cc_d, [], [t_small], "s")
    K.op("act", [t_small], [t_small], lambda h: h.activation(out=scb[:], in_=cc[:], func=AF.Silu))

    def make_ada(l, stack):
        t_sm = Tok("adasmall")
        K.dma("sp", adabs[l][:], adab_d[l], [], [t_sm], "s")
        K.dma("sp", nmixs[l][:], nmix_d[l], [], [t_sm], "s")
        K.dma("sp", nffns[l][:], nffn_d[l], [], [t_sm], "s")
        wsl = [K.sb(stack, "adaw%d" % i, [128, 8, 512], BF16) for i in range(2)]
        wsl_t = [Tok("adaw%d" % i) for i in range(2)]

        def dma(hs):
            K.dma("pool", wsl[hs % 2][:], adaw_d[l][:, :, hs * 512:(hs + 1) * 512], [], [wsl_t[hs % 2]], "adaw")

        def sec(hs):
            s = hs % 2
            if hs == 0:
                dma(0)
            if hs + 1 < 12:
                dma(hs + 1)
            for mt in range(4):
                K.mm([wsl_t[s], t_small], [pt[6]], ps[6][:, mt * 2:mt * 2 + 2],
                     [(wsl[s][:, kc, mt * 128:(mt + 1) * 128], scb[:, kc, :]) for kc in range(8)])
            K.op("dve", [pt[6], t_sm], [t_mod], lambda h: h.tensor_tensor(
                out=modTs[l][:, hs * 4:(hs + 1) * 4, :],
                in0=ps[6][:, 0:8].rearrange("p (a b) -> p a b", b=2),
                in1=adabs[l][:, hs * 4:(hs + 1) * 4].unsqueeze(2).to_broadcast([128, 4, 2]),
                op=ALU.add))

        def tail():
            for (dst, gain, sc_) in ((amixs[l], nmixs[l], 1), (affns[l], nffns[l], 4)):
                K.op("dve", [t_mod, t_sm], [t_mod], lambda h, dst=dst, gain=gain, sc_=sc_: h.scalar_tensor_tensor(
                    out=dst[:], in0=modTs[l][:, sc_ * 8:(sc_ + 1) * 8, :], scalar=1.0,
                    in1=gain[:].unsqueeze(2).to_broadcast([128, 8, 2]),
                    op0=ALU.add, op1=ALU.mult))
        return [lambda hs=hs: sec(hs) for hs in range(12)] + [tail]

    with contextlib.ExitStack() as ph:
        ada0 = make_ada(0, ph)
        for b in range(5):
            c0, n = blk_cols(b)
            src = x_d[:, :, c0:c0 + n] if b < 4 else ctx_d[:, :, :]
            for hf in range(2):
                K.dma("sp", xT[:, hf * 4:(hf + 1) * 4, c0:c0 + n], src[:, hf * 4:(hf + 1) * 4, :], [], [xT_t[b]], "xin")
        for hs in range(12):
            ada0[hs]()
        ada0[12]()
        K.barrier()

    rot = {"ps": 0}

    def norm_mod(l, which, hT, hT_t, blocks):
        a = amix if which == "mix" else affn
        sec = 0 if which == "mix" else 3
        blocks = list(blocks)
        with contextlib.ExitStack() as ph:
            sq1 = K.sb(ph, "nm_sq", [128, 8, 512], BF16)
            sq = [sq1, sq1]
            sq1_t = Tok()
            sq_t = [sq1_t, sq1_t]
            rs = [K.sb(ph, "nm_rs%d" % i, [128, 512], F32) for i in range(2)]
            rs_t = [Tok() for i in range(2)]
            tmp = [K.sb(ph, "nm_tmp%d" % i, [128, 512], F32) for i in range(4)]
            tmp_t = [Tok() for i in range(4)]
            epsb = K.sb(ph, "nm_eps", [128, 1], F32)
            eps_t = Tok()
            K.op("dve", [], [eps_t], lambda h: h.memset(epsb[:], EPS))

            def stats(bi):
                b = blocks[bi]
                c0, n = blk_cols(b)
                s = bi % 2
                pb = bi % 2
                K.op("dve", [xT_t[b]], [sq_t[s]], lambda h: h.tensor_tensor(
                    out=sq[s][:, :, :n], in0=xT[:, :, c0:c0 + n], in1=xT[:, :, c0:c0 + n], op=ALU.mult))
                K.mm([sq_t[s], t_const], [pt[pb]], ps[pb][:, :n],
                     [(onesb[:], sq[s][:, kc, :n]) for kc in range(8)])
                K.op("act", [pt[pb], eps_t], [rs_t[s]], lambda h: h.activation(
                    out=rs[s][:, :n], in_=ps[pb][:, :n], func=AF.Ln, bias=epsb[:], scale=1.0 / D))
                K.op("act", [rs_t[s]], [rs_t[s]], lambda h: h.activation(
                    out=rs[s][:, :n], in_=rs[s][:, :n], func=AF.Exp, scale=-0.5))

            ti = 0
            stats(0)
            for bi, b in enumerate(blocks):
                if bi + 1 < len(blocks):
                    stats(bi + 1)
                c0, n = blk_cols(b)
                r = 0 if b < 4 else 1
                s = bi % 2
                for kc in range(8):
                    u = ti % 4
                    ti += 1
                    eng = "pool" if kc % 4 == 3 else "dve"
                    K.op(eng, [xT_t[b], rs_t[s]], [tmp_t[u]], lambda h, u=u, kc=kc: h.tensor_tensor(
                        out=tmp[u][:, :n], in0=xT[:, kc, c0:c0 + n], in1=rs[s][:, :n], op=ALU.mult))
                    K.op("act", [tmp_t[u], t_mod], [hT_t[b]], lambda h, u=u, kc=kc: h.activation(
                        out=hT[:, kc, c0:c0 + n], in_=tmp[u][:, :n], func=AF.Identity,
                        bias=modT[:, sec * 8 + kc, r:r + 1], scale=a[:, kc, r:r + 1]))
            K.barrier()

    def ffn(l, hT, hT_t, blocks):
        with contextlib.ExitStack() as ph:
            chunks = [(i * 512, 512) for i in range(5)] + [(2560, 256)]
            w1s = [K.sb(ph, "w1s%d" % i, [128, 8, 512], BF16) for i in range(2)]
            w3s = [K.sb(ph, "w3s%d" % i, [128, 8, 512], BF16) for i in range(2)]
            w2s = [K.sb(ph, "w2s%d" % i, [128, 4, D], BF16) for i in range(2)]
            w_t = [Tok() for i in range(2)]
            s1 = [K.sb(ph, "ff_s1%d" % i, [128, 512], F32) for i in range(2)]
            s1_t = [Tok() for i in range(2)]
            g = [K.sb(ph, "ff_g%d" % i, [128, 4, 512], BF16) for i in range(2)]
            g_t = [Tok() for i in range(2)]

            def load(ch):
                f0, nf = chunks[ch]
                s = ch % 2
                K.dma("pool", w1s[s][:, :, :nf], w1_d[l][:, :, f0:f0 + nf], [], [w_t[s]], "ffw")
                K.dma("pool", w3s[s][:, :, :nf], w3_d[l][:, :, f0:f0 + nf], [], [w_t[s]], "ffw")
                K.dma("pool", w2s[s][:, :nf // 128, :], w2_d[l][:, f0 // 128:(f0 + nf) // 128, :], [], [w_t[s]], "ffw")
            load(0)
            ada_next = make_ada(l + 1, ph) if l + 1 < 2 else None
            norm_mod(l, "ffn", hT, hT_t, blocks)
            cnt = 0
            gi = 0
            oi = 0
            for ch in range(6):
                if ch + 1 < 6:
                    load(ch + 1)
                f0, nf = chunks[ch]
                s = ch % 2
                nft = nf // 128
                for b in blocks:
                    c0, n = blk_cols(b)
                    r = 0 if b < 4 else 1
                    gs = gi % 2
                    gi += 1
                    for ft in range(nft):
                        pa = cnt % 2
                        pbk = 2 + cnt % 2
                        u = cnt % 2
                        cnt += 1
                        K.mm([w_t[s], hT_t[b]], [pt[pa]], ps[pa][:, :n],
                             [(w1s[s][:, kc, ft * 128:(ft + 1) * 128], hT[:, kc, c0:c0 + n]) for kc in range(8)])
                        K.mm([w_t[s], hT_t[b]], [pt[pbk]], ps[pbk][:, :n],
                             [(w3s[s][:, kc, ft * 128:(ft + 1) * 128], hT[:, kc, c0:c0 + n]) for kc in range(8)])
                        K.op("act", [pt[pa]], [s1_t[u]], lambda h, u=u, pa=pa, n=n: h.activation(
                            out=s1[u][:, :n], in_=ps[pa][:, :n], func=AF.Silu))
                        K.op("dve", [s1_t[u], pt[pbk]], [g_t[gs]], lambda h, u=u, pbk=pbk, n=n, gs=gs, ft=ft: h.tensor_tensor(
                            out=g[gs][:, ft, :n], in0=s1[u][:, :n], in1=ps[pbk][:, :n], op=ALU.mult))
                    for dt in range(8):
                        pc = 4 + oi % 3
                        oi += 1
                        K.mm([w_t[s], g_t[gs]], [pt[pc]], ps[pc][:, :n],
                             [(w2s[s][:, ft, dt * 128:(dt + 1) * 128], g[gs][:, ft, :n]) for ft in range(nft)])
                        K.op("dve", [pt[pc], t_mod, xT_t[b]], [xT_t[b]], lambda h, pc=pc, dt=dt, c0=c0, n=n, r=r: h.scalar_tensor_tensor(
                            out=xT[:, dt, c0:c0 + n], in0=ps[pc][:, :n], scalar=modT[:, 40 + dt, r:r + 1],
                            in1=xT[:, dt, c0:c0 + n], op0=ALU.mult, op1=ALU.add))
                if ada_next is not None:
                    ada_next[2 * ch]()
                    ada_next[2 * ch + 1]()
            if ada_next is not None:
                ada_next[12]()
            K.barrier()

    def lin_fm(wt, w, col0, M, hT, hT_t, b, c0, n, pb, wtok):
        K.mm([wtok, hT_t[b]], [pt[pb]], ps[pb][:M, :n],
             [(w[:, kc, col0:col0 + M], hT[:, kc, c0:c0 + n]) for kc in range(8)])

    def out_proj_update(pairs_fn, reads, b, c0, n, r, pbs):
        for dt in range(8):
            pc = pbs[dt % len(pbs)]
            K.mm(reads, [pt[pc]], ps[pc][:, :n], pairs_fn(dt))
            K.op("dve", [pt[pc], t_mod, xT_t[b]], [xT_t[b]], lambda h, pc=pc, dt=dt: h.scalar_tensor_tensor(
                out=xT[:, dt, c0:c0 + n], in0=ps[pc][:, :n], scalar=modT[:, 16 + dt, r:r + 1],
                in1=xT[:, dt, c0:c0 + n], op0=ALU.mult, op1=ALU.add))

    def rstd_from_ps(pb, P, n, inv, rt, rs, rs_tok, eps_ap, eps_t):
        K.op("act", [pt[pb], eps_t], [rs_tok], lambda h: h.activation(
            out=rt[:P, :n], in_=ps[pb][:P, :n], func=AF.Sqrt, bias=eps_ap[:P, :], scale=inv))
        K.op("dve", [rs_tok], [rs_tok], lambda h: h.reciprocal(out=rs[:P, :n], in_=rt[:P, :n]))

    def fnet(l, hT, hT_t, need_ctx):
        with contextlib.ExitStack() as ph:
            Z = K.sb(ph, "fn_Z", [128, 18, 256], BF16)
            Z_t = Tok()
            wfn = K.sb(ph, "fn_w", [128, 8, 256], BF16)
            fnw = K.sb(ph, "fn_fw", [128, 2, 256], BF16)
            wo = K.sb(ph, "fn_wo", [128, 2, D], BF16)
            c64 = K.sb(ph, "fn_c64", [128, 4, 128], BF16)
            w_t = Tok()
            K.dma("pool", wfn[:], win_d[l][:, :, 0:256], [], [w_t], "fnw")
            K.dma("pool", fnw[:], fnetw_d[l], [], [w_t], "fnw")
            K.dma("pool", wo[:], wout_d[l][:, 0:2, :], [], [w_t], "fnw")
            K.dma("sp", c64[:], c64_d, [], [w_t], "fnw")
            cts = [K.sb(ph, "fn_ct%d" % i, [128, 2, 16, 512], BF16) for i in range(2)]
            ct_t = [Tok() for i in range(2)]
            Psb = [K.sb(ph, "fn_P%d" % i, [128, 2, 512], BF16) for i in range(2)]
            P_t = [Tok() for i in range(2)]
            Ysb = K.sb(ph, "fn_Y", [128, 2, 512], BF16)
            Y_t = Tok()
            yf = K.sb(ph, "fn_yf", [128, 2, 512], BF16)
            yf_t = Tok()
            for ti in range(18):
                pb = ti % 2
                b = min(ti // 4, 4)
                K.mm([w_t, hT_t[b]], [pt[pb]], ps[pb][:, :256],
                     [(hT[:, kc, ti * 128:(ti + 1) * 128], wfn[:, kc, :]) for kc in range(8)])
                if ti % 2 == 0:
                    K.op("dve", [pt[pb]], [Z_t], lambda h, ti=ti, pb=pb: h.tensor_copy(out=Z[:, ti, :], in_=ps[pb][:, :256]))
                else:
                    K.op("act", [pt[pb]], [Z_t], lambda h, ti=ti, pb=pb: h.activation(out=Z[:, ti, :], in_=ps[pb][:, :256], func=AF.Copy))
            jobs = [(kb, 0) for kb in range(4)] + ([(0, 1)] if need_ctx else [])
            for ji, (kb, isctx) in enumerate(jobs):
                s = ji % 2
                if not isctx:
                    ntt, n, tt0, c0, b, r = 16, 512, 0, kb * 512, kb, 0
                    K.dma("sp", cts[s][:, 0, :, :], ct_d[:, :, kb * 512:(kb + 1) * 512], [], [ct_t[s]], "ct")
                    K.dma("sp", cts[s][:, 1, :, :], st_d[:, :, kb * 512:(kb + 1) * 512], [], [ct_t[s]], "ct")
                else:
                    ntt, n, tt0, c0, b, r = 2, 256, 16, T, 4, 1
                    K.dma("sp", cts[s][:, 0, 0:2, 0:256], c256_d, [], [ct_t[s]], "ct")
                    K.dma("sp", cts[s][:, 1, 0:2, 0:256], s256_d, [], [ct_t[s]], "ct")
                for chc in range(2):
                    u = chc
                    for cs in range(2):
                        pb = cs
                        K.mm([Z_t, ct_t[s]], [pt[pb]], ps[pb][:, :n],
                             [(Z[:, tt0 + tt, chc * 128:(chc + 1) * 128], cts[s][:, cs, tt, :n]) for tt in range(ntt)])
                        if cs == 0:
                            K.op("dve", [pt[pb]], [P_t[u]], lambda h, u=u, pb=pb, cs=cs: h.tensor_copy(out=Psb[u][:, cs, :n], in_=ps[pb][:, :n]))
                        else:
                            K.op("act", [pt[pb]], [P_t[u]], lambda h, u=u, pb=pb, cs=cs: h.activation(out=Psb[u][:, cs, :n], in_=ps[pb][:, :n], func=AF.Copy))
                    pb = 2 + chc
                    K.mm([P_t[u], w_t], [pt[pb]], ps[pb][:, :n],
                         [(c64[:, 2 * isctx + 0, :], Psb[u][:, 0, :n]), (c64[:, 2 * isctx + 1, :], Psb[u][:, 1, :n])])
                    K.op("dve", [pt[pb]], [Y_t], lambda h, pb=pb, chc=chc: h.tensor_copy(out=Ysb[:, chc, :n], in_=ps[pb][:, :n]))
                for c2 in range(2):
                    pb = 4 + c2
                    K.mm([Y_t, w_t], [pt[pb]], ps[pb][:, :n],
                         [(fnw[:, c1, c2 * 128:(c2 + 1) * 128], Ysb[:, c1, :n]) for c1 in range(2)])
                    K.op("act", [pt[pb]], [yf_t], lambda h, pb=pb, c2=c2: h.activation(out=yf[:, c2, :n], in_=ps[pb][:, :n], func=AF.Copy))
                out_proj_update(lambda dt: [(wo[:, c2, dt * 128:(dt + 1) * 128], yf[:, c2, :n]) for c2 in range(2)],
                                [yf_t, w_t], b, c0, n, r, [0, 1, 2, 3, 4, 5, 6])
            K.barrier()

    def natt(l, hT, hT_t, need_ctx):
        with contextlib.ExitStack() as ph:
            kT = K.sb(ph, "na_kT", [128, 3, NT], BF16)
            kT_t = Tok()
            V = K.sb(ph, "na_V", [128, 18, 6, 65], BF16)
            V_t = Tok()
            gq = K.sb(ph, "na_gq", [128, 1], F32)
            gk = K.sb(ph, "na_gk", [128, 1], F32)
            epsb = K.sb(ph, "na_eps", [128, 1], F32)
            bones = K.sb(ph, "na_bones", [128, 128], BF16)
            g_t = Tok()
            K.dma("sp", gq[:], naq_d[l], [], [g_t], "nag")
            K.dma("sp", gk[:], nak_d[l], [], [g_t], "nag")
            K.op("dve", [g_t], [g_t], lambda h: h.tensor_scalar(out=gq[:], in0=gq[:], scalar1=0.125, scalar2=None, op0=ALU.mult))
            K.op("dve", [], [g_t], lambda h: h.memset(epsb[:], EPS))
            K.op("dve", [], [g_t], lambda h: h.memset(bones[:], 0.0))
            K.op("dve", [g_t], [g_t], lambda h: h.memset(bones[0:64, 0:64], 1.0))
            K.op("dve", [g_t], [g_t], lambda h: h.memset(bones[64:128, 64:128], 1.0))
            K.op("dve", [], [V_t], lambda h: h.memset(V[:, :, :, 64:65], 1.0))
            sq = [K.sb(ph, "na_sq%d" % i, [128, 512], BF16) for i in range(2)]
            sq_t = [Tok() for i in range(2)]
            rtl = [K.sb(ph, "na_rt%d" % i, [128, 512], F32) for i in range(2)]
            rs = [K.sb(ph, "na_rs%d" % i, [128, 512], F32) for i in range(2)]
            rs_t = [Tok() for i in range(2)]

            def qk_norm(w, wtok, gain, dst, dst_t, dcol, b, c0, n, cnt):
                for hp in range(3):
                    u = (cnt[0]) % 2
                    cnt[0] += 1
                    pa, pbk = u, 2 + u
                    lin_fm(None, w, hp * 128, 128, hT, hT_t, b, c0, n, pa, wtok)
                    K.op("act", [pt[pa]], [sq_t[u]], lambda h, u=u, pa=pa: h.activation(out=sq[u][:, :n], in_=ps[pa][:, :n], func=AF.Square))
                    K.mm([sq_t[u], g_t], [pt[pbk]], ps[pbk][:, :n], [(bones[:], sq[u][:, :n])])
                    K.op("act", [pt[pbk], g_t], [rs_t[u]], lambda h, u=u, pbk=pbk: h.activation(
                        out=rtl[u][:, :n], in_=ps[pbk][:, :n], func=AF.Ln, bias=epsb[:], scale=1.0 / 64))
                    K.op("act", [rs_t[u]], [rs_t[u]], lambda h, u=u: h.activation(
                        out=rs[u][:, :n], in_=rtl[u][:, :n], func=AF.Exp, scale=-0.5))
                    K.op("dve", [pt[pa], rs_t[u], g_t], [dst_t], lambda h, u=u, pa=pa, hp=hp: h.scalar_tensor_tensor(
                        out=dst[:, hp, dcol:dcol + n], in0=ps[pa][:, :n], scalar=gain[:, 0:1], in1=rs[u][:, :n],
                        op0=ALU.mult, op1=ALU.mult))

            cnt = [0]
            BT = K.sb(ph, "na_BT", [128, 72, 128], BF16)
            bt_t = Tok()
            for i3 in range(3):
                K.dma("pool", BT[:, 24 * i3:24 * (i3 + 1), :], bt_d[l][:, 24 * i3:24 * (i3 + 1), :], [], [bt_t], "nabt")
            for i3 in range(3):
                K.op("act", [bt_t], [bt_t], lambda h, i3=i3: h.activation(
                    out=BT[:, 24 * i3:24 * (i3 + 1), :], in_=BT[:, 24 * i3:24 * (i3 + 1), :], func=AF.Exp))
            with contextlib.ExitStack() as p1:
                wk = K.sb(p1, "na_wk", [128, 8, 384], BF16)
                wv = K.sb(p1, "na_wv", [128, 8, 384], BF16)
                wk_t = Tok()
                K.dma("pool", wk[:], win_d[l][:, :, 640:1024], [], [wk_t], "naw")
                K.dma("pool", wv[:], win_d[l][:, :, 1024:1408], [], [wk_t], "naw")
                for b in range(5):
                    c0, n = blk_cols(b)
                    qk_norm(wk, wk_t, gk, kT, kT_t, c0, b, c0, n, cnt)
                for ti in range(18):
                    pb = 4 + ti % 2
                    b = min(ti // 4, 4)
                    K.mm([wk_t, hT_t[b]], [pt[pb]], ps[pb][:, :384],
                         [(hT[:, kc, ti * 128:(ti + 1) * 128], wv[:, kc, :]) for kc in range(8)])
                    K.op("act" if ti % 2 else "dve", [pt[pb]], [V_t],
                         (lambda h, ti=ti, pb=pb: h.activation(out=V[:, ti, :, 0:64], in_=ps[pb][:, :384].rearrange("p (a b) -> p a b", b=64), func=AF.Copy))
                         if ti % 2 else
                         (lambda h, ti=ti, pb=pb: h.tensor_copy(out=V[:, ti, :, 0:64], in_=ps[pb][:, :384].rearrange("p (a b) -> p a b", b=64))))
                K.barrier()
            wq = K.sb(ph, "na_wq", [128, 8, 384], BF16)
            wo = K.sb(ph, "na_wo", [128, 3, D], BF16)
            w_t = Tok()
            K.dma("pool", wq[:], win_d[l][:, :, 256:640], [], [w_t], "naw2")
            K.dma("pool", wo[:], wout_d[l][:, 2:5, :], [], [w_t], "naw2")
            qT2 = [K.sb(ph, "na_qT%d" % i, [128, 3, 512], BF16) for i in range(2)]
            qT2_t = [Tok(), Tok()]
            PT = [K.sb(ph, "na_PT%d" % i, [128, 7, 128], BF16) for i in range(2)]
            PT_t = [Tok() for i in range(2)]
            rec = [K.sb(ph, "na_rec%d" % i, [128, 6], F32) for i in range(2)]
            rec_t = [Tok(), Tok()]
            Osb = [K.sb(ph, "na_O%d" % i, [128, 6, 64], BF16) for i in range(2)]
            O_t = [Tok(), Tok()]
            yna2 = [K.sb(ph, "na_y%d" % i, [128, 3, 512], BF16) for i in range(2)]
            yna2_t = [Tok(), Tok()]
            hcount = 0
            nblocks = list(range(5) if need_ctx else range(4))
            pending = [None]
            c0_, n_ = blk_cols(nblocks[0])
            qk_norm(wq, w_t, gq, qT2[0], qT2_t[0], 0, nblocks[0], c0_, n_, cnt)
            for bi, b in enumerate(nblocks):
                c0, n = blk_cols(b)
                r = 0 if b < 4 else 1
                qT, qT_t = qT2[bi % 2], qT2_t[bi % 2]
                yna, yna_t = yna2[bi % 2], yna2_t[bi % 2]
                tile_chunks = []
                for qi in range(n // 128):
                    if b < 4:
                        i = b * 4 + qi
                        if i <= 1:
                            js = list(range(0, 4))
                        elif i >= 14:
                            js = list(range(12, 16))
                        else:
                            js = list(range(i - 2, i + 3))
                        chunks = []
                        for j in js:
                            dl = j - i
                            v = (dl + 9) if 2 <= i <= 13 else (dl + 3)
                            chunks.append((j, v))
                        chunks += [(16, None), (17, None)]
                    else:
                        chunks = [(16, None), (17, None)]
                    tile_chunks.append(chunks)
                items = [(qi, hd) for qi in range(n // 128) for hd in range(6)]

                def stage1(k):
                    qi, hd = items[k]
                    chunks = tile_chunks[qi]
                    nch = len(chunks)
                    hp, r0 = hd // 2, (hd % 2) * 64
                    u = (hbase + k) % 2
                    pS = (0 + 2 * u, 1 + 2 * u)
                    for c, (kt, v) in enumerate(chunks):
                        pb = pS[c // 4]
                        pairs = [(kT[r0:r0 + 64, hp, kt * 128:(kt + 1) * 128], qT[r0:r0 + 64, hp, qi * 128:(qi + 1) * 128])]
                        K.mm([kT_t, qT_t, w_t, t_const], [pt[pb]], ps[pb][:, (c % 4) * 128:(c % 4 + 1) * 128], pairs)
                    n0 = min(nch, 4)
                    K.op("act", [pt[pS[0]]], [PT_t[u]], lambda h: h.activation(
                        out=PT[u][:, 0:n0, :], in_=ps[pS[0]][:, 0:n0 * 128].rearrange("p (a b) -> p a b", b=128), func=AF.Exp))
                    if nch > 4:
                        n1 = nch - 4
                        K.op("act", [pt[pS[1]]], [PT_t[u]], lambda h: h.activation(
                            out=PT[u][:, 4:4 + n1, :], in_=ps[pS[1]][:, 0:n1 * 128].rearrange("p (a b) -> p a b", b=128), func=AF.Exp))
                    loc = [v for (kt, v) in chunks if v is not None]
                    if loc:
                        nl, v0 = len(loc), loc[0]
                        K.op("dve", [PT_t[u], bt_t], [PT_t[u]], lambda h: h.tensor_tensor(
                            out=PT[u][:, 0:nl, :], in0=PT[u][:, 0:nl, :], in1=BT[:, hd * 12 + v0:hd * 12 + v0 + nl, :], op=ALU.mult))

                def stage2(k):
                    qi, hd = items[k]
                    chunks = tile_chunks[qi]
                    u = (hbase + k) % 2
                    po = 4 + qi % 2
                    K.mm([PT_t[u], V_t], [pt[po]], ps[po][:, hd * 65:(hd + 1) * 65],
                         [(PT[u][:, c, :], V[:, kt, hd, :]) for c, (kt, v) in enumerate(chunks)])

                def fin_dve(qi):
                    po = 4 + qi % 2
                    oi = qi % 2
                    ov = ps[po][:, 0:390].rearrange("p (a b) -> p a b", b=65)
                    K.op("dve", [pt[po]], [rec_t[oi]], lambda h: h.reciprocal(out=rec[oi][:], in_=ov[:, :, 64]))
                    K.op("dve", [pt[po], rec_t[oi]], [O_t[oi]], lambda h: h.tensor_tensor(
                        out=Osb[oi][:], in0=ov[:, :, 0:64], in1=rec[oi][:].unsqueeze(2).to_broadcast([128, 6, 64]), op=ALU.mult))

                def fin_pe(qi):
                    oi = qi % 2
                    Of = Osb[oi][:].rearrange("p a b -> p (a b)")
                    for hp in range(3):
                        K.op("pe", [O_t[oi], t_const], [ptb], lambda h, hp=hp: h.transpose(
                            psb[:, hp * 128:(hp + 1) * 128], Of[:, hp * 128:(hp + 1) * 128], identb[:]))
                    K.op("act", [ptb], [yna_t], lambda h: h.activation(
                        out=yna[:, :, qi * 128:(qi + 1) * 128], in_=psb[:, 0:384].rearrange("p (a b) -> p a b", b=128), func=AF.Copy))

                hbase = hcount
                nit = len(items)
                for k in range(nit + 2):
                    if k == 6 and pending[0] is not None:
                        pending[0]()
                        pending[0] = None
                    if k < nit:
                        stage1(k)
                    if 1 <= k <= nit:
                        stage2(k - 1)
                        if items[k - 1][1] == 5:
                            fin_dve(items[k - 1][0])
                    if 2 <= k <= nit + 1 and items[k - 2][1] == 5:
                        fin_pe(items[k - 2][0])
                hcount += nit
                if bi + 1 < len(nblocks):
                    c0_, n_ = blk_cols(nblocks[bi + 1])
                    qk_norm(wq, w_t, gq, qT2[(bi + 1) % 2], qT2_t[(bi + 1) % 2], 0, nblocks[bi + 1], c0_, n_, cnt)
                def _op(yna=yna, yna_t=yna_t, b=b, c0=c0, n=n, r=r):
                    out_proj_update(lambda dt: [(wo[:, hp, dt * 128:(dt + 1) * 128], yna[:, hp, :n]) for hp in range(3)],
                                    [yna_t, w_t], b, c0, n, r, [6])
                if pending[0] is not None:
                    pending[0]()
                pending[0] = _op
            if pending[0] is not None:
                pending[0]()
                pending[0] = None
            K.barrier()

    def gla(l, hT, hT_t, need_ctx):
        NB = 256
        QS = 48 ** -0.5
        with contextlib.ExitStack() as ph:
            qr = K.sb(ph, "gl_qr", [128, 2, NT], BF16)
            kr = K.sb(ph, "gl_kr", [128, 2, NT], BF16)
            aT = K.sb(ph, "gl_aT", [32, NT], BF16)
            st_t = [Tok() for i in range(5)]
            gvc = K.sb(ph, "gl_gvc", [128, 2, 384], BF16)
            sgc = K.sb(ph, "gl_sgc", [96, 4, NCTX], BF16)
            gvc_t = Tok()
            aw = K.sb(ph, "gl_aw", [32, 2, 256], BF16)
            nab = K.sb(ph, "gl_nab", [128, 2, 2], F32)
            gon = K.sb(ph, "gl_gon", [96, 1], F32)
            msk = K.sb(ph, "gl_msk", [128, 2, 2, 128], BF16)
            mscan = K.sb(ph, "gl_mscan", [128, 512], BF16)
            epsb = K.sb(ph, "gl_eps", [128, 1], F32)
            oneb = K.sb(ph, "gl_one", [128, 1], F32)
            w_t = Tok()
            K.dma("pool", aw[:], aw_d[l], [], [w_t], "glw")
            K.dma("sp", nab[:], ab_d[l], [], [w_t], "glw")
            K.dma("sp", gon[:], gon_d[l], [], [w_t], "glw")
            K.dma("sp", msk[:], gmask_d, [], [w_t], "glw")
            K.dma("sp", mscan[:], mscan_d, [], [w_t], "glw")
            K.op("dve", [w_t], [w_t], lambda h: h.tensor_scalar(out=nab[:], in0=nab[:], scalar1=-1.0, scalar2=None, op0=ALU.mult))
            K.op("dve", [], [w_t], lambda h: h.memset(epsb[:], EPS))
            K.op("dve", [], [w_t], lambda h: h.memset(oneb[:], 1.0))
            with contextlib.ExitStack() as p0:
                wg = K.sb(p0, "gl_wg", [128, 8, 1024], BF16)
                wga = K.sb(p0, "gl_wga", [128, 8, 32], BF16)
                w0_t = Tok()
                K.dma("pool", wg[:], wg_d[l], [], [w0_t], "glw0")
                K.dma("pool", wga[:], win_d[l][:, :, 2560:2592], [], [w0_t], "glw0")
                wgv = K.sb(p0, "gl_wgv", [128, 8, 384], BF16)
                wgg = K.sb(p0, "gl_wgg", [128, 8, 384], BF16)
                K.dma("pool", wgv[:], win_d[l][:, :, 1792:2176], [], [w0_t], "glw0")
                K.dma("pool", wgg[:], win_d[l][:, :, 2176:2560], [], [w0_t], "glw0")
                gvs = K.sb(p0, "gl_gvs", [128, 4, 384], BF16)
                sgs = K.sb(p0, "gl_sgs", [96, 4, 512], BF16)
                stg_t = Tok()
                cosb = K.sb(p0, "gl_cos", [128, 512], F32)
                sinb = K.sb(p0, "gl_sin", [128, 512], F32)
                cs_t = Tok()
                t1 = [K.sb(p0, "gl_t1%d" % i, [128, 512], F32) for i in range(2)]
                t2 = [K.sb(p0, "gl_t2%d" % i, [128, 512], F32) for i in range(2)]
                r_t = [Tok(), Tok()]
                cnt = 0
                for b5 in range(5):
                    c0, n = blk_cols(b5)
                    lat = b5 < 4
                    if lat:
                        K.dma("sp", cosb[:, :n], cos_d[:, c0:c0 + n], [], [cs_t], "cs")
                        K.dma("sp", sinb[:, :n], sin_d[:, c0:c0 + n], [], [cs_t], "cs")
                    lin_fm(None, wga, 0, 32, hT, hT_t, b5, c0, n, 6, w0_t)
                    K.op("act", [pt[6]], [st_t[b5]], lambda h: h.activation(out=aT[:, c0:c0 + n], in_=ps[6][:32, :n], func=AF.Copy))
                    for hp in range(2):
                        for which in range(2):
                            base = which * 512
                            dst = qr if which == 0 else kr
                            u = cnt % 2
                            cnt += 1
                            pa, pbk = (0, 1) if u == 0 else (2, 3)
                            lin_fm(None, wg, base + hp * 128, 128, hT, hT_t, b5, c0, n, pa, w0_t)
                            if lat:
                                lin_fm(None, wg, base + 256 + hp * 128, 128, hT, hT_t, b5, c0, n, pbk, w0_t)
                                K.op("dve", [pt[pa], cs_t], [r_t[u]], lambda h: h.tensor_tensor(out=t1[u][:, :n], in0=ps[pa][:, :n], in1=cosb[:, :n], op=ALU.mult))
                                K.op("dve", [pt[pbk], cs_t, r_t[u]], [r_t[u]], lambda h: h.tensor_tensor(out=t2[u][:, :n], in0=ps[pbk][:, :n], in1=sinb[:, :n], op=ALU.mult))
                                K.op("pool", [r_t[u]], [st_t[b5]], lambda h: h.tensor_tensor(out=dst[:, hp, c0:c0 + n], in0=t1[u][:, :n], in1=t2[u][:, :n], op=ALU.add))
                            else:
                                K.op("act", [pt[pa]], [st_t[b5]], lambda h: h.activation(out=dst[:, hp, c0:c0 + n], in_=ps[pa][:, :n], func=AF.Copy))
                    for tt in range(n // 128):
                        pv = 4 + tt % 2
                        K.mm([w0_t, hT_t[b5]], [pt[pv]], ps[pv][:, :384],
                             [(hT[:, kc, c0 + tt * 128:c0 + (tt + 1) * 128], wgv[:, kc, :]) for kc in range(8)])
                        if lat:
                            K.op("act", [pt[pv]], [stg_t], lambda h, tt=tt, pv=pv: h.activation(out=gvs[:, tt, :], in_=ps[pv][:, :384], func=AF.Copy))
                        else:
                            K.op("act", [pt[pv]], [gvc_t], lambda h, tt=tt, pv=pv: h.activation(out=gvc[:, tt, :], in_=ps[pv][:, :384], func=AF.Copy))
                    if lat or need_ctx:
                        for hd in range(4):
                            pv = 4 + hd % 2
                            lin_fm(None, wgg, hd * 96, 96, hT, hT_t, b5, c0, n, pv, w0_t)
                            if lat:
                                K.op("act", [pt[pv]], [stg_t], lambda h, hd=hd, pv=pv: h.activation(out=sgs[:, hd, :n], in_=ps[pv][:96, :n], func=AF.Silu))
                            else:
                                K.op("act", [pt[pv]], [gvc_t], lambda h, hd=hd, pv=pv: h.activation(out=sgc[:, hd, :n], in_=ps[pv][:96, :n], func=AF.Silu))
                    if lat:
                        for tt in range(4):
                            K.op("pool", [stg_t], [hT_t[b5]], lambda h, tt=tt: h.tensor_copy(out=hT[:, tt, c0:c0 + 384], in_=gvs[:, tt, :]))
                        for hd in range(4):
                            K.op("pool", [stg_t], [hT_t[b5]], lambda h, hd=hd: h.tensor_copy(out=hT[:96, 4 + hd, c0:c0 + 512], in_=sgs[:, hd, :]))
                K.barrier()
            wo = K.sb(ph, "gl_wo", [96, 4, D], BF16)
            w2_t = Tok()
            K.dma("pool", wo[:], woutg_d[l], [], [w2_t], "glw2")
            oF = K.sb(ph, "gl_oF", [96, 4, NT], BF16)
            oF_t = [Tok() for i in range(5)]
            R = []
            for d in range(2):
                r_ = dict(
                    e1=K.sb(ph, "gl_e1_%d" % d, [128, 512], F32),
                    bpos=K.sb(ph, "gl_bp_%d" % d, [128, 512], F32), eb=K.sb(ph, "gl_eb_%d" % d, [128, 512], F32),
                    qin=K.sb(ph, "gl_qin_%d" % d, [128, 2, 512], BF16), kin=K.sb(ph, "gl_kin_%d" % d, [128, 2, 512], BF16),
                    ktok=K.sb(ph, "gl_ktok_%d" % d, [128, 4, 2, 128], BF16),
                    dec=K.sb(ph, "gl_dec_%d" % d, [128, 2, 4], F32), AM=K.sb(ph, "gl_AM_%d" % d, [128, 4, 128], BF16),
                    S=K.sb(ph, "gl_S_%d" % d, [128, 2, 192], F32), Sb=K.sb(ph, "gl_Sb_%d" % d, [128, 2, 192], BF16),
                    osum=K.sb(ph, "gl_osum_%d" % d, [96, 4, NB], F32),
                    sg_t=Tok(), n_t=Tok(),
                    g_t=Tok(), qk_t=Tok(), ktok_t=Tok(), gv_t=Tok(), dec_t=Tok(), AM_t=Tok(), S_t=Tok(), Sb_t=Tok(), os_t=Tok(),
                    BA=3 * d, BB=3 * d + 1, BO=3 * d + 2)
                R.append(r_)

            def block_prep(d, b5):
                r_ = R[d]
                c0, n = blk_cols(b5)
                nt = n // 128
                BB = r_["BB"]
                e1, bpos, eb = r_["e1"], r_["bpos"], r_["eb"]
                lsb = e1
                enb = eb
                gt = r_["g_t"]
                for hp in range(2):
                    K.mm([st_t[b5], w_t], [pt[BB]], ps[BB][:, :n], [(aw[:, d, hp * 128:(hp + 1) * 128], aT[:, c0:c0 + n])])
                    yield
                    K.op("act", [pt[BB], w_t], [gt], lambda h: h.activation(
                        out=e1[:, :n], in_=ps[BB][:, :n], func=AF.Exp, bias=nab[:, d, hp:hp + 1], scale=-1.0))
                    yield
                    K.op("act", [gt, w_t], [gt], lambda h: h.activation(out=lsb[:, :n], in_=e1[:, :n], func=AF.Ln, bias=oneb[:], scale=1.0))
                    yield
                    K.op("dve", [gt, w_t], [gt], lambda h: h.tensor_tensor_scan(
                        out=bpos[:, :n], data0=mscan[:, :n], data1=lsb[:, :n], initial=0.0, op0=ALU.mult, op1=ALU.add))
                    yield
                    bsel = bpos
                    if d == 1:
                        K.op("dve", [gt], [gt], lambda h: h.tensor_tensor(out=e1[:, :n], in0=lsb[:, :n], in1=bpos[:, :n], op=ALU.subtract))
                        yield
                        K.op("dve", [gt], [gt], lambda h: h.tensor_tensor(
                            out=e1[:, :n].rearrange("p (a b) -> p a b", b=128), in0=e1[:, :n].rearrange("p (a b) -> p a b", b=128),
                            in1=bpos[:, :n].rearrange("p (a b) -> p a b", b=128)[:, :, 127:128].to_broadcast([128, n // 128, 128]), op=ALU.add))
                        yield
                        bsel = e1
                    K.op("act", [gt], [gt], lambda h, bsel=bsel: h.activation(out=eb[:, :n], in_=bsel[:, :n], func=AF.Exp, scale=-1.0 / 16))
                    yield
                    sel = 127 if d == 0 else 0
                    K.op("dve", [gt], [r_["dec_t"]], lambda h: h.tensor_copy(
                        out=r_["dec"][:, hp, 0:nt], in_=eb[:, :n].rearrange("p (a b) -> p a b", b=128)[:, :, sel]))
                    yield
                    K.op("dve", [gt, st_t[b5]], [r_["qk_t"]], lambda h: h.scalar_tensor_tensor(
                        out=r_["qin"][:, hp, :n], in0=qr[:, hp, c0:c0 + n], scalar=QS, in1=eb[:, :n], op0=ALU.mult, op1=ALU.mult))
                    yield
                    K.op("act", [gt, r_["dec_t"], r_["qk_t"]], [gt], lambda h, bsel=bsel: h.activation(out=enb[:, :n], in_=bsel[:, :n], func=AF.Exp, scale=1.0 / 16))
                    yield
                    K.op("dve", [gt, st_t[b5]], [r_["qk_t"]], lambda h: h.tensor_tensor(
                        out=r_["kin"][:, hp, :n], in0=kr[:, hp, c0:c0 + n], in1=enb[:, :n], op=ALU.mult))
                    yield
                    for tt in range(nt):
                        K.op("pe", [r_["qk_t"], t_const], [ptb], lambda h, tt=tt: h.transpose(
                            psb[:, d * 512 + tt * 128:d * 512 + (tt + 1) * 128], r_["kin"][:, hp, tt * 128:(tt + 1) * 128], identb[:]))
                        yield
                    K.op("act", [ptb], [r_["ktok_t"]], lambda h: h.activation(
                        out=r_["ktok"][:, 0:nt, hp, :], in_=psb[:, d * 512:d * 512 + nt * 128].rearrange("p (t f) -> p t f", t=nt), func=AF.Copy))
                    yield

            def scan_tile(d, b5, tt, want_o, second):
                r_ = R[d]
                c0, n = blk_cols(b5)
                tc0 = tt * 128
                BA, BB, BO = r_["BA"], r_["BB"], r_["BO"]
                qin, kin, Sb, S, AM, dec, ktok = r_["qin"], r_["kin"], r_["Sb"], r_["S"], r_["AM"], r_["dec"], r_["ktok"]
                if b5 < 4:
                    gv_tok = hT_t[b5]

                    def gvf(col, w):
                        return hT[:, tt, c0 + col:c0 + col + w]
                else:
                    gv_tok = gvc_t

                    def gvf(col, w):
                        return gvc[:, tt, col:col + w]
                if want_o:
                    for hd in (0, 2, 1, 3):
                        hp, r0 = hd // 2, (hd % 2) * 64
                        pa = BA if hd % 2 == 0 else BB
                        K.mm([r_["qk_t"]], [pt[pa]], ps[pa][:, (hd // 2) * 128:(hd // 2 + 1) * 128],
                             [(kin[r0:r0 + 48, hp, tc0:tc0 + 128], qin[r0:r0 + 48, hp, tc0:tc0 + 128])])
                        yield
                    for par, pa in ((0, BA), (1, BB)):
                        K.op("dve", [pt[pa], w_t], [r_["AM_t"]], lambda h, par=par, pa=pa: h.tensor_tensor(
                            out=AM[:, par:4:2, :], in0=ps[pa][:, 0:256].rearrange("p (a b) -> p a b", b=128),
                            in1=msk[:, d, :, :], op=ALU.mult))
                        yield
                    for hd in range(4):
                        hp, r0 = hd // 2, (hd % 2) * 64
                        K.mm([r_["AM_t"], gv_tok, r_["Sb_t"], r_["qk_t"]], [pt[BO]], ps[BO][:96, hd * 128:(hd + 1) * 128],
                             [(gvf(hd * 96, 96), AM[:, hd, :]),
                              (Sb[r0:r0 + 48, hp, (hd % 2) * 96:(hd % 2 + 1) * 96], qin[r0:r0 + 48, hp, tc0:tc0 + 128])])
                        yield
                    ov = ps[BO][:96, :].rearrange("p (a b) -> p a b", b=128)
                    if not second:
                        K.op("act", [pt[BO]], [oF_t[b5]], lambda h: h.activation(
                            out=oF[:, :, c0 + tc0:c0 + tc0 + 128], in_=ov, func=AF.Copy))
                    else:
                        K.op("dve", [pt[BO], oF_t[b5]], [r_["os_t"]], lambda h: h.tensor_tensor(
                            out=r_["osum"][:, :, (tt % 2) * 128:(tt % 2 + 1) * 128], in0=ov, in1=oF[:, :, c0 + tc0:c0 + tc0 + 128], op=ALU.add))
                    yield
                for hp in range(2):
                    K.mm([r_["ktok_t"], gv_tok], [pt[BA]], ps[BA][:, hp * 192:(hp + 1) * 192],
                         [(ktok[:, tt, hp, :], gvf(hp * 192, 192))])
                    yield
                for hp in range(2):
                    K.op("act", [r_["S_t"], r_["dec_t"]], [r_["S_t"]], lambda h, hp=hp: h.activation(
                        out=S[:, hp, :], in_=S[:, hp, :], func=AF.Identity, scale=dec[:, hp, tt:tt + 1]))
                    yield
                    K.op("dve", [pt[BA], r_["S_t"], r_["dec_t"]], [r_["S_t"]], lambda h, hp=hp: h.scalar_tensor_tensor(
                        out=S[:, hp, :], in0=ps[BA][:, hp * 192:(hp + 1) * 192], scalar=dec[:, hp, tt:tt + 1], in1=S[:, hp, :],
                        op0=ALU.mult, op1=ALU.add))
                    yield
                K.op("act", [r_["S_t"]], [r_["Sb_t"]], lambda h: h.activation(out=Sb[:], in_=S[:], func=AF.Copy))
                yield

            def finalize(d, b5, half):
                r_ = R[d]
                c0 = blk_cols(b5)[0] + half * 256
                n = 256
                r = 0 if b5 < 4 else 1
                osum = r_["osum"]
                BA, BB, BO = r_["BA"], r_["BB"], r_["BO"]
                if b5 < 4:
                    sg_t = hT_t[b5]

                    def sgf(hd):
                        return hT[:96, 4 + hd, c0:c0 + n]
                else:
                    sg_t = gvc_t

                    def sgf(hd):
                        return sgc[:, hd, :n]
                sq = r_["AM"][:96, 0:2, :].rearrange("p a b -> p (a b)")
                am_t = r_["AM_t"]
                rt = r_["e1"]
                n_t, gt = r_["n_t"], r_["g_t"]
                for hd in range(4):
                    K.op("act", [r_["os_t"]], [n_t, am_t], lambda h, hd=hd: h.activation(out=sq[:, :n], in_=osum[:, hd, :n], func=AF.Square))
                    yield
                    K.mm([n_t, am_t, t_const], [pt[BB]], ps[BB][:96, :n], [(onesb[:96, :96], sq[:, :n])])
                    yield
                    K.op("act", [pt[BB], w_t], [n_t, gt], lambda h: h.activation(
                        out=rt[:96, :n], in_=ps[BB][:96, :n], func=AF.Ln, bias=epsb[:96, :], scale=1.0 / 96))
                    yield
                    K.op("act", [n_t], [n_t, gt], lambda h: h.activation(out=rt[:96, :n], in_=rt[:96, :n], func=AF.Exp, scale=-0.5))
                    yield
                    K.op("dve", [n_t, gt, w_t], [r_["os_t"]], lambda h, hd=hd: h.scalar_tensor_tensor(
                        out=osum[:, hd, :n], in0=osum[:, hd, :n], scalar=gon[:, 0:1], in1=rt[:96, :n], op0=ALU.mult, op1=ALU.mult))
                    yield
                    K.op("dve", [r_["os_t"], sg_t], [sg_t], lambda h, hd=hd: h.tensor_tensor(
                        out=sgf(hd), in0=osum[:, hd, :n], in1=sgf(hd), op=ALU.mult))
                    yield
                for dt in range(8):
                    pc = BA if dt % 2 == 0 else BO
                    K.mm([sg_t, w2_t], [pt[pc]], ps[pc][:, :n], [(wo[:, hd, dt * 128:(dt + 1) * 128], sgf(hd)) for hd in range(4)])
                    yield
                    K.op("dve", [pt[pc], t_mod, xT_t[b5]], [xT_t[b5]], lambda h, dt=dt, pc=pc: h.scalar_tensor_tensor(
                        out=xT[:, dt, c0:c0 + n], in0=ps[pc][:, :n], scalar=modT[:, 16 + dt, r:r + 1],
                        in1=xT[:, dt, c0:c0 + n], op0=ALU.mult, op1=ALU.add))
                    yield

            def dir_gen(d):
                r_ = R[d]
                K.op("dve", [], [r_["S_t"]], lambda h: h.memset(r_["S"][:], 0.0))
                K.op("dve", [], [r_["Sb_t"]], lambda h: h.memset(r_["Sb"][:], 0.0))
                order = [4] + ([0, 1, 2, 3] if d == 0 else [3, 2, 1, 0])
                for b5 in order:
                    want_o = (b5 < 4) or need_ctx
                    if b5 == 4:
                        second = (d == 1)
                    else:
                        second = (b5 >= 2) if d == 0 else (b5 <= 1)
                    nt = blk_cols(b5)[1] // 128
                    yield from block_prep(d, b5)
                    tiles = list(range(nt)) if d == 0 else list(range(nt - 1, -1, -1))
                    for i_, tt in enumerate(tiles):
                        yield from scan_tile(d, b5, tt, want_o, second)
                        if want_o and second and i_ % 2 == 1:
                            yield from finalize(d, b5, tt // 2)
                    yield "STEP"

            def run_step(gens):
                active = list(gens)
                while active:
                    for g in list(active):
                        if next(g) == "STEP":
                            active.remove(g)
            gF, gB = dir_gen(0), dir_gen(1)
            run_step([gF])
            run_step([gB])
            for s_ in range(4):
                run_step([gF, gB])
            K.barrier()

    def mixers(l, hT, hT_t, need_ctx):
        if "nofn" not in debug:
            fnet(l, hT, hT_t, need_ctx)
        if "nona" not in debug:
            natt(l, hT, hT_t, need_ctx)
        if "nogla" not in debug:
            gla(l, hT, hT_t, need_ctx)

    for l in range(2):
        need_ctx = (l == 0)
        modT.t, amix.t, affn.t = modTs[l], amixs[l], affns[l]
        with contextlib.ExitStack() as lay:
            hT = K.sb(lay, "hT", [128, 8, NT], BF16)
            hT_t = [Tok("hT%d" % i) for i in range(5)]
            if "ffn" not in debug:
                norm_mod(l, "mix", hT, hT_t, range(5))
                mixers(l, hT, hT_t, need_ctx)
            blocks = range(5) if need_ctx else range(4)
            if "noffn" not in debug:
                ffn(l, hT, hT_t, blocks)
            K.barrier()
        if "ffn" in debug or "l0" in debug:
            break

    t_out = Tok("out")
    for name, (buf, shape, dt) in dbg.items():
        dd = K.dram("dbg_" + name, shape, dt, kind="ExternalOutput")
        K.barrier()
        K.dma("sp", dd, buf[:], [], [t_out], "out")
        K.outs["dbg_" + name] = shape

    for b in range(4):
        c0, n = blk_cols(b)
        for hf in range(2):
            K.dma("sp", out_d[:, hf * 4:(hf + 1) * 4, c0:c0 + n], xT[:, hf * 4:(hf + 1) * 4, c0:c0 + n], [xT_t[b]], [t_out], "out")
    S = t_out.dsem["sp"]
    nc.sync.wait_ge(S.sem, S.count * 16)
    es.close()
    return K


_CONSTS = {}


def _consts():
    if _CONSTS:
        return _CONSTS
    bf = ml_dtypes.bfloat16
    t = np.arange(T, dtype=np.int64)
    ang = 2.0 * np.pi * ((t[:, None] * t[None, :]) % T).astype(np.float64) / T
    _CONSTS["c_ct"] = np.ascontiguousarray(np.cos(ang).reshape(16, 128, T).transpose(1, 0, 2)).astype(bf)
    _CONSTS["c_st"] = np.ascontiguousarray(np.sin(ang).reshape(16, 128, T).transpose(1, 0, 2)).astype(bf)
    t2 = np.arange(NCTX, dtype=np.int64)
    ang2 = 2.0 * np.pi * ((t2[:, None] * t2[None, :]) % NCTX).astype(np.float64) / NCTX
    _CONSTS["c_c256"] = np.ascontiguousarray(np.cos(ang2).reshape(2, 128, NCTX).transpose(1, 0, 2)).astype(bf)
    _CONSTS["c_s256"] = np.ascontiguousarray(np.sin(ang2).reshape(2, 128, NCTX).transpose(1, 0, 2)).astype(bf)
    g = np.arange(64)
    a64 = 2.0 * np.pi * ((g[:, None] * g[None, :]) % 64) / 64.0
    c64 = np.zeros((128, 4, 128))
    for blk in range(2):
        sl = slice(64 * blk, 64 * blk + 64)
        c64[sl, 0, sl] = np.cos(a64) / np.sqrt(T * 64.0)
        c64[sl, 1, sl] = -np.sin(a64) / np.sqrt(T * 64.0)
        c64[sl, 2, sl] = np.cos(a64) / np.sqrt(NCTX * 64.0)
        c64[sl, 3, sl] = -np.sin(a64) / np.sqrt(NCTX * 64.0)
    _CONSTS["c_c64"] = c64.astype(bf)
    cosT = np.zeros((128, T), np.float32)
    sinT = np.zeros((128, T), np.float32)
    row = (t // 64).astype(np.float64)
    col = (t % 64).astype(np.float64)
    for p in range(128):
        d = p % 64
        if d >= 48:
            continue
        within = d % 24
        j = within % 12
        inv = 10000.0 ** (-j / 12.0)
        pos = row if d < 24 else col
        cosT[p] = np.cos(pos * inv)
        sinT[p] = np.sin(pos * inv) * (-1.0 if within < 12 else 1.0)
    _CONSTS["c_cos"] = cosT
    _CONSTS["c_sin"] = sinT
    s_ = np.arange(128)
    gm = np.zeros((128, 2, 2, 128), np.float32)
    gm[:, 0, :, :] = (s_[:, None] <= s_[None, :])[:, None, :]
    gm[:, 1, :, :] = (s_[:, None] >= s_[None, :])[:, None, :]
    _CONSTS["c_gmask"] = gm.astype(bf)
    ms = np.ones((128, 512), np.float32)
    ms[:, 0::128] = 0.0
    _CONSTS["c_mscan"] = ms.astype(bf)
    return _CONSTS


def _host_prep(inputs):
    f = lambda a: np.ascontiguousarray(np.asarray(a, dtype=np.float32))
    x, c, ctx, c_ctx = f(inputs["x"]), f(inputs["c"]), f(inputs["ctx"]), f(inputs["c_ctx"])
    shared = {}
    shared["ada_w"] = f(f(inputs["ada_w"]).reshape(2, 8, 128, 6144).transpose(0, 2, 1, 3))
    shared["ada_b"] = f(f(inputs["ada_b"]).reshape(2, 48, 128).transpose(0, 2, 1))
    shared["norm_mix"] = f(f(inputs["norm_mix"]).reshape(2, 8, 128).transpose(0, 2, 1))
    shared["norm_ffn"] = f(f(inputs["norm_ffn"]).reshape(2, 8, 128).transpose(0, 2, 1))
    shared["ident"] = np.eye(128, dtype=np.float32)
    shared["ffn_w1"] = f(f(inputs["ffn_w1"]).reshape(2, 8, 128, DFF).transpose(0, 2, 1, 3))
    shared["ffn_w3"] = f(f(inputs["ffn_w3"]).reshape(2, 8, 128, DFF).transpose(0, 2, 1, 3))
    shared["ffn_w2"] = f(f(inputs["ffn_w2"]).reshape(2, 22, 128, D).transpose(0, 2, 1, 3))
    w_in = f(inputs["w_in"])
    shared["w_in"] = f(w_in.reshape(2, 8, 128, NIN).transpose(0, 2, 1, 3))
    wg = np.zeros((2, D, 1024), np.float32)
    dd = np.arange(48)
    partner = np.where((dd % 24) < 12, dd + 12, dd - 12)
    for hh in range(4):
        wg[:, :, 64 * hh:64 * hh + 48] = w_in[:, :, 1408 + 48 * hh + dd]
        wg[:, :, 256 + 64 * hh:256 + 64 * hh + 48] = w_in[:, :, 1408 + 48 * hh + partner]
        wg[:, :, 512 + 64 * hh:512 + 64 * hh + 48] = w_in[:, :, 1600 + 48 * hh + dd]
        wg[:, :, 768 + 64 * hh:768 + 64 * hh + 48] = w_in[:, :, 1600 + 48 * hh + partner]
    shared["w_g"] = f(wg.reshape(2, 8, 128, 1024).transpose(0, 2, 1, 3))
    shared["fnet_w"] = f(f(inputs["fnet_w"]).reshape(2, 2, 128, 256).transpose(0, 2, 1, 3))
    w_out = f(inputs["w_out"])
    shared["w_out"] = f(w_out.reshape(2, 8, 128, D).transpose(0, 2, 1, 3))
    shared["w_out_g"] = f(w_out[:, 640:, :].reshape(2, 4, 96, D).transpose(0, 2, 1, 3))
    shared["na_q"] = f(np.tile(f(inputs["na_q_norm"]), (1, 2))[:, :, None])
    shared["na_k"] = f(np.tile(f(inputs["na_k_norm"]), (1, 2))[:, :, None])
    rpb = f(inputs["na_rpb"])
    pk = np.arange(128)
    a_, kc_ = pk // 64, pk % 64
    b_, c_ = pk // 64, pk % 64
    dcm = np.clip(kc_[:, None] - c_[None, :], -15, 15) + 15
    cq0 = np.clip(c_ - 8, 0, 48)
    col_ok = (kc_[:, None] >= cq0[None, :]) & (kc_[:, None] < cq0[None, :] + 16)
    bt = np.full((2, 128, 72, 128), NEG, np.float32)
    ents = [(dl, 0) for dl in range(-3, 4)] + [(-2, 1), (-1, 0), (0, 0), (1, 0), (2, 1)]
    for e, (dl, msk_) in enumerate(ents):
        drm = 2 * dl + a_[:, None] - b_[None, :] + 7
        ok = col_ok & (drm >= 0) & (drm <= 14)
        if msk_ and dl == -2:
            ok = ok & ((2 * dl + a_[:, None]) >= (-4 + b_[None, :]))
        if msk_ and dl == 2:
            ok = ok & ((2 * dl + a_[:, None]) <= (3 + b_[None, :]))
        drc = np.clip(drm, 0, 14)
        for hh in range(6):
            vals = rpb[:, hh][:, drc, dcm]
            bt[:, :, hh * 12 + e, :] = np.where(ok[None], vals, NEG)
    shared["na_bt"] = bt
    aw = np.zeros((2, 32, 2, 256), np.float32)
    ab = np.zeros((2, 128, 2, 2), np.float32)
    gaw = f(inputs["gla_alpha_w"]); gab = f(inputs["gla_alpha_b"])
    for dr_ in range(2):
        for hh in range(4):
            aw[:, 16 * dr_:16 * dr_ + 16, dr_, 64 * hh:64 * hh + 48] = gaw[:, dr_, :, 48 * hh:48 * hh + 48]
            ab[:, 64 * (hh % 2):64 * (hh % 2) + 48, dr_, hh // 2] = gab[:, dr_, 48 * hh:48 * hh + 48]
    shared["gla_aw"] = aw
    shared["gla_ab"] = ab
    shared["gla_gon"] = f(f(inputs["gla_o_norm"])[:, :, None])
    shared.update(_consts())
    per_core = []
    for b in range(8):
        m = dict(shared)
        m["x"] = f(x[b].T.reshape(8, 128, T).transpose(1, 0, 2))
        m["ctx"] = f(ctx[b].T.reshape(8, 128, NCTX).transpose(1, 0, 2))
        ccv = np.stack([c[b], c_ctx], axis=-1)
        m["cc"] = f(ccv.reshape(8, 128, 2).transpose(1, 0, 2))
        per_core.append(m)
    return per_core


_DEBUG = tuple(x for x in os.environ.get("KDEBUG", "").split(",") if x)
LAST = {}


def kernel(**inputs):
    in_maps = _host_prep(inputs)
    K = build_program(debug=_DEBUG)
    ncores = int(os.environ.get("KCORES", "8"))
    res = run_bass_kernel_spmd(K.nc, in_maps[:ncores], core_ids=list(range(ncores)))
    LAST["res"] = res
    outs = []
    for r in res.results:
        o = np.asarray(r["out"])
        outs.append(o.transpose(1, 0, 2).reshape(D, T).T)
    return np.ascontiguousarray(np.stack(outs, axis=0)).astype(np.float32)
```

```python
import contextlib
import os
import numpy as np
import ml_dtypes
import concourse.bass as bass
import concourse.mybir as mybir
from concourse.bass_utils import run_bass_kernel_spmd

F32 = mybir.dt.float32
BF16 = mybir.dt.bfloat16
AF = mybir.ActivationFunctionType
ALU = mybir.AluOpType

D = 1024
T = 2048
NCTX = 256
NT = T + NCTX
DFF = 2816
NIN = 2592
EPS = 1e-6
NEG = -30000.0
SAME_ENGINE_RAW = True


class Tok:
    __slots__ = ("w", "r", "dsem", "name")

    def __init__(self, name=""):
        self.w = []
        self.r = {}
        self.dsem = None
        self.name = name


class Src:
    def __init__(self, name, sem, unit, h=None):
        self.name = name
        self.sem = sem
        self.unit = unit
        self.h = h
        self.count = 0
        self.seen = {}


class Ctx:
    def __init__(self):
        self.nc = bass.Bass("TRN2", target_bir_lowering=False)
        nc = self.nc
        self.es = contextlib.ExitStack()
        self.engs = {}
        for nm, h in (("pe", nc.tensor), ("act", nc.scalar), ("dve", nc.vector),
                      ("pool", nc.gpsimd), ("sp", nc.sync)):
            sem = self.es.enter_context(nc.semaphore("sem_" + nm))
            self.engs[nm] = Src(nm, sem, 1, h)
        self.dsems = []
        self.outs = {}
        self.n_inst = 0

    def dram(self, name, shape, dt, kind="ExternalInput"):
        return self.nc.dram_tensor(name, list(shape), dt, kind=kind).ap()

    def sb(self, stack, name, shape, dt):
        self.n_sb = getattr(self, "n_sb", 0) + 1
        return stack.enter_context(self.nc.sbuf_tensor("sb%d_%s" % (self.n_sb, name), list(shape), dt))

    def new_dsem(self, name):
        sem = self.es.enter_context(self.nc.semaphore("d_" + name + str(len(self.dsems))))
        s = Src("dma_" + name, sem, 16)
        self.dsems.append(s)
        return s

    def _wait_deps(self, E, reads, writes):
        deps = {}
        for t in reads:
            for (s, n) in t.w:
                deps[s] = max(deps.get(s, 0), n)
        for t in writes:
            for (s, n) in t.w:
                deps[s] = max(deps.get(s, 0), n)
            for s, n in t.r.items():
                deps[s] = max(deps.get(s, 0), n)
        for s, n in deps.items():
            if s is E and (E.name == "pe" or not SAME_ENGINE_RAW):
                continue
            if E.seen.get(s, 0) < n:
                E.h.wait_ge(s.sem, n * s.unit)
                E.seen[s] = n

    def op(self, eng, reads, writes, emit):
        E = self.engs[eng]
        self._wait_deps(E, reads, writes)
        ins = emit(E.h)
        E.count += 1
        ins.then_inc(E.sem, 1)
        self.n_inst += 1
        for t in reads:
            t.r[E] = E.count
        for t in writes:
            t.w = [(E, E.count)]
            t.r = {}
        return ins

    def mm(self, reads, writes, out, pairs, first=True, last=True):
        E = self.engs["pe"]
        self._wait_deps(E, reads, writes)
        n = len(pairs)
        ins = None
        for i, (l, r) in enumerate(pairs):
            ins = E.h.matmul(out, l, r, start=(first and i == 0), stop=(last and i == n - 1))
            self.n_inst += 1
        E.count += 1
        ins.then_inc(E.sem, 1)
        for t in reads:
            t.r[E] = E.count
        for t in writes:
            t.w = [(E, E.count)]
            t.r = {}

    def transpose(self, reads, writes, out, in_, ident):
        return self.op("pe", reads, writes, lambda h: h.transpose(out, in_, ident))

    def dma(self, q, out, in_, reads, writes, name="x"):
        E = self.engs[q]
        self._wait_deps(E, reads, writes)
        wt = writes[0]
        if wt.dsem is None:
            wt.dsem = {}
        if q not in wt.dsem:
            wt.dsem[q] = self.new_dsem(name + q)
        S = wt.dsem[q]
        ins = E.h.dma_start(out=out, in_=in_)
        S.count += 1
        ins.then_inc(S.sem, 16)
        self.n_inst += 1
        for t in reads:
            t.r[S] = S.count
        for t in writes:
            t.w = [(s_, n_) for (s_, n_) in t.w if (s_.unit == 16 and s_ is not S)] + [(S, S.count)]
            t.r = {}

    def barrier(self):
        allsrc = list(self.engs.values()) + self.dsems
        for E in self.engs.values():
            for s in allsrc:
                if s is E or s.count == 0:
                    continue
                if E.seen.get(s, 0) < s.count:
                    E.h.wait_ge(s.sem, s.count * s.unit)
                    E.seen[s] = s.count


def build_program(debug=()):
    K = Ctx()
    nc = K.nc
    es = K.es
    dbg = {}

    x_d = K.dram("x", [128, 8, T], F32)
    ctx_d = K.dram("ctx", [128, 8, NCTX], F32)
    cc_d = K.dram("cc", [128, 8, 2], F32)
    adaw_d = K.dram("ada_w", [2, 128, 8, 6144], F32)
    adab_d = K.dram("ada_b", [2, 128, 48], F32)
    nmix_d = K.dram("norm_mix", [2, 128, 8], F32)
    nffn_d = K.dram("norm_ffn", [2, 128, 8], F32)
    out_d = K.dram("out", [128, 8, T], F32, kind="ExternalOutput")
    ident_d = K.dram("ident", [128, 128], F32)
    w1_d = K.dram("ffn_w1", [2, 128, 8, DFF], F32)
    win_d = K.dram("w_in", [2, 128, 8, NIN], F32)
    wg_d = K.dram("w_g", [2, 128, 8, 1024], F32)
    fnetw_d = K.dram("fnet_w", [2, 128, 2, 256], F32)
    wout_d = K.dram("w_out", [2, 128, 8, D], F32)
    woutg_d = K.dram("w_out_g", [2, 96, 4, D], F32)
    naq_d = K.dram("na_q", [2, 128, 1], F32)
    nak_d = K.dram("na_k", [2, 128, 1], F32)
    bt_d = K.dram("na_bt", [2, 128, 72, 128], F32)
    aw_d = K.dram("gla_aw", [2, 32, 2, 256], F32)
    ab_d = K.dram("gla_ab", [2, 128, 2, 2], F32)
    gon_d = K.dram("gla_gon", [2, 96, 1], F32)
    ct_d = K.dram("c_ct", [128, 16, T], BF16)
    st_d = K.dram("c_st", [128, 16, T], BF16)
    c256_d = K.dram("c_c256", [128, 2, 256], BF16)
    s256_d = K.dram("c_s256", [128, 2, 256], BF16)
    c64_d = K.dram("c_c64", [128, 4, 128], BF16)
    cos_d = K.dram("c_cos", [128, T], F32)
    sin_d = K.dram("c_sin", [128, T], F32)
    gmask_d = K.dram("c_gmask", [128, 2, 2, 128], BF16)
    mscan_d = K.dram("c_mscan", [128, 512], BF16)
    w3_d = K.dram("ffn_w3", [2, 128, 8, DFF], F32)
    w2_d = K.dram("ffn_w2", [2, 128, 22, D], F32)

    top = es
    xT = K.sb(top, "xT", [128, 8, NT], F32)
    xT_t = [Tok("xT%d" % i) for i in range(5)]
    identf = K.sb(top, "identf", [128, 128], F32)
    identb = K.sb(top, "identb", [128, 128], BF16)
    onesb = K.sb(top, "onesb", [128, 128], BF16)
    cc = K.sb(top, "cc", [128, 8, 2], F32)
    scb = K.sb(top, "scb", [128, 8, 2], BF16)
    class Cur:
        def __init__(self):
            self.t = None

        def __getitem__(self, k):
            return self.t[k]
    modT = Cur()
    modTs = [K.sb(top, "modT%d" % i, [128, 48, 2], F32) for i in range(2)]
    amix = Cur()
    affn = Cur()
    amixs = [K.sb(top, "amix%d" % i, [128, 8, 2], F32) for i in range(2)]
    affns = [K.sb(top, "affn%d" % i, [128, 8, 2], F32) for i in range(2)]
    adabs = [K.sb(top, "adab%d" % i, [128, 48], F32) for i in range(2)]
    nmixs = [K.sb(top, "nmix%d" % i, [128, 8], F32) for i in range(2)]
    nffns = [K.sb(top, "nffn%d" % i, [128, 8], F32) for i in range(2)]
    t_const = Tok("const")
    t_mod = Tok("mod")
    t_small = Tok("small")

    ps = [es.enter_context(nc.psum_tensor("ps%d" % i, [128, 512], F32)) for i in range(7)]
    pt = [Tok("ps%d" % i) for i in range(7)]
    psb = es.enter_context(nc.psum_tensor("psb", [128, 1024], BF16))
    ptb = Tok("psb")

    def blk_cols(b):
        return (b * 512, 512) if b < 4 else (T, NCTX)

    K.dma("sp", identf[:], ident_d, [], [t_const], "c")
    K.op("dve", [t_const], [t_const], lambda h: h.tensor_copy(out=identb[:], in_=identf[:]))
    K.op("dve", [], [t_const], lambda h: h.memset(onesb[:], 1.0))
    K.dma("sp", cc[:], cc_d, [], [t_small], "s")
    K.op("act", [t_small], [t_small], lambda h: h.activation(out=scb[:], in_=cc[:], func=AF.Silu))

    def make_ada(l, stack):
        t_sm = Tok("adasmall")
        K.dma("sp", adabs[l][:], adab_d[l], [], [t_sm], "s")
        K.dma("sp", nmixs[l][:], nmix_d[l], [], [t_sm], "s")
        K.dma("sp", nffns[l][:], nffn_d[l], [], [t_sm], "s")
        wsl = [K.sb(stack, "adaw%d" % i, [128, 8, 512], BF16) for i in range(2)]
        wsl_t = [Tok("adaw%d" % i) for i in range(2)]

        def dma(hs):
            K.dma("pool", wsl[hs % 2][:], adaw_d[l][:, :, hs * 512:(hs + 1) * 512], [], [wsl_t[hs % 2]], "adaw")

        def sec(hs):
            s = hs % 2
            if hs == 0:
                dma(0)
            if hs + 1 < 12:
                dma(hs + 1)
            for mt in range(4):
                K.mm([wsl_t[s], t_small], [pt[6]], ps[6][:, mt * 2:mt * 2 + 2],
                     [(wsl[s][:, kc, mt * 128:(mt + 1) * 128], scb[:, kc, :]) for kc in range(8)])
            K.op("dve", [pt[6], t_sm], [t_mod], lambda h: h.tensor_tensor(
                out=modTs[l][:, hs * 4:(hs + 1) * 4, :],
                in0=ps[6][:, 0:8].rearrange("p (a b) -> p a b", b=2),
                in1=adabs[l][:, hs * 4:(hs + 1) * 4].unsqueeze(2).to_broadcast([128, 4, 2]),
                op=ALU.add))

        def tail():
            for (dst, gain, sc_) in ((amixs[l], nmixs[l], 1), (affns[l], nffns[l], 4)):
                K.op("dve", [t_mod, t_sm], [t_mod], lambda h, dst=dst, gain=gain, sc_=sc_: h.scalar_tensor_tensor(
                    out=dst[:], in0=modTs[l][:, sc_ * 8:(sc_ + 1) * 8, :], scalar=1.0,
                    in1=gain[:].unsqueeze(2).to_broadcast([128, 8, 2]),
                    op0=ALU.add, op1=ALU.mult))
        return [lambda hs=hs: sec(hs) for hs in range(12)] + [tail]

    with contextlib.ExitStack() as ph:
        ada0 = make_ada(0, ph)
        for b in range(5):
            c0, n = blk_cols(b)
            src = x_d[:, :, c0:c0 + n] if b < 4 else ctx_d[:, :, :]
            for hf in range(2):
                K.dma("sp", xT[:, hf * 4:(hf + 1) * 4, c0:c0 + n], src[:, hf * 4:(hf + 1) * 4, :], [], [xT_t[b]], "xin")
        for hs in range(12):
            ada0[hs]()
        ada0[12]()
        K.barrier()

    rot = {"ps": 0}

    def norm_mod(l, which, hT, hT_t, blocks):
        a = amix if which == "mix" else affn
        sec = 0 if which == "mix" else 3
        blocks = list(blocks)
        with contextlib.ExitStack() as ph:
            sq1 = K.sb(ph, "nm_sq", [128, 8, 512], BF16)
            sq = [sq1, sq1]
            sq1_t = Tok()
            sq_t = [sq1_t, sq1_t]
            rs = [K.sb(ph, "nm_rs%d" % i, [128, 512], F32) for i in range(2)]
            rs_t = [Tok() for i in range(2)]
            tmp = [K.sb(ph, "nm_tmp%d" % i, [128, 512], F32) for i in range(3)]
            tmp_t = [Tok() for i in range(3)]
            epsb = K.sb(ph, "nm_eps", [128, 1], F32)
            eps_t = Tok()
            K.op("dve", [], [eps_t], lambda h: h.memset(epsb[:], EPS))

            def stats(bi):
                b = blocks[bi]
                c0, n = blk_cols(b)
                s = bi % 2
                pb = bi % 2
                K.op("dve", [xT_t[b]], [sq_t[s]], lambda h: h.tensor_tensor(
                    out=sq[s][:, :, :n], in0=xT[:, :, c0:c0 + n], in1=xT[:, :, c0:c0 + n], op=ALU.mult))
                K.mm([sq_t[s], t_const], [pt[pb]], ps[pb][:, :n],
                     [(onesb[:], sq[s][:, kc, :n]) for kc in range(8)])
                K.op("act", [pt[pb], eps_t], [rs_t[s]], lambda h: h.activation(
                    out=rs[s][:, :n], in_=ps[pb][:, :n], func=AF.Ln, bias=epsb[:], scale=1.0 / D))
                K.op("act", [rs_t[s]], [rs_t[s]], lambda h: h.activation(
                    out=rs[s][:, :n], in_=rs[s][:, :n], func=AF.Exp, scale=-0.5))

            ti = 0
            stats(0)
            for bi, b in enumerate(blocks):
                if bi + 1 < len(blocks):
                    stats(bi + 1)
                c0, n = blk_cols(b)
                r = 0 if b < 4 else 1
                s = bi % 2
                for kc in range(8):
                    u = ti % 3
                    ti += 1
                    eng = "pool" if kc % 4 == 3 else "dve"
                    K.op(eng, [xT_t[b], rs_t[s]], [tmp_t[u]], lambda h, u=u, kc=kc: h.tensor_tensor(
                        out=tmp[u][:, :n], in0=xT[:, kc, c0:c0 + n], in1=rs[s][:, :n], op=ALU.mult))
                    K.op("act", [tmp_t[u], t_mod], [hT_t[b]], lambda h, u=u, kc=kc: h.activation(
                        out=hT[:, kc, c0:c0 + n], in_=tmp[u][:, :n], func=AF.Identity,
                        bias=modT[:, sec * 8 + kc, r:r + 1], scale=a[:, kc, r:r + 1]))
            K.barrier()

    def ffn(l, hT, hT_t, blocks):
        with contextlib.ExitStack() as ph:
            chunks = [(i * 512, 512) for i in range(5)] + [(2560, 256)]
            w1s = [K.sb(ph, "w1s%d" % i, [128, 8, 512], BF16) for i in range(2)]
            w3s = [K.sb(ph, "w3s%d" % i, [128, 8, 512], BF16) for i in range(2)]
            w2s = [K.sb(ph, "w2s%d" % i, [128, 4, D], BF16) for i in range(2)]
            w_t = [Tok() for i in range(2)]
            s1 = [K.sb(ph, "ff_s1%d" % i, [128, 512], F32) for i in range(2)]
            s1_t = [Tok() for i in range(2)]
            g = [K.sb(ph, "ff_g%d" % i, [128, 4, 512], BF16) for i in range(2)]
            g_t = [Tok() for i in range(2)]

            def load(ch):
                f0, nf = chunks[ch]
                s = ch % 2
                K.dma("pool", w1s[s][:, :, :nf], w1_d[l][:, :, f0:f0 + nf], [], [w_t[s]], "ffw")
                K.dma("pool", w3s[s][:, :, :nf], w3_d[l][:, :, f0:f0 + nf], [], [w_t[s]], "ffw")
                K.dma("pool", w2s[s][:, :nf // 128, :], w2_d[l][:, f0 // 128:(f0 + nf) // 128, :], [], [w_t[s]], "ffw")
            load(0)
            ada_next = make_ada(l + 1, ph) if l + 1 < 2 else None
            norm_mod(l, "ffn", hT, hT_t, blocks)
            cnt = 0
            gi = 0
            oi = 0
            for ch in range(6):
                if ch + 1 < 6:
                    load(ch + 1)
                f0, nf = chunks[ch]
                s = ch % 2
                nft = nf // 128
                for b in blocks:
                    c0, n = blk_cols(b)
                    r = 0 if b < 4 else 1
                    gs = gi % 2
                    gi += 1
                    for ft in range(nft):
                        pa = cnt % 2
                        pbk = 2 + cnt % 2
                        u = cnt % 2
                        cnt += 1
                        K.mm([w_t[s], hT_t[b]], [pt[pa]], ps[pa][:, :n],
                             [(w1s[s][:, kc, ft * 128:(ft + 1) * 128], hT[:, kc, c0:c0 + n]) for kc in range(8)])
                        K.mm([w_t[s], hT_t[b]], [pt[pbk]], ps[pbk][:, :n],
                             [(w3s[s][:, kc, ft * 128:(ft + 1) * 128], hT[:, kc, c0:c0 + n]) for kc in range(8)])
                        K.op("act", [pt[pa]], [s1_t[u]], lambda h, u=u, pa=pa, n=n: h.activation(
                            out=s1[u][:, :n], in_=ps[pa][:, :n], func=AF.Silu))
                        K.op("dve", [s1_t[u], pt[pbk]], [g_t[gs]], lambda h, u=u, pbk=pbk, n=n, gs=gs, ft=ft: h.tensor_tensor(
                            out=g[gs][:, ft, :n], in0=s1[u][:, :n], in1=ps[pbk][:, :n], op=ALU.mult))
                    for dt in range(8):
                        pc = 4 + oi % 3
                        oi += 1
                        K.mm([w_t[s], g_t[gs]], [pt[pc]], ps[pc][:, :n],
                             [(w2s[s][:, ft, dt * 128:(dt + 1) * 128], g[gs][:, ft, :n]) for ft in range(nft)])
                        K.op("dve", [pt[pc], t_mod, xT_t[b]], [xT_t[b]], lambda h, pc=pc, dt=dt, c0=c0, n=n, r=r: h.scalar_tensor_tensor(
                            out=xT[:, dt, c0:c0 + n], in0=ps[pc][:, :n], scalar=modT[:, 40 + dt, r:r + 1],
                            in1=xT[:, dt, c0:c0 + n], op0=ALU.mult, op1=ALU.add))
                if ada_next is not None:
                    ada_next[2 * ch]()
                    ada_next[2 * ch + 1]()
            if ada_next is not None:
                ada_next[12]()
            K.barrier()

    def lin_fm(wt, w, col0, M, hT, hT_t, b, c0, n, pb, wtok):
        K.mm([wtok, hT_t[b]], [pt[pb]], ps[pb][:M, :n],
             [(w[:, kc, col0:col0 + M], hT[:, kc, c0:c0 + n]) for kc in range(8)])

    def out_proj_update(pairs_fn, reads, b, c0, n, r, pbs):
        for dt in range(8):
            pc = pbs[dt % len(pbs)]
            K.mm(reads, [pt[pc]], ps[pc][:, :n], pairs_fn(dt))
            K.op("dve", [pt[pc], t_mod, xT_t[b]], [xT_t[b]], lambda h, pc=pc, dt=dt: h.scalar_tensor_tensor(
                out=xT[:, dt, c0:c0 + n], in0=ps[pc][:, :n], scalar=modT[:, 16 + dt, r:r + 1],
                in1=xT[:, dt, c0:c0 + n], op0=ALU.mult, op1=ALU.add))

    def rstd_from_ps(pb, P, n, inv, rt, rs, rs_tok, eps_ap, eps_t):
        K.op("act", [pt[pb], eps_t], [rs_tok], lambda h: h.activation(
            out=rt[:P, :n], in_=ps[pb][:P, :n], func=AF.Sqrt, bias=eps_ap[:P, :], scale=inv))
        K.op("dve", [rs_tok], [rs_tok], lambda h: h.reciprocal(out=rs[:P, :n], in_=rt[:P, :n]))

    def fnet(l, hT, hT_t, need_ctx, pre=None):
        with contextlib.ExitStack() as ph:
            wfn = K.sb(ph, "fn_w", [128, 8, 256], BF16)
            fnw = K.sb(ph, "fn_fw", [128, 2, 256], BF16)
            wo = K.sb(ph, "fn_wo", [128, 2, D], BF16)
            c64 = K.sb(ph, "fn_c64", [128, 4, 128], BF16)
            w_t = Tok()
            K.dma("pool", wfn[:], win_d[l][:, :, 0:256], [], [w_t], "fnw")
            K.dma("pool", fnw[:], fnetw_d[l], [], [w_t], "fnw")
            K.dma("pool", wo[:], wout_d[l][:, 0:2, :], [], [w_t], "fnw")
            K.dma("sp", c64[:], c64_d, [], [w_t], "fnw")
            cts = [K.sb(ph, "fn_ct%d" % i, [128, 2, 16, 512], BF16) for i in range(2)]
            ct_t = [Tok() for i in range(2)]
            jobs = [(kb, 0) for kb in range(4)] + ([(0, 1)] if need_ctx else [])

            def issue_tables(ji):
                kb, isctx = jobs[ji]
                s = ji % 2
                if not isctx:
                    K.dma("sp", cts[s][:, 0, :, :], ct_d[:, :, kb * 512:(kb + 1) * 512], [], [ct_t[s]], "ct")
                    K.dma("sp", cts[s][:, 1, :, :], st_d[:, :, kb * 512:(kb + 1) * 512], [], [ct_t[s]], "ct")
                else:
                    K.dma("sp", cts[s][:, 0, 0:2, 0:256], c256_d, [], [ct_t[s]], "ct")
                    K.dma("sp", cts[s][:, 1, 0:2, 0:256], s256_d, [], [ct_t[s]], "ct")
            issue_tables(0)
            issue_tables(1)
            if pre is not None:
                pre()
            Z = K.sb(ph, "fn_Z", [128, 18, 256], BF16)
            Z_t = Tok()
            Psb = [K.sb(ph, "fn_P%d" % i, [128, 2, 512], BF16) for i in range(2)]
            P_t = [Tok() for i in range(2)]
            Ysb = K.sb(ph, "fn_Y", [128, 2, 512], BF16)
            Y_t = Tok()
            yf = K.sb(ph, "fn_yf", [128, 2, 512], BF16)
            yf_t = Tok()
            for ti in range(18):
                pb = ti % 2
                b = min(ti // 4, 4)
                K.mm([w_t, hT_t[b]], [pt[pb]], ps[pb][:, :256],
                     [(hT[:, kc, ti * 128:(ti + 1) * 128], wfn[:, kc, :]) for kc in range(8)])
                if ti % 2 == 0:
                    K.op("dve", [pt[pb]], [Z_t], lambda h, ti=ti, pb=pb: h.tensor_copy(out=Z[:, ti, :], in_=ps[pb][:, :256]))
                else:
                    K.op("act", [pt[pb]], [Z_t], lambda h, ti=ti, pb=pb: h.activation(out=Z[:, ti, :], in_=ps[pb][:, :256], func=AF.Copy))
            for ji, (kb, isctx) in enumerate(jobs):
                s = ji % 2
                if not isctx:
                    ntt, n, tt0, c0, b, r = 16, 512, 0, kb * 512, kb, 0
                else:
                    ntt, n, tt0, c0, b, r = 2, 256, 16, T, 4, 1
                for chc in range(2):
                    u = chc
                    for cs in range(2):
                        pb = cs
                        K.mm([Z_t, ct_t[s]], [pt[pb]], ps[pb][:, :n],
                             [(Z[:, tt0 + tt, chc * 128:(chc + 1) * 128], cts[s][:, cs, tt, :n]) for tt in range(ntt)])
                        if cs == 0:
                            K.op("dve", [pt[pb]], [P_t[u]], lambda h, u=u, pb=pb, cs=cs: h.tensor_copy(out=Psb[u][:, cs, :n], in_=ps[pb][:, :n]))
                        else:
                            K.op("act", [pt[pb]], [P_t[u]], lambda h, u=u, pb=pb, cs=cs: h.activation(out=Psb[u][:, cs, :n], in_=ps[pb][:, :n], func=AF.Copy))
                    pb = 2 + chc
                    K.mm([P_t[u], w_t], [pt[pb]], ps[pb][:, :n],
                         [(c64[:, 2 * isctx + 0, :], Psb[u][:, 0, :n]), (c64[:, 2 * isctx + 1, :], Psb[u][:, 1, :n])])
                    K.op("dve", [pt[pb]], [Y_t], lambda h, pb=pb, chc=chc: h.tensor_copy(out=Ysb[:, chc, :n], in_=ps[pb][:, :n]))
                for c2 in range(2):
                    pb = 4 + c2
                    K.mm([Y_t, w_t], [pt[pb]], ps[pb][:, :n],
                         [(fnw[:, c1, c2 * 128:(c2 + 1) * 128], Ysb[:, c1, :n]) for c1 in range(2)])
                    K.op("act", [pt[pb]], [yf_t], lambda h, pb=pb, c2=c2: h.activation(out=yf[:, c2, :n], in_=ps[pb][:, :n], func=AF.Copy))
                out_proj_update(lambda dt: [(wo[:, c2, dt * 128:(dt + 1) * 128], yf[:, c2, :n]) for c2 in range(2)],
                                [yf_t, w_t], b, c0, n, r, [0, 1, 2, 3, 4, 5, 6])
                if ji + 2 < len(jobs):
                    issue_tables(ji + 2)
            K.barrier()

    def natt(l, hT, hT_t, need_ctx):
        with contextlib.ExitStack() as ph:
            kT = K.sb(ph, "na_kT", [128, 3, NT], BF16)
            kT_t = Tok()
            V = K.sb(ph, "na_V", [128, 18, 6, 65], BF16)
            V_t = Tok()
            gq = K.sb(ph, "na_gq", [128, 1], F32)
            gk = K.sb(ph, "na_gk", [128, 1], F32)
            epsb = K.sb(ph, "na_eps", [128, 1], F32)
            bones = K.sb(ph, "na_bones", [128, 128], BF16)
            g_t = Tok()
            K.dma("sp", gq[:], naq_d[l], [], [g_t], "nag")
            K.dma("sp", gk[:], nak_d[l], [], [g_t], "nag")
            K.op("dve", [g_t], [g_t], lambda h: h.tensor_scalar(out=gq[:], in0=gq[:], scalar1=0.125, scalar2=None, op0=ALU.mult))
            K.op("dve", [], [g_t], lambda h: h.memset(epsb[:], EPS))
            K.op("dve", [], [g_t], lambda h: h.memset(bones[:], 0.0))
            K.op("dve", [g_t], [g_t], lambda h: h.memset(bones[0:64, 0:64], 1.0))
            K.op("dve", [g_t], [g_t], lambda h: h.memset(bones[64:128, 64:128], 1.0))
            K.op("dve", [], [V_t], lambda h: h.memset(V[:, :, :, 64:65], 1.0))
            sq = [K.sb(ph, "na_sq%d" % i, [128, 512], BF16) for i in range(2)]
            sq_t = [Tok() for i in range(2)]
            rtl = [K.sb(ph, "na_rt%d" % i, [128, 512], F32) for i in range(2)]
            rs = [K.sb(ph, "na_rs%d" % i, [128, 512], F32) for i in range(2)]
            rs_t = [Tok() for i in range(2)]

            def qk_norm(w, wtok, gain, dst, dst_t, dcol, b, c0, n, cnt):
                for hp in range(3):
                    u = (cnt[0]) % 2
                    cnt[0] += 1
                    pa, pbk = u, 2 + u
                    lin_fm(None, w, hp * 128, 128, hT, hT_t, b, c0, n, pa, wtok)
                    K.op("act", [pt[pa]], [sq_t[u]], lambda h, u=u, pa=pa: h.activation(out=sq[u][:, :n], in_=ps[pa][:, :n], func=AF.Square))
                    K.mm([sq_t[u], g_t], [pt[pbk]], ps[pbk][:, :n], [(bones[:], sq[u][:, :n])])
                    K.op("act", [pt[pbk], g_t], [rs_t[u]], lambda h, u=u, pbk=pbk: h.activation(
                        out=rtl[u][:, :n], in_=ps[pbk][:, :n], func=AF.Ln, bias=epsb[:], scale=1.0 / 64))
                    K.op("act", [rs_t[u]], [rs_t[u]], lambda h, u=u: h.activation(
                        out=rs[u][:, :n], in_=rtl[u][:, :n], func=AF.Exp, scale=-0.5))
                    K.op("dve", [pt[pa], rs_t[u], g_t], [dst_t], lambda h, u=u, pa=pa, hp=hp: h.scalar_tensor_tensor(
                        out=dst[:, hp, dcol:dcol + n], in0=ps[pa][:, :n], scalar=gain[:, 0:1], in1=rs[u][:, :n],
                        op0=ALU.mult, op1=ALU.mult))

            cnt = [0]
            BT = K.sb(ph, "na_BT", [128, 72, 128], BF16)
            bt_t = Tok()
            for i3 in range(3):
                K.dma("pool", BT[:, 24 * i3:24 * (i3 + 1), :], bt_d[l][:, 24 * i3:24 * (i3 + 1), :], [], [bt_t], "nabt")
            for i3 in range(3):
                K.op("act", [bt_t], [bt_t], lambda h, i3=i3: h.activation(
                    out=BT[:, 24 * i3:24 * (i3 + 1), :], in_=BT[:, 24 * i3:24 * (i3 + 1), :], func=AF.Exp))
            with contextlib.ExitStack() as p1:
                wk = K.sb(p1, "na_wk", [128, 8, 384], BF16)
                wv = K.sb(p1, "na_wv", [128, 8, 384], BF16)
                wk_t = Tok()
                K.dma("pool", wk[:], win_d[l][:, :, 640:1024], [], [wk_t], "naw")
                K.dma("pool", wv[:], win_d[l][:, :, 1024:1408], [], [wk_t], "naw")
                for b in range(5):
                    c0, n = blk_cols(b)
                    qk_norm(wk, wk_t, gk, kT, kT_t, c0, b, c0, n, cnt)
                for ti in range(18):
                    pb = 4 + ti % 2
                    b = min(ti // 4, 4)
                    K.mm([wk_t, hT_t[b]], [pt[pb]], ps[pb][:, :384],
                         [(hT[:, kc, ti * 128:(ti + 1) * 128], wv[:, kc, :]) for kc in range(8)])
                    K.op("act" if ti % 2 else "dve", [pt[pb]], [V_t],
                         (lambda h, ti=ti, pb=pb: h.activation(out=V[:, ti, :, 0:64], in_=ps[pb][:, :384].rearrange("p (a b) -> p a b", b=64), func=AF.Copy))
                         if ti % 2 else
                         (lambda h, ti=ti, pb=pb: h.tensor_copy(out=V[:, ti, :, 0:64], in_=ps[pb][:, :384].rearrange("p (a b) -> p a b", b=64))))
                K.barrier()
            wq = K.sb(ph, "na_wq", [128, 8, 384], BF16)
            wo = K.sb(ph, "na_wo", [128, 3, D], BF16)
            w_t = Tok()
            K.dma("pool", wq[:], win_d[l][:, :, 256:640], [], [w_t], "naw2")
            K.dma("pool", wo[:], wout_d[l][:, 2:5, :], [], [w_t], "naw2")
            qT2 = [K.sb(ph, "na_qT%d" % i, [128, 3, 512], BF16) for i in range(2)]
            qT2_t = [Tok(), Tok()]
            PT = [K.sb(ph, "na_PT%d" % i, [128, 7, 128], BF16) for i in range(2)]
            PT_t = [Tok() for i in range(2)]
            rec = [K.sb(ph, "na_rec%d" % i, [128, 6], F32) for i in range(2)]
            rec_t = [Tok(), Tok()]
            Osb = [K.sb(ph, "na_O%d" % i, [128, 6, 64], BF16) for i in range(2)]
            O_t = [Tok(), Tok()]
            yna2 = [K.sb(ph, "na_y%d" % i, [128, 3, 512], BF16) for i in range(2)]
            yna2_t = [Tok(), Tok()]
            hcount = 0
            nblocks = list(range(5) if need_ctx else range(4))
            pending = [None]
            c0_, n_ = blk_cols(nblocks[0])
            qk_norm(wq, w_t, gq, qT2[0], qT2_t[0], 0, nblocks[0], c0_, n_, cnt)
            for bi, b in enumerate(nblocks):
                c0, n = blk_cols(b)
                r = 0 if b < 4 else 1
                qT, qT_t = qT2[bi % 2], qT2_t[bi % 2]
                yna, yna_t = yna2[bi % 2], yna2_t[bi % 2]
                tile_chunks = []
                for qi in range(n // 128):
                    if b < 4:
                        i = b * 4 + qi
                        if i <= 1:
                            js = list(range(0, 4))
                        elif i >= 14:
                            js = list(range(12, 16))
                        else:
                            js = list(range(i - 2, i + 3))
                        chunks = []
                        for j in js:
                            dl = j - i
                            v = (dl + 9) if 2 <= i <= 13 else (dl + 3)
                            chunks.append((j, v))
                        chunks += [(16, None), (17, None)]
                    else:
                        chunks = [(16, None), (17, None)]
                    tile_chunks.append(chunks)
                items = [(qi, hd) for qi in range(n // 128) for hd in range(6)]

                def stage1(k):
                    qi, hd = items[k]
                    chunks = tile_chunks[qi]
                    nch = len(chunks)
                    hp, r0 = hd // 2, (hd % 2) * 64
                    u = (hbase + k) % 2
                    pS = (0 + 2 * u, 1 + 2 * u)
                    for c, (kt, v) in enumerate(chunks):
                        pb = pS[c // 4]
                        pairs = [(kT[r0:r0 + 64, hp, kt * 128:(kt + 1) * 128], qT[r0:r0 + 64, hp, qi * 128:(qi + 1) * 128])]
                        K.mm([kT_t, qT_t, w_t, t_const], [pt[pb]], ps[pb][:, (c % 4) * 128:(c % 4 + 1) * 128], pairs)
                    n0 = min(nch, 4)
                    K.op("act", [pt[pS[0]]], [PT_t[u]], lambda h: h.activation(
                        out=PT[u][:, 0:n0, :], in_=ps[pS[0]][:, 0:n0 * 128].rearrange("p (a b) -> p a b", b=128), func=AF.Exp))
                    if nch > 4:
                        n1 = nch - 4
                        K.op("act", [pt[pS[1]]], [PT_t[u]], lambda h: h.activation(
                            out=PT[u][:, 4:4 + n1, :], in_=ps[pS[1]][:, 0:n1 * 128].rearrange("p (a b) -> p a b", b=128), func=AF.Exp))
                    loc = [v for (kt, v) in chunks if v is not None]
                    if loc:
                        nl, v0 = len(loc), loc[0]
                        K.op("dve", [PT_t[u], bt_t], [PT_t[u]], lambda h: h.tensor_tensor(
                            out=PT[u][:, 0:nl, :], in0=PT[u][:, 0:nl, :], in1=BT[:, hd * 12 + v0:hd * 12 + v0 + nl, :], op=ALU.mult))

                def stage2(k):
                    qi, hd = items[k]
                    chunks = tile_chunks[qi]
                    u = (hbase + k) % 2
                    po = 4 + qi % 2
                    K.mm([PT_t[u], V_t], [pt[po]], ps[po][:, hd * 65:(hd + 1) * 65],
                         [(PT[u][:, c, :], V[:, kt, hd, :]) for c, (kt, v) in enumerate(chunks)])

                def fin_dve(qi):
                    po = 4 + qi % 2
                    oi = qi % 2
                    ov = ps[po][:, 0:390].rearrange("p (a b) -> p a b", b=65)
                    K.op("dve", [pt[po]], [rec_t[oi]], lambda h: h.reciprocal(out=rec[oi][:], in_=ov[:, :, 64]))
                    K.op("dve", [pt[po], rec_t[oi]], [O_t[oi]], lambda h: h.tensor_tensor(
                        out=Osb[oi][:], in0=ov[:, :, 0:64], in1=rec[oi][:].unsqueeze(2).to_broadcast([128, 6, 64]), op=ALU.mult))

                def fin_pe(qi):
                    oi = qi % 2
                    Of = Osb[oi][:].rearrange("p a b -> p (a b)")
                    for hp in range(3):
                        K.op("pe", [O_t[oi], t_const], [ptb], lambda h, hp=hp: h.transpose(
                            psb[:, hp * 128:(hp + 1) * 128], Of[:, hp * 128:(hp + 1) * 128], identb[:]))
                    K.op("act", [ptb], [yna_t], lambda h: h.activation(
                        out=yna[:, :, qi * 128:(qi + 1) * 128], in_=psb[:, 0:384].rearrange("p (a b) -> p a b", b=128), func=AF.Copy))

                hbase = hcount
                nit = len(items)
                for k in range(nit + 2):
                    if k == 6 and pending[0] is not None:
                        pending[0]()
                        pending[0] = None
                    if k < nit:
                        stage1(k)
                    if 1 <= k <= nit:
                        stage2(k - 1)
                        if items[k - 1][1] == 5:
                            fin_dve(items[k - 1][0])
                    if 2 <= k <= nit + 1 and items[k - 2][1] == 5:
                        fin_pe(items[k - 2][0])
                hcount += nit
                if bi + 1 < len(nblocks):
                    c0_, n_ = blk_cols(nblocks[bi + 1])
                    qk_norm(wq, w_t, gq, qT2[(bi + 1) % 2], qT2_t[(bi + 1) % 2], 0, nblocks[bi + 1], c0_, n_, cnt)
                def _op(yna=yna, yna_t=yna_t, b=b, c0=c0, n=n, r=r):
                    out_proj_update(lambda dt: [(wo[:, hp, dt * 128:(dt + 1) * 128], yna[:, hp, :n]) for hp in range(3)],
                                    [yna_t, w_t], b, c0, n, r, [6])
                if pending[0] is not None:
                    pending[0]()
                pending[0] = _op
            if pending[0] is not None:
                pending[0]()
                pending[0] = None
            K.barrier()

    def gla(l, hT, hT_t, need_ctx):
        NB = 256
        QS = 48 ** -0.5
        with contextlib.ExitStack() as ph:
            qr = K.sb(ph, "gl_qr", [128, 2, NT], BF16)
            kr = K.sb(ph, "gl_kr", [128, 2, NT], BF16)
            aT = K.sb(ph, "gl_aT", [32, NT], BF16)
            st_t = [Tok() for i in range(5)]
            gvc = K.sb(ph, "gl_gvc", [128, 2, 384], BF16)
            sgc = K.sb(ph, "gl_sgc", [96, 4, NCTX], BF16)
            gvc_t = Tok()
            aw = K.sb(ph, "gl_aw", [32, 2, 256], BF16)
            nab = K.sb(ph, "gl_nab", [128, 2, 2], F32)
            gon = K.sb(ph, "gl_gon", [96, 1], F32)
            msk = K.sb(ph, "gl_msk", [128, 2, 2, 128], BF16)
            mscan = K.sb(ph, "gl_mscan", [128, 512], BF16)
            epsb = K.sb(ph, "gl_eps", [128, 1], F32)
            oneb = K.sb(ph, "gl_one", [128, 1], F32)
            w_t = Tok()
            K.dma("pool", aw[:], aw_d[l], [], [w_t], "glw")
            K.dma("sp", nab[:], ab_d[l], [], [w_t], "glw")
            K.dma("sp", gon[:], gon_d[l], [], [w_t], "glw")
            K.dma("sp", msk[:], gmask_d, [], [w_t], "glw")
            K.dma("sp", mscan[:], mscan_d, [], [w_t], "glw")
            K.op("dve", [w_t], [w_t], lambda h: h.tensor_scalar(out=nab[:], in0=nab[:], scalar1=-1.0, scalar2=None, op0=ALU.mult))
            K.op("dve", [], [w_t], lambda h: h.memset(epsb[:], EPS))
            K.op("dve", [], [w_t], lambda h: h.memset(oneb[:], 1.0))
            with contextlib.ExitStack() as p0:
                wg = K.sb(p0, "gl_wg", [128, 8, 1024], BF16)
                wga = K.sb(p0, "gl_wga", [128, 8, 32], BF16)
                w0_t = Tok()
                K.dma("pool", wg[:], wg_d[l], [], [w0_t], "glw0")
                K.dma("pool", wga[:], win_d[l][:, :, 2560:2592], [], [w0_t], "glw0")
                wgv = K.sb(p0, "gl_wgv", [128, 8, 384], BF16)
                wgg = K.sb(p0, "gl_wgg", [128, 8, 384], BF16)
                K.dma("pool", wgv[:], win_d[l][:, :, 1792:2176], [], [w0_t], "glw0")
                K.dma("pool", wgg[:], win_d[l][:, :, 2176:2560], [], [w0_t], "glw0")
                gvs = K.sb(p0, "gl_gvs", [128, 4, 384], BF16)
                sgs = K.sb(p0, "gl_sgs", [96, 4, 512], BF16)
                stg_t = Tok()
                cosb = K.sb(p0, "gl_cos", [128, 512], F32)
                sinb = K.sb(p0, "gl_sin", [128, 512], F32)
                cs_t = Tok()
                t1 = [K.sb(p0, "gl_t1%d" % i, [128, 512], F32) for i in range(2)]
                t2 = [K.sb(p0, "gl_t2%d" % i, [128, 512], F32) for i in range(2)]
                r_t = [Tok(), Tok()]
                cnt = 0
                for b5 in range(5):
                    c0, n = blk_cols(b5)
                    lat = b5 < 4
                    if lat:
                        K.dma("sp", cosb[:, :n], cos_d[:, c0:c0 + n], [], [cs_t], "cs")
                        K.dma("sp", sinb[:, :n], sin_d[:, c0:c0 + n], [], [cs_t], "cs")
                    lin_fm(None, wga, 0, 32, hT, hT_t, b5, c0, n, 6, w0_t)
                    K.op("act", [pt[6]], [st_t[b5]], lambda h: h.activation(out=aT[:, c0:c0 + n], in_=ps[6][:32, :n], func=AF.Copy))
                    for hp in range(2):
                        for which in range(2):
                            base = which * 512
                            dst = qr if which == 0 else kr
                            u = cnt % 2
                            cnt += 1
                            pa, pbk = (0, 1) if u == 0 else (2, 3)
                            lin_fm(None, wg, base + hp * 128, 128, hT, hT_t, b5, c0, n, pa, w0_t)
                            if lat:
                                lin_fm(None, wg, base + 256 + hp * 128, 128, hT, hT_t, b5, c0, n, pbk, w0_t)
                                K.op("dve", [pt[pa], cs_t], [r_t[u]], lambda h: h.tensor_tensor(out=t1[u][:, :n], in0=ps[pa][:, :n], in1=cosb[:, :n], op=ALU.mult))
                                K.op("dve", [pt[pbk], cs_t, r_t[u]], [r_t[u]], lambda h: h.tensor_tensor(out=t2[u][:, :n], in0=ps[pbk][:, :n], in1=sinb[:, :n], op=ALU.mult))
                                K.op("pool", [r_t[u]], [st_t[b5]], lambda h: h.tensor_tensor(out=dst[:, hp, c0:c0 + n], in0=t1[u][:, :n], in1=t2[u][:, :n], op=ALU.add))
                            else:
                                K.op("act", [pt[pa]], [st_t[b5]], lambda h: h.activation(out=dst[:, hp, c0:c0 + n], in_=ps[pa][:, :n], func=AF.Copy))
                    for tt in range(n // 128):
                        pv = 4 + tt % 2
                        K.mm([w0_t, hT_t[b5]], [pt[pv]], ps[pv][:, :384],
                             [(hT[:, kc, c0 + tt * 128:c0 + (tt + 1) * 128], wgv[:, kc, :]) for kc in range(8)])
                        if lat:
                            K.op("act", [pt[pv]], [stg_t], lambda h, tt=tt, pv=pv: h.activation(out=gvs[:, tt, :], in_=ps[pv][:, :384], func=AF.Copy))
                        else:
                            K.op("act", [pt[pv]], [gvc_t], lambda h, tt=tt, pv=pv: h.activation(out=gvc[:, tt, :], in_=ps[pv][:, :384], func=AF.Copy))
                    if lat or need_ctx:
                        for hd in range(4):
                            pv = 4 + hd % 2
                            lin_fm(None, wgg, hd * 96, 96, hT, hT_t, b5, c0, n, pv, w0_t)
                            if lat:
                                K.op("act", [pt[pv]], [stg_t], lambda h, hd=hd, pv=pv: h.activation(out=sgs[:, hd, :n], in_=ps[pv][:96, :n], func=AF.Silu))
                            else:
                                K.op("act", [pt[pv]], [gvc_t], lambda h, hd=hd, pv=pv: h.activation(out=sgc[:, hd, :n], in_=ps[pv][:96, :n], func=AF.Silu))
                    if lat:
                        for tt in range(4):
                            K.op("pool", [stg_t], [hT_t[b5]], lambda h, tt=tt: h.tensor_copy(out=hT[:, tt, c0:c0 + 384], in_=gvs[:, tt, :]))
                        for hd in range(4):
                            K.op("pool", [stg_t], [hT_t[b5]], lambda h, hd=hd: h.tensor_copy(out=hT[:96, 4 + hd, c0:c0 + 512], in_=sgs[:, hd, :]))
                K.barrier()
            wo = K.sb(ph, "gl_wo", [96, 4, D], BF16)
            w2_t = Tok()
            K.dma("pool", wo[:], woutg_d[l], [], [w2_t], "glw2")
            oF = K.sb(ph, "gl_oF", [96, 4, NT], BF16)
            oF_t = [Tok() for i in range(5)]
            R = []
            for d in range(2):
                r_ = dict(
                    e1=K.sb(ph, "gl_e1_%d" % d, [128, 512], F32),
                    bpos=K.sb(ph, "gl_bp_%d" % d, [128, 512], F32), eb=K.sb(ph, "gl_eb_%d" % d, [128, 512], F32),
                    qin=K.sb(ph, "gl_qin_%d" % d, [128, 2, 512], BF16), kin=K.sb(ph, "gl_kin_%d" % d, [128, 2, 512], BF16),
                    ktok=K.sb(ph, "gl_ktok_%d" % d, [128, 4, 2, 128], BF16),
                    dec=K.sb(ph, "gl_dec_%d" % d, [128, 2, 4], F32), AM=K.sb(ph, "gl_AM_%d" % d, [128, 4, 128], BF16),
                    S=K.sb(ph, "gl_S_%d" % d, [128, 2, 192], F32), Sb=K.sb(ph, "gl_Sb_%d" % d, [128, 2, 192], BF16),
                    osum=K.sb(ph, "gl_osum_%d" % d, [96, 4, NB], F32),
                    sg_t=Tok(), n_t=Tok(),
                    g_t=Tok(), qk_t=Tok(), ktok_t=Tok(), gv_t=Tok(), dec_t=Tok(), AM_t=Tok(), S_t=Tok(), Sb_t=Tok(), os_t=Tok(),
                    BA=3 * d, BB=3 * d + 1, BO=3 * d + 2)
                R.append(r_)

            def block_prep(d, b5):
                r_ = R[d]
                c0, n = blk_cols(b5)
                nt = n // 128
                BB = r_["BB"]
                e1, bpos, eb = r_["e1"], r_["bpos"], r_["eb"]
                lsb = e1
                enb = eb
                gt = r_["g_t"]
                for hp in range(2):
                    K.mm([st_t[b5], w_t], [pt[BB]], ps[BB][:, :n], [(aw[:, d, hp * 128:(hp + 1) * 128], aT[:, c0:c0 + n])])
                    yield
                    K.op("act", [pt[BB], w_t], [gt], lambda h: h.activation(
                        out=e1[:, :n], in_=ps[BB][:, :n], func=AF.Exp, bias=nab[:, d, hp:hp + 1], scale=-1.0))
                    yield
                    K.op("act", [gt, w_t], [gt], lambda h: h.activation(out=lsb[:, :n], in_=e1[:, :n], func=AF.Ln, bias=oneb[:], scale=1.0))
                    yield
                    K.op("dve", [gt, w_t], [gt], lambda h: h.tensor_tensor_scan(
                        out=bpos[:, :n], data0=mscan[:, :n], data1=lsb[:, :n], initial=0.0, op0=ALU.mult, op1=ALU.add))
                    yield
                    bsel = bpos
                    if d == 1:
                        K.op("dve", [gt], [gt], lambda h: h.tensor_tensor(out=e1[:, :n], in0=lsb[:, :n], in1=bpos[:, :n], op=ALU.subtract))
                        yield
                        K.op("dve", [gt], [gt], lambda h: h.tensor_tensor(
                            out=e1[:, :n].rearrange("p (a b) -> p a b", b=128), in0=e1[:, :n].rearrange("p (a b) -> p a b", b=128),
                            in1=bpos[:, :n].rearrange("p (a b) -> p a b", b=128)[:, :, 127:128].to_broadcast([128, n // 128, 128]), op=ALU.add))
                        yield
                        bsel = e1
                    K.op("act", [gt], [gt], lambda h, bsel=bsel: h.activation(out=eb[:, :n], in_=bsel[:, :n], func=AF.Exp, scale=-1.0 / 16))
                    yield
                    sel = 127 if d == 0 else 0
                    K.op("dve", [gt], [r_["dec_t"]], lambda h: h.tensor_copy(
                        out=r_["dec"][:, hp, 0:nt], in_=eb[:, :n].rearrange("p (a b) -> p a b", b=128)[:, :, sel]))
                    yield
                    K.op("dve", [gt, st_t[b5]], [r_["qk_t"]], lambda h: h.scalar_tensor_tensor(
                        out=r_["qin"][:, hp, :n], in0=qr[:, hp, c0:c0 + n], scalar=QS, in1=eb[:, :n], op0=ALU.mult, op1=ALU.mult))
                    yield
                    K.op("act", [gt, r_["dec_t"], r_["qk_t"]], [gt], lambda h, bsel=bsel: h.activation(out=enb[:, :n], in_=bsel[:, :n], func=AF.Exp, scale=1.0 / 16))
                    yield
                    K.op("dve", [gt, st_t[b5]], [r_["qk_t"]], lambda h: h.tensor_tensor(
                        out=r_["kin"][:, hp, :n], in0=kr[:, hp, c0:c0 + n], in1=enb[:, :n], op=ALU.mult))
                    yield
                    for tt in range(nt):
                        K.op("pe", [r_["qk_t"], t_const], [ptb], lambda h, tt=tt: h.transpose(
                            psb[:, d * 512 + tt * 128:d * 512 + (tt + 1) * 128], r_["kin"][:, hp, tt * 128:(tt + 1) * 128], identb[:]))
                        yield
                    K.op("act", [ptb], [r_["ktok_t"]], lambda h: h.activation(
                        out=r_["ktok"][:, 0:nt, hp, :], in_=psb[:, d * 512:d * 512 + nt * 128].rearrange("p (t f) -> p t f", t=nt), func=AF.Copy))
                    yield

            def scan_tile(d, b5, tt, want_o, second):
                r_ = R[d]
                c0, n = blk_cols(b5)
                tc0 = tt * 128
                BA, BB, BO = r_["BA"], r_["BB"], r_["BO"]
                qin, kin, Sb, S, AM, dec, ktok = r_["qin"], r_["kin"], r_["Sb"], r_["S"], r_["AM"], r_["dec"], r_["ktok"]
                if b5 < 4:
                    gv_tok = hT_t[b5]

                    def gvf(col, w):
                        return hT[:, tt, c0 + col:c0 + col + w]
                else:
                    gv_tok = gvc_t

                    def gvf(col, w):
                        return gvc[:, tt, col:col + w]
                if want_o:
                    for hd in (0, 2, 1, 3):
                        hp, r0 = hd // 2, (hd % 2) * 64
                        pa = BA if hd % 2 == 0 else BB
                        K.mm([r_["qk_t"]], [pt[pa]], ps[pa][:, (hd // 2) * 128:(hd // 2 + 1) * 128],
                             [(kin[r0:r0 + 48, hp, tc0:tc0 + 128], qin[r0:r0 + 48, hp, tc0:tc0 + 128])])
                        yield
                    for par, pa in ((0, BA), (1, BB)):
                        K.op("dve", [pt[pa], w_t], [r_["AM_t"]], lambda h, par=par, pa=pa: h.tensor_tensor(
                            out=AM[:, par:4:2, :], in0=ps[pa][:, 0:256].rearrange("p (a b) -> p a b", b=128),
                            in1=msk[:, d, :, :], op=ALU.mult))
                        yield
                    for hd in range(4):
                        hp, r0 = hd // 2, (hd % 2) * 64
                        K.mm([r_["AM_t"], gv_tok, r_["Sb_t"], r_["qk_t"]], [pt[BO]], ps[BO][:96, hd * 128:(hd + 1) * 128],
                             [(gvf(hd * 96, 96), AM[:, hd, :]),
                              (Sb[r0:r0 + 48, hp, (hd % 2) * 96:(hd % 2 + 1) * 96], qin[r0:r0 + 48, hp, tc0:tc0 + 128])])
                        yield
                    ov = ps[BO][:96, :].rearrange("p (a b) -> p a b", b=128)
                    if not second:
                        K.op("act", [pt[BO]], [oF_t[b5]], lambda h: h.activation(
                            out=oF[:, :, c0 + tc0:c0 + tc0 + 128], in_=ov, func=AF.Copy))
                    else:
                        K.op("dve", [pt[BO], oF_t[b5]], [r_["os_t"]], lambda h: h.tensor_tensor(
                            out=r_["osum"][:, :, (tt % 2) * 128:(tt % 2 + 1) * 128], in0=ov, in1=oF[:, :, c0 + tc0:c0 + tc0 + 128], op=ALU.add))
                    yield
                for hp in range(2):
                    K.mm([r_["ktok_t"], gv_tok], [pt[BA]], ps[BA][:, hp * 192:(hp + 1) * 192],
                         [(ktok[:, tt, hp, :], gvf(hp * 192, 192))])
                    yield
                for hp in range(2):
                    K.op("act", [r_["S_t"], r_["dec_t"]], [r_["S_t"]], lambda h, hp=hp: h.activation(
                        out=S[:, hp, :], in_=S[:, hp, :], func=AF.Identity, scale=dec[:, hp, tt:tt + 1]))
                    yield
                    K.op("dve", [pt[BA], r_["S_t"], r_["dec_t"]], [r_["S_t"]], lambda h, hp=hp: h.scalar_tensor_tensor(
                        out=S[:, hp, :], in0=ps[BA][:, hp * 192:(hp + 1) * 192], scalar=dec[:, hp, tt:tt + 1], in1=S[:, hp, :],
                        op0=ALU.mult, op1=ALU.add))
                    yield
                K.op("act", [r_["S_t"]], [r_["Sb_t"]], lambda h: h.activation(out=Sb[:], in_=S[:], func=AF.Copy))
                yield

            def finalize(d, b5, half):
                r_ = R[d]
                c0 = blk_cols(b5)[0] + half * 256
                n = 256
                r = 0 if b5 < 4 else 1
                osum = r_["osum"]
                BA, BB, BO = r_["BA"], r_["BB"], r_["BO"]
                if b5 < 4:
                    sg_t = hT_t[b5]

                    def sgf(hd):
                        return hT[:96, 4 + hd, c0:c0 + n]
                else:
                    sg_t = gvc_t

                    def sgf(hd):
                        return sgc[:, hd, :n]
                sq = r_["AM"][:96, 0:2, :].rearrange("p a b -> p (a b)")
                am_t = r_["AM_t"]
                rt = r_["e1"]
                n_t, gt = r_["n_t"], r_["g_t"]
                for hd in range(4):
                    K.op("act", [r_["os_t"]], [n_t, am_t], lambda h, hd=hd: h.activation(out=sq[:, :n], in_=osum[:, hd, :n], func=AF.Square))
                    yield
                    K.mm([n_t, am_t, t_const], [pt[BB]], ps[BB][:96, :n], [(onesb[:96, :96], sq[:, :n])])
                    yield
                    K.op("act", [pt[BB], w_t], [n_t, gt], lambda h: h.activation(
                        out=rt[:96, :n], in_=ps[BB][:96, :n], func=AF.Ln, bias=epsb[:96, :], scale=1.0 / 96))
                    yield
                    K.op("act", [n_t], [n_t, gt], lambda h: h.activation(out=rt[:96, :n], in_=rt[:96, :n], func=AF.Exp, scale=-0.5))
                    yield
                    K.op("dve", [n_t, gt, w_t], [r_["os_t"]], lambda h, hd=hd: h.scalar_tensor_tensor(
                        out=osum[:, hd, :n], in0=osum[:, hd, :n], scalar=gon[:, 0:1], in1=rt[:96, :n], op0=ALU.mult, op1=ALU.mult))
                    yield
                    K.op("dve", [r_["os_t"], sg_t], [sg_t], lambda h, hd=hd: h.tensor_tensor(
                        out=sgf(hd), in0=osum[:, hd, :n], in1=sgf(hd), op=ALU.mult))
                    yield
                for dt in range(8):
                    pc = BA if dt % 2 == 0 else BO
                    K.mm([sg_t, w2_t], [pt[pc]], ps[pc][:, :n], [(wo[:, hd, dt * 128:(dt + 1) * 128], sgf(hd)) for hd in range(4)])
                    yield
                    K.op("dve", [pt[pc], t_mod, xT_t[b5]], [xT_t[b5]], lambda h, dt=dt, pc=pc: h.scalar_tensor_tensor(
                        out=xT[:, dt, c0:c0 + n], in0=ps[pc][:, :n], scalar=modT[:, 16 + dt, r:r + 1],
                        in1=xT[:, dt, c0:c0 + n], op0=ALU.mult, op1=ALU.add))
                    yield

            def dir_gen(d):
                r_ = R[d]
                K.op("dve", [], [r_["S_t"]], lambda h: h.memset(r_["S"][:], 0.0))
                K.op("dve", [], [r_["Sb_t"]], lambda h: h.memset(r_["Sb"][:], 0.0))
                order = [4] + ([0, 1, 2, 3] if d == 0 else [3, 2, 1, 0])
                for b5 in order:
                    want_o = (b5 < 4) or need_ctx
                    if b5 == 4:
                        second = (d == 1)
                    else:
                        second = (b5 >= 2) if d == 0 else (b5 <= 1)
                    nt = blk_cols(b5)[1] // 128
                    yield from block_prep(d, b5)
                    tiles = list(range(nt)) if d == 0 else list(range(nt - 1, -1, -1))
                    for i_, tt in enumerate(tiles):
                        yield from scan_tile(d, b5, tt, want_o, second)
                        if want_o and second and i_ % 2 == 1:
                            yield from finalize(d, b5, tt // 2)
                    yield "STEP"

            def run_step(gens):
                active = list(gens)
                while active:
                    for g in list(active):
                        if next(g) == "STEP":
                            active.remove(g)
            gF, gB = dir_gen(0), dir_gen(1)
            run_step([gF])
            run_step([gB])
            for s_ in range(4):
                run_step([gF, gB])
            K.barrier()

    def mixers(l, hT, hT_t, need_ctx):
        nm = lambda: norm_mod(l, "mix", hT, hT_t, range(5))
        if "nofn" not in debug:
            fnet(l, hT, hT_t, need_ctx, pre=nm)
        else:
            nm()
        if "nona" not in debug:
            natt(l, hT, hT_t, need_ctx)
        if "nogla" not in debug:
            gla(l, hT, hT_t, need_ctx)

    for l in range(2):
        need_ctx = (l == 0)
        modT.t, amix.t, affn.t = modTs[l], amixs[l], affns[l]
        with contextlib.ExitStack() as lay:
            hT = K.sb(lay, "hT", [128, 8, NT], BF16)
            hT_t = [Tok("hT%d" % i) for i in range(5)]
            if "ffn" not in debug:
                mixers(l, hT, hT_t, need_ctx)
            blocks = range(5) if need_ctx else range(4)
            if "noffn" not in debug:
                ffn(l, hT, hT_t, blocks)
            K.barrier()
        if "ffn" in debug or "l0" in debug:
            break

    t_out = Tok("out")
    for name, (buf, shape, dt) in dbg.items():
        dd = K.dram("dbg_" + name, shape, dt, kind="ExternalOutput")
        K.barrier()
        K.dma("sp", dd, buf[:], [], [t_out], "out")
        K.outs["dbg_" + name] = shape

    for b in range(4):
        c0, n = blk_cols(b)
        for hf in range(2):
            K.dma("sp", out_d[:, hf * 4:(hf + 1) * 4, c0:c0 + n], xT[:, hf * 4:(hf + 1) * 4, c0:c0 + n], [xT_t[b]], [t_out], "out")
    S = t_out.dsem["sp"]
    nc.sync.wait_ge(S.sem, S.count * 16)
    es.close()
    return K


_CONSTS = {}


def _consts():
    if _CONSTS:
        return _CONSTS
    bf = ml_dtypes.bfloat16
    t = np.arange(T, dtype=np.int64)
    ang = 2.0 * np.pi * ((t[:, None] * t[None, :]) % T).astype(np.float64) / T
    _CONSTS["c_ct"] = np.ascontiguousarray(np.cos(ang).reshape(16, 128, T).transpose(1, 0, 2)).astype(bf)
    _CONSTS["c_st"] = np.ascontiguousarray(np.sin(ang).reshape(16, 128, T).transpose(1, 0, 2)).astype(bf)
    t2 = np.arange(NCTX, dtype=np.int64)
    ang2 = 2.0 * np.pi * ((t2[:, None] * t2[None, :]) % NCTX).astype(np.float64) / NCTX
    _CONSTS["c_c256"] = np.ascontiguousarray(np.cos(ang2).reshape(2, 128, NCTX).transpose(1, 0, 2)).astype(bf)
    _CONSTS["c_s256"] = np.ascontiguousarray(np.sin(ang2).reshape(2, 128, NCTX).transpose(1, 0, 2)).astype(bf)
    g = np.arange(64)
    a64 = 2.0 * np.pi * ((g[:, None] * g[None, :]) % 64) / 64.0
    c64 = np.zeros((128, 4, 128))
    for blk in range(2):
        sl = slice(64 * blk, 64 * blk + 64)
        c64[sl, 0, sl] = np.cos(a64) / np.sqrt(T * 64.0)
        c64[sl, 1, sl] = -np.sin(a64) / np.sqrt(T * 64.0)
        c64[sl, 2, sl] = np.cos(a64) / np.sqrt(NCTX * 64.0)
        c64[sl, 3, sl] = -np.sin(a64) / np.sqrt(NCTX * 64.0)
    _CONSTS["c_c64"] = c64.astype(bf)
    cosT = np.zeros((128, T), np.float32)
    sinT = np.zeros((128, T), np.float32)
    row = (t // 64).astype(np.float64)
    col = (t % 64).astype(np.float64)
    for p in range(128):
        d = p % 64
        if d >= 48:
            continue
        within = d % 24
        j = within % 12
        inv = 10000.0 ** (-j / 12.0)
        pos = row if d < 24 else col
        cosT[p] = np.cos(pos * inv)
        sinT[p] = np.sin(pos * inv) * (-1.0 if within < 12 else 1.0)
    _CONSTS["c_cos"] = cosT
    _CONSTS["c_sin"] = sinT
    s_ = np.arange(128)
    gm = np.zeros((128, 2, 2, 128), np.float32)
    gm[:, 0, :, :] = (s_[:, None] <= s_[None, :])[:, None, :]
    gm[:, 1, :, :] = (s_[:, None] >= s_[None, :])[:, None, :]
    _CONSTS["c_gmask"] = gm.astype(bf)
    ms = np.ones((128, 512), np.float32)
    ms[:, 0::128] = 0.0
    _CONSTS["c_mscan"] = ms.astype(bf)
    return _CONSTS


def _host_prep(inputs):
    f = lambda a: np.ascontiguousarray(np.asarray(a, dtype=np.float32))
    x, c, ctx, c_ctx = f(inputs["x"]), f(inputs["c"]), f(inputs["ctx"]), f(inputs["c_ctx"])
    shared = {}
    shared["ada_w"] = f(f(inputs["ada_w"]).reshape(2, 8, 128, 6144).transpose(0, 2, 1, 3))
    shared["ada_b"] = f(f(inputs["ada_b"]).reshape(2, 48, 128).transpose(0, 2, 1))
    shared["norm_mix"] = f(f(inputs["norm_mix"]).reshape(2, 8, 128).transpose(0, 2, 1))
    shared["norm_ffn"] = f(f(inputs["norm_ffn"]).reshape(2, 8, 128).transpose(0, 2, 1))
    shared["ident"] = np.eye(128, dtype=np.float32)
    shared["ffn_w1"] = f(f(inputs["ffn_w1"]).reshape(2, 8, 128, DFF).transpose(0, 2, 1, 3))
    shared["ffn_w3"] = f(f(inputs["ffn_w3"]).reshape(2, 8, 128, DFF).transpose(0, 2, 1, 3))
    shared["ffn_w2"] = f(f(inputs["ffn_w2"]).reshape(2, 22, 128, D).transpose(0, 2, 1, 3))
    w_in = f(inputs["w_in"])
    shared["w_in"] = f(w_in.reshape(2, 8, 128, NIN).transpose(0, 2, 1, 3))
    wg = np.zeros((2, D, 1024), np.float32)
    dd = np.arange(48)
    partner = np.where((dd % 24) < 12, dd + 12, dd - 12)
    for hh in range(4):
        wg[:, :, 64 * hh:64 * hh + 48] = w_in[:, :, 1408 + 48 * hh + dd]
        wg[:, :, 256 + 64 * hh:256 + 64 * hh + 48] = w_in[:, :, 1408 + 48 * hh + partner]
        wg[:, :, 512 + 64 * hh:512 + 64 * hh + 48] = w_in[:, :, 1600 + 48 * hh + dd]
        wg[:, :, 768 + 64 * hh:768 + 64 * hh + 48] = w_in[:, :, 1600 + 48 * hh + partner]
    shared["w_g"] = f(wg.reshape(2, 8, 128, 1024).transpose(0, 2, 1, 3))
    shared["fnet_w"] = f(f(inputs["fnet_w"]).reshape(2, 2, 128, 256).transpose(0, 2, 1, 3))
    w_out = f(inputs["w_out"])
    shared["w_out"] = f(w_out.reshape(2, 8, 128, D).transpose(0, 2, 1, 3))
    shared["w_out_g"] = f(w_out[:, 640:, :].reshape(2, 4, 96, D).transpose(0, 2, 1, 3))
    shared["na_q"] = f(np.tile(f(inputs["na_q_norm"]), (1, 2))[:, :, None])
    shared["na_k"] = f(np.tile(f(inputs["na_k_norm"]), (1, 2))[:, :, None])
    rpb = f(inputs["na_rpb"])
    pk = np.arange(128)
    a_, kc_ = pk // 64, pk % 64
    b_, c_ = pk // 64, pk % 64
    dcm = np.clip(kc_[:, None] - c_[None, :], -15, 15) + 15
    cq0 = np.clip(c_ - 8, 0, 48)
    col_ok = (kc_[:, None] >= cq0[None, :]) & (kc_[:, None] < cq0[None, :] + 16)
    bt = np.full((2, 128, 72, 128), NEG, np.float32)
    ents = [(dl, 0) for dl in range(-3, 4)] + [(-2, 1), (-1, 0), (0, 0), (1, 0), (2, 1)]
    for e, (dl, msk_) in enumerate(ents):
        drm = 2 * dl + a_[:, None] - b_[None, :] + 7
        ok = col_ok & (drm >= 0) & (drm <= 14)
        if msk_ and dl == -2:
            ok = ok & ((2 * dl + a_[:, None]) >= (-4 + b_[None, :]))
        if msk_ and dl == 2:
            ok = ok & ((2 * dl + a_[:, None]) <= (3 + b_[None, :]))
        drc = np.clip(drm, 0, 14)
        for hh in range(6):
            vals = rpb[:, hh][:, drc, dcm]
            bt[:, :, hh * 12 + e, :] = np.where(ok[None], vals, NEG)
    shared["na_bt"] = bt
    aw = np.zeros((2, 32, 2, 256), np.float32)
    ab = np.zeros((2, 128, 2, 2), np.float32)
    gaw = f(inputs["gla_alpha_w"]); gab = f(inputs["gla_alpha_b"])
    for dr_ in range(2):
        for hh in range(4):
            aw[:, 16 * dr_:16 * dr_ + 16, dr_, 64 * hh:64 * hh + 48] = gaw[:, dr_, :, 48 * hh:48 * hh + 48]
            ab[:, 64 * (hh % 2):64 * (hh % 2) + 48, dr_, hh // 2] = gab[:, dr_, 48 * hh:48 * hh + 48]
    shared["gla_aw"] = aw
    shared["gla_ab"] = ab
    shared["gla_gon"] = f(f(inputs["gla_o_norm"])[:, :, None])
    shared.update(_consts())
    per_core = []
    for b in range(8):
        m = dict(shared)
        m["x"] = f(x[b].T.reshape(8, 128, T).transpose(1, 0, 2))
        m["ctx"] = f(ctx[b].T.reshape(8, 128, NCTX).transpose(1, 0, 2))
        ccv = np.stack([c[b], c_ctx], axis=-1)
        m["cc"] = f(ccv.reshape(8, 128, 2).transpose(1, 0, 2))
        per_core.append(m)
    return per_core


_DEBUG = tuple(x for x in os.environ.get("KDEBUG", "").split(",") if x)
LAST = {}


def kernel(**inputs):
    in_maps = _host_prep(inputs)
    K = build_program(debug=_DEBUG)
    ncores = int(os.environ.get("KCORES", "8"))
    res = run_bass_kernel_spmd(K.nc, in_maps[:ncores], core_ids=list(range(ncores)))
    LAST["res"] = res
    outs = []
    for r in res.results:
        o = np.asarray(r["out"])
        outs.append(o.transpose(1, 0, 2).reshape(D, T).T)
    return np.ascontiguousarray(np.stack(outs, axis=0)).astype(np.float32)
```

```python
import contextlib
import os
import numpy as np
import ml_dtypes
import concourse.bass as bass
import concourse.mybir as mybir
from concourse.bass_utils import run_bass_kernel_spmd

F32 = mybir.dt.float32
BF16 = mybir.dt.bfloat16
AF = mybir.ActivationFunctionType
ALU = mybir.AluOpType

D = 1024
T = 2048
NCTX = 256
NT = T + NCTX
DFF = 2816
NIN = 2592
EPS = 1e-6
NEG = -30000.0
SAME_ENGINE_RAW = True


class Tok:
    __slots__ = ("w", "r", "dsem", "name")

    def __init__(self, name=""):
        self.w = []
        self.r = {}
        self.dsem = None
        self.name = name


class Src:
    def __init__(self, name, sem, unit, h=None):
        self.name = name
        self.sem = sem
        self.unit = unit
        self.h = h
        self.count = 0
        self.seen = {}


class Ctx:
    def __init__(self):
        self.nc = bass.Bass("TRN2", target_bir_lowering=False)
        nc = self.nc
        self.es = contextlib.ExitStack()
        self.engs = {}
        for nm, h in (("pe", nc.tensor), ("act", nc.scalar), ("dve", nc.vector),
                      ("pool", nc.gpsimd), ("sp", nc.sync)):
            sem = self.es.enter_context(nc.semaphore("sem_" + nm))
            self.engs[nm] = Src(nm, sem, 1, h)
        self.dsems = []
        self.outs = {}
        self.n_inst = 0

    def dram(self, name, shape, dt, kind="ExternalInput"):
        return self.nc.dram_tensor(name, list(shape), dt, kind=kind).ap()

    def sb(self, stack, name, shape, dt):
        self.n_sb = getattr(self, "n_sb", 0) + 1
        return stack.enter_context(self.nc.sbuf_tensor("sb%d_%s" % (self.n_sb, name), list(shape), dt))

    def new_dsem(self, name):
        sem = self.es.enter_context(self.nc.semaphore("d_" + name + str(len(self.dsems))))
        s = Src("dma_" + name, sem, 16)
        self.dsems.append(s)
        return s

    def _wait_deps(self, E, reads, writes):
        deps = {}
        for t in reads:
            for (s, n) in t.w:
                deps[s] = max(deps.get(s, 0), n)
        for t in writes:
            for (s, n) in t.w:
                deps[s] = max(deps.get(s, 0), n)
            for s, n in t.r.items():
                deps[s] = max(deps.get(s, 0), n)
        for s, n in deps.items():
            if s is E and (E.name == "pe" or not SAME_ENGINE_RAW):
                continue
            if E.seen.get(s, 0) < n:
                E.h.wait_ge(s.sem, n * s.unit)
                E.seen[s] = n

    def op(self, eng, reads, writes, emit):
        E = self.engs[eng]
        self._wait_deps(E, reads, writes)
        ins = emit(E.h)
        E.count += 1
        ins.then_inc(E.sem, 1)
        self.n_inst += 1
        for t in reads:
            t.r[E] = E.count
        for t in writes:
            t.w = [(E, E.count)]
            t.r = {}
        return ins

    def mm(self, reads, writes, out, pairs, first=True, last=True):
        E = self.engs["pe"]
        self._wait_deps(E, reads, writes)
        n = len(pairs)
        ins = None
        for i, (l, r) in enumerate(pairs):
            ins = E.h.matmul(out, l, r, start=(first and i == 0), stop=(last and i == n - 1))
            self.n_inst += 1
        E.count += 1
        ins.then_inc(E.sem, 1)
        for t in reads:
            t.r[E] = E.count
        for t in writes:
            t.w = [(E, E.count)]
            t.r = {}

    def transpose(self, reads, writes, out, in_, ident):
        return self.op("pe", reads, writes, lambda h: h.transpose(out, in_, ident))

    def dma(self, q, out, in_, reads, writes, name="x"):
        E = self.engs[q]
        self._wait_deps(E, reads, writes)
        wt = writes[0]
        if wt.dsem is None:
            wt.dsem = {}
        if q not in wt.dsem:
            wt.dsem[q] = self.new_dsem(name + q)
        S = wt.dsem[q]
        ins = E.h.dma_start(out=out, in_=in_)
        S.count += 1
        ins.then_inc(S.sem, 16)
        self.n_inst += 1
        for t in reads:
            t.r[S] = S.count
        for t in writes:
            t.w = [(s_, n_) for (s_, n_) in t.w if (s_.unit == 16 and s_ is not S)] + [(S, S.count)]
            t.r = {}

    def barrier(self):
        allsrc = list(self.engs.values()) + self.dsems
        for E in self.engs.values():
            for s in allsrc:
                if s is E or s.count == 0:
                    continue
                if E.seen.get(s, 0) < s.count:
                    E.h.wait_ge(s.sem, s.count * s.unit)
                    E.seen[s] = s.count


def build_program(debug=()):
    K = Ctx()
    nc = K.nc
    es = K.es
    dbg = {}

    x_d = K.dram("x", [128, 8, T], F32)
    ctx_d = K.dram("ctx", [128, 8, NCTX], F32)
    cc_d = K.dram("cc", [128, 8, 2], F32)
    adaw_d = K.dram("ada_w", [2, 128, 8, 6144], F32)
    adab_d = K.dram("ada_b", [2, 128, 48], F32)
    nmix_d = K.dram("norm_mix", [2, 128, 8], F32)
    nffn_d = K.dram("norm_ffn", [2, 128, 8], F32)
    out_d = K.dram("out", [128, 8, T], F32, kind="ExternalOutput")
    ident_d = K.dram("ident", [128, 128], F32)
    w1_d = K.dram("ffn_w1", [2, 128, 8, DFF], F32)
    win_d = K.dram("w_in", [2, 128, 8, NIN], F32)
    wg_d = K.dram("w_g", [2, 128, 8, 1024], F32)
    fnetw_d = K.dram("fnet_w", [2, 128, 2, 256], F32)
    wout_d = K.dram("w_out", [2, 128, 8, D], F32)
    woutg_d = K.dram("w_out_g", [2, 96, 4, D], F32)
    naq_d = K.dram("na_q", [2, 128, 1], F32)
    nak_d = K.dram("na_k", [2, 128, 1], F32)
    bt_d = K.dram("na_bt", [2, 128, 72, 128], F32)
    aw_d = K.dram("gla_aw", [2, 32, 2, 256], F32)
    ab_d = K.dram("gla_ab", [2, 128, 2, 2], F32)
    gon_d = K.dram("gla_gon", [2, 96, 1], F32)
    ct_d = K.dram("c_ct", [128, 16, T], BF16)
    st_d = K.dram("c_st", [128, 16, T], BF16)
    c256_d = K.dram("c_c256", [128, 2, 256], BF16)
    s256_d = K.dram("c_s256", [128, 2, 256], BF16)
    c64_d = K.dram("c_c64", [128, 4, 128], BF16)
    cos_d = K.dram("c_cos", [128, T], F32)
    sin_d = K.dram("c_sin", [128, T], F32)
    gmask_d = K.dram("c_gmask", [128, 2, 2, 128], BF16)
    mscan_d = K.dram("c_mscan", [128, 512], BF16)
    w3_d = K.dram("ffn_w3", [2, 128, 8, DFF], F32)
    w2_d = K.dram("ffn_w2", [2, 128, 22, D], F32)

    top = es
    xT = K.sb(top, "xT", [128, 8, NT], F32)
    xT_t = [Tok("xT%d" % i) for i in range(5)]
    identf = K.sb(top, "identf", [128, 128], F32)
    identb = K.sb(top, "identb", [128, 128], BF16)
    onesb = K.sb(top, "onesb", [128, 128], BF16)
    cc = K.sb(top, "cc", [128, 8, 2], F32)
    scb = K.sb(top, "scb", [128, 8, 2], BF16)
    class Cur:
        def __init__(self):
            self.t = None

        def __getitem__(self, k):
            return self.t[k]
    modT = Cur()
    modTs = [K.sb(top, "modT%d" % i, [128, 48, 2], F32) for i in range(2)]
    amix = Cur()
    affn = Cur()
    amixs = [K.sb(top, "amix%d" % i, [128, 8, 2], F32) for i in range(2)]
    affns = [K.sb(top, "affn%d" % i, [128, 8, 2], F32) for i in range(2)]
    adabs = [K.sb(top, "adab%d" % i, [128, 48], F32) for i in range(2)]
    nmixs = [K.sb(top, "nmix%d" % i, [128, 8], F32) for i in range(2)]
    nffns = [K.sb(top, "nffn%d" % i, [128, 8], F32) for i in range(2)]
    t_const = Tok("const")
    t_mod = Tok("mod")
    t_small = Tok("small")

    ps = [es.enter_context(nc.psum_tensor("ps%d" % i, [128, 512], F32)) for i in range(7)]
    pt = [Tok("ps%d" % i) for i in range(7)]
    psb = es.enter_context(nc.psum_tensor("psb", [128, 1024], BF16))
    ptb = Tok("psb")

    def blk_cols(b):
        return (b * 512, 512) if b < 4 else (T, NCTX)

    K.dma("sp", identf[:], ident_d, [], [t_const], "c")
    K.op("dve", [t_const], [t_const], lambda h: h.tensor_copy(out=identb[:], in_=identf[:]))
    K.op("dve", [], [t_const], lambda h: h.memset(onesb[:], 1.0))
    K.dma("sp", cc[:], cc_d, [], [t_small], "s")
    K.op("act", [t_small], [t_small], lambda h: h.activation(out=scb[:], in_=cc[:], func=AF.Silu))

    def make_ada(l, stack):
        t_sm = Tok("adasmall")
        K.dma("sp", adabs[l][:], adab_d[l], [], [t_sm], "s")
        K.dma("sp", nmixs[l][:], nmix_d[l], [], [t_sm], "s")
        K.dma("sp", nffns[l][:], nffn_d[l], [], [t_sm], "s")
        wsl = [K.sb(stack, "adaw%d" % i, [128, 8, 512], BF16) for i in range(2)]
        wsl_t = [Tok("adaw%d" % i) for i in range(2)]

        def dma(hs):
            K.dma("pool", wsl[hs % 2][:], adaw_d[l][:, :, hs * 512:(hs + 1) * 512], [], [wsl_t[hs % 2]], "adaw")

        def sec(hs):
            s = hs % 2
            if hs == 0:
                dma(0)
            if hs + 1 < 12:
                dma(hs + 1)
            for mt in range(4):
                K.mm([wsl_t[s], t_small], [pt[6]], ps[6][:, mt * 2:mt * 2 + 2],
                     [(wsl[s][:, kc, mt * 128:(mt + 1) * 128], scb[:, kc, :]) for kc in range(8)])
            K.op("dve", [pt[6], t_sm], [t_mod], lambda h: h.tensor_tensor(
                out=modTs[l][:, hs * 4:(hs + 1) * 4, :],
                in0=ps[6][:, 0:8].rearrange("p (a b) -> p a b", b=2),
                in1=adabs[l][:, hs * 4:(hs + 1) * 4].unsqueeze(2).to_broadcast([128, 4, 2]),
                op=ALU.add))

        def tail():
            for (dst, gain, sc_) in ((amixs[l], nmixs[l], 1), (affns[l], nffns[l], 4)):
                K.op("dve", [t_mod, t_sm], [t_mod], lambda h, dst=dst, gain=gain, sc_=sc_: h.scalar_tensor_tensor(
                    out=dst[:], in0=modTs[l][:, sc_ * 8:(sc_ + 1) * 8, :], scalar=1.0,
                    in1=gain[:].unsqueeze(2).to_broadcast([128, 8, 2]),
                    op0=ALU.add, op1=ALU.mult))
        return [lambda hs=hs: sec(hs) for hs in range(12)] + [tail]

    with contextlib.ExitStack() as ph:
        ada0 = make_ada(0, ph)
        for b in range(5):
            c0, n = blk_cols(b)
            src = x_d[:, :, c0:c0 + n] if b < 4 else ctx_d[:, :, :]
            for hf in range(2):
                K.dma("sp", xT[:, hf * 4:(hf + 1) * 4, c0:c0 + n], src[:, hf * 4:(hf + 1) * 4, :], [], [xT_t[b]], "xin")
        for hs in range(12):
            ada0[hs]()
        ada0[12]()
        K.barrier()

    rot = {"ps": 0}

    def norm_mod(l, which, hT, hT_t, blocks):
        a = amix if which == "mix" else affn
        sec = 0 if which == "mix" else 3
        blocks = list(blocks)
        with contextlib.ExitStack() as ph:
            sq1 = K.sb(ph, "nm_sq", [128, 8, 512], BF16)
            sq = [sq1, sq1]
            sq1_t = Tok()
            sq_t = [sq1_t, sq1_t]
            rs = [K.sb(ph, "nm_rs%d" % i, [128, 512], F32) for i in range(2)]
            rs_t = [Tok() for i in range(2)]
            tmp = [K.sb(ph, "nm_tmp%d" % i, [128, 512], F32) for i in range(3)]
            tmp_t = [Tok() for i in range(3)]
            epsb = K.sb(ph, "nm_eps", [128, 1], F32)
            eps_t = Tok()
            K.op("dve", [], [eps_t], lambda h: h.memset(epsb[:], EPS))

            def stats(bi):
                b = blocks[bi]
                c0, n = blk_cols(b)
                s = bi % 2
                pb = bi % 2
                K.op("dve", [xT_t[b]], [sq_t[s]], lambda h: h.tensor_tensor(
                    out=sq[s][:, :, :n], in0=xT[:, :, c0:c0 + n], in1=xT[:, :, c0:c0 + n], op=ALU.mult))
                K.mm([sq_t[s], t_const], [pt[pb]], ps[pb][:, :n],
                     [(onesb[:], sq[s][:, kc, :n]) for kc in range(8)])
                K.op("act", [pt[pb], eps_t], [rs_t[s]], lambda h: h.activation(
                    out=rs[s][:, :n], in_=ps[pb][:, :n], func=AF.Ln, bias=epsb[:], scale=1.0 / D))
                K.op("act", [rs_t[s]], [rs_t[s]], lambda h: h.activation(
                    out=rs[s][:, :n], in_=rs[s][:, :n], func=AF.Exp, scale=-0.5))

            ti = 0
            stats(0)
            for bi, b in enumerate(blocks):
                if bi + 1 < len(blocks):
                    stats(bi + 1)
                c0, n = blk_cols(b)
                r = 0 if b < 4 else 1
                s = bi % 2
                for kc in range(8):
                    u = ti % 3
                    ti += 1
                    eng = "pool" if kc % 4 == 3 else "dve"
                    K.op(eng, [xT_t[b], rs_t[s]], [tmp_t[u]], lambda h, u=u, kc=kc: h.tensor_tensor(
                        out=tmp[u][:, :n], in0=xT[:, kc, c0:c0 + n], in1=rs[s][:, :n], op=ALU.mult))
                    K.op("act", [tmp_t[u], t_mod], [hT_t[b]], lambda h, u=u, kc=kc: h.activation(
                        out=hT[:, kc, c0:c0 + n], in_=tmp[u][:, :n], func=AF.Identity,
                        bias=modT[:, sec * 8 + kc, r:r + 1], scale=a[:, kc, r:r + 1]))
            K.barrier()

    def ffn(l, hT, hT_t, blocks):
        with contextlib.ExitStack() as ph:
            chunks = [(i * 512, 512) for i in range(5)] + [(2560, 256)]
            w1s = [K.sb(ph, "w1s%d" % i, [128, 8, 512], BF16) for i in range(2)]
            w3s = [K.sb(ph, "w3s%d" % i, [128, 8, 512], BF16) for i in range(2)]
            w2s = [K.sb(ph, "w2s%d" % i, [128, 4, D], BF16) for i in range(2)]
            w_t = [Tok() for i in range(2)]
            s1 = [K.sb(ph, "ff_s1%d" % i, [128, 512], F32) for i in range(2)]
            s1_t = [Tok() for i in range(2)]
            g = [K.sb(ph, "ff_g%d" % i, [128, 4, 512], BF16) for i in range(2)]
            g_t = [Tok() for i in range(2)]

            def load(ch):
                f0, nf = chunks[ch]
                s = ch % 2
                K.dma("pool", w1s[s][:, :, :nf], w1_d[l][:, :, f0:f0 + nf], [], [w_t[s]], "ffw")
                K.dma("pool", w3s[s][:, :, :nf], w3_d[l][:, :, f0:f0 + nf], [], [w_t[s]], "ffw")
                K.dma("pool", w2s[s][:, :nf // 128, :], w2_d[l][:, f0 // 128:(f0 + nf) // 128, :], [], [w_t[s]], "ffw")
            load(0)
            ada_next = make_ada(l + 1, ph) if l + 1 < 2 else None
            norm_mod(l, "ffn", hT, hT_t, blocks)
            cnt = 0
            gi = 0
            oi = 0
            for ch in range(6):
                if ch + 1 < 6:
                    load(ch + 1)
                f0, nf = chunks[ch]
                s = ch % 2
                nft = nf // 128
                for b in blocks:
                    c0, n = blk_cols(b)
                    r = 0 if b < 4 else 1
                    gs = gi % 2
                    gi += 1
                    for ft in range(nft):
                        pa = cnt % 2
                        pbk = 2 + cnt % 2
                        u = cnt % 2
                        cnt += 1
                        K.mm([w_t[s], hT_t[b]], [pt[pa]], ps[pa][:, :n],
                             [(w1s[s][:, kc, ft * 128:(ft + 1) * 128], hT[:, kc, c0:c0 + n]) for kc in range(8)])
                        K.mm([w_t[s], hT_t[b]], [pt[pbk]], ps[pbk][:, :n],
                             [(w3s[s][:, kc, ft * 128:(ft + 1) * 128], hT[:, kc, c0:c0 + n]) for kc in range(8)])
                        K.op("act", [pt[pa]], [s1_t[u]], lambda h, u=u, pa=pa, n=n: h.activation(
                            out=s1[u][:, :n], in_=ps[pa][:, :n], func=AF.Silu))
                        K.op("dve", [s1_t[u], pt[pbk]], [g_t[gs]], lambda h, u=u, pbk=pbk, n=n, gs=gs, ft=ft: h.tensor_tensor(
                            out=g[gs][:, ft, :n], in0=s1[u][:, :n], in1=ps[pbk][:, :n], op=ALU.mult))
                    for dt in range(8):
                        pc = 4 + oi % 3
                        oi += 1
                        K.mm([w_t[s], g_t[gs]], [pt[pc]], ps[pc][:, :n],
                             [(w2s[s][:, ft, dt * 128:(dt + 1) * 128], g[gs][:, ft, :n]) for ft in range(nft)])
                        K.op("dve", [pt[pc], t_mod, xT_t[b]], [xT_t[b]], lambda h, pc=pc, dt=dt, c0=c0, n=n, r=r: h.scalar_tensor_tensor(
                            out=xT[:, dt, c0:c0 + n], in0=ps[pc][:, :n], scalar=modT[:, 40 + dt, r:r + 1],
                            in1=xT[:, dt, c0:c0 + n], op0=ALU.mult, op1=ALU.add))
                if ada_next is not None:
                    ada_next[2 * ch]()
                    ada_next[2 * ch + 1]()
            if ada_next is not None:
                ada_next[12]()
            K.barrier()

    def lin_fm(wt, w, col0, M, hT, hT_t, b, c0, n, pb, wtok):
        K.mm([wtok, hT_t[b]], [pt[pb]], ps[pb][:M, :n],
             [(w[:, kc, col0:col0 + M], hT[:, kc, c0:c0 + n]) for kc in range(8)])

    def out_proj_update(pairs_fn, reads, b, c0, n, r, pbs):
        for dt in range(8):
            pc = pbs[dt % len(pbs)]
            K.mm(reads, [pt[pc]], ps[pc][:, :n], pairs_fn(dt))
            K.op("dve", [pt[pc], t_mod, xT_t[b]], [xT_t[b]], lambda h, pc=pc, dt=dt: h.scalar_tensor_tensor(
                out=xT[:, dt, c0:c0 + n], in0=ps[pc][:, :n], scalar=modT[:, 16 + dt, r:r + 1],
                in1=xT[:, dt, c0:c0 + n], op0=ALU.mult, op1=ALU.add))

    def rstd_from_ps(pb, P, n, inv, rt, rs, rs_tok, eps_ap, eps_t):
        K.op("act", [pt[pb], eps_t], [rs_tok], lambda h: h.activation(
            out=rt[:P, :n], in_=ps[pb][:P, :n], func=AF.Sqrt, bias=eps_ap[:P, :], scale=inv))
        K.op("dve", [rs_tok], [rs_tok], lambda h: h.reciprocal(out=rs[:P, :n], in_=rt[:P, :n]))

    def fnet(l, hT, hT_t, need_ctx, pre=None):
        with contextlib.ExitStack() as ph:
            wfn = K.sb(ph, "fn_w", [128, 8, 256], BF16)
            fnw = K.sb(ph, "fn_fw", [128, 2, 256], BF16)
            wo = K.sb(ph, "fn_wo", [128, 2, D], BF16)
            c64 = K.sb(ph, "fn_c64", [128, 4, 128], BF16)
            w_t = Tok()
            K.dma("pool", wfn[:], win_d[l][:, :, 0:256], [], [w_t], "fnw")
            K.dma("pool", fnw[:], fnetw_d[l], [], [w_t], "fnw")
            K.dma("pool", wo[:], wout_d[l][:, 0:2, :], [], [w_t], "fnw")
            K.dma("sp", c64[:], c64_d, [], [w_t], "fnw")
            cts = [K.sb(ph, "fn_ct%d" % i, [128, 2, 16, 512], BF16) for i in range(2)]
            ct_t = [Tok() for i in range(2)]
            jobs = [(kb, 0) for kb in range(4)] + ([(0, 1)] if need_ctx else [])

            def issue_tables(ji):
                kb, isctx = jobs[ji]
                s = ji % 2
                if not isctx:
                    K.dma("sp", cts[s][:, 0, :, :], ct_d[:, :, kb * 512:(kb + 1) * 512], [], [ct_t[s]], "ct")
                    K.dma("sp", cts[s][:, 1, :, :], st_d[:, :, kb * 512:(kb + 1) * 512], [], [ct_t[s]], "ct")
                else:
                    K.dma("sp", cts[s][:, 0, 0:2, 0:256], c256_d, [], [ct_t[s]], "ct")
                    K.dma("sp", cts[s][:, 1, 0:2, 0:256], s256_d, [], [ct_t[s]], "ct")
            issue_tables(0)
            issue_tables(1)
            if pre is not None:
                pre()
            Z = K.sb(ph, "fn_Z", [128, 18, 256], BF16)
            Z_t = Tok()
            Psb = [K.sb(ph, "fn_P%d" % i, [128, 2, 512], BF16) for i in range(2)]
            P_t = [Tok() for i in range(2)]
            Ysb = K.sb(ph, "fn_Y", [128, 2, 512], BF16)
            Y_t = Tok()
            yf = K.sb(ph, "fn_yf", [128, 2, 512], BF16)
            yf_t = Tok()
            for ti in range(18):
                pb = ti % 2
                b = min(ti // 4, 4)
                K.mm([w_t, hT_t[b]], [pt[pb]], ps[pb][:, :256],
                     [(hT[:, kc, ti * 128:(ti + 1) * 128], wfn[:, kc, :]) for kc in range(8)])
                if ti % 2 == 0:
                    K.op("dve", [pt[pb]], [Z_t], lambda h, ti=ti, pb=pb: h.tensor_copy(out=Z[:, ti, :], in_=ps[pb][:, :256]))
                else:
                    K.op("act", [pt[pb]], [Z_t], lambda h, ti=ti, pb=pb: h.activation(out=Z[:, ti, :], in_=ps[pb][:, :256], func=AF.Copy))
            for ji, (kb, isctx) in enumerate(jobs):
                s = ji % 2
                if not isctx:
                    ntt, n, tt0, c0, b, r = 16, 512, 0, kb * 512, kb, 0
                else:
                    ntt, n, tt0, c0, b, r = 2, 256, 16, T, 4, 1
                for chc in range(2):
                    u = chc
                    for cs in range(2):
                        pb = cs
                        K.mm([Z_t, ct_t[s]], [pt[pb]], ps[pb][:, :n],
                             [(Z[:, tt0 + tt, chc * 128:(chc + 1) * 128], cts[s][:, cs, tt, :n]) for tt in range(ntt)])
                        if cs == 0:
                            K.op("dve", [pt[pb]], [P_t[u]], lambda h, u=u, pb=pb, cs=cs: h.tensor_copy(out=Psb[u][:, cs, :n], in_=ps[pb][:, :n]))
                        else:
                            K.op("act", [pt[pb]], [P_t[u]], lambda h, u=u, pb=pb, cs=cs: h.activation(out=Psb[u][:, cs, :n], in_=ps[pb][:, :n], func=AF.Copy))
                    pb = 2 + chc
                    K.mm([P_t[u], w_t], [pt[pb]], ps[pb][:, :n],
                         [(c64[:, 2 * isctx + 0, :], Psb[u][:, 0, :n]), (c64[:, 2 * isctx + 1, :], Psb[u][:, 1, :n])])
                    K.op("dve", [pt[pb]], [Y_t], lambda h, pb=pb, chc=chc: h.tensor_copy(out=Ysb[:, chc, :n], in_=ps[pb][:, :n]))
                for c2 in range(2):
                    pb = 4 + c2
                    K.mm([Y_t, w_t], [pt[pb]], ps[pb][:, :n],
                         [(fnw[:, c1, c2 * 128:(c2 + 1) * 128], Ysb[:, c1, :n]) for c1 in range(2)])
                    K.op("act", [pt[pb]], [yf_t], lambda h, pb=pb, c2=c2: h.activation(out=yf[:, c2, :n], in_=ps[pb][:, :n], func=AF.Copy))
                out_proj_update(lambda dt: [(wo[:, c2, dt * 128:(dt + 1) * 128], yf[:, c2, :n]) for c2 in range(2)],
                                [yf_t, w_t], b, c0, n, r, [0, 1, 2, 3, 4, 5, 6])
                if ji + 2 < len(jobs):
                    issue_tables(ji + 2)
            K.barrier()

    def natt(l, hT, hT_t, need_ctx, wkv=None):
        with contextlib.ExitStack() as ph:
            kT = K.sb(ph, "na_kT", [128, 3, NT], BF16)
            kT_t = Tok()
            V = K.sb(ph, "na_V", [128, 18, 6, 65], BF16)
            V_t = Tok()
            gq = K.sb(ph, "na_gq", [128, 1], F32)
            gk = K.sb(ph, "na_gk", [128, 1], F32)
            epsb = K.sb(ph, "na_eps", [128, 1], F32)
            bones = K.sb(ph, "na_bones", [128, 128], BF16)
            g_t = Tok()
            K.dma("sp", gq[:], naq_d[l], [], [g_t], "nag")
            K.dma("sp", gk[:], nak_d[l], [], [g_t], "nag")
            K.op("dve", [g_t], [g_t], lambda h: h.tensor_scalar(out=gq[:], in0=gq[:], scalar1=0.125, scalar2=None, op0=ALU.mult))
            K.op("dve", [], [g_t], lambda h: h.memset(epsb[:], EPS))
            K.op("dve", [], [g_t], lambda h: h.memset(bones[:], 0.0))
            K.op("dve", [g_t], [g_t], lambda h: h.memset(bones[0:64, 0:64], 1.0))
            K.op("dve", [g_t], [g_t], lambda h: h.memset(bones[64:128, 64:128], 1.0))
            K.op("dve", [], [V_t], lambda h: h.memset(V[:, :, :, 64:65], 1.0))
            sq = [K.sb(ph, "na_sq%d" % i, [128, 512], BF16) for i in range(2)]
            sq_t = [Tok() for i in range(2)]
            rtl = [K.sb(ph, "na_rt%d" % i, [128, 512], F32) for i in range(2)]
            rs = [K.sb(ph, "na_rs%d" % i, [128, 512], F32) for i in range(2)]
            rs_t = [Tok() for i in range(2)]

            def qk_norm(w, wtok, gain, dst, dst_t, dcol, b, c0, n, cnt):
                for hp in range(3):
                    u = (cnt[0]) % 2
                    cnt[0] += 1
                    pa, pbk = u, 2 + u
                    lin_fm(None, w, hp * 128, 128, hT, hT_t, b, c0, n, pa, wtok)
                    K.op("act", [pt[pa]], [sq_t[u]], lambda h, u=u, pa=pa: h.activation(out=sq[u][:, :n], in_=ps[pa][:, :n], func=AF.Square))
                    K.mm([sq_t[u], g_t], [pt[pbk]], ps[pbk][:, :n], [(bones[:], sq[u][:, :n])])
                    K.op("act", [pt[pbk], g_t], [rs_t[u]], lambda h, u=u, pbk=pbk: h.activation(
                        out=rtl[u][:, :n], in_=ps[pbk][:, :n], func=AF.Ln, bias=epsb[:], scale=1.0 / 64))
                    K.op("act", [rs_t[u]], [rs_t[u]], lambda h, u=u: h.activation(
                        out=rs[u][:, :n], in_=rtl[u][:, :n], func=AF.Exp, scale=-0.5))
                    K.op("dve", [pt[pa], rs_t[u], g_t], [dst_t], lambda h, u=u, pa=pa, hp=hp: h.scalar_tensor_tensor(
                        out=dst[:, hp, dcol:dcol + n], in0=ps[pa][:, :n], scalar=gain[:, 0:1], in1=rs[u][:, :n],
                        op0=ALU.mult, op1=ALU.mult))

            cnt = [0]
            BT = K.sb(ph, "na_BT", [128, 72, 128], BF16)
            bt_t = Tok()
            for i3 in range(3):
                K.dma("pool", BT[:, 24 * i3:24 * (i3 + 1), :], bt_d[l][:, 24 * i3:24 * (i3 + 1), :], [], [bt_t], "nabt")
            for i3 in range(3):
                K.op("act", [bt_t], [bt_t], lambda h, i3=i3: h.activation(
                    out=BT[:, 24 * i3:24 * (i3 + 1), :], in_=BT[:, 24 * i3:24 * (i3 + 1), :], func=AF.Exp))
            wq = K.sb(ph, "na_wq", [128, 8, 384], BF16)
            wo = K.sb(ph, "na_wo", [128, 3, D], BF16)
            w_t = Tok()
            with contextlib.ExitStack() as p1:
                if wkv is None:
                    wk = K.sb(p1, "na_wk", [128, 8, 384], BF16)
                    wv = K.sb(p1, "na_wv", [128, 8, 384], BF16)
                    wk_t = Tok()
                    K.dma("pool", wk[:], win_d[l][:, :, 640:1024], [], [wk_t], "naw")
                    K.dma("pool", wv[:], win_d[l][:, :, 1024:1408], [], [wk_t], "naw")
                else:
                    wk, wv, wk_t = wkv
                K.dma("pool", wq[:], win_d[l][:, :, 256:640], [], [w_t], "naw2")
                K.dma("pool", wo[:], wout_d[l][:, 2:5, :], [], [w_t], "naw2")
                for b in range(5):
                    c0, n = blk_cols(b)
                    qk_norm(wk, wk_t, gk, kT, kT_t, c0, b, c0, n, cnt)
                for ti in range(18):
                    pb = 4 + ti % 2
                    b = min(ti // 4, 4)
                    K.mm([wk_t, hT_t[b]], [pt[pb]], ps[pb][:, :384],
                         [(hT[:, kc, ti * 128:(ti + 1) * 128], wv[:, kc, :]) for kc in range(8)])
                    K.op("act" if ti % 2 else "dve", [pt[pb]], [V_t],
                         (lambda h, ti=ti, pb=pb: h.activation(out=V[:, ti, :, 0:64], in_=ps[pb][:, :384].rearrange("p (a b) -> p a b", b=64), func=AF.Copy))
                         if ti % 2 else
                         (lambda h, ti=ti, pb=pb: h.tensor_copy(out=V[:, ti, :, 0:64], in_=ps[pb][:, :384].rearrange("p (a b) -> p a b", b=64))))
                K.barrier()
            qT2 = [K.sb(ph, "na_qT%d" % i, [128, 3, 512], BF16) for i in range(2)]
            qT2_t = [Tok(), Tok()]
            PT = [K.sb(ph, "na_PT%d" % i, [128, 7, 128], BF16) for i in range(2)]
            PT_t = [Tok() for i in range(2)]
            rec = [K.sb(ph, "na_rec%d" % i, [128, 6], F32) for i in range(2)]
            rec_t = [Tok(), Tok()]
            Osb = [K.sb(ph, "na_O%d" % i, [128, 6, 64], BF16) for i in range(2)]
            O_t = [Tok(), Tok()]
            yna2 = [K.sb(ph, "na_y%d" % i, [128, 3, 512], BF16) for i in range(2)]
            yna2_t = [Tok(), Tok()]
            hcount = 0
            nblocks = list(range(5) if need_ctx else range(4))
            pending = [None]
            c0_, n_ = blk_cols(nblocks[0])
            qk_norm(wq, w_t, gq, qT2[0], qT2_t[0], 0, nblocks[0], c0_, n_, cnt)
            for bi, b in enumerate(nblocks):
                c0, n = blk_cols(b)
                r = 0 if b < 4 else 1
                qT, qT_t = qT2[bi % 2], qT2_t[bi % 2]
                yna, yna_t = yna2[bi % 2], yna2_t[bi % 2]
                tile_chunks = []
                for qi in range(n // 128):
                    if b < 4:
                        i = b * 4 + qi
                        if i <= 1:
                            js = list(range(0, 4))
                        elif i >= 14:
                            js = list(range(12, 16))
                        else:
                            js = list(range(i - 2, i + 3))
                        chunks = []
                        for j in js:
                            dl = j - i
                            v = (dl + 9) if 2 <= i <= 13 else (dl + 3)
                            chunks.append((j, v))
                        chunks += [(16, None), (17, None)]
                    else:
                        chunks = [(16, None), (17, None)]
                    tile_chunks.append(chunks)
                items = [(qi, hd) for qi in range(n // 128) for hd in range(6)]

                def stage1(k):
                    qi, hd = items[k]
                    chunks = tile_chunks[qi]
                    nch = len(chunks)
                    hp, r0 = hd // 2, (hd % 2) * 64
                    u = (hbase + k) % 2
                    pS = (0 + 2 * u, 1 + 2 * u)
                    for c, (kt, v) in enumerate(chunks):
                        pb = pS[c // 4]
                        pairs = [(kT[r0:r0 + 64, hp, kt * 128:(kt + 1) * 128], qT[r0:r0 + 64, hp, qi * 128:(qi + 1) * 128])]
                        K.mm([kT_t, qT_t, w_t, t_const], [pt[pb]], ps[pb][:, (c % 4) * 128:(c % 4 + 1) * 128], pairs)
                    n0 = min(nch, 4)
                    K.op("act", [pt[pS[0]]], [PT_t[u]], lambda h: h.activation(
                        out=PT[u][:, 0:n0, :], in_=ps[pS[0]][:, 0:n0 * 128].rearrange("p (a b) -> p a b", b=128), func=AF.Exp))
                    if nch > 4:
                        n1 = nch - 4
                        K.op("act", [pt[pS[1]]], [PT_t[u]], lambda h: h.activation(
                            out=PT[u][:, 4:4 + n1, :], in_=ps[pS[1]][:, 0:n1 * 128].rearrange("p (a b) -> p a b", b=128), func=AF.Exp))
                    loc = [v for (kt, v) in chunks if v is not None]
                    if loc:
                        nl, v0 = len(loc), loc[0]
                        K.op("dve", [PT_t[u], bt_t], [PT_t[u]], lambda h: h.tensor_tensor(
                            out=PT[u][:, 0:nl, :], in0=PT[u][:, 0:nl, :], in1=BT[:, hd * 12 + v0:hd * 12 + v0 + nl, :], op=ALU.mult))

                def stage2(k):
                    qi, hd = items[k]
                    chunks = tile_chunks[qi]
                    u = (hbase + k) % 2
                    po = 4 + qi % 2
                    K.mm([PT_t[u], V_t], [pt[po]], ps[po][:, hd * 65:(hd + 1) * 65],
                         [(PT[u][:, c, :], V[:, kt, hd, :]) for c, (kt, v) in enumerate(chunks)])

                def fin_dve(qi):
                    po = 4 + qi % 2
                    oi = qi % 2
                    ov = ps[po][:, 0:390].rearrange("p (a b) -> p a b", b=65)
                    K.op("dve", [pt[po]], [rec_t[oi]], lambda h: h.reciprocal(out=rec[oi][:], in_=ov[:, :, 64]))
                    K.op("dve", [pt[po], rec_t[oi]], [O_t[oi]], lambda h: h.tensor_tensor(
                        out=Osb[oi][:], in0=ov[:, :, 0:64], in1=rec[oi][:].unsqueeze(2).to_broadcast([128, 6, 64]), op=ALU.mult))

                def fin_pe(qi):
                    oi = qi % 2
                    Of = Osb[oi][:].rearrange("p a b -> p (a b)")
                    for hp in range(3):
                        K.op("pe", [O_t[oi], t_const], [ptb], lambda h, hp=hp: h.transpose(
                            psb[:, hp * 128:(hp + 1) * 128], Of[:, hp * 128:(hp + 1) * 128], identb[:]))
                    K.op("act", [ptb], [yna_t], lambda h: h.activation(
                        out=yna[:, :, qi * 128:(qi + 1) * 128], in_=psb[:, 0:384].rearrange("p (a b) -> p a b", b=128), func=AF.Copy))

                hbase = hcount
                nit = len(items)
                for k in range(nit + 2):
                    if k == 6 and pending[0] is not None:
                        pending[0]()
                        pending[0] = None
                    if k < nit:
                        stage1(k)
                    if 1 <= k <= nit:
                        stage2(k - 1)
                        if items[k - 1][1] == 5:
                            fin_dve(items[k - 1][0])
                    if 2 <= k <= nit + 1 and items[k - 2][1] == 5:
                        fin_pe(items[k - 2][0])
                hcount += nit
                if bi + 1 < len(nblocks):
                    c0_, n_ = blk_cols(nblocks[bi + 1])
                    qk_norm(wq, w_t, gq, qT2[(bi + 1) % 2], qT2_t[(bi + 1) % 2], 0, nblocks[bi + 1], c0_, n_, cnt)
                def _op(yna=yna, yna_t=yna_t, b=b, c0=c0, n=n, r=r):
                    out_proj_update(lambda dt: [(wo[:, hp, dt * 128:(dt + 1) * 128], yna[:, hp, :n]) for hp in range(3)],
                                    [yna_t, w_t], b, c0, n, r, [6])
                if pending[0] is not None:
                    pending[0]()
                pending[0] = _op
            if pending[0] is not None:
                pending[0]()
                pending[0] = None
            K.barrier()

    def gla(l, hT, hT_t, need_ctx):
        NB = 256
        QS = 48 ** -0.5
        with contextlib.ExitStack() as ph:
            qr = K.sb(ph, "gl_qr", [128, 2, NT], BF16)
            kr = K.sb(ph, "gl_kr", [128, 2, NT], BF16)
            aT = K.sb(ph, "gl_aT", [32, NT], BF16)
            st_t = [Tok() for i in range(5)]
            gvc = K.sb(ph, "gl_gvc", [128, 2, 384], BF16)
            sgc = K.sb(ph, "gl_sgc", [96, 4, NCTX], BF16)
            gvc_t = Tok()
            aw = K.sb(ph, "gl_aw", [32, 2, 256], BF16)
            nab = K.sb(ph, "gl_nab", [128, 2, 2], F32)
            gon = K.sb(ph, "gl_gon", [96, 1], F32)
            msk = K.sb(ph, "gl_msk", [128, 2, 2, 128], BF16)
            mscan = K.sb(ph, "gl_mscan", [128, 512], BF16)
            epsb = K.sb(ph, "gl_eps", [128, 1], F32)
            oneb = K.sb(ph, "gl_one", [128, 1], F32)
            w_t = Tok()
            K.dma("pool", aw[:], aw_d[l], [], [w_t], "glw")
            K.dma("sp", nab[:], ab_d[l], [], [w_t], "glw")
            K.dma("sp", gon[:], gon_d[l], [], [w_t], "glw")
            K.dma("sp", msk[:], gmask_d, [], [w_t], "glw")
            K.dma("sp", mscan[:], mscan_d, [], [w_t], "glw")
            K.op("dve", [w_t], [w_t], lambda h: h.tensor_scalar(out=nab[:], in0=nab[:], scalar1=-1.0, scalar2=None, op0=ALU.mult))
            K.op("dve", [], [w_t], lambda h: h.memset(epsb[:], EPS))
            K.op("dve", [], [w_t], lambda h: h.memset(oneb[:], 1.0))
            wo = K.sb(ph, "gl_wo", [96, 4, D], BF16)
            w2_t = Tok()
            with contextlib.ExitStack() as p0:
                wg = K.sb(p0, "gl_wg", [128, 8, 1024], BF16)
                wga = K.sb(p0, "gl_wga", [128, 8, 32], BF16)
                w0_t = Tok()
                K.dma("pool", wg[:], wg_d[l], [], [w0_t], "glw0")
                K.dma("pool", wga[:], win_d[l][:, :, 2560:2592], [], [w0_t], "glw0")
                wgv = K.sb(p0, "gl_wgv", [128, 8, 384], BF16)
                wgg = K.sb(p0, "gl_wgg", [128, 8, 384], BF16)
                K.dma("pool", wgv[:], win_d[l][:, :, 1792:2176], [], [w0_t], "glw0")
                K.dma("pool", wgg[:], win_d[l][:, :, 2176:2560], [], [w0_t], "glw0")
                K.dma("pool", wo[:], woutg_d[l], [], [w2_t], "glw2")
                gvs = K.sb(p0, "gl_gvs", [128, 4, 384], BF16)
                sgs = K.sb(p0, "gl_sgs", [96, 4, 512], BF16)
                stg_t = Tok()
                cosb = K.sb(p0, "gl_cos", [128, 512], F32)
                sinb = K.sb(p0, "gl_sin", [128, 512], F32)
                cs_t = Tok()
                t1 = [K.sb(p0, "gl_t1%d" % i, [128, 512], F32) for i in range(2)]
                t2 = [K.sb(p0, "gl_t2%d" % i, [128, 512], F32) for i in range(2)]
                r_t = [Tok(), Tok()]
                cnt = 0
                for b5 in range(5):
                    c0, n = blk_cols(b5)
                    lat = b5 < 4
                    if lat:
                        K.dma("sp", cosb[:, :n], cos_d[:, c0:c0 + n], [], [cs_t], "cs")
                        K.dma("sp", sinb[:, :n], sin_d[:, c0:c0 + n], [], [cs_t], "cs")
                    lin_fm(None, wga, 0, 32, hT, hT_t, b5, c0, n, 6, w0_t)
                    K.op("act", [pt[6]], [st_t[b5]], lambda h: h.activation(out=aT[:, c0:c0 + n], in_=ps[6][:32, :n], func=AF.Copy))
                    for hp in range(2):
                        for which in range(2):
                            base = which * 512
                            dst = qr if which == 0 else kr
                            u = cnt % 2
                            cnt += 1
                            pa, pbk = (0, 1) if u == 0 else (2, 3)
                            lin_fm(None, wg, base + hp * 128, 128, hT, hT_t, b5, c0, n, pa, w0_t)
                            if lat:
                                lin_fm(None, wg, base + 256 + hp * 128, 128, hT, hT_t, b5, c0, n, pbk, w0_t)
                                K.op("dve", [pt[pa], cs_t], [r_t[u]], lambda h: h.tensor_tensor(out=t1[u][:, :n], in0=ps[pa][:, :n], in1=cosb[:, :n], op=ALU.mult))
                                K.op("dve", [pt[pbk], cs_t, r_t[u]], [r_t[u]], lambda h: h.tensor_tensor(out=t2[u][:, :n], in0=ps[pbk][:, :n], in1=sinb[:, :n], op=ALU.mult))
                                K.op("pool", [r_t[u]], [st_t[b5]], lambda h: h.tensor_tensor(out=dst[:, hp, c0:c0 + n], in0=t1[u][:, :n], in1=t2[u][:, :n], op=ALU.add))
                            else:
                                K.op("act", [pt[pa]], [st_t[b5]], lambda h: h.activation(out=dst[:, hp, c0:c0 + n], in_=ps[pa][:, :n], func=AF.Copy))
                    for tt in range(n // 128):
                        pv = 4 + tt % 2
                        K.mm([w0_t, hT_t[b5]], [pt[pv]], ps[pv][:, :384],
                             [(hT[:, kc, c0 + tt * 128:c0 + (tt + 1) * 128], wgv[:, kc, :]) for kc in range(8)])
                        if lat:
                            K.op("act", [pt[pv]], [stg_t], lambda h, tt=tt, pv=pv: h.activation(out=gvs[:, tt, :], in_=ps[pv][:, :384], func=AF.Copy))
                        else:
                            K.op("act", [pt[pv]], [gvc_t], lambda h, tt=tt, pv=pv: h.activation(out=gvc[:, tt, :], in_=ps[pv][:, :384], func=AF.Copy))
                    if lat or need_ctx:
                        for hd in range(4):
                            pv = 4 + hd % 2
                            lin_fm(None, wgg, hd * 96, 96, hT, hT_t, b5, c0, n, pv, w0_t)
                            if lat:
                                K.op("act", [pt[pv]], [stg_t], lambda h, hd=hd, pv=pv: h.activation(out=sgs[:, hd, :n], in_=ps[pv][:96, :n], func=AF.Silu))
                            else:
                                K.op("act", [pt[pv]], [gvc_t], lambda h, hd=hd, pv=pv: h.activation(out=sgc[:, hd, :n], in_=ps[pv][:96, :n], func=AF.Silu))
                    if lat:
                        for tt in range(4):
                            K.op("pool", [stg_t], [hT_t[b5]], lambda h, tt=tt: h.tensor_copy(out=hT[:, tt, c0:c0 + 384], in_=gvs[:, tt, :]))
                        for hd in range(4):
                            K.op("pool", [stg_t], [hT_t[b5]], lambda h, hd=hd: h.tensor_copy(out=hT[:96, 4 + hd, c0:c0 + 512], in_=sgs[:, hd, :]))
                K.barrier()
            oF = K.sb(ph, "gl_oF", [96, 4, NT], BF16)
            oF_t = [Tok() for i in range(5)]
            R = []
            for d in range(2):
                r_ = dict(
                    e1=K.sb(ph, "gl_e1_%d" % d, [128, 512], F32),
                    bpos=K.sb(ph, "gl_bp_%d" % d, [128, 512], F32), eb=K.sb(ph, "gl_eb_%d" % d, [128, 512], F32),
                    qin=K.sb(ph, "gl_qin_%d" % d, [128, 2, 512], BF16), kin=K.sb(ph, "gl_kin_%d" % d, [128, 2, 512], BF16),
                    ktok=K.sb(ph, "gl_ktok_%d" % d, [128, 4, 2, 128], BF16),
                    dec=K.sb(ph, "gl_dec_%d" % d, [128, 2, 4], F32), AM=K.sb(ph, "gl_AM_%d" % d, [128, 4, 128], BF16),
                    S=K.sb(ph, "gl_S_%d" % d, [128, 2, 192], F32), Sb=K.sb(ph, "gl_Sb_%d" % d, [128, 2, 192], BF16),
                    osum=K.sb(ph, "gl_osum_%d" % d, [96, 4, NB], F32),
                    sg_t=Tok(), n_t=Tok(),
                    g_t=Tok(), qk_t=Tok(), ktok_t=Tok(), gv_t=Tok(), dec_t=Tok(), AM_t=Tok(), S_t=Tok(), Sb_t=Tok(), os_t=Tok(),
                    BA=3 * d, BB=3 * d + 1, BO=3 * d + 2)
                R.append(r_)

            def block_prep(d, b5):
                r_ = R[d]
                c0, n = blk_cols(b5)
                nt = n // 128
                BB = r_["BB"]
                e1, bpos, eb = r_["e1"], r_["bpos"], r_["eb"]
                lsb = e1
                enb = eb
                gt = r_["g_t"]
                for hp in range(2):
                    K.mm([st_t[b5], w_t], [pt[BB]], ps[BB][:, :n], [(aw[:, d, hp * 128:(hp + 1) * 128], aT[:, c0:c0 + n])])
                    yield
                    K.op("act", [pt[BB], w_t], [gt], lambda h: h.activation(
                        out=e1[:, :n], in_=ps[BB][:, :n], func=AF.Exp, bias=nab[:, d, hp:hp + 1], scale=-1.0))
                    yield
                    K.op("act", [gt, w_t], [gt], lambda h: h.activation(out=lsb[:, :n], in_=e1[:, :n], func=AF.Ln, bias=oneb[:], scale=1.0))
                    yield
                    K.op("dve", [gt, w_t], [gt], lambda h: h.tensor_tensor_scan(
                        out=bpos[:, :n], data0=mscan[:, :n], data1=lsb[:, :n], initial=0.0, op0=ALU.mult, op1=ALU.add))
                    yield
                    bsel = bpos
                    if d == 1:
                        K.op("dve", [gt], [gt], lambda h: h.tensor_tensor(out=e1[:, :n], in0=lsb[:, :n], in1=bpos[:, :n], op=ALU.subtract))
                        yield
                        K.op("dve", [gt], [gt], lambda h: h.tensor_tensor(
                            out=e1[:, :n].rearrange("p (a b) -> p a b", b=128), in0=e1[:, :n].rearrange("p (a b) -> p a b", b=128),
                            in1=bpos[:, :n].rearrange("p (a b) -> p a b", b=128)[:, :, 127:128].to_broadcast([128, n // 128, 128]), op=ALU.add))
                        yield
                        bsel = e1
                    K.op("act", [gt], [gt], lambda h, bsel=bsel: h.activation(out=eb[:, :n], in_=bsel[:, :n], func=AF.Exp, scale=-1.0 / 16))
                    yield
                    sel = 127 if d == 0 else 0
                    K.op("dve", [gt], [r_["dec_t"]], lambda h: h.tensor_copy(
                        out=r_["dec"][:, hp, 0:nt], in_=eb[:, :n].rearrange("p (a b) -> p a b", b=128)[:, :, sel]))
                    yield
                    K.op("dve", [gt, st_t[b5]], [r_["qk_t"]], lambda h: h.scalar_tensor_tensor(
                        out=r_["qin"][:, hp, :n], in0=qr[:, hp, c0:c0 + n], scalar=QS, in1=eb[:, :n], op0=ALU.mult, op1=ALU.mult))
                    yield
                    K.op("act", [gt, r_["dec_t"], r_["qk_t"]], [gt], lambda h, bsel=bsel: h.activation(out=enb[:, :n], in_=bsel[:, :n], func=AF.Exp, scale=1.0 / 16))
                    yield
                    K.op("dve", [gt, st_t[b5]], [r_["qk_t"]], lambda h: h.tensor_tensor(
                        out=r_["kin"][:, hp, :n], in0=kr[:, hp, c0:c0 + n], in1=enb[:, :n], op=ALU.mult))
                    yield
                    for tt in range(nt):
                        K.op("pe", [r_["qk_t"], t_const], [ptb], lambda h, tt=tt: h.transpose(
                            psb[:, d * 512 + tt * 128:d * 512 + (tt + 1) * 128], r_["kin"][:, hp, tt * 128:(tt + 1) * 128], identb[:]))
                        yield
                    K.op("act", [ptb], [r_["ktok_t"]], lambda h: h.activation(
                        out=r_["ktok"][:, 0:nt, hp, :], in_=psb[:, d * 512:d * 512 + nt * 128].rearrange("p (t f) -> p t f", t=nt), func=AF.Copy))
                    yield

            def scan_tile(d, b5, tt, want_o, second):
                r_ = R[d]
                c0, n = blk_cols(b5)
                tc0 = tt * 128
                BA, BB, BO = r_["BA"], r_["BB"], r_["BO"]
                qin, kin, Sb, S, AM, dec, ktok = r_["qin"], r_["kin"], r_["Sb"], r_["S"], r_["AM"], r_["dec"], r_["ktok"]
                if b5 < 4:
                    gv_tok = hT_t[b5]

                    def gvf(col, w):
                        return hT[:, tt, c0 + col:c0 + col + w]
                else:
                    gv_tok = gvc_t

                    def gvf(col, w):
                        return gvc[:, tt, col:col + w]
                if want_o:
                    for hd in (0, 2, 1, 3):
                        hp, r0 = hd // 2, (hd % 2) * 64
                        pa = BA if hd % 2 == 0 else BB
                        K.mm([r_["qk_t"]], [pt[pa]], ps[pa][:, (hd // 2) * 128:(hd // 2 + 1) * 128],
                             [(kin[r0:r0 + 48, hp, tc0:tc0 + 128], qin[r0:r0 + 48, hp, tc0:tc0 + 128])])
                        yield
                    for par, pa in ((0, BA), (1, BB)):
                        K.op("dve", [pt[pa], w_t], [r_["AM_t"]], lambda h, par=par, pa=pa: h.tensor_tensor(
                            out=AM[:, par:4:2, :], in0=ps[pa][:, 0:256].rearrange("p (a b) -> p a b", b=128),
                            in1=msk[:, d, :, :], op=ALU.mult))
                        yield
                    for hd in range(4):
                        hp, r0 = hd // 2, (hd % 2) * 64
                        K.mm([r_["AM_t"], gv_tok, r_["Sb_t"], r_["qk_t"]], [pt[BO]], ps[BO][:96, hd * 128:(hd + 1) * 128],
                             [(gvf(hd * 96, 96), AM[:, hd, :]),
                              (Sb[r0:r0 + 48, hp, (hd % 2) * 96:(hd % 2 + 1) * 96], qin[r0:r0 + 48, hp, tc0:tc0 + 128])])
                        yield
                    ov = ps[BO][:96, :].rearrange("p (a b) -> p a b", b=128)
                    if not second:
                        K.op("act", [pt[BO]], [oF_t[b5]], lambda h: h.activation(
                            out=oF[:, :, c0 + tc0:c0 + tc0 + 128], in_=ov, func=AF.Copy))
                    else:
                        K.op("dve", [pt[BO], oF_t[b5]], [r_["os_t"]], lambda h: h.tensor_tensor(
                            out=r_["osum"][:, :, (tt % 2) * 128:(tt % 2 + 1) * 128], in0=ov, in1=oF[:, :, c0 + tc0:c0 + tc0 + 128], op=ALU.add))
                    yield
                for hp in range(2):
                    K.mm([r_["ktok_t"], gv_tok], [pt[BA]], ps[BA][:, hp * 192:(hp + 1) * 192],
                         [(ktok[:, tt, hp, :], gvf(hp * 192, 192))])
                    yield
                for hp in range(2):
                    K.op("act", [r_["S_t"], r_["dec_t"]], [r_["S_t"]], lambda h, hp=hp: h.activation(
                        out=S[:, hp, :], in_=S[:, hp, :], func=AF.Identity, scale=dec[:, hp, tt:tt + 1]))
                    yield
                    K.op("dve", [pt[BA], r_["S_t"], r_["dec_t"]], [r_["S_t"]], lambda h, hp=hp: h.scalar_tensor_tensor(
                        out=S[:, hp, :], in0=ps[BA][:, hp * 192:(hp + 1) * 192], scalar=dec[:, hp, tt:tt + 1], in1=S[:, hp, :],
                        op0=ALU.mult, op1=ALU.add))
                    yield
                K.op("act", [r_["S_t"]], [r_["Sb_t"]], lambda h: h.activation(out=Sb[:], in_=S[:], func=AF.Copy))
                yield

            def finalize(d, b5, half):
                r_ = R[d]
                c0 = blk_cols(b5)[0] + half * 256
                n = 256
                r = 0 if b5 < 4 else 1
                osum = r_["osum"]
                BA, BB, BO = r_["BA"], r_["BB"], r_["BO"]
                if b5 < 4:
                    sg_t = hT_t[b5]

                    def sgf(hd):
                        return hT[:96, 4 + hd, c0:c0 + n]
                else:
                    sg_t = gvc_t

                    def sgf(hd):
                        return sgc[:, hd, :n]
                sq = r_["AM"][:96, 0:2, :].rearrange("p a b -> p (a b)")
                am_t = r_["AM_t"]
                rt = r_["e1"]
                n_t, gt = r_["n_t"], r_["g_t"]
                for hd in range(4):
                    K.op("act", [r_["os_t"]], [n_t, am_t], lambda h, hd=hd: h.activation(out=sq[:, :n], in_=osum[:, hd, :n], func=AF.Square))
                    yield
                    K.mm([n_t, am_t, t_const], [pt[BB]], ps[BB][:96, :n], [(onesb[:96, :96], sq[:, :n])])
                    yield
                    K.op("act", [pt[BB], w_t], [n_t, gt], lambda h: h.activation(
                        out=rt[:96, :n], in_=ps[BB][:96, :n], func=AF.Ln, bias=epsb[:96, :], scale=1.0 / 96))
                    yield
                    K.op("act", [n_t], [n_t, gt], lambda h: h.activation(out=rt[:96, :n], in_=rt[:96, :n], func=AF.Exp, scale=-0.5))
                    yield
                    K.op("dve", [n_t, gt, w_t], [r_["os_t"]], lambda h, hd=hd: h.scalar_tensor_tensor(
                        out=osum[:, hd, :n], in0=osum[:, hd, :n], scalar=gon[:, 0:1], in1=rt[:96, :n], op0=ALU.mult, op1=ALU.mult))
                    yield
                    K.op("dve", [r_["os_t"], sg_t], [sg_t], lambda h, hd=hd: h.tensor_tensor(
                        out=sgf(hd), in0=osum[:, hd, :n], in1=sgf(hd), op=ALU.mult))
                    yield
                for dt in range(8):
                    pc = BA if dt % 2 == 0 else BO
                    K.mm([sg_t, w2_t], [pt[pc]], ps[pc][:, :n], [(wo[:, hd, dt * 128:(dt + 1) * 128], sgf(hd)) for hd in range(4)])
                    yield
                    K.op("dve", [pt[pc], t_mod, xT_t[b5]], [xT_t[b5]], lambda h, dt=dt, pc=pc: h.scalar_tensor_tensor(
                        out=xT[:, dt, c0:c0 + n], in0=ps[pc][:, :n], scalar=modT[:, 16 + dt, r:r + 1],
                        in1=xT[:, dt, c0:c0 + n], op0=ALU.mult, op1=ALU.add))
                    yield

            def dir_gen(d):
                r_ = R[d]
                K.op("dve", [], [r_["S_t"]], lambda h: h.memset(r_["S"][:], 0.0))
                K.op("dve", [], [r_["Sb_t"]], lambda h: h.memset(r_["Sb"][:], 0.0))
                order = [4] + ([0, 1, 2, 3] if d == 0 else [3, 2, 1, 0])
                for b5 in order:
                    want_o = (b5 < 4) or need_ctx
                    if b5 == 4:
                        second = (d == 1)
                    else:
                        second = (b5 >= 2) if d == 0 else (b5 <= 1)
                    nt = blk_cols(b5)[1] // 128
                    yield from block_prep(d, b5)
                    tiles = list(range(nt)) if d == 0 else list(range(nt - 1, -1, -1))
                    for i_, tt in enumerate(tiles):
                        yield from scan_tile(d, b5, tt, want_o, second)
                        if want_o and second and i_ % 2 == 1:
                            yield from finalize(d, b5, tt // 2)
                    yield "STEP"

            def run_step(gens):
                active = list(gens)
                while active:
                    for g in list(active):
                        if next(g) == "STEP":
                            active.remove(g)
            gF, gB = dir_gen(0), dir_gen(1)
            run_step([gF])
            run_step([gB])
            for s_ in range(4):
                run_step([gF, gB])
            K.barrier()

    def mixers(l, hT, hT_t, need_ctx):
        nm = lambda: norm_mod(l, "mix", hT, hT_t, range(5))
        with contextlib.ExitStack() as pw:
            wkv = None
            if False:
                wk = K.sb(pw, "na_wk", [128, 8, 384], BF16)
                wv = K.sb(pw, "na_wv", [128, 8, 384], BF16)
                wk_t = Tok()
                K.dma("pool", wk[:], win_d[l][:, :, 640:1024], [], [wk_t], "naw")
                K.dma("pool", wv[:], win_d[l][:, :, 1024:1408], [], [wk_t], "naw")
                wkv = (wk, wv, wk_t)
            if "nofn" not in debug:
                fnet(l, hT, hT_t, need_ctx, pre=nm)
            else:
                nm()
            if "nona" not in debug:
                natt(l, hT, hT_t, need_ctx, wkv=wkv)
            K.barrier()
        if "nogla" not in debug:
            gla(l, hT, hT_t, need_ctx)

    for l in range(2):
        need_ctx = (l == 0)
        modT.t, amix.t, affn.t = modTs[l], amixs[l], affns[l]
        with contextlib.ExitStack() as lay:
            hT = K.sb(lay, "hT", [128, 8, NT], BF16)
            hT_t = [Tok("hT%d" % i) for i in range(5)]
            if "ffn" not in debug:
                mixers(l, hT, hT_t, need_ctx)
            blocks = range(5) if need_ctx else range(4)
            if "noffn" not in debug:
                ffn(l, hT, hT_t, blocks)
            K.barrier()
        if "ffn" in debug or "l0" in debug:
            break

    t_out = Tok("out")
    for name, (buf, shape, dt) in dbg.items():
        dd = K.dram("dbg_" + name, shape, dt, kind="ExternalOutput")
        K.barrier()
        K.dma("sp", dd, buf[:], [], [t_out], "out")
        K.outs["dbg_" + name] = shape

    for b in range(4):
        c0, n = blk_cols(b)
        for hf in range(2):
            K.dma("sp", out_d[:, hf * 4:(hf + 1) * 4, c0:c0 + n], xT[:, hf * 4:(hf + 1) * 4, c0:c0 + n], [xT_t[b]], [t_out], "out")
    S = t_out.dsem["sp"]
    nc.sync.wait_ge(S.sem, S.count * 16)
    es.close()
    return K


_CONSTS = {}


def _consts():
    if _CONSTS:
        return _CONSTS
    bf = ml_dtypes.bfloat16
    t = np.arange(T, dtype=np.int64)
    ang = 2.0 * np.pi * ((t[:, None] * t[None, :]) % T).astype(np.float64) / T
    _CONSTS["c_ct"] = np.ascontiguousarray(np.cos(ang).reshape(16, 128, T).transpose(1, 0, 2)).astype(bf)
    _CONSTS["c_st"] = np.ascontiguousarray(np.sin(ang).reshape(16, 128, T).transpose(1, 0, 2)).astype(bf)
    t2 = np.arange(NCTX, dtype=np.int64)
    ang2 = 2.0 * np.pi * ((t2[:, None] * t2[None, :]) % NCTX).astype(np.float64) / NCTX
    _CONSTS["c_c256"] = np.ascontiguousarray(np.cos(ang2).reshape(2, 128, NCTX).transpose(1, 0, 2)).astype(bf)
    _CONSTS["c_s256"] = np.ascontiguousarray(np.sin(ang2).reshape(2, 128, NCTX).transpose(1, 0, 2)).astype(bf)
    g = np.arange(64)
    a64 = 2.0 * np.pi * ((g[:, None] * g[None, :]) % 64) / 64.0
    c64 = np.zeros((128, 4, 128))
    for blk in range(2):
        sl = slice(64 * blk, 64 * blk + 64)
        c64[sl, 0, sl] = np.cos(a64) / np.sqrt(T * 64.0)
        c64[sl, 1, sl] = -np.sin(a64) / np.sqrt(T * 64.0)
        c64[sl, 2, sl] = np.cos(a64) / np.sqrt(NCTX * 64.0)
        c64[sl, 3, sl] = -np.sin(a64) / np.sqrt(NCTX * 64.0)
    _CONSTS["c_c64"] = c64.astype(bf)
    cosT = np.zeros((128, T), np.float32)
    sinT = np.zeros((128, T), np.float32)
    row = (t // 64).astype(np.float64)
    col = (t % 64).astype(np.float64)
    for p in range(128):
        d = p % 64
        if d >= 48:
            continue
        within = d % 24
        j = within % 12
        inv = 10000.0 ** (-j / 12.0)
        pos = row if d < 24 else col
        cosT[p] = np.cos(pos * inv)
        sinT[p] = np.sin(pos * inv) * (-1.0 if within < 12 else 1.0)
    _CONSTS["c_cos"] = cosT
    _CONSTS["c_sin"] = sinT
    s_ = np.arange(128)
    gm = np.zeros((128, 2, 2, 128), np.float32)
    gm[:, 0, :, :] = (s_[:, None] <= s_[None, :])[:, None, :]
    gm[:, 1, :, :] = (s_[:, None] >= s_[None, :])[:, None, :]
    _CONSTS["c_gmask"] = gm.astype(bf)
    ms = np.ones((128, 512), np.float32)
    ms[:, 0::128] = 0.0
    _CONSTS["c_mscan"] = ms.astype(bf)
    return _CONSTS


def _host_prep(inputs):
    f = lambda a: np.ascontiguousarray(np.asarray(a, dtype=np.float32))
    x, c, ctx, c_ctx = f(inputs["x"]), f(inputs["c"]), f(inputs["ctx"]), f(inputs["c_ctx"])
    shared = {}
    shared["ada_w"] = f(f(inputs["ada_w"]).reshape(2, 8, 128, 6144).transpose(0, 2, 1, 3))
    shared["ada_b"] = f(f(inputs["ada_b"]).reshape(2, 48, 128).transpose(0, 2, 1))
    shared["norm_mix"] = f(f(inputs["norm_mix"]).reshape(2, 8, 128).transpose(0, 2, 1))
    shared["norm_ffn"] = f(f(inputs["norm_ffn"]).reshape(2, 8, 128).transpose(0, 2, 1))
    shared["ident"] = np.eye(128, dtype=np.float32)
    shared["ffn_w1"] = f(f(inputs["ffn_w1"]).reshape(2, 8, 128, DFF).transpose(0, 2, 1, 3))
    shared["ffn_w3"] = f(f(inputs["ffn_w3"]).reshape(2, 8, 128, DFF).transpose(0, 2, 1, 3))
    shared["ffn_w2"] = f(f(inputs["ffn_w2"]).reshape(2, 22, 128, D).transpose(0, 2, 1, 3))
    w_in = f(inputs["w_in"])
    shared["w_in"] = f(w_in.reshape(2, 8, 128, NIN).transpose(0, 2, 1, 3))
    wg = np.zeros((2, D, 1024), np.float32)
    dd = np.arange(48)
    partner = np.where((dd % 24) < 12, dd + 12, dd - 12)
    for hh in range(4):
        wg[:, :, 64 * hh:64 * hh + 48] = w_in[:, :, 1408 + 48 * hh + dd]
        wg[:, :, 256 + 64 * hh:256 + 64 * hh + 48] = w_in[:, :, 1408 + 48 * hh + partner]
        wg[:, :, 512 + 64 * hh:512 + 64 * hh + 48] = w_in[:, :, 1600 + 48 * hh + dd]
        wg[:, :, 768 + 64 * hh:768 + 64 * hh + 48] = w_in[:, :, 1600 + 48 * hh + partner]
    shared["w_g"] = f(wg.reshape(2, 8, 128, 1024).transpose(0, 2, 1, 3))
    shared["fnet_w"] = f(f(inputs["fnet_w"]).reshape(2, 2, 128, 256).transpose(0, 2, 1, 3))
    w_out = f(inputs["w_out"])
    shared["w_out"] = f(w_out.reshape(2, 8, 128, D).transpose(0, 2, 1, 3))
    shared["w_out_g"] = f(w_out[:, 640:, :].reshape(2, 4, 96, D).transpose(0, 2, 1, 3))
    shared["na_q"] = f(np.tile(f(inputs["na_q_norm"]), (1, 2))[:, :, None])
    shared["na_k"] = f(np.tile(f(inputs["na_k_norm"]), (1, 2))[:, :, None])
    rpb = f(inputs["na_rpb"])
    pk = np.arange(128)
    a_, kc_ = pk // 64, pk % 64
    b_, c_ = pk // 64, pk % 64
    dcm = np.clip(kc_[:, None] - c_[None, :], -15, 15) + 15
    cq0 = np.clip(c_ - 8, 0, 48)
    col_ok = (kc_[:, None] >= cq0[None, :]) & (kc_[:, None] < cq0[None, :] + 16)
    bt = np.full((2, 128, 72, 128), NEG, np.float32)
    ents = [(dl, 0) for dl in range(-3, 4)] + [(-2, 1), (-1, 0), (0, 0), (1, 0), (2, 1)]
    for e, (dl, msk_) in enumerate(ents):
        drm = 2 * dl + a_[:, None] - b_[None, :] + 7
        ok = col_ok & (drm >= 0) & (drm <= 14)
        if msk_ and dl == -2:
            ok = ok & ((2 * dl + a_[:, None]) >= (-4 + b_[None, :]))
        if msk_ and dl == 2:
            ok = ok & ((2 * dl + a_[:, None]) <= (3 + b_[None, :]))
        drc = np.clip(drm, 0, 14)
        for hh in range(6):
            vals = rpb[:, hh][:, drc, dcm]
            bt[:, :, hh * 12 + e, :] = np.where(ok[None], vals, NEG)
    shared["na_bt"] = bt
    aw = np.zeros((2, 32, 2, 256), np.float32)
    ab = np.zeros((2, 128, 2, 2), np.float32)
    gaw = f(inputs["gla_alpha_w"]); gab = f(inputs["gla_alpha_b"])
    for dr_ in range(2):
        for hh in range(4):
            aw[:, 16 * dr_:16 * dr_ + 16, dr_, 64 * hh:64 * hh + 48] = gaw[:, dr_, :, 48 * hh:48 * hh + 48]
            ab[:, 64 * (hh % 2):64 * (hh % 2) + 48, dr_, hh // 2] = gab[:, dr_, 48 * hh:48 * hh + 48]
    shared["gla_aw"] = aw
    shared["gla_ab"] = ab
    shared["gla_gon"] = f(f(inputs["gla_o_norm"])[:, :, None])
    shared.update(_consts())
    per_core = []
    for b in range(8):
        m = dict(shared)
        m["x"] = f(x[b].T.reshape(8, 128, T).transpose(1, 0, 2))
        m["ctx"] = f(ctx[b].T.reshape(8, 128, NCTX).transpose(1, 0, 2))
        ccv = np.stack([c[b], c_ctx], axis=-1)
        m["cc"] = f(ccv.reshape(8, 128, 2).transpose(1, 0, 2))
        per_core.append(m)
    return per_core


_DEBUG = tuple(x for x in os.environ.get("KDEBUG", "").split(",") if x)
LAST = {}


def kernel(**inputs):
    in_maps = _host_prep(inputs)
    K = build_program(debug=_DEBUG)
    ncores = int(os.environ.get("KCORES", "8"))
    res = run_bass_kernel_spmd(K.nc, in_maps[:ncores], core_ids=list(range(ncores)))
    LAST["res"] = res
    outs = []
    for r in res.results:
        o = np.asarray(r["out"])
        outs.append(o.transpose(1, 0, 2).reshape(D, T).T)
    return np.ascontiguousarray(np.stack(outs, axis=0)).astype(np.float32)
```

```python
import contextlib
import os
import numpy as np
import ml_dtypes
import concourse.bass as bass
import concourse.mybir as mybir
from concourse.bass_utils import run_bass_kernel_spmd

F32 = mybir.dt.float32
BF16 = mybir.dt.bfloat16
AF = mybir.ActivationFunctionType
ALU = mybir.AluOpType

D = 1024
T = 2048
NCTX = 256
NT = T + NCTX
DFF = 2816
NIN = 2592
EPS = 1e-6
NEG = -30000.0
SAME_ENGINE_RAW = True


class Tok:
    __slots__ = ("w", "r", "dsem", "name")

    def __init__(self, name=""):
        self.w = []
        self.r = {}
        self.dsem = None
        self.name = name


class Src:
    def __init__(self, name, sem, unit, h=None):
        self.name = name
        self.sem = sem
        self.unit = unit
        self.h = h
        self.count = 0
        self.seen = {}


class Ctx:
    def __init__(self):
        self.nc = bass.Bass("TRN2", target_bir_lowering=False)
        nc = self.nc
        self.es = contextlib.ExitStack()
        self.engs = {}
        for nm, h in (("pe", nc.tensor), ("act", nc.scalar), ("dve", nc.vector),
                      ("pool", nc.gpsimd), ("sp", nc.sync)):
            sem = self.es.enter_context(nc.semaphore("sem_" + nm))
            self.engs[nm] = Src(nm, sem, 1, h)
        self.dsems = []
        self.outs = {}
        self.n_inst = 0

    def dram(self, name, shape, dt, kind="ExternalInput"):
        return self.nc.dram_tensor(name, list(shape), dt, kind=kind).ap()

    def sb(self, stack, name, shape, dt):
        self.n_sb = getattr(self, "n_sb", 0) + 1
        return stack.enter_context(self.nc.sbuf_tensor("sb%d_%s" % (self.n_sb, name), list(shape), dt))

    def new_dsem(self, name):
        sem = self.es.enter_context(self.nc.semaphore("d_" + name + str(len(self.dsems))))
        s = Src("dma_" + name, sem, 16)
        self.dsems.append(s)
        return s

    def _wait_deps(self, E, reads, writes):
        deps = {}
        for t in reads:
            for (s, n) in t.w:
                deps[s] = max(deps.get(s, 0), n)
        for t in writes:
            for (s, n) in t.w:
                deps[s] = max(deps.get(s, 0), n)
            for s, n in t.r.items():
                deps[s] = max(deps.get(s, 0), n)
        for s, n in deps.items():
            if s is E and (E.name == "pe" or not SAME_ENGINE_RAW):
                continue
            if E.seen.get(s, 0) < n:
                E.h.wait_ge(s.sem, n * s.unit)
                E.seen[s] = n

    def op(self, eng, reads, writes, emit):
        E = self.engs[eng]
        self._wait_deps(E, reads, writes)
        ins = emit(E.h)
        E.count += 1
        ins.then_inc(E.sem, 1)
        self.n_inst += 1
        for t in reads:
            t.r[E] = E.count
        for t in writes:
            t.w = [(E, E.count)]
            t.r = {}
        return ins

    def mm(self, reads, writes, out, pairs, first=True, last=True):
        E = self.engs["pe"]
        self._wait_deps(E, reads, writes)
        n = len(pairs)
        ins = None
        for i, (l, r) in enumerate(pairs):
            ins = E.h.matmul(out, l, r, start=(first and i == 0), stop=(last and i == n - 1))
            self.n_inst += 1
        E.count += 1
        ins.then_inc(E.sem, 1)
        for t in reads:
            t.r[E] = E.count
        for t in writes:
            t.w = [(E, E.count)]
            t.r = {}

    def transpose(self, reads, writes, out, in_, ident):
        return self.op("pe", reads, writes, lambda h: h.transpose(out, in_, ident))

    def dma(self, q, out, in_, reads, writes, name="x"):
        E = self.engs[q]
        self._wait_deps(E, reads, writes)
        wt = writes[0]
        if wt.dsem is None:
            wt.dsem = {}
        if q not in wt.dsem:
            wt.dsem[q] = self.new_dsem(name + q)
        S = wt.dsem[q]
        ins = E.h.dma_start(out=out, in_=in_)
        S.count += 1
        ins.then_inc(S.sem, 16)
        self.n_inst += 1
        for t in reads:
            t.r[S] = S.count
        for t in writes:
            t.w = [(s_, n_) for (s_, n_) in t.w if (s_.unit == 16 and s_ is not S)] + [(S, S.count)]
            t.r = {}

    def barrier(self):
        allsrc = list(self.engs.values()) + self.dsems
        for E in self.engs.values():
            for s in allsrc:
                if s is E or s.count == 0:
                    continue
                if E.seen.get(s, 0) < s.count:
                    E.h.wait_ge(s.sem, s.count * s.unit)
                    E.seen[s] = s.count


def build_program(debug=()):
    K = Ctx()
    nc = K.nc
    es = K.es
    dbg = {}

    x_d = K.dram("x", [128, 8, T], F32)
    ctx_d = K.dram("ctx", [128, 8, NCTX], F32)
    cc_d = K.dram("cc", [128, 8, 2], F32)
    adaw_d = K.dram("ada_w", [2, 128, 8, 6144], F32)
    adab_d = K.dram("ada_b", [2, 128, 48], F32)
    nmix_d = K.dram("norm_mix", [2, 128, 8], F32)
    nffn_d = K.dram("norm_ffn", [2, 128, 8], F32)
    out_d = K.dram("out", [128, 8, T], F32, kind="ExternalOutput")
    ident_d = K.dram("ident", [128, 128], F32)
    w1_d = K.dram("ffn_w1", [2, 128, 8, DFF], F32)
    win_d = K.dram("w_in", [2, 128, 8, NIN], F32)
    wg_d = K.dram("w_g", [2, 128, 8, 1024], F32)
    fnetw_d = K.dram("fnet_w", [2, 128, 2, 256], F32)
    wout_d = K.dram("w_out", [2, 128, 8, D], F32)
    woutg_d = K.dram("w_out_g", [2, 96, 4, D], F32)
    naq_d = K.dram("na_q", [2, 128, 1], F32)
    nak_d = K.dram("na_k", [2, 128, 1], F32)
    bt_d = K.dram("na_bt", [2, 128, 72, 128], F32)
    aw_d = K.dram("gla_aw", [2, 32, 2, 256], F32)
    ab_d = K.dram("gla_ab", [2, 128, 2, 2], F32)
    gon_d = K.dram("gla_gon", [2, 96, 1], F32)
    ct_d = K.dram("c_ct", [128, 16, T], BF16)
    st_d = K.dram("c_st", [128, 16, T], BF16)
    c256_d = K.dram("c_c256", [128, 2, 256], BF16)
    s256_d = K.dram("c_s256", [128, 2, 256], BF16)
    c64_d = K.dram("c_c64", [128, 4, 128], BF16)
    cos_d = K.dram("c_cos", [128, T], F32)
    sin_d = K.dram("c_sin", [128, T], F32)
    gmask_d = K.dram("c_gmask", [128, 2, 2, 128], BF16)
    mscan_d = K.dram("c_mscan", [128, 512], BF16)
    w3_d = K.dram("ffn_w3", [2, 128, 8, DFF], F32)
    w2_d = K.dram("ffn_w2", [2, 128, 22, D], F32)

    top = es
    xT = K.sb(top, "xT", [128, 8, NT], F32)
    xT_t = [Tok("xT%d" % i) for i in range(5)]
    identf = K.sb(top, "identf", [128, 128], F32)
    identb = K.sb(top, "identb", [128, 128], BF16)
    onesb = K.sb(top, "onesb", [128, 128], BF16)
    cc = K.sb(top, "cc", [128, 8, 2], F32)
    scb = K.sb(top, "scb", [128, 8, 2], BF16)
    class Cur:
        def __init__(self):
            self.t = None

        def __getitem__(self, k):
            return self.t[k]
    modT = Cur()
    modTs = [K.sb(top, "modT%d" % i, [128, 48, 2], F32) for i in range(2)]
    amix = Cur()
    affn = Cur()
    amixs = [K.sb(top, "amix%d" % i, [128, 8, 2], F32) for i in range(2)]
    affns = [K.sb(top, "affn%d" % i, [128, 8, 2], F32) for i in range(2)]
    adabs = [K.sb(top, "adab%d" % i, [128, 48], F32) for i in range(2)]
    nmixs = [K.sb(top, "nmix%d" % i, [128, 8], F32) for i in range(2)]
    nffns = [K.sb(top, "nffn%d" % i, [128, 8], F32) for i in range(2)]
    t_const = Tok("const")
    t_mod = Tok("mod")
    t_small = Tok("small")

    ps = [es.enter_context(nc.psum_tensor("ps%d" % i, [128, 512], F32)) for i in range(7)]
    pt = [Tok("ps%d" % i) for i in range(7)]
    psb = es.enter_context(nc.psum_tensor("psb", [128, 1024], BF16))
    ptb = Tok("psb")

    def blk_cols(b):
        return (b * 512, 512) if b < 4 else (T, NCTX)

    K.dma("sp", identf[:], ident_d, [], [t_const], "c")
    K.op("dve", [t_const], [t_const], lambda h: h.tensor_copy(out=identb[:], in_=identf[:]))
    K.op("dve", [], [t_const], lambda h: h.memset(onesb[:], 1.0))
    K.dma("sp", cc[:], cc_d, [], [t_small], "s")
    K.op("act", [t_small], [t_small], lambda h: h.activation(out=scb[:], in_=cc[:], func=AF.Silu))

    def make_ada(l, stack):
        t_sm = Tok("adasmall")
        K.dma("sp", adabs[l][:], adab_d[l], [], [t_sm], "s")
        K.dma("sp", nmixs[l][:], nmix_d[l], [], [t_sm], "s")
        K.dma("sp", nffns[l][:], nffn_d[l], [], [t_sm], "s")
        wsl = [K.sb(stack, "adaw%d" % i, [128, 8, 512], BF16) for i in range(2)]
        wsl_t = [Tok("adaw%d" % i) for i in range(2)]

        def dma(hs):
            K.dma("pool", wsl[hs % 2][:], adaw_d[l][:, :, hs * 512:(hs + 1) * 512], [], [wsl_t[hs % 2]], "adaw")

        def sec(hs):
            s = hs % 2
            if hs == 0:
                dma(0)
            if hs + 1 < 12:
                dma(hs + 1)
            for mt in range(4):
                K.mm([wsl_t[s], t_small], [pt[6]], ps[6][:, mt * 2:mt * 2 + 2],
                     [(wsl[s][:, kc, mt * 128:(mt + 1) * 128], scb[:, kc, :]) for kc in range(8)])
            K.op("dve", [pt[6], t_sm], [t_mod], lambda h: h.tensor_tensor(
                out=modTs[l][:, hs * 4:(hs + 1) * 4, :],
                in0=ps[6][:, 0:8].rearrange("p (a b) -> p a b", b=2),
                in1=adabs[l][:, hs * 4:(hs + 1) * 4].unsqueeze(2).to_broadcast([128, 4, 2]),
                op=ALU.add))

        def tail():
            for (dst, gain, sc_) in ((amixs[l], nmixs[l], 1), (affns[l], nffns[l], 4)):
                K.op("dve", [t_mod, t_sm], [t_mod], lambda h, dst=dst, gain=gain, sc_=sc_: h.scalar_tensor_tensor(
                    out=dst[:], in0=modTs[l][:, sc_ * 8:(sc_ + 1) * 8, :], scalar=1.0,
                    in1=gain[:].unsqueeze(2).to_broadcast([128, 8, 2]),
                    op0=ALU.add, op1=ALU.mult))
        return [lambda hs=hs: sec(hs) for hs in range(12)] + [tail]

    with contextlib.ExitStack() as ph:
        ada0 = make_ada(0, ph)
        for b in range(5):
            c0, n = blk_cols(b)
            src = x_d[:, :, c0:c0 + n] if b < 4 else ctx_d[:, :, :]
            for hf in range(2):
                K.dma("sp", xT[:, hf * 4:(hf + 1) * 4, c0:c0 + n], src[:, hf * 4:(hf + 1) * 4, :], [], [xT_t[b]], "xin")
        for hs in range(12):
            ada0[hs]()
        ada0[12]()
        K.barrier()

    rot = {"ps": 0}

    def norm_mod(l, which, hT, hT_t, blocks):
        a = amix if which == "mix" else affn
        sec = 0 if which == "mix" else 3
        blocks = list(blocks)
        with contextlib.ExitStack() as ph:
            sq1 = K.sb(ph, "nm_sq", [128, 8, 512], BF16)
            sq = [sq1, sq1]
            sq1_t = Tok()
            sq_t = [sq1_t, sq1_t]
            rs = [K.sb(ph, "nm_rs%d" % i, [128, 512], F32) for i in range(2)]
            rs_t = [Tok() for i in range(2)]
            tmp = [K.sb(ph, "nm_tmp%d" % i, [128, 512], F32) for i in range(3)]
            tmp_t = [Tok() for i in range(3)]
            epsb = K.sb(ph, "nm_eps", [128, 1], F32)
            eps_t = Tok()
            K.op("dve", [], [eps_t], lambda h: h.memset(epsb[:], EPS))

            def stats(bi):
                b = blocks[bi]
                c0, n = blk_cols(b)
                s = bi % 2
                pb = bi % 2
                K.op("dve", [xT_t[b]], [sq_t[s]], lambda h: h.tensor_tensor(
                    out=sq[s][:, :, :n], in0=xT[:, :, c0:c0 + n], in1=xT[:, :, c0:c0 + n], op=ALU.mult))
                K.mm([sq_t[s], t_const], [pt[pb]], ps[pb][:, :n],
                     [(onesb[:], sq[s][:, kc, :n]) for kc in range(8)])
                K.op("act", [pt[pb], eps_t], [rs_t[s]], lambda h: h.activation(
                    out=rs[s][:, :n], in_=ps[pb][:, :n], func=AF.Ln, bias=epsb[:], scale=1.0 / D))
                K.op("act", [rs_t[s]], [rs_t[s]], lambda h: h.activation(
                    out=rs[s][:, :n], in_=rs[s][:, :n], func=AF.Exp, scale=-0.5))

            ti = 0
            stats(0)
            for bi, b in enumerate(blocks):
                if bi + 1 < len(blocks):
                    stats(bi + 1)
                c0, n = blk_cols(b)
                r = 0 if b < 4 else 1
                s = bi % 2
                for kc in range(8):
                    u = ti % 3
                    ti += 1
                    eng = "pool" if kc % 4 == 3 else "dve"
                    K.op(eng, [xT_t[b], rs_t[s]], [tmp_t[u]], lambda h, u=u, kc=kc: h.tensor_tensor(
                        out=tmp[u][:, :n], in0=xT[:, kc, c0:c0 + n], in1=rs[s][:, :n], op=ALU.mult))
                    K.op("act", [tmp_t[u], t_mod], [hT_t[b]], lambda h, u=u, kc=kc: h.activation(
                        out=hT[:, kc, c0:c0 + n], in_=tmp[u][:, :n], func=AF.Identity,
                        bias=modT[:, sec * 8 + kc, r:r + 1], scale=a[:, kc, r:r + 1]))
            K.barrier()

    def ffn(l, hT, hT_t, blocks):
        with contextlib.ExitStack() as ph:
            chunks = [(i * 512, 512) for i in range(5)] + [(2560, 256)]
            w1s = [K.sb(ph, "w1s%d" % i, [128, 8, 512], BF16) for i in range(2)]
            w3s = [K.sb(ph, "w3s%d" % i, [128, 8, 512], BF16) for i in range(2)]
            w2s = [K.sb(ph, "w2s%d" % i, [128, 4, D], BF16) for i in range(2)]
            w_t = [Tok() for i in range(2)]
            s1 = [K.sb(ph, "ff_s1%d" % i, [128, 512], F32) for i in range(2)]
            s1_t = [Tok() for i in range(2)]
            g = [K.sb(ph, "ff_g%d" % i, [128, 4, 512], BF16) for i in range(2)]
            g_t = [Tok() for i in range(2)]

            def load(ch):
                f0, nf = chunks[ch]
                s = ch % 2
                K.dma("pool", w1s[s][:, :, :nf], w1_d[l][:, :, f0:f0 + nf], [], [w_t[s]], "ffw")
                K.dma("pool", w3s[s][:, :, :nf], w3_d[l][:, :, f0:f0 + nf], [], [w_t[s]], "ffw")
                K.dma("pool", w2s[s][:, :nf // 128, :], w2_d[l][:, f0 // 128:(f0 + nf) // 128, :], [], [w_t[s]], "ffw")
            load(0)
            ada_next = make_ada(l + 1, ph) if l + 1 < 2 else None
            norm_mod(l, "ffn", hT, hT_t, blocks)
            cnt = 0
            gi = 0
            oi = 0
            for ch in range(6):
                if ch + 1 < 6:
                    load(ch + 1)
                f0, nf = chunks[ch]
                s = ch % 2
                nft = nf // 128
                for b in blocks:
                    c0, n = blk_cols(b)
                    r = 0 if b < 4 else 1
                    gs = gi % 2
                    gi += 1
                    for ft in range(nft):
                        pa = cnt % 2
                        pbk = 2 + cnt % 2
                        u = cnt % 2
                        cnt += 1
                        K.mm([w_t[s], hT_t[b]], [pt[pa]], ps[pa][:, :n],
                             [(w1s[s][:, kc, ft * 128:(ft + 1) * 128], hT[:, kc, c0:c0 + n]) for kc in range(8)])
                        K.mm([w_t[s], hT_t[b]], [pt[pbk]], ps[pbk][:, :n],
                             [(w3s[s][:, kc, ft * 128:(ft + 1) * 128], hT[:, kc, c0:c0 + n]) for kc in range(8)])
                        K.op("act", [pt[pa]], [s1_t[u]], lambda h, u=u, pa=pa, n=n: h.activation(
                            out=s1[u][:, :n], in_=ps[pa][:, :n], func=AF.Silu))
                        K.op("dve", [s1_t[u], pt[pbk]], [g_t[gs]], lambda h, u=u, pbk=pbk, n=n, gs=gs, ft=ft: h.tensor_tensor(
                            out=g[gs][:, ft, :n], in0=s1[u][:, :n], in1=ps[pbk][:, :n], op=ALU.mult))
                    for dt in range(8):
                        pc = 4 + oi % 3
                        oi += 1
                        K.mm([w_t[s], g_t[gs]], [pt[pc]], ps[pc][:, :n],
                             [(w2s[s][:, ft, dt * 128:(dt + 1) * 128], g[gs][:, ft, :n]) for ft in range(nft)])
                        K.op("dve", [pt[pc], t_mod, xT_t[b]], [xT_t[b]], lambda h, pc=pc, dt=dt, c0=c0, n=n, r=r: h.scalar_tensor_tensor(
                            out=xT[:, dt, c0:c0 + n], in0=ps[pc][:, :n], scalar=modT[:, 40 + dt, r:r + 1],
                            in1=xT[:, dt, c0:c0 + n], op0=ALU.mult, op1=ALU.add))
                if ada_next is not None:
                    ada_next[2 * ch]()
                    ada_next[2 * ch + 1]()
            if ada_next is not None:
                ada_next[12]()
            K.barrier()

    def lin_fm(wt, w, col0, M, hT, hT_t, b, c0, n, pb, wtok):
        K.mm([wtok, hT_t[b]], [pt[pb]], ps[pb][:M, :n],
             [(w[:, kc, col0:col0 + M], hT[:, kc, c0:c0 + n]) for kc in range(8)])

    def out_proj_update(pairs_fn, reads, b, c0, n, r, pbs):
        for dt in range(8):
            pc = pbs[dt % len(pbs)]
            K.mm(reads, [pt[pc]], ps[pc][:, :n], pairs_fn(dt))
            K.op("dve", [pt[pc], t_mod, xT_t[b]], [xT_t[b]], lambda h, pc=pc, dt=dt: h.scalar_tensor_tensor(
                out=xT[:, dt, c0:c0 + n], in0=ps[pc][:, :n], scalar=modT[:, 16 + dt, r:r + 1],
                in1=xT[:, dt, c0:c0 + n], op0=ALU.mult, op1=ALU.add))

    def rstd_from_ps(pb, P, n, inv, rt, rs, rs_tok, eps_ap, eps_t):
        K.op("act", [pt[pb], eps_t], [rs_tok], lambda h: h.activation(
            out=rt[:P, :n], in_=ps[pb][:P, :n], func=AF.Sqrt, bias=eps_ap[:P, :], scale=inv))
        K.op("dve", [rs_tok], [rs_tok], lambda h: h.reciprocal(out=rs[:P, :n], in_=rt[:P, :n]))

    def fnet(l, hT, hT_t, need_ctx, pre=None):
        with contextlib.ExitStack() as ph:
            wfn = K.sb(ph, "fn_w", [128, 8, 256], BF16)
            fnw = K.sb(ph, "fn_fw", [128, 2, 256], BF16)
            wo = K.sb(ph, "fn_wo", [128, 2, D], BF16)
            c64 = K.sb(ph, "fn_c64", [128, 4, 128], BF16)
            w_t = Tok()
            K.dma("pool", wfn[:], win_d[l][:, :, 0:256], [], [w_t], "fnw")
            K.dma("pool", fnw[:], fnetw_d[l], [], [w_t], "fnw")
            K.dma("pool", wo[:], wout_d[l][:, 0:2, :], [], [w_t], "fnw")
            K.dma("sp", c64[:], c64_d, [], [w_t], "fnw")
            cts = [K.sb(ph, "fn_ct%d" % i, [128, 2, 16, 512], BF16) for i in range(2)]
            ct_t = [Tok() for i in range(2)]
            jobs = [(kb, 0) for kb in range(4)] + ([(0, 1)] if need_ctx else [])

            def issue_tables(ji):
                kb, isctx = jobs[ji]
                s = ji % 2
                if not isctx:
                    K.dma("sp", cts[s][:, 0, :, :], ct_d[:, :, kb * 512:(kb + 1) * 512], [], [ct_t[s]], "ct")
                    K.dma("sp", cts[s][:, 1, :, :], st_d[:, :, kb * 512:(kb + 1) * 512], [], [ct_t[s]], "ct")
                else:
                    K.dma("sp", cts[s][:, 0, 0:2, 0:256], c256_d, [], [ct_t[s]], "ct")
                    K.dma("sp", cts[s][:, 1, 0:2, 0:256], s256_d, [], [ct_t[s]], "ct")
            issue_tables(0)
            issue_tables(1)
            if pre is not None:
                pre()
            Z = K.sb(ph, "fn_Z", [128, 18, 256], BF16)
            Z_t = Tok()
            Psb = [K.sb(ph, "fn_P%d" % i, [128, 2, 512], BF16) for i in range(2)]
            P_t = [Tok() for i in range(2)]
            Ysb = K.sb(ph, "fn_Y", [128, 2, 512], BF16)
            Y_t = Tok()
            yf = K.sb(ph, "fn_yf", [128, 2, 512], BF16)
            yf_t = Tok()
            for ti in range(18):
                pb = ti % 2
                b = min(ti // 4, 4)
                K.mm([w_t, hT_t[b]], [pt[pb]], ps[pb][:, :256],
                     [(hT[:, kc, ti * 128:(ti + 1) * 128], wfn[:, kc, :]) for kc in range(8)])
                if ti % 2 == 0:
                    K.op("dve", [pt[pb]], [Z_t], lambda h, ti=ti, pb=pb: h.tensor_copy(out=Z[:, ti, :], in_=ps[pb][:, :256]))
                else:
                    K.op("act", [pt[pb]], [Z_t], lambda h, ti=ti, pb=pb: h.activation(out=Z[:, ti, :], in_=ps[pb][:, :256], func=AF.Copy))
            for ji, (kb, isctx) in enumerate(jobs):
                s = ji % 2
                if not isctx:
                    ntt, n, tt0, c0, b, r = 16, 512, 0, kb * 512, kb, 0
                else:
                    ntt, n, tt0, c0, b, r = 2, 256, 16, T, 4, 1
                for chc in range(2):
                    u = chc
                    for cs in range(2):
                        pb = cs
                        K.mm([Z_t, ct_t[s]], [pt[pb]], ps[pb][:, :n],
                             [(Z[:, tt0 + tt, chc * 128:(chc + 1) * 128], cts[s][:, cs, tt, :n]) for tt in range(ntt)])
                        if cs == 0:
                            K.op("dve", [pt[pb]], [P_t[u]], lambda h, u=u, pb=pb, cs=cs: h.tensor_copy(out=Psb[u][:, cs, :n], in_=ps[pb][:, :n]))
                        else:
                            K.op("act", [pt[pb]], [P_t[u]], lambda h, u=u, pb=pb, cs=cs: h.activation(out=Psb[u][:, cs, :n], in_=ps[pb][:, :n], func=AF.Copy))
                    pb = 2 + chc
                    K.mm([P_t[u], w_t], [pt[pb]], ps[pb][:, :n],
                         [(c64[:, 2 * isctx + 0, :], Psb[u][:, 0, :n]), (c64[:, 2 * isctx + 1, :], Psb[u][:, 1, :n])])
                    K.op("dve", [pt[pb]], [Y_t], lambda h, pb=pb, chc=chc: h.tensor_copy(out=Ysb[:, chc, :n], in_=ps[pb][:, :n]))
                for c2 in range(2):
                    pb = 4 + c2
                    K.mm([Y_t, w_t], [pt[pb]], ps[pb][:, :n],
                         [(fnw[:, c1, c2 * 128:(c2 + 1) * 128], Ysb[:, c1, :n]) for c1 in range(2)])
                    K.op("act", [pt[pb]], [yf_t], lambda h, pb=pb, c2=c2: h.activation(out=yf[:, c2, :n], in_=ps[pb][:, :n], func=AF.Copy))
                out_proj_update(lambda dt: [(wo[:, c2, dt * 128:(dt + 1) * 128], yf[:, c2, :n]) for c2 in range(2)],
                                [yf_t, w_t], b, c0, n, r, [0, 1, 2, 3, 4, 5, 6])
                if ji + 2 < len(jobs):
                    issue_tables(ji + 2)
            K.barrier()

    def natt(l, hT, hT_t, need_ctx, wkv=None):
        with contextlib.ExitStack() as ph:
            kT = K.sb(ph, "na_kT", [128, 3, NT], BF16)
            kT_t = Tok()
            V = K.sb(ph, "na_V", [128, 18, 6, 65], BF16)
            V_t = Tok()
            gq = K.sb(ph, "na_gq", [128, 1], F32)
            gk = K.sb(ph, "na_gk", [128, 1], F32)
            epsb = K.sb(ph, "na_eps", [128, 1], F32)
            bones = K.sb(ph, "na_bones", [128, 128], BF16)
            g_t = Tok()
            K.dma("sp", gq[:], naq_d[l], [], [g_t], "nag")
            K.dma("sp", gk[:], nak_d[l], [], [g_t], "nag")
            K.op("dve", [g_t], [g_t], lambda h: h.tensor_scalar(out=gq[:], in0=gq[:], scalar1=0.125, scalar2=None, op0=ALU.mult))
            K.op("dve", [], [g_t], lambda h: h.memset(epsb[:], EPS))
            K.op("dve", [], [g_t], lambda h: h.memset(bones[:], 0.0))
            K.op("dve", [g_t], [g_t], lambda h: h.memset(bones[0:64, 0:64], 1.0))
            K.op("dve", [g_t], [g_t], lambda h: h.memset(bones[64:128, 64:128], 1.0))
            K.op("dve", [], [V_t], lambda h: h.memset(V[:, :, :, 64:65], 1.0))
            sq = [K.sb(ph, "na_sq%d" % i, [128, 512], BF16) for i in range(2)]
            sq_t = [Tok() for i in range(2)]
            rtl = [K.sb(ph, "na_rt%d" % i, [128, 512], F32) for i in range(2)]
            rs = [K.sb(ph, "na_rs%d" % i, [128, 512], F32) for i in range(2)]
            rs_t = [Tok() for i in range(2)]

            def qk_norm(w, wtok, gain, dst, dst_t, dcol, b, c0, n, cnt):
                for hp in range(3):
                    u = (cnt[0]) % 2
                    cnt[0] += 1
                    pa, pbk = u, 2 + u
                    lin_fm(None, w, hp * 128, 128, hT, hT_t, b, c0, n, pa, wtok)
                    K.op("act", [pt[pa]], [sq_t[u]], lambda h, u=u, pa=pa: h.activation(out=sq[u][:, :n], in_=ps[pa][:, :n], func=AF.Square))
                    K.mm([sq_t[u], g_t], [pt[pbk]], ps[pbk][:, :n], [(bones[:], sq[u][:, :n])])
                    K.op("act", [pt[pbk], g_t], [rs_t[u]], lambda h, u=u, pbk=pbk: h.activation(
                        out=rtl[u][:, :n], in_=ps[pbk][:, :n], func=AF.Ln, bias=epsb[:], scale=1.0 / 64))
                    K.op("act", [rs_t[u]], [rs_t[u]], lambda h, u=u: h.activation(
                        out=rs[u][:, :n], in_=rtl[u][:, :n], func=AF.Exp, scale=-0.5))
                    K.op("dve", [pt[pa], rs_t[u], g_t], [dst_t], lambda h, u=u, pa=pa, hp=hp: h.scalar_tensor_tensor(
                        out=dst[:, hp, dcol:dcol + n], in0=ps[pa][:, :n], scalar=gain[:, 0:1], in1=rs[u][:, :n],
                        op0=ALU.mult, op1=ALU.mult))

            cnt = [0]
            BT = K.sb(ph, "na_BT", [128, 72, 128], BF16)
            bt_t = Tok()
            for i3 in range(3):
                K.dma("pool", BT[:, 24 * i3:24 * (i3 + 1), :], bt_d[l][:, 24 * i3:24 * (i3 + 1), :], [], [bt_t], "nabt")
            for i3 in range(3):
                K.op("act", [bt_t], [bt_t], lambda h, i3=i3: h.activation(
                    out=BT[:, 24 * i3:24 * (i3 + 1), :], in_=BT[:, 24 * i3:24 * (i3 + 1), :], func=AF.Exp))
            wq = K.sb(ph, "na_wq", [128, 8, 384], BF16)
            wo = K.sb(ph, "na_wo", [128, 3, D], BF16)
            w_t = Tok()
            with contextlib.ExitStack() as p1:
                if wkv is None:
                    wk = K.sb(p1, "na_wk", [128, 8, 384], BF16)
                    wv = K.sb(p1, "na_wv", [128, 8, 384], BF16)
                    wk_t = Tok()
                    wv_t = Tok()
                    K.dma("pool", wk[:], win_d[l][:, :, 640:1024], [], [wk_t], "naw")
                    K.dma("pool", wv[:], win_d[l][:, :, 1024:1408], [], [wv_t], "nawv")
                else:
                    wk, wv, wk_t = wkv
                    wv_t = wk_t
                K.dma("pool", wq[:], win_d[l][:, :, 256:640], [], [w_t], "naw2")
                K.dma("pool", wo[:], wout_d[l][:, 2:5, :], [], [w_t], "naw2")
                for b in range(5):
                    c0, n = blk_cols(b)
                    qk_norm(wk, wk_t, gk, kT, kT_t, c0, b, c0, n, cnt)
                for ti in range(18):
                    pb = 4 + ti % 2
                    b = min(ti // 4, 4)
                    K.mm([wv_t, hT_t[b]], [pt[pb]], ps[pb][:, :384],
                         [(hT[:, kc, ti * 128:(ti + 1) * 128], wv[:, kc, :]) for kc in range(8)])
                    K.op("act" if ti % 2 else "dve", [pt[pb]], [V_t],
                         (lambda h, ti=ti, pb=pb: h.activation(out=V[:, ti, :, 0:64], in_=ps[pb][:, :384].rearrange("p (a b) -> p a b", b=64), func=AF.Copy))
                         if ti % 2 else
                         (lambda h, ti=ti, pb=pb: h.tensor_copy(out=V[:, ti, :, 0:64], in_=ps[pb][:, :384].rearrange("p (a b) -> p a b", b=64))))
                K.barrier()
            qT2 = [K.sb(ph, "na_qT%d" % i, [128, 3, 512], BF16) for i in range(2)]
            qT2_t = [Tok(), Tok()]
            PT = [K.sb(ph, "na_PT%d" % i, [128, 7, 128], BF16) for i in range(2)]
            PT_t = [Tok() for i in range(2)]
            rec = [K.sb(ph, "na_rec%d" % i, [128, 6], F32) for i in range(2)]
            rec_t = [Tok(), Tok()]
            Osb = [K.sb(ph, "na_O%d" % i, [128, 6, 64], BF16) for i in range(2)]
            O_t = [Tok(), Tok()]
            yna2 = [K.sb(ph, "na_y%d" % i, [128, 3, 512], BF16) for i in range(2)]
            yna2_t = [Tok(), Tok()]
            hcount = 0
            nblocks = list(range(5) if need_ctx else range(4))
            pending = [None]
            c0_, n_ = blk_cols(nblocks[0])
            qk_norm(wq, w_t, gq, qT2[0], qT2_t[0], 0, nblocks[0], c0_, n_, cnt)
            for bi, b in enumerate(nblocks):
                c0, n = blk_cols(b)
                r = 0 if b < 4 else 1
                qT, qT_t = qT2[bi % 2], qT2_t[bi % 2]
                yna, yna_t = yna2[bi % 2], yna2_t[bi % 2]
                tile_chunks = []
                for qi in range(n // 128):
                    if b < 4:
                        i = b * 4 + qi
                        if i <= 1:
                            js = list(range(0, 4))
                        elif i >= 14:
                            js = list(range(12, 16))
                        else:
                            js = list(range(i - 2, i + 3))
                        chunks = []
                        for j in js:
                            dl = j - i
                            v = (dl + 9) if 2 <= i <= 13 else (dl + 3)
                            chunks.append((j, v))
                        chunks += [(16, None), (17, None)]
                    else:
                        chunks = [(16, None), (17, None)]
                    tile_chunks.append(chunks)
                items = [(qi, hd) for qi in range(n // 128) for hd in range(6)]

                def stage1(k):
                    qi, hd = items[k]
                    chunks = tile_chunks[qi]
                    nch = len(chunks)
                    hp, r0 = hd // 2, (hd % 2) * 64
                    u = (hbase + k) % 2
                    pS = (0 + 2 * u, 1 + 2 * u)
                    for c, (kt, v) in enumerate(chunks):
                        pb = pS[c // 4]
                        pairs = [(kT[r0:r0 + 64, hp, kt * 128:(kt + 1) * 128], qT[r0:r0 + 64, hp, qi * 128:(qi + 1) * 128])]
                        K.mm([kT_t, qT_t, w_t, t_const], [pt[pb]], ps[pb][:, (c % 4) * 128:(c % 4 + 1) * 128], pairs)
                    n0 = min(nch, 4)
                    K.op("act", [pt[pS[0]]], [PT_t[u]], lambda h: h.activation(
                        out=PT[u][:, 0:n0, :], in_=ps[pS[0]][:, 0:n0 * 128].rearrange("p (a b) -> p a b", b=128), func=AF.Exp))
                    if nch > 4:
                        n1 = nch - 4
                        K.op("act", [pt[pS[1]]], [PT_t[u]], lambda h: h.activation(
                            out=PT[u][:, 4:4 + n1, :], in_=ps[pS[1]][:, 0:n1 * 128].rearrange("p (a b) -> p a b", b=128), func=AF.Exp))
                    loc = [v for (kt, v) in chunks if v is not None]
                    if loc:
                        nl, v0 = len(loc), loc[0]
                        K.op("dve", [PT_t[u], bt_t], [PT_t[u]], lambda h: h.tensor_tensor(
                            out=PT[u][:, 0:nl, :], in0=PT[u][:, 0:nl, :], in1=BT[:, hd * 12 + v0:hd * 12 + v0 + nl, :], op=ALU.mult))

                def stage2(k):
                    qi, hd = items[k]
                    chunks = tile_chunks[qi]
                    u = (hbase + k) % 2
                    po = 4 + qi % 2
                    K.mm([PT_t[u], V_t], [pt[po]], ps[po][:, hd * 65:(hd + 1) * 65],
                         [(PT[u][:, c, :], V[:, kt, hd, :]) for c, (kt, v) in enumerate(chunks)])

                def fin_dve(qi):
                    po = 4 + qi % 2
                    oi = qi % 2
                    ov = ps[po][:, 0:390].rearrange("p (a b) -> p a b", b=65)
                    K.op("dve", [pt[po]], [rec_t[oi]], lambda h: h.reciprocal(out=rec[oi][:], in_=ov[:, :, 64]))
                    K.op("dve", [pt[po], rec_t[oi]], [O_t[oi]], lambda h: h.tensor_tensor(
                        out=Osb[oi][:], in0=ov[:, :, 0:64], in1=rec[oi][:].unsqueeze(2).to_broadcast([128, 6, 64]), op=ALU.mult))

                def fin_pe(qi):
                    oi = qi % 2
                    Of = Osb[oi][:].rearrange("p a b -> p (a b)")
                    for hp in range(3):
                        K.op("pe", [O_t[oi], t_const], [ptb], lambda h, hp=hp: h.transpose(
                            psb[:, hp * 128:(hp + 1) * 128], Of[:, hp * 128:(hp + 1) * 128], identb[:]))
                    K.op("act", [ptb], [yna_t], lambda h: h.activation(
                        out=yna[:, :, qi * 128:(qi + 1) * 128], in_=psb[:, 0:384].rearrange("p (a b) -> p a b", b=128), func=AF.Copy))

                hbase = hcount
                nit = len(items)
                for k in range(nit + 2):
                    if k == 6 and pending[0] is not None:
                        pending[0]()
                        pending[0] = None
                    if k < nit:
                        stage1(k)
                    if 1 <= k <= nit:
                        stage2(k - 1)
                        if items[k - 1][1] == 5:
                            fin_dve(items[k - 1][0])
                    if 2 <= k <= nit + 1 and items[k - 2][1] == 5:
                        fin_pe(items[k - 2][0])
                hcount += nit
                if bi + 1 < len(nblocks):
                    c0_, n_ = blk_cols(nblocks[bi + 1])
                    qk_norm(wq, w_t, gq, qT2[(bi + 1) % 2], qT2_t[(bi + 1) % 2], 0, nblocks[bi + 1], c0_, n_, cnt)
                def _op(yna=yna, yna_t=yna_t, b=b, c0=c0, n=n, r=r):
                    out_proj_update(lambda dt: [(wo[:, hp, dt * 128:(dt + 1) * 128], yna[:, hp, :n]) for hp in range(3)],
                                    [yna_t, w_t], b, c0, n, r, [6])
                if pending[0] is not None:
                    pending[0]()
                pending[0] = _op
            if pending[0] is not None:
                pending[0]()
                pending[0] = None
            K.barrier()

    def gla(l, hT, hT_t, need_ctx):
        NB = 256
        QS = 48 ** -0.5
        with contextlib.ExitStack() as ph:
            qr = K.sb(ph, "gl_qr", [128, 2, NT], BF16)
            kr = K.sb(ph, "gl_kr", [128, 2, NT], BF16)
            aT = K.sb(ph, "gl_aT", [32, NT], BF16)
            st_t = [Tok() for i in range(5)]
            gvc = K.sb(ph, "gl_gvc", [128, 2, 384], BF16)
            sgc = K.sb(ph, "gl_sgc", [96, 4, NCTX], BF16)
            gvc_t = Tok()
            aw = K.sb(ph, "gl_aw", [32, 2, 256], BF16)
            nab = K.sb(ph, "gl_nab", [128, 2, 2], F32)
            gon = K.sb(ph, "gl_gon", [96, 1], F32)
            msk = K.sb(ph, "gl_msk", [128, 2, 2, 128], BF16)
            mscan = K.sb(ph, "gl_mscan", [128, 512], BF16)
            epsb = K.sb(ph, "gl_eps", [128, 1], F32)
            oneb = K.sb(ph, "gl_one", [128, 1], F32)
            w_t = Tok()
            K.dma("pool", aw[:], aw_d[l], [], [w_t], "glw")
            K.dma("sp", nab[:], ab_d[l], [], [w_t], "glw")
            K.dma("sp", gon[:], gon_d[l], [], [w_t], "glw")
            K.dma("sp", msk[:], gmask_d, [], [w_t], "glw")
            K.dma("sp", mscan[:], mscan_d, [], [w_t], "glw")
            K.op("dve", [w_t], [w_t], lambda h: h.tensor_scalar(out=nab[:], in0=nab[:], scalar1=-1.0, scalar2=None, op0=ALU.mult))
            K.op("dve", [], [w_t], lambda h: h.memset(epsb[:], EPS))
            K.op("dve", [], [w_t], lambda h: h.memset(oneb[:], 1.0))
            wo = K.sb(ph, "gl_wo", [96, 4, D], BF16)
            w2_t = Tok()
            with contextlib.ExitStack() as p0:
                wg = K.sb(p0, "gl_wg", [128, 8, 1024], BF16)
                wga = K.sb(p0, "gl_wga", [128, 8, 32], BF16)
                w0_t = Tok()
                wga_t, wgv_t, wgg_t = Tok(), Tok(), Tok()
                K.dma("pool", wga[:], win_d[l][:, :, 2560:2592], [], [wga_t], "glw0a")
                K.dma("pool", wg[:], wg_d[l], [], [w0_t], "glw0")
                wgv = K.sb(p0, "gl_wgv", [128, 8, 384], BF16)
                wgg = K.sb(p0, "gl_wgg", [128, 8, 384], BF16)
                K.dma("pool", wgv[:], win_d[l][:, :, 1792:2176], [], [wgv_t], "glw0v")
                K.dma("pool", wgg[:], win_d[l][:, :, 2176:2560], [], [wgg_t], "glw0g")
                K.dma("pool", wo[:], woutg_d[l], [], [w2_t], "glw2")
                gvs = K.sb(p0, "gl_gvs", [128, 4, 384], BF16)
                sgs = K.sb(p0, "gl_sgs", [96, 4, 512], BF16)
                stg_t = Tok()
                cosb = K.sb(p0, "gl_cos", [128, 512], F32)
                sinb = K.sb(p0, "gl_sin", [128, 512], F32)
                cs_t = Tok()
                t1 = [K.sb(p0, "gl_t1%d" % i, [128, 512], F32) for i in range(2)]
                t2 = [K.sb(p0, "gl_t2%d" % i, [128, 512], F32) for i in range(2)]
                r_t = [Tok(), Tok()]
                cnt = 0
                for b5 in range(5):
                    c0, n = blk_cols(b5)
                    lat = b5 < 4
                    if lat:
                        K.dma("sp", cosb[:, :n], cos_d[:, c0:c0 + n], [], [cs_t], "cs")
                        K.dma("sp", sinb[:, :n], sin_d[:, c0:c0 + n], [], [cs_t], "cs")
                    lin_fm(None, wga, 0, 32, hT, hT_t, b5, c0, n, 6, wga_t)
                    K.op("act", [pt[6]], [st_t[b5]], lambda h: h.activation(out=aT[:, c0:c0 + n], in_=ps[6][:32, :n], func=AF.Copy))
                    for hp in range(2):
                        for which in range(2):
                            base = which * 512
                            dst = qr if which == 0 else kr
                            u = cnt % 2
                            cnt += 1
                            pa, pbk = (0, 1) if u == 0 else (2, 3)
                            lin_fm(None, wg, base + hp * 128, 128, hT, hT_t, b5, c0, n, pa, w0_t)
                            if lat:
                                lin_fm(None, wg, base + 256 + hp * 128, 128, hT, hT_t, b5, c0, n, pbk, w0_t)
                                K.op("dve", [pt[pa], cs_t], [r_t[u]], lambda h: h.tensor_tensor(out=t1[u][:, :n], in0=ps[pa][:, :n], in1=cosb[:, :n], op=ALU.mult))
                                K.op("dve", [pt[pbk], cs_t, r_t[u]], [r_t[u]], lambda h: h.tensor_tensor(out=t2[u][:, :n], in0=ps[pbk][:, :n], in1=sinb[:, :n], op=ALU.mult))
                                K.op("pool", [r_t[u]], [st_t[b5]], lambda h: h.tensor_tensor(out=dst[:, hp, c0:c0 + n], in0=t1[u][:, :n], in1=t2[u][:, :n], op=ALU.add))
                            else:
                                K.op("act", [pt[pa]], [st_t[b5]], lambda h: h.activation(out=dst[:, hp, c0:c0 + n], in_=ps[pa][:, :n], func=AF.Copy))
                    for tt in range(n // 128):
                        pv = 4 + tt % 2
                        K.mm([wgv_t, hT_t[b5]], [pt[pv]], ps[pv][:, :384],
                             [(hT[:, kc, c0 + tt * 128:c0 + (tt + 1) * 128], wgv[:, kc, :]) for kc in range(8)])
                        if lat:
                            K.op("act", [pt[pv]], [stg_t], lambda h, tt=tt, pv=pv: h.activation(out=gvs[:, tt, :], in_=ps[pv][:, :384], func=AF.Copy))
                        else:
                            K.op("act", [pt[pv]], [gvc_t], lambda h, tt=tt, pv=pv: h.activation(out=gvc[:, tt, :], in_=ps[pv][:, :384], func=AF.Copy))
                    if lat or need_ctx:
                        for hd in range(4):
                            pv = 4 + hd % 2
                            lin_fm(None, wgg, hd * 96, 96, hT, hT_t, b5, c0, n, pv, wgg_t)
                            if lat:
                                K.op("act", [pt[pv]], [stg_t], lambda h, hd=hd, pv=pv: h.activation(out=sgs[:, hd, :n], in_=ps[pv][:96, :n], func=AF.Silu))
                            else:
                                K.op("act", [pt[pv]], [gvc_t], lambda h, hd=hd, pv=pv: h.activation(out=sgc[:, hd, :n], in_=ps[pv][:96, :n], func=AF.Silu))
                    if lat:
                        for tt in range(4):
                            K.op("pool", [stg_t], [hT_t[b5]], lambda h, tt=tt: h.tensor_copy(out=hT[:, tt, c0:c0 + 384], in_=gvs[:, tt, :]))
                        for hd in range(4):
                            K.op("pool", [stg_t], [hT_t[b5]], lambda h, hd=hd: h.tensor_copy(out=hT[:96, 4 + hd, c0:c0 + 512], in_=sgs[:, hd, :]))
                K.barrier()
            oF = K.sb(ph, "gl_oF", [96, 4, NT], BF16)
            oF_t = [Tok() for i in range(5)]
            R = []
            for d in range(2):
                r_ = dict(
                    e1=K.sb(ph, "gl_e1_%d" % d, [128, 512], F32),
                    bpos=K.sb(ph, "gl_bp_%d" % d, [128, 512], F32), eb=K.sb(ph, "gl_eb_%d" % d, [128, 512], F32),
                    qin=K.sb(ph, "gl_qin_%d" % d, [128, 2, 512], BF16), kin=K.sb(ph, "gl_kin_%d" % d, [128, 2, 512], BF16),
                    ktok=K.sb(ph, "gl_ktok_%d" % d, [128, 4, 2, 128], BF16),
                    dec=K.sb(ph, "gl_dec_%d" % d, [128, 2, 4], F32), AM=K.sb(ph, "gl_AM_%d" % d, [128, 4, 128], BF16),
                    S=K.sb(ph, "gl_S_%d" % d, [128, 2, 192], F32), Sb=K.sb(ph, "gl_Sb_%d" % d, [128, 2, 192], BF16),
                    osum=K.sb(ph, "gl_osum_%d" % d, [96, 4, NB], F32),
                    sg_t=Tok(), n_t=Tok(),
                    g_t=Tok(), qk_t=Tok(), ktok_t=Tok(), gv_t=Tok(), dec_t=Tok(), AM_t=Tok(), S_t=Tok(), Sb_t=Tok(), os_t=Tok(),
                    BA=3 * d, BB=3 * d + 1, BO=3 * d + 2)
                R.append(r_)

            def block_prep(d, b5):
                r_ = R[d]
                c0, n = blk_cols(b5)
                nt = n // 128
                BB = r_["BB"]
                e1, bpos, eb = r_["e1"], r_["bpos"], r_["eb"]
                lsb = e1
                enb = eb
                gt = r_["g_t"]
                for hp in range(2):
                    K.mm([st_t[b5], w_t], [pt[BB]], ps[BB][:, :n], [(aw[:, d, hp * 128:(hp + 1) * 128], aT[:, c0:c0 + n])])
                    yield
                    K.op("act", [pt[BB], w_t], [gt], lambda h: h.activation(
                        out=e1[:, :n], in_=ps[BB][:, :n], func=AF.Exp, bias=nab[:, d, hp:hp + 1], scale=-1.0))
                    yield
                    K.op("act", [gt, w_t], [gt], lambda h: h.activation(out=lsb[:, :n], in_=e1[:, :n], func=AF.Ln, bias=oneb[:], scale=1.0))
                    yield
                    K.op("dve", [gt, w_t], [gt], lambda h: h.tensor_tensor_scan(
                        out=bpos[:, :n], data0=mscan[:, :n], data1=lsb[:, :n], initial=0.0, op0=ALU.mult, op1=ALU.add))
                    yield
                    bsel = bpos
                    if d == 1:
                        K.op("dve", [gt], [gt], lambda h: h.tensor_tensor(out=e1[:, :n], in0=lsb[:, :n], in1=bpos[:, :n], op=ALU.subtract))
                        yield
                        K.op("dve", [gt], [gt], lambda h: h.tensor_tensor(
                            out=e1[:, :n].rearrange("p (a b) -> p a b", b=128), in0=e1[:, :n].rearrange("p (a b) -> p a b", b=128),
                            in1=bpos[:, :n].rearrange("p (a b) -> p a b", b=128)[:, :, 127:128].to_broadcast([128, n // 128, 128]), op=ALU.add))
                        yield
                        bsel = e1
                    K.op("act", [gt], [gt], lambda h, bsel=bsel: h.activation(out=eb[:, :n], in_=bsel[:, :n], func=AF.Exp, scale=-1.0 / 16))
                    yield
                    sel = 127 if d == 0 else 0
                    K.op("dve", [gt], [r_["dec_t"]], lambda h: h.tensor_copy(
                        out=r_["dec"][:, hp, 0:nt], in_=eb[:, :n].rearrange("p (a b) -> p a b", b=128)[:, :, sel]))
                    yield
                    K.op("dve", [gt, st_t[b5]], [r_["qk_t"]], lambda h: h.scalar_tensor_tensor(
                        out=r_["qin"][:, hp, :n], in0=qr[:, hp, c0:c0 + n], scalar=QS, in1=eb[:, :n], op0=ALU.mult, op1=ALU.mult))
                    yield
                    K.op("act", [gt, r_["dec_t"], r_["qk_t"]], [gt], lambda h, bsel=bsel: h.activation(out=enb[:, :n], in_=bsel[:, :n], func=AF.Exp, scale=1.0 / 16))
                    yield
                    K.op("dve", [gt, st_t[b5]], [r_["qk_t"]], lambda h: h.tensor_tensor(
                        out=r_["kin"][:, hp, :n], in0=kr[:, hp, c0:c0 + n], in1=enb[:, :n], op=ALU.mult))
                    yield
                    for tt in range(nt):
                        K.op("pe", [r_["qk_t"], t_const], [ptb], lambda h, tt=tt: h.transpose(
                            psb[:, d * 512 + tt * 128:d * 512 + (tt + 1) * 128], r_["kin"][:, hp, tt * 128:(tt + 1) * 128], identb[:]))
                        yield
                    K.op("act", [ptb], [r_["ktok_t"]], lambda h: h.activation(
                        out=r_["ktok"][:, 0:nt, hp, :], in_=psb[:, d * 512:d * 512 + nt * 128].rearrange("p (t f) -> p t f", t=nt), func=AF.Copy))
                    yield

            def scan_tile(d, b5, tt, want_o, second):
                r_ = R[d]
                c0, n = blk_cols(b5)
                tc0 = tt * 128
                BA, BB, BO = r_["BA"], r_["BB"], r_["BO"]
                qin, kin, Sb, S, AM, dec, ktok = r_["qin"], r_["kin"], r_["Sb"], r_["S"], r_["AM"], r_["dec"], r_["ktok"]
                if b5 < 4:
                    gv_tok = hT_t[b5]

                    def gvf(col, w):
                        return hT[:, tt, c0 + col:c0 + col + w]
                else:
                    gv_tok = gvc_t

                    def gvf(col, w):
                        return gvc[:, tt, col:col + w]
                if want_o:
                    for hd in (0, 2, 1, 3):
                        hp, r0 = hd // 2, (hd % 2) * 64
                        pa = BA if hd % 2 == 0 else BB
                        K.mm([r_["qk_t"]], [pt[pa]], ps[pa][:, (hd // 2) * 128:(hd // 2 + 1) * 128],
                             [(kin[r0:r0 + 48, hp, tc0:tc0 + 128], qin[r0:r0 + 48, hp, tc0:tc0 + 128])])
                        yield
                    for par, pa in ((0, BA), (1, BB)):
                        K.op("dve", [pt[pa], w_t], [r_["AM_t"]], lambda h, par=par, pa=pa: h.tensor_tensor(
                            out=AM[:, par:4:2, :], in0=ps[pa][:, 0:256].rearrange("p (a b) -> p a b", b=128),
                            in1=msk[:, d, :, :], op=ALU.mult))
                        yield
                    for hd in range(4):
                        hp, r0 = hd // 2, (hd % 2) * 64
                        K.mm([r_["AM_t"], gv_tok, r_["Sb_t"], r_["qk_t"]], [pt[BO]], ps[BO][:96, hd * 128:(hd + 1) * 128],
                             [(gvf(hd * 96, 96), AM[:, hd, :]),
                              (Sb[r0:r0 + 48, hp, (hd % 2) * 96:(hd % 2 + 1) * 96], qin[r0:r0 + 48, hp, tc0:tc0 + 128])])
                        yield
                    ov = ps[BO][:96, :].rearrange("p (a b) -> p a b", b=128)
                    if not second:
                        K.op("act", [pt[BO]], [oF_t[b5]], lambda h: h.activation(
                            out=oF[:, :, c0 + tc0:c0 + tc0 + 128], in_=ov, func=AF.Copy))
                    else:
                        K.op("dve", [pt[BO], oF_t[b5]], [r_["os_t"]], lambda h: h.tensor_tensor(
                            out=r_["osum"][:, :, (tt % 2) * 128:(tt % 2 + 1) * 128], in0=ov, in1=oF[:, :, c0 + tc0:c0 + tc0 + 128], op=ALU.add))
                    yield
                for hp in range(2):
                    K.mm([r_["ktok_t"], gv_tok], [pt[BA]], ps[BA][:, hp * 192:(hp + 1) * 192],
                         [(ktok[:, tt, hp, :], gvf(hp * 192, 192))])
                    yield
                for hp in range(2):
                    K.op("act", [r_["S_t"], r_["dec_t"]], [r_["S_t"]], lambda h, hp=hp: h.activation(
                        out=S[:, hp, :], in_=S[:, hp, :], func=AF.Identity, scale=dec[:, hp, tt:tt + 1]))
                    yield
                    K.op("dve", [pt[BA], r_["S_t"], r_["dec_t"]], [r_["S_t"]], lambda h, hp=hp: h.scalar_tensor_tensor(
                        out=S[:, hp, :], in0=ps[BA][:, hp * 192:(hp + 1) * 192], scalar=dec[:, hp, tt:tt + 1], in1=S[:, hp, :],
                        op0=ALU.mult, op1=ALU.add))
                    yield
                K.op("act", [r_["S_t"]], [r_["Sb_t"]], lambda h: h.activation(out=Sb[:], in_=S[:], func=AF.Copy))
                yield

            def finalize(d, b5, half):
                r_ = R[d]
                c0 = blk_cols(b5)[0] + half * 256
                n = 256
                r = 0 if b5 < 4 else 1
                osum = r_["osum"]
                BA, BB, BO = r_["BA"], r_["BB"], r_["BO"]
                if b5 < 4:
                    sg_t = hT_t[b5]

                    def sgf(hd):
                        return hT[:96, 4 + hd, c0:c0 + n]
                else:
                    sg_t = gvc_t

                    def sgf(hd):
                        return sgc[:, hd, :n]
                sq = r_["AM"][:96, 0:2, :].rearrange("p a b -> p (a b)")
                am_t = r_["AM_t"]
                rt = r_["e1"]
                n_t, gt = r_["n_t"], r_["g_t"]
                for hd in range(4):
                    K.op("act", [r_["os_t"]], [n_t, am_t], lambda h, hd=hd: h.activation(out=sq[:, :n], in_=osum[:, hd, :n], func=AF.Square))
                    yield
                    K.mm([n_t, am_t, t_const], [pt[BB]], ps[BB][:96, :n], [(onesb[:96, :96], sq[:, :n])])
                    yield
                    K.op("act", [pt[BB], w_t], [n_t, gt], lambda h: h.activation(
                        out=rt[:96, :n], in_=ps[BB][:96, :n], func=AF.Ln, bias=epsb[:96, :], scale=1.0 / 96))
                    yield
                    K.op("act", [n_t], [n_t, gt], lambda h: h.activation(out=rt[:96, :n], in_=rt[:96, :n], func=AF.Exp, scale=-0.5))
                    yield
                    K.op("dve", [n_t, gt, w_t], [r_["os_t"]], lambda h, hd=hd: h.scalar_tensor_tensor(
                        out=osum[:, hd, :n], in0=osum[:, hd, :n], scalar=gon[:, 0:1], in1=rt[:96, :n], op0=ALU.mult, op1=ALU.mult))
                    yield
                    K.op("dve", [r_["os_t"], sg_t], [sg_t], lambda h, hd=hd: h.tensor_tensor(
                        out=sgf(hd), in0=osum[:, hd, :n], in1=sgf(hd), op=ALU.mult))
                    yield
                for dt in range(8):
                    pc = BA if dt % 2 == 0 else BO
                    K.mm([sg_t, w2_t], [pt[pc]], ps[pc][:, :n], [(wo[:, hd, dt * 128:(dt + 1) * 128], sgf(hd)) for hd in range(4)])
                    yield
                    K.op("dve", [pt[pc], t_mod, xT_t[b5]], [xT_t[b5]], lambda h, dt=dt, pc=pc: h.scalar_tensor_tensor(
                        out=xT[:, dt, c0:c0 + n], in0=ps[pc][:, :n], scalar=modT[:, 16 + dt, r:r + 1],
                        in1=xT[:, dt, c0:c0 + n], op0=ALU.mult, op1=ALU.add))
                    yield

            def dir_gen(d):
                r_ = R[d]
                K.op("dve", [], [r_["S_t"]], lambda h: h.memset(r_["S"][:], 0.0))
                K.op("dve", [], [r_["Sb_t"]], lambda h: h.memset(r_["Sb"][:], 0.0))
                order = [4] + ([0, 1, 2, 3] if d == 0 else [3, 2, 1, 0])
                for b5 in order:
                    want_o = (b5 < 4) or need_ctx
                    if b5 == 4:
                        second = (d == 1)
                    else:
                        second = (b5 >= 2) if d == 0 else (b5 <= 1)
                    nt = blk_cols(b5)[1] // 128
                    yield from block_prep(d, b5)
                    tiles = list(range(nt)) if d == 0 else list(range(nt - 1, -1, -1))
                    for i_, tt in enumerate(tiles):
                        yield from scan_tile(d, b5, tt, want_o, second)
                        if want_o and second and i_ % 2 == 1:
                            yield from finalize(d, b5, tt // 2)
                    yield "STEP"

            def run_step(gens):
                active = list(gens)
                while active:
                    for g in list(active):
                        if next(g) == "STEP":
                            active.remove(g)
            gF, gB = dir_gen(0), dir_gen(1)
            run_step([gF])
            run_step([gB])
            for s_ in range(4):
                run_step([gF, gB])
            K.barrier()

    def mixers(l, hT, hT_t, need_ctx):
        nm = lambda: norm_mod(l, "mix", hT, hT_t, range(5))
        with contextlib.ExitStack() as pw:
            wkv = None
            if False:
                wk = K.sb(pw, "na_wk", [128, 8, 384], BF16)
                wv = K.sb(pw, "na_wv", [128, 8, 384], BF16)
                wk_t = Tok()
                K.dma("pool", wk[:], win_d[l][:, :, 640:1024], [], [wk_t], "naw")
                K.dma("pool", wv[:], win_d[l][:, :, 1024:1408], [], [wk_t], "naw")
                wkv = (wk, wv, wk_t)
            if "nofn" not in debug:
                fnet(l, hT, hT_t, need_ctx, pre=nm)
            else:
                nm()
            if "nona" not in debug:
                natt(l, hT, hT_t, need_ctx, wkv=wkv)
            K.barrier()
        if "nogla" not in debug:
            gla(l, hT, hT_t, need_ctx)

    for l in range(2):
        need_ctx = (l == 0)
        modT.t, amix.t, affn.t = modTs[l], amixs[l], affns[l]
        with contextlib.ExitStack() as lay:
            hT = K.sb(lay, "hT", [128, 8, NT], BF16)
            hT_t = [Tok("hT%d" % i) for i in range(5)]
            if "ffn" not in debug:
                mixers(l, hT, hT_t, need_ctx)
            blocks = range(5) if need_ctx else range(4)
            if "noffn" not in debug:
                ffn(l, hT, hT_t, blocks)
            K.barrier()
        if "ffn" in debug or "l0" in debug:
            break

    t_out = Tok("out")
    for name, (buf, shape, dt) in dbg.items():
        dd = K.dram("dbg_" + name, shape, dt, kind="ExternalOutput")
        K.barrier()
        K.dma("sp", dd, buf[:], [], [t_out], "out")
        K.outs["dbg_" + name] = shape

    for b in range(4):
        c0, n = blk_cols(b)
        for hf in range(2):
            K.dma("sp", out_d[:, hf * 4:(hf + 1) * 4, c0:c0 + n], xT[:, hf * 4:(hf + 1) * 4, c0:c0 + n], [xT_t[b]], [t_out], "out")
    S = t_out.dsem["sp"]
    nc.sync.wait_ge(S.sem, S.count * 16)
    es.close()
    return K


_CONSTS = {}


def _consts():
    if _CONSTS:
        return _CONSTS
    bf = ml_dtypes.bfloat16
    t = np.arange(T, dtype=np.int64)
    ang = 2.0 * np.pi * ((t[:, None] * t[None, :]) % T).astype(np.float64) / T
    _CONSTS["c_ct"] = np.ascontiguousarray(np.cos(ang).reshape(16, 128, T).transpose(1, 0, 2)).astype(bf)
    _CONSTS["c_st"] = np.ascontiguousarray(np.sin(ang).reshape(16, 128, T).transpose(1, 0, 2)).astype(bf)
    t2 = np.arange(NCTX, dtype=np.int64)
    ang2 = 2.0 * np.pi * ((t2[:, None] * t2[None, :]) % NCTX).astype(np.float64) / NCTX
    _CONSTS["c_c256"] = np.ascontiguousarray(np.cos(ang2).reshape(2, 128, NCTX).transpose(1, 0, 2)).astype(bf)
    _CONSTS["c_s256"] = np.ascontiguousarray(np.sin(ang2).reshape(2, 128, NCTX).transpose(1, 0, 2)).astype(bf)
    g = np.arange(64)
    a64 = 2.0 * np.pi * ((g[:, None] * g[None, :]) % 64) / 64.0
    c64 = np.zeros((128, 4, 128))
    for blk in range(2):
        sl = slice(64 * blk, 64 * blk + 64)
        c64[sl, 0, sl] = np.cos(a64) / np.sqrt(T * 64.0)
        c64[sl, 1, sl] = -np.sin(a64) / np.sqrt(T * 64.0)
        c64[sl, 2, sl] = np.cos(a64) / np.sqrt(NCTX * 64.0)
        c64[sl, 3, sl] = -np.sin(a64) / np.sqrt(NCTX * 64.0)
    _CONSTS["c_c64"] = c64.astype(bf)
    cosT = np.zeros((128, T), np.float32)
    sinT = np.zeros((128, T), np.float32)
    row = (t // 64).astype(np.float64)
    col = (t % 64).astype(np.float64)
    for p in range(128):
        d = p % 64
        if d >= 48:
            continue
        within = d % 24
        j = within % 12
        inv = 10000.0 ** (-j / 12.0)
        pos = row if d < 24 else col
        cosT[p] = np.cos(pos * inv)
        sinT[p] = np.sin(pos * inv) * (-1.0 if within < 12 else 1.0)
    _CONSTS["c_cos"] = cosT
    _CONSTS["c_sin"] = sinT
    s_ = np.arange(128)
    gm = np.zeros((128, 2, 2, 128), np.float32)
    gm[:, 0, :, :] = (s_[:, None] <= s_[None, :])[:, None, :]
    gm[:, 1, :, :] = (s_[:, None] >= s_[None, :])[:, None, :]
    _CONSTS["c_gmask"] = gm.astype(bf)
    ms = np.ones((128, 512), np.float32)
    ms[:, 0::128] = 0.0
    _CONSTS["c_mscan"] = ms.astype(bf)
    return _CONSTS


def _host_prep(inputs):
    f = lambda a: np.ascontiguousarray(np.asarray(a, dtype=np.float32))
    x, c, ctx, c_ctx = f(inputs["x"]), f(inputs["c"]), f(inputs["ctx"]), f(inputs["c_ctx"])
    shared = {}
    shared["ada_w"] = f(f(inputs["ada_w"]).reshape(2, 8, 128, 6144).transpose(0, 2, 1, 3))
    shared["ada_b"] = f(f(inputs["ada_b"]).reshape(2, 48, 128).transpose(0, 2, 1))
    shared["norm_mix"] = f(f(inputs["norm_mix"]).reshape(2, 8, 128).transpose(0, 2, 1))
    shared["norm_ffn"] = f(f(inputs["norm_ffn"]).reshape(2, 8, 128).transpose(0, 2, 1))
    shared["ident"] = np.eye(128, dtype=np.float32)
    shared["ffn_w1"] = f(f(inputs["ffn_w1"]).reshape(2, 8, 128, DFF).transpose(0, 2, 1, 3))
    shared["ffn_w3"] = f(f(inputs["ffn_w3"]).reshape(2, 8, 128, DFF).transpose(0, 2, 1, 3))
    shared["ffn_w2"] = f(f(inputs["ffn_w2"]).reshape(2, 22, 128, D).transpose(0, 2, 1, 3))
    w_in = f(inputs["w_in"])
    shared["w_in"] = f(w_in.reshape(2, 8, 128, NIN).transpose(0, 2, 1, 3))
    wg = np.zeros((2, D, 1024), np.float32)
    dd = np.arange(48)
    partner = np.where((dd % 24) < 12, dd + 12, dd - 12)
    for hh in range(4):
        wg[:, :, 64 * hh:64 * hh + 48] = w_in[:, :, 1408 + 48 * hh + dd]
        wg[:, :, 256 + 64 * hh:256 + 64 * hh + 48] = w_in[:, :, 1408 + 48 * hh + partner]
        wg[:, :, 512 + 64 * hh:512 + 64 * hh + 48] = w_in[:, :, 1600 + 48 * hh + dd]
        wg[:, :, 768 + 64 * hh:768 + 64 * hh + 48] = w_in[:, :, 1600 + 48 * hh + partner]
    shared["w_g"] = f(wg.reshape(2, 8, 128, 1024).transpose(0, 2, 1, 3))
    shared["fnet_w"] = f(f(inputs["fnet_w"]).reshape(2, 2, 128, 256).transpose(0, 2, 1, 3))
    w_out = f(inputs["w_out"])
    shared["w_out"] = f(w_out.reshape(2, 8, 128, D).transpose(0, 2, 1, 3))
    shared["w_out_g"] = f(w_out[:, 640:, :].reshape(2, 4, 96, D).transpose(0, 2, 1, 3))
    shared["na_q"] = f(np.tile(f(inputs["na_q_norm"]), (1, 2))[:, :, None])
    shared["na_k"] = f(np.tile(f(inputs["na_k_norm"]), (1, 2))[:, :, None])
    rpb = f(inputs["na_rpb"])
    pk = np.arange(128)
    a_, kc_ = pk // 64, pk % 64
    b_, c_ = pk // 64, pk % 64
    dcm = np.clip(kc_[:, None] - c_[None, :], -15, 15) + 15
    cq0 = np.clip(c_ - 8, 0, 48)
    col_ok = (kc_[:, None] >= cq0[None, :]) & (kc_[:, None] < cq0[None, :] + 16)
    bt = np.full((2, 128, 72, 128), NEG, np.float32)
    ents = [(dl, 0) for dl in range(-3, 4)] + [(-2, 1), (-1, 0), (0, 0), (1, 0), (2, 1)]
    for e, (dl, msk_) in enumerate(ents):
        drm = 2 * dl + a_[:, None] - b_[None, :] + 7
        ok = col_ok & (drm >= 0) & (drm <= 14)
        if msk_ and dl == -2:
            ok = ok & ((2 * dl + a_[:, None]) >= (-4 + b_[None, :]))
        if msk_ and dl == 2:
            ok = ok & ((2 * dl + a_[:, None]) <= (3 + b_[None, :]))
        drc = np.clip(drm, 0, 14)
        for hh in range(6):
            vals = rpb[:, hh][:, drc, dcm]
            bt[:, :, hh * 12 + e, :] = np.where(ok[None], vals, NEG)
    shared["na_bt"] = bt
    aw = np.zeros((2, 32, 2, 256), np.float32)
    ab = np.zeros((2, 128, 2, 2), np.float32)
    gaw = f(inputs["gla_alpha_w"]); gab = f(inputs["gla_alpha_b"])
    for dr_ in range(2):
        for hh in range(4):
            aw[:, 16 * dr_:16 * dr_ + 16, dr_, 64 * hh:64 * hh + 48] = gaw[:, dr_, :, 48 * hh:48 * hh + 48]
            ab[:, 64 * (hh % 2):64 * (hh % 2) + 48, dr_, hh // 2] = gab[:, dr_, 48 * hh:48 * hh + 48]
    shared["gla_aw"] = aw
    shared["gla_ab"] = ab
    shared["gla_gon"] = f(f(inputs["gla_o_norm"])[:, :, None])
    shared.update(_consts())
    per_core = []
    for b in range(8):
        m = dict(shared)
        m["x"] = f(x[b].T.reshape(8, 128, T).transpose(1, 0, 2))
        m["ctx"] = f(ctx[b].T.reshape(8, 128, NCTX).transpose(1, 0, 2))
        ccv = np.stack([c[b], c_ctx], axis=-1)
        m["cc"] = f(ccv.reshape(8, 128, 2).transpose(1, 0, 2))
        per_core.append(m)
    return per_core


_DEBUG = tuple(x for x in os.environ.get("KDEBUG", "").split(",") if x)
LAST = {}


def kernel(**inputs):
    in_maps = _host_prep(inputs)
    K = build_program(debug=_DEBUG)
    ncores = int(os.environ.get("KCORES", "8"))
    res = run_bass_kernel_spmd(K.nc, in_maps[:ncores], core_ids=list(range(ncores)))
    LAST["res"] = res
    outs = []
    for r in res.results:
        o = np.asarray(r["out"])
        outs.append(o.transpose(1, 0, 2).reshape(D, T).T)
    return np.ascontiguousarray(np.stack(outs, axis=0)).astype(np.float32)
```

```python
import contextlib
import os
import numpy as np
import ml_dtypes
import concourse.bass as bass
import concourse.mybir as mybir
from concourse.bass_utils import run_bass_kernel_spmd

F32 = mybir.dt.float32
BF16 = mybir.dt.bfloat16
AF = mybir.ActivationFunctionType
ALU = mybir.AluOpType

D = 1024
T = 2048
NCTX = 256
NT = T + NCTX
DFF = 2816
NIN = 2592
EPS = 1e-6
NEG = -30000.0
SAME_ENGINE_RAW = True


class Tok:
    __slots__ = ("w", "r", "dsem", "name")

    def __init__(self, name=""):
        self.w = []
        self.r = {}
        self.dsem = None
        self.name = name


class Src:
    def __init__(self, name, sem, unit, h=None):
        self.name = name
        self.sem = sem
        self.unit = unit
        self.h = h
        self.count = 0
        self.seen = {}


class Ctx:
    def __init__(self):
        self.nc = bass.Bass("TRN2", target_bir_lowering=False)
        nc = self.nc
        self.es = contextlib.ExitStack()
        self.engs = {}
        for nm, h in (("pe", nc.tensor), ("act", nc.scalar), ("dve", nc.vector),
                      ("pool", nc.gpsimd), ("sp", nc.sync)):
            sem = self.es.enter_context(nc.semaphore("sem_" + nm))
            self.engs[nm] = Src(nm, sem, 1, h)
        self.dsems = []
        self.outs = {}
        self.n_inst = 0

    def dram(self, name, shape, dt, kind="ExternalInput"):
        return self.nc.dram_tensor(name, list(shape), dt, kind=kind).ap()

    def sb(self, stack, name, shape, dt):
        self.n_sb = getattr(self, "n_sb", 0) + 1
        return stack.enter_context(self.nc.sbuf_tensor("sb%d_%s" % (self.n_sb, name), list(shape), dt))

    def new_dsem(self, name):
        sem = self.es.enter_context(self.nc.semaphore("d_" + name + str(len(self.dsems))))
        s = Src("dma_" + name, sem, 16)
        self.dsems.append(s)
        return s

    def _wait_deps(self, E, reads, writes):
        deps = {}
        for t in reads:
            for (s, n) in t.w:
                deps[s] = max(deps.get(s, 0), n)
        for t in writes:
            for (s, n) in t.w:
                deps[s] = max(deps.get(s, 0), n)
            for s, n in t.r.items():
                deps[s] = max(deps.get(s, 0), n)
        for s, n in deps.items():
            if s is E and (E.name == "pe" or not SAME_ENGINE_RAW):
                continue
            if E.seen.get(s, 0) < n:
                E.h.wait_ge(s.sem, n * s.unit)
                E.seen[s] = n

    def op(self, eng, reads, writes, emit):
        E = self.engs[eng]
        self._wait_deps(E, reads, writes)
        ins = emit(E.h)
        E.count += 1
        ins.then_inc(E.sem, 1)
        self.n_inst += 1
        for t in reads:
            t.r[E] = E.count
        for t in writes:
            t.w = [(E, E.count)]
            t.r = {}
        return ins

    def mm(self, reads, writes, out, pairs, first=True, last=True):
        E = self.engs["pe"]
        self._wait_deps(E, reads, writes)
        n = len(pairs)
        ins = None
        for i, (l, r) in enumerate(pairs):
            ins = E.h.matmul(out, l, r, start=(first and i == 0), stop=(last and i == n - 1))
            self.n_inst += 1
        E.count += 1
        ins.then_inc(E.sem, 1)
        for t in reads:
            t.r[E] = E.count
        for t in writes:
            t.w = [(E, E.count)]
            t.r = {}

    def transpose(self, reads, writes, out, in_, ident):
        return self.op("pe", reads, writes, lambda h: h.transpose(out, in_, ident))

    def dma(self, q, out, in_, reads, writes, name="x"):
        E = self.engs[q]
        self._wait_deps(E, reads, writes)
        wt = writes[0]
        if wt.dsem is None:
            wt.dsem = {}
        if q not in wt.dsem:
            wt.dsem[q] = self.new_dsem(name + q)
        S = wt.dsem[q]
        ins = E.h.dma_start(out=out, in_=in_)
        S.count += 1
        ins.then_inc(S.sem, 16)
        self.n_inst += 1
        for t in reads:
            t.r[S] = S.count
        for t in writes:
            t.w = [(s_, n_) for (s_, n_) in t.w if (s_.unit == 16 and s_ is not S)] + [(S, S.count)]
            t.r = {}

    def barrier(self):
        allsrc = list(self.engs.values()) + self.dsems
        for E in self.engs.values():
            for s in allsrc:
                if s is E or s.count == 0:
                    continue
                if E.seen.get(s, 0) < s.count:
                    E.h.wait_ge(s.sem, s.count * s.unit)
                    E.seen[s] = s.count


def build_program(debug=()):
    K = Ctx()
    nc = K.nc
    es = K.es
    dbg = {}

    x_d = K.dram("x", [128, 8, T], F32)
    ctx_d = K.dram("ctx", [128, 8, NCTX], F32)
    cc_d = K.dram("cc", [128, 8, 2], F32)
    adaw_d = K.dram("ada_w", [2, 128, 8, 6144], F32)
    adab_d = K.dram("ada_b", [2, 128, 48], F32)
    nmix_d = K.dram("norm_mix", [2, 128, 8], F32)
    nffn_d = K.dram("norm_ffn", [2, 128, 8], F32)
    out_d = K.dram("out", [128, 8, T], F32, kind="ExternalOutput")
    ident_d = K.dram("ident", [128, 128], F32)
    w1_d = K.dram("ffn_w1", [2, 128, 8, DFF], F32)
    win_d = K.dram("w_in", [2, 128, 8, NIN], F32)
    wg_d = K.dram("w_g", [2, 128, 8, 1024], F32)
    fnetw_d = K.dram("fnet_w", [2, 128, 2, 256], F32)
    wout_d = K.dram("w_out", [2, 128, 8, D], F32)
    woutg_d = K.dram("w_out_g", [2, 96, 4, D], F32)
    naq_d = K.dram("na_q", [2, 128, 1], F32)
    nak_d = K.dram("na_k", [2, 128, 1], F32)
    bt_d = K.dram("na_bt", [2, 128, 72, 128], F32)
    aw_d = K.dram("gla_aw", [2, 32, 2, 256], F32)
    ab_d = K.dram("gla_ab", [2, 128, 2, 2], F32)
    gon_d = K.dram("gla_gon", [2, 96, 1], F32)
    ct_d = K.dram("c_ct", [128, 16, T], BF16)
    st_d = K.dram("c_st", [128, 16, T], BF16)
    c256_d = K.dram("c_c256", [128, 2, 256], BF16)
    s256_d = K.dram("c_s256", [128, 2, 256], BF16)
    c64_d = K.dram("c_c64", [128, 4, 128], BF16)
    cos_d = K.dram("c_cos", [128, T], F32)
    sin_d = K.dram("c_sin", [128, T], F32)
    gmask_d = K.dram("c_gmask", [128, 2, 2, 128], BF16)
    mscan_d = K.dram("c_mscan", [128, 512], BF16)
    w3_d = K.dram("ffn_w3", [2, 128, 8, DFF], F32)
    w2_d = K.dram("ffn_w2", [2, 128, 22, D], F32)

    top = es
    xT = K.sb(top, "xT", [128, 8, NT], F32)
    xT_t = [Tok("xT%d" % i) for i in range(5)]
    identf = K.sb(top, "identf", [128, 128], F32)
    identb = K.sb(top, "identb", [128, 128], BF16)
    onesb = K.sb(top, "onesb", [128, 128], BF16)
    cc = K.sb(top, "cc", [128, 8, 2], F32)
    scb = K.sb(top, "scb", [128, 8, 2], BF16)
    class Cur:
        def __init__(self):
            self.t = None

        def __getitem__(self, k):
            return self.t[k]
    modT = Cur()
    modTs = [K.sb(top, "modT%d" % i, [128, 48, 2], F32) for i in range(2)]
    amix = Cur()
    affn = Cur()
    amixs = [K.sb(top, "amix%d" % i, [128, 8, 2], F32) for i in range(2)]
    affns = [K.sb(top, "affn%d" % i, [128, 8, 2], F32) for i in range(2)]
    adabs = [K.sb(top, "adab%d" % i, [128, 48], F32) for i in range(2)]
    nmixs = [K.sb(top, "nmix%d" % i, [128, 8], F32) for i in range(2)]
    nffns = [K.sb(top, "nffn%d" % i, [128, 8], F32) for i in range(2)]
    t_const = Tok("const")
    t_mod = Tok("mod")
    t_small = Tok("small")

    ps = [es.enter_context(nc.psum_tensor("ps%d" % i, [128, 512], F32)) for i in range(7)]
    pt = [Tok("ps%d" % i) for i in range(7)]
    psb = es.enter_context(nc.psum_tensor("psb", [128, 1024], BF16))
    ptb = Tok("psb")

    def blk_cols(b):
        return (b * 512, 512) if b < 4 else (T, NCTX)

    K.dma("sp", identf[:], ident_d, [], [t_const], "c")
    K.op("dve", [t_const], [t_const], lambda h: h.tensor_copy(out=identb[:], in_=identf[:]))
    K.op("dve", [], [t_const], lambda h: h.memset(onesb[:], 1.0))
    K.dma("sp", cc[:], cc_d, [], [t_small], "s")
    K.op("act", [t_small], [t_small], lambda h: h.activation(out=scb[:], in_=cc[:], func=AF.Silu))

    def make_ada(l, stack):
        t_sm = Tok("adasmall")
        K.dma("sp", adabs[l][:], adab_d[l], [], [t_sm], "s")
        K.dma("sp", nmixs[l][:], nmix_d[l], [], [t_sm], "s")
        K.dma("sp", nffns[l][:], nffn_d[l], [], [t_sm], "s")
        wsl = [K.sb(stack, "adaw%d" % i, [128, 8, 512], BF16) for i in range(2)]
        wsl_t = [Tok("adaw%d" % i) for i in range(2)]

        def dma(hs):
            K.dma("pool", wsl[hs % 2][:], adaw_d[l][:, :, hs * 512:(hs + 1) * 512], [], [wsl_t[hs % 2]], "adaw")

        def sec(hs):
            s = hs % 2
            if hs == 0:
                dma(0)
            if hs + 1 < 12:
                dma(hs + 1)
            for mt in range(4):
                K.mm([wsl_t[s], t_small], [pt[6]], ps[6][:, mt * 2:mt * 2 + 2],
                     [(wsl[s][:, kc, mt * 128:(mt + 1) * 128], scb[:, kc, :]) for kc in range(8)])
            K.op("dve", [pt[6], t_sm], [t_mod], lambda h: h.tensor_tensor(
                out=modTs[l][:, hs * 4:(hs + 1) * 4, :],
                in0=ps[6][:, 0:8].rearrange("p (a b) -> p a b", b=2),
                in1=adabs[l][:, hs * 4:(hs + 1) * 4].unsqueeze(2).to_broadcast([128, 4, 2]),
                op=ALU.add))

        def tail():
            for (dst, gain, sc_) in ((amixs[l], nmixs[l], 1), (affns[l], nffns[l], 4)):
                K.op("dve", [t_mod, t_sm], [t_mod], lambda h, dst=dst, gain=gain, sc_=sc_: h.scalar_tensor_tensor(
                    out=dst[:], in0=modTs[l][:, sc_ * 8:(sc_ + 1) * 8, :], scalar=1.0,
                    in1=gain[:].unsqueeze(2).to_broadcast([128, 8, 2]),
                    op0=ALU.add, op1=ALU.mult))
        return [lambda hs=hs: sec(hs) for hs in range(12)] + [tail]

    with contextlib.ExitStack() as ph:
        ada0 = make_ada(0, ph)
        for b in range(5):
            c0, n = blk_cols(b)
            src = x_d[:, :, c0:c0 + n] if b < 4 else ctx_d[:, :, :]
            for hf in range(2):
                K.dma("sp", xT[:, hf * 4:(hf + 1) * 4, c0:c0 + n], src[:, hf * 4:(hf + 1) * 4, :], [], [xT_t[b]], "xin")
        for hs in range(12):
            ada0[hs]()
        ada0[12]()
        K.barrier()

    rot = {"ps": 0}
    t_out = Tok("out")
    out_done = set()

    def emit_out(b):
        c0, n = blk_cols(b)
        for hf in range(2):
            K.dma("sp", out_d[:, hf * 4:(hf + 1) * 4, c0:c0 + n], xT[:, hf * 4:(hf + 1) * 4, c0:c0 + n], [xT_t[b]], [t_out], "out")
        out_done.add(b)

    def norm_mod(l, which, hT, hT_t, blocks):
        a = amix if which == "mix" else affn
        sec = 0 if which == "mix" else 3
        blocks = list(blocks)
        with contextlib.ExitStack() as ph:
            sq1 = K.sb(ph, "nm_sq", [128, 8, 512], BF16)
            sq = [sq1, sq1]
            sq1_t = Tok()
            sq_t = [sq1_t, sq1_t]
            rs = [K.sb(ph, "nm_rs%d" % i, [128, 512], F32) for i in range(2)]
            rs_t = [Tok() for i in range(2)]
            tmp = [K.sb(ph, "nm_tmp%d" % i, [128, 512], F32) for i in range(3)]
            tmp_t = [Tok() for i in range(3)]
            epsb = K.sb(ph, "nm_eps", [128, 1], F32)
            eps_t = Tok()
            K.op("dve", [], [eps_t], lambda h: h.memset(epsb[:], EPS))

            def stats(bi):
                b = blocks[bi]
                c0, n = blk_cols(b)
                s = bi % 2
                pb = bi % 2
                K.op("dve", [xT_t[b]], [sq_t[s]], lambda h: h.tensor_tensor(
                    out=sq[s][:, :, :n], in0=xT[:, :, c0:c0 + n], in1=xT[:, :, c0:c0 + n], op=ALU.mult))
                K.mm([sq_t[s], t_const], [pt[pb]], ps[pb][:, :n],
                     [(onesb[:], sq[s][:, kc, :n]) for kc in range(8)])
                K.op("act", [pt[pb], eps_t], [rs_t[s]], lambda h: h.activation(
                    out=rs[s][:, :n], in_=ps[pb][:, :n], func=AF.Ln, bias=epsb[:], scale=1.0 / D))
                K.op("act", [rs_t[s]], [rs_t[s]], lambda h: h.activation(
                    out=rs[s][:, :n], in_=rs[s][:, :n], func=AF.Exp, scale=-0.5))

            ti = 0
            stats(0)
            for bi, b in enumerate(blocks):
                if bi + 1 < len(blocks):
                    stats(bi + 1)
                c0, n = blk_cols(b)
                r = 0 if b < 4 else 1
                s = bi % 2
                for kc in range(8):
                    u = ti % 3
                    ti += 1
                    eng = "pool" if kc % 4 == 3 else "dve"
                    K.op(eng, [xT_t[b], rs_t[s]], [tmp_t[u]], lambda h, u=u, kc=kc: h.tensor_tensor(
                        out=tmp[u][:, :n], in0=xT[:, kc, c0:c0 + n], in1=rs[s][:, :n], op=ALU.mult))
                    K.op("act", [tmp_t[u], t_mod], [hT_t[b]], lambda h, u=u, kc=kc: h.activation(
                        out=hT[:, kc, c0:c0 + n], in_=tmp[u][:, :n], func=AF.Identity,
                        bias=modT[:, sec * 8 + kc, r:r + 1], scale=a[:, kc, r:r + 1]))
            K.barrier()

    def ffn(l, hT, hT_t, blocks):
        with contextlib.ExitStack() as ph:
            chunks = [(i * 512, 512) for i in range(5)] + [(2560, 256)]
            w1s = [K.sb(ph, "w1s%d" % i, [128, 8, 512], BF16) for i in range(2)]
            w3s = [K.sb(ph, "w3s%d" % i, [128, 8, 512], BF16) for i in range(2)]
            w2s = [K.sb(ph, "w2s%d" % i, [128, 4, D], BF16) for i in range(2)]
            w_t = [Tok() for i in range(2)]
            s1 = [K.sb(ph, "ff_s1%d" % i, [128, 512], F32) for i in range(2)]
            s1_t = [Tok() for i in range(2)]
            g = [K.sb(ph, "ff_g%d" % i, [128, 4, 512], BF16) for i in range(2)]
            g_t = [Tok() for i in range(2)]

            def load(ch):
                f0, nf = chunks[ch]
                s = ch % 2
                K.dma("pool", w1s[s][:, :, :nf], w1_d[l][:, :, f0:f0 + nf], [], [w_t[s]], "ffw")
                K.dma("pool", w3s[s][:, :, :nf], w3_d[l][:, :, f0:f0 + nf], [], [w_t[s]], "ffw")
                K.dma("pool", w2s[s][:, :nf // 128, :], w2_d[l][:, f0 // 128:(f0 + nf) // 128, :], [], [w_t[s]], "ffw")
            load(0)
            ada_next = make_ada(l + 1, ph) if l + 1 < 2 else None
            norm_mod(l, "ffn", hT, hT_t, blocks)
            cnt = 0
            gi = 0
            oi = 0
            for ch in range(6):
                if ch + 1 < 6:
                    load(ch + 1)
                f0, nf = chunks[ch]
                s = ch % 2
                nft = nf // 128
                for b in blocks:
                    c0, n = blk_cols(b)
                    r = 0 if b < 4 else 1
                    gs = gi % 2
                    gi += 1
                    for ft in range(nft):
                        pa = cnt % 2
                        pbk = 2 + cnt % 2
                        u = cnt % 2
                        cnt += 1
                        K.mm([w_t[s], hT_t[b]], [pt[pa]], ps[pa][:, :n],
                             [(w1s[s][:, kc, ft * 128:(ft + 1) * 128], hT[:, kc, c0:c0 + n]) for kc in range(8)])
                        K.mm([w_t[s], hT_t[b]], [pt[pbk]], ps[pbk][:, :n],
                             [(w3s[s][:, kc, ft * 128:(ft + 1) * 128], hT[:, kc, c0:c0 + n]) for kc in range(8)])
                        K.op("act", [pt[pa]], [s1_t[u]], lambda h, u=u, pa=pa, n=n: h.activation(
                            out=s1[u][:, :n], in_=ps[pa][:, :n], func=AF.Silu))
                        K.op("dve", [s1_t[u], pt[pbk]], [g_t[gs]], lambda h, u=u, pbk=pbk, n=n, gs=gs, ft=ft: h.tensor_tensor(
                            out=g[gs][:, ft, :n], in0=s1[u][:, :n], in1=ps[pbk][:, :n], op=ALU.mult))
                    for dt in range(8):
                        pc = 4 + oi % 3
                        oi += 1
                        K.mm([w_t[s], g_t[gs]], [pt[pc]], ps[pc][:, :n],
                             [(w2s[s][:, ft, dt * 128:(dt + 1) * 128], g[gs][:, ft, :n]) for ft in range(nft)])
                        K.op("dve", [pt[pc], t_mod, xT_t[b]], [xT_t[b]], lambda h, pc=pc, dt=dt, c0=c0, n=n, r=r: h.scalar_tensor_tensor(
                            out=xT[:, dt, c0:c0 + n], in0=ps[pc][:, :n], scalar=modT[:, 40 + dt, r:r + 1],
                            in1=xT[:, dt, c0:c0 + n], op0=ALU.mult, op1=ALU.add))
                    if l == 1 and ch == 5 and b < 4:
                        emit_out(b)
                if ada_next is not None:
                    ada_next[2 * ch]()
                    ada_next[2 * ch + 1]()
            if ada_next is not None:
                ada_next[12]()
            K.barrier()

    def lin_fm(wt, w, col0, M, hT, hT_t, b, c0, n, pb, wtok):
        K.mm([wtok, hT_t[b]], [pt[pb]], ps[pb][:M, :n],
             [(w[:, kc, col0:col0 + M], hT[:, kc, c0:c0 + n]) for kc in range(8)])

    def out_proj_update(pairs_fn, reads, b, c0, n, r, pbs):
        for dt in range(8):
            pc = pbs[dt % len(pbs)]
            K.mm(reads, [pt[pc]], ps[pc][:, :n], pairs_fn(dt))
            K.op("dve", [pt[pc], t_mod, xT_t[b]], [xT_t[b]], lambda h, pc=pc, dt=dt: h.scalar_tensor_tensor(
                out=xT[:, dt, c0:c0 + n], in0=ps[pc][:, :n], scalar=modT[:, 16 + dt, r:r + 1],
                in1=xT[:, dt, c0:c0 + n], op0=ALU.mult, op1=ALU.add))

    def rstd_from_ps(pb, P, n, inv, rt, rs, rs_tok, eps_ap, eps_t):
        K.op("act", [pt[pb], eps_t], [rs_tok], lambda h: h.activation(
            out=rt[:P, :n], in_=ps[pb][:P, :n], func=AF.Sqrt, bias=eps_ap[:P, :], scale=inv))
        K.op("dve", [rs_tok], [rs_tok], lambda h: h.reciprocal(out=rs[:P, :n], in_=rt[:P, :n]))

    def fnet(l, hT, hT_t, need_ctx, pre=None):
        with contextlib.ExitStack() as ph:
            wfn = K.sb(ph, "fn_w", [128, 8, 256], BF16)
            fnw = K.sb(ph, "fn_fw", [128, 2, 256], BF16)
            wo = K.sb(ph, "fn_wo", [128, 2, D], BF16)
            c64 = K.sb(ph, "fn_c64", [128, 4, 128], BF16)
            w_t = Tok()
            K.dma("pool", wfn[:], win_d[l][:, :, 0:256], [], [w_t], "fnw")
            K.dma("pool", fnw[:], fnetw_d[l], [], [w_t], "fnw")
            K.dma("pool", wo[:], wout_d[l][:, 0:2, :], [], [w_t], "fnw")
            K.dma("sp", c64[:], c64_d, [], [w_t], "fnw")
            cts = [K.sb(ph, "fn_ct%d" % i, [128, 2, 16, 512], BF16) for i in range(2)]
            ct_t = [Tok() for i in range(2)]
            jobs = [(kb, 0) for kb in range(4)] + ([(0, 1)] if need_ctx else [])

            def issue_tables(ji):
                kb, isctx = jobs[ji]
                s = ji % 2
                if not isctx:
                    K.dma("sp", cts[s][:, 0, :, :], ct_d[:, :, kb * 512:(kb + 1) * 512], [], [ct_t[s]], "ct")
                    K.dma("sp", cts[s][:, 1, :, :], st_d[:, :, kb * 512:(kb + 1) * 512], [], [ct_t[s]], "ct")
                else:
                    K.dma("sp", cts[s][:, 0, 0:2, 0:256], c256_d, [], [ct_t[s]], "ct")
                    K.dma("sp", cts[s][:, 1, 0:2, 0:256], s256_d, [], [ct_t[s]], "ct")
            issue_tables(0)
            issue_tables(1)
            if pre is not None:
                pre()
            Z = K.sb(ph, "fn_Z", [128, 18, 256], BF16)
            Z_t = Tok()
            Psb = [K.sb(ph, "fn_P%d" % i, [128, 2, 512], BF16) for i in range(2)]
            P_t = [Tok() for i in range(2)]
            Ysb = K.sb(ph, "fn_Y", [128, 2, 512], BF16)
            Y_t = Tok()
            yf = K.sb(ph, "fn_yf", [128, 2, 512], BF16)
            yf_t = Tok()
            for ti in range(18):
                pb = ti % 2
                b = min(ti // 4, 4)
                K.mm([w_t, hT_t[b]], [pt[pb]], ps[pb][:, :256],
                     [(hT[:, kc, ti * 128:(ti + 1) * 128], wfn[:, kc, :]) for kc in range(8)])
                if ti % 2 == 0:
                    K.op("dve", [pt[pb]], [Z_t], lambda h, ti=ti, pb=pb: h.tensor_copy(out=Z[:, ti, :], in_=ps[pb][:, :256]))
                else:
                    K.op("act", [pt[pb]], [Z_t], lambda h, ti=ti, pb=pb: h.activation(out=Z[:, ti, :], in_=ps[pb][:, :256], func=AF.Copy))
            for ji, (kb, isctx) in enumerate(jobs):
                s = ji % 2
                if not isctx:
                    ntt, n, tt0, c0, b, r = 16, 512, 0, kb * 512, kb, 0
                else:
                    ntt, n, tt0, c0, b, r = 2, 256, 16, T, 4, 1
                for chc in range(2):
                    u = chc
                    for cs in range(2):
                        pb = cs
                        K.mm([Z_t, ct_t[s]], [pt[pb]], ps[pb][:, :n],
                             [(Z[:, tt0 + tt, chc * 128:(chc + 1) * 128], cts[s][:, cs, tt, :n]) for tt in range(ntt)])
                        if cs == 0:
                            K.op("dve", [pt[pb]], [P_t[u]], lambda h, u=u, pb=pb, cs=cs: h.tensor_copy(out=Psb[u][:, cs, :n], in_=ps[pb][:, :n]))
                        else:
                            K.op("act", [pt[pb]], [P_t[u]], lambda h, u=u, pb=pb, cs=cs: h.activation(out=Psb[u][:, cs, :n], in_=ps[pb][:, :n], func=AF.Copy))
                    pb = 2 + chc
                    K.mm([P_t[u], w_t], [pt[pb]], ps[pb][:, :n],
                         [(c64[:, 2 * isctx + 0, :], Psb[u][:, 0, :n]), (c64[:, 2 * isctx + 1, :], Psb[u][:, 1, :n])])
                    K.op("dve", [pt[pb]], [Y_t], lambda h, pb=pb, chc=chc: h.tensor_copy(out=Ysb[:, chc, :n], in_=ps[pb][:, :n]))
                for c2 in range(2):
                    pb = 4 + c2
                    K.mm([Y_t, w_t], [pt[pb]], ps[pb][:, :n],
                         [(fnw[:, c1, c2 * 128:(c2 + 1) * 128], Ysb[:, c1, :n]) for c1 in range(2)])
                    K.op("act", [pt[pb]], [yf_t], lambda h, pb=pb, c2=c2: h.activation(out=yf[:, c2, :n], in_=ps[pb][:, :n], func=AF.Copy))
                out_proj_update(lambda dt: [(wo[:, c2, dt * 128:(dt + 1) * 128], yf[:, c2, :n]) for c2 in range(2)],
                                [yf_t, w_t], b, c0, n, r, [0, 1, 2, 3, 4, 5, 6])
                if ji + 2 < len(jobs):
                    issue_tables(ji + 2)
            K.barrier()

    def natt(l, hT, hT_t, need_ctx, wkv=None):
        with contextlib.ExitStack() as ph:
            kT = K.sb(ph, "na_kT", [128, 3, NT], BF16)
            kT_t = Tok()
            V = K.sb(ph, "na_V", [128, 18, 6, 65], BF16)
            V_t = Tok()
            gq = K.sb(ph, "na_gq", [128, 1], F32)
            gk = K.sb(ph, "na_gk", [128, 1], F32)
            epsb = K.sb(ph, "na_eps", [128, 1], F32)
            bones = K.sb(ph, "na_bones", [128, 128], BF16)
            g_t = Tok()
            K.dma("sp", gq[:], naq_d[l], [], [g_t], "nag")
            K.dma("sp", gk[:], nak_d[l], [], [g_t], "nag")
            K.op("dve", [g_t], [g_t], lambda h: h.tensor_scalar(out=gq[:], in0=gq[:], scalar1=0.125, scalar2=None, op0=ALU.mult))
            K.op("dve", [], [g_t], lambda h: h.memset(epsb[:], EPS))
            K.op("dve", [], [g_t], lambda h: h.memset(bones[:], 0.0))
            K.op("dve", [g_t], [g_t], lambda h: h.memset(bones[0:64, 0:64], 1.0))
            K.op("dve", [g_t], [g_t], lambda h: h.memset(bones[64:128, 64:128], 1.0))
            K.op("dve", [], [V_t], lambda h: h.memset(V[:, :, :, 64:65], 1.0))
            sq = [K.sb(ph, "na_sq%d" % i, [128, 512], BF16) for i in range(2)]
            sq_t = [Tok() for i in range(2)]
            rtl = [K.sb(ph, "na_rt%d" % i, [128, 512], F32) for i in range(2)]
            rs = [K.sb(ph, "na_rs%d" % i, [128, 512], F32) for i in range(2)]
            rs_t = [Tok() for i in range(2)]

            def qk_norm(w, wtok, gain, dst, dst_t, dcol, b, c0, n, cnt):
                for hp in range(3):
                    u = (cnt[0]) % 2
                    cnt[0] += 1
                    pa, pbk = u, 2 + u
                    lin_fm(None, w, hp * 128, 128, hT, hT_t, b, c0, n, pa, wtok)
                    K.op("act", [pt[pa]], [sq_t[u]], lambda h, u=u, pa=pa: h.activation(out=sq[u][:, :n], in_=ps[pa][:, :n], func=AF.Square))
                    K.mm([sq_t[u], g_t], [pt[pbk]], ps[pbk][:, :n], [(bones[:], sq[u][:, :n])])
                    K.op("act", [pt[pbk], g_t], [rs_t[u]], lambda h, u=u, pbk=pbk: h.activation(
                        out=rtl[u][:, :n], in_=ps[pbk][:, :n], func=AF.Ln, bias=epsb[:], scale=1.0 / 64))
                    K.op("act", [rs_t[u]], [rs_t[u]], lambda h, u=u: h.activation(
                        out=rs[u][:, :n], in_=rtl[u][:, :n], func=AF.Exp, scale=-0.5))
                    K.op("dve", [pt[pa], rs_t[u], g_t], [dst_t], lambda h, u=u, pa=pa, hp=hp: h.scalar_tensor_tensor(
                        out=dst[:, hp, dcol:dcol + n], in0=ps[pa][:, :n], scalar=gain[:, 0:1], in1=rs[u][:, :n],
                        op0=ALU.mult, op1=ALU.mult))

            cnt = [0]
            BT = K.sb(ph, "na_BT", [128, 72, 128], BF16)
            bt_t = Tok()
            for i3 in range(3):
                K.dma("pool", BT[:, 24 * i3:24 * (i3 + 1), :], bt_d[l][:, 24 * i3:24 * (i3 + 1), :], [], [bt_t], "nabt")
            for i3 in range(3):
                K.op("act", [bt_t], [bt_t], lambda h, i3=i3: h.activation(
                    out=BT[:, 24 * i3:24 * (i3 + 1), :], in_=BT[:, 24 * i3:24 * (i3 + 1), :], func=AF.Exp))
            wq = K.sb(ph, "na_wq", [128, 8, 384], BF16)
            wo = K.sb(ph, "na_wo", [128, 3, D], BF16)
            w_t = Tok()
            with contextlib.ExitStack() as p1:
                if wkv is None:
                    wk = K.sb(p1, "na_wk", [128, 8, 384], BF16)
                    wv = K.sb(p1, "na_wv", [128, 8, 384], BF16)
                    wk_t = Tok()
                    wv_t = Tok()
                    K.dma("pool", wk[:], win_d[l][:, :, 640:1024], [], [wk_t], "naw")
                    K.dma("pool", wv[:], win_d[l][:, :, 1024:1408], [], [wv_t], "nawv")
                else:
                    wk, wv, wk_t = wkv
                    wv_t = wk_t
                K.dma("pool", wq[:], win_d[l][:, :, 256:640], [], [w_t], "naw2")
                K.dma("pool", wo[:], wout_d[l][:, 2:5, :], [], [w_t], "naw2")
                for b in range(5):
                    c0, n = blk_cols(b)
                    qk_norm(wk, wk_t, gk, kT, kT_t, c0, b, c0, n, cnt)
                for ti in range(18):
                    pb = 4 + ti % 2
                    b = min(ti // 4, 4)
                    K.mm([wv_t, hT_t[b]], [pt[pb]], ps[pb][:, :384],
                         [(hT[:, kc, ti * 128:(ti + 1) * 128], wv[:, kc, :]) for kc in range(8)])
                    K.op("act" if ti % 2 else "dve", [pt[pb]], [V_t],
                         (lambda h, ti=ti, pb=pb: h.activation(out=V[:, ti, :, 0:64], in_=ps[pb][:, :384].rearrange("p (a b) -> p a b", b=64), func=AF.Copy))
                         if ti % 2 else
                         (lambda h, ti=ti, pb=pb: h.tensor_copy(out=V[:, ti, :, 0:64], in_=ps[pb][:, :384].rearrange("p (a b) -> p a b", b=64))))
                K.barrier()
            qT2 = [K.sb(ph, "na_qT%d" % i, [128, 3, 512], BF16) for i in range(2)]
            qT2_t = [Tok(), Tok()]
            PT = [K.sb(ph, "na_PT%d" % i, [128, 7, 128], BF16) for i in range(2)]
            PT_t = [Tok() for i in range(2)]
            rec = [K.sb(ph, "na_rec%d" % i, [128, 6], F32) for i in range(2)]
            rec_t = [Tok(), Tok()]
            Osb = [K.sb(ph, "na_O%d" % i, [128, 6, 64], BF16) for i in range(2)]
            O_t = [Tok(), Tok()]
            yna2 = [K.sb(ph, "na_y%d" % i, [128, 3, 512], BF16) for i in range(2)]
            yna2_t = [Tok(), Tok()]
            hcount = 0
            nblocks = list(range(5) if need_ctx else range(4))
            pending = [None]
            c0_, n_ = blk_cols(nblocks[0])
            qk_norm(wq, w_t, gq, qT2[0], qT2_t[0], 0, nblocks[0], c0_, n_, cnt)
            for bi, b in enumerate(nblocks):
                c0, n = blk_cols(b)
                r = 0 if b < 4 else 1
                qT, qT_t = qT2[bi % 2], qT2_t[bi % 2]
                yna, yna_t = yna2[bi % 2], yna2_t[bi % 2]
                tile_chunks = []
                for qi in range(n // 128):
                    if b < 4:
                        i = b * 4 + qi
                        if i <= 1:
                            js = list(range(0, 4))
                        elif i >= 14:
                            js = list(range(12, 16))
                        else:
                            js = list(range(i - 2, i + 3))
                        chunks = []
                        for j in js:
                            dl = j - i
                            v = (dl + 9) if 2 <= i <= 13 else (dl + 3)
                            chunks.append((j, v))
                        chunks += [(16, None), (17, None)]
                    else:
                        chunks = [(16, None), (17, None)]
                    tile_chunks.append(chunks)
                items = [(qi, hd) for qi in range(n // 128) for hd in range(6)]

                def stage1(k):
                    qi, hd = items[k]
                    chunks = tile_chunks[qi]
                    nch = len(chunks)
                    hp, r0 = hd // 2, (hd % 2) * 64
                    u = (hbase + k) % 2
                    pS = (0 + 2 * u, 1 + 2 * u)
                    for c, (kt, v) in enumerate(chunks):
                        pb = pS[c // 4]
                        pairs = [(kT[r0:r0 + 64, hp, kt * 128:(kt + 1) * 128], qT[r0:r0 + 64, hp, qi * 128:(qi + 1) * 128])]
                        K.mm([kT_t, qT_t, w_t, t_const], [pt[pb]], ps[pb][:, (c % 4) * 128:(c % 4 + 1) * 128], pairs)
                    n0 = min(nch, 4)
                    K.op("act", [pt[pS[0]]], [PT_t[u]], lambda h: h.activation(
                        out=PT[u][:, 0:n0, :], in_=ps[pS[0]][:, 0:n0 * 128].rearrange("p (a b) -> p a b", b=128), func=AF.Exp))
                    if nch > 4:
                        n1 = nch - 4
                        K.op("act", [pt[pS[1]]], [PT_t[u]], lambda h: h.activation(
                            out=PT[u][:, 4:4 + n1, :], in_=ps[pS[1]][:, 0:n1 * 128].rearrange("p (a b) -> p a b", b=128), func=AF.Exp))
                    loc = [v for (kt, v) in chunks if v is not None]
                    if loc:
                        nl, v0 = len(loc), loc[0]
                        K.op("dve", [PT_t[u], bt_t], [PT_t[u]], lambda h: h.tensor_tensor(
                            out=PT[u][:, 0:nl, :], in0=PT[u][:, 0:nl, :], in1=BT[:, hd * 12 + v0:hd * 12 + v0 + nl, :], op=ALU.mult))

                def stage2(k):
                    qi, hd = items[k]
                    chunks = tile_chunks[qi]
                    u = (hbase + k) % 2
                    po = 4 + qi % 2
                    K.mm([PT_t[u], V_t], [pt[po]], ps[po][:, hd * 65:(hd + 1) * 65],
                         [(PT[u][:, c, :], V[:, kt, hd, :]) for c, (kt, v) in enumerate(chunks)])

                def fin_dve(qi):
                    po = 4 + qi % 2
                    oi = qi % 2
                    ov = ps[po][:, 0:390].rearrange("p (a b) -> p a b", b=65)
                    K.op("dve", [pt[po]], [rec_t[oi]], lambda h: h.reciprocal(out=rec[oi][:], in_=ov[:, :, 64]))
                    K.op("dve", [pt[po], rec_t[oi]], [O_t[oi]], lambda h: h.tensor_tensor(
                        out=Osb[oi][:], in0=ov[:, :, 0:64], in1=rec[oi][:].unsqueeze(2).to_broadcast([128, 6, 64]), op=ALU.mult))

                def fin_pe(qi):
                    oi = qi % 2
                    Of = Osb[oi][:].rearrange("p a b -> p (a b)")
                    for hp in range(3):
                        K.op("pe", [O_t[oi], t_const], [ptb], lambda h, hp=hp: h.transpose(
                            psb[:, hp * 128:(hp + 1) * 128], Of[:, hp * 128:(hp + 1) * 128], identb[:]))
                    K.op("act", [ptb], [yna_t], lambda h: h.activation(
                        out=yna[:, :, qi * 128:(qi + 1) * 128], in_=psb[:, 0:384].rearrange("p (a b) -> p a b", b=128), func=AF.Copy))

                hbase = hcount
                nit = len(items)
                for k in range(nit + 2):
                    if k == 6 and pending[0] is not None:
                        pending[0]()
                        pending[0] = None
                    if k < nit:
                        stage1(k)
                    if 1 <= k <= nit:
                        stage2(k - 1)
                        if items[k - 1][1] == 5:
                            fin_dve(items[k - 1][0])
                    if 2 <= k <= nit + 1 and items[k - 2][1] == 5:
                        fin_pe(items[k - 2][0])
                hcount += nit
                if bi + 1 < len(nblocks):
                    c0_, n_ = blk_cols(nblocks[bi + 1])
                    qk_norm(wq, w_t, gq, qT2[(bi + 1) % 2], qT2_t[(bi + 1) % 2], 0, nblocks[bi + 1], c0_, n_, cnt)
                def _op(yna=yna, yna_t=yna_t, b=b, c0=c0, n=n, r=r):
                    out_proj_update(lambda dt: [(wo[:, hp, dt * 128:(dt + 1) * 128], yna[:, hp, :n]) for hp in range(3)],
                                    [yna_t, w_t], b, c0, n, r, [6])
                if pending[0] is not None:
                    pending[0]()
                pending[0] = _op
            if pending[0] is not None:
                pending[0]()
                pending[0] = None
            K.barrier()

    def gla(l, hT, hT_t, need_ctx):
        NB = 256
        QS = 48 ** -0.5
        with contextlib.ExitStack() as ph:
            qr = K.sb(ph, "gl_qr", [128, 2, NT], BF16)
            kr = K.sb(ph, "gl_kr", [128, 2, NT], BF16)
            aT = K.sb(ph, "gl_aT", [32, NT], BF16)
            st_t = [Tok() for i in range(5)]
            gvc = K.sb(ph, "gl_gvc", [128, 2, 384], BF16)
            sgc = K.sb(ph, "gl_sgc", [96, 4, NCTX], BF16)
            gvc_t = Tok()
            aw = K.sb(ph, "gl_aw", [32, 2, 256], BF16)
            nab = K.sb(ph, "gl_nab", [128, 2, 2], F32)
            gon = K.sb(ph, "gl_gon", [96, 1], F32)
            msk = K.sb(ph, "gl_msk", [128, 2, 2, 128], BF16)
            mscan = K.sb(ph, "gl_mscan", [128, 512], BF16)
            epsb = K.sb(ph, "gl_eps", [128, 1], F32)
            oneb = K.sb(ph, "gl_one", [128, 1], F32)
            w_t = Tok()
            K.dma("pool", aw[:], aw_d[l], [], [w_t], "glw")
            K.dma("sp", nab[:], ab_d[l], [], [w_t], "glw")
            K.dma("sp", gon[:], gon_d[l], [], [w_t], "glw")
            K.dma("sp", msk[:], gmask_d, [], [w_t], "glw")
            K.dma("sp", mscan[:], mscan_d, [], [w_t], "glw")
            K.op("dve", [w_t], [w_t], lambda h: h.tensor_scalar(out=nab[:], in0=nab[:], scalar1=-1.0, scalar2=None, op0=ALU.mult))
            K.op("dve", [], [w_t], lambda h: h.memset(epsb[:], EPS))
            K.op("dve", [], [w_t], lambda h: h.memset(oneb[:], 1.0))
            wo = K.sb(ph, "gl_wo", [96, 4, D], BF16)
            w2_t = Tok()
            with contextlib.ExitStack() as p0:
                wg = K.sb(p0, "gl_wg", [128, 8, 1024], BF16)
                wga = K.sb(p0, "gl_wga", [128, 8, 32], BF16)
                w0_t = Tok()
                wga_t, wgv_t, wgg_t = Tok(), Tok(), Tok()
                K.dma("pool", wga[:], win_d[l][:, :, 2560:2592], [], [wga_t], "glw0a")
                K.dma("pool", wg[:], wg_d[l], [], [w0_t], "glw0")
                wgv = K.sb(p0, "gl_wgv", [128, 8, 384], BF16)
                wgg = K.sb(p0, "gl_wgg", [128, 8, 384], BF16)
                K.dma("pool", wgv[:], win_d[l][:, :, 1792:2176], [], [wgv_t], "glw0v")
                K.dma("pool", wgg[:], win_d[l][:, :, 2176:2560], [], [wgg_t], "glw0g")
                K.dma("pool", wo[:], woutg_d[l], [], [w2_t], "glw2")
                gvs = K.sb(p0, "gl_gvs", [128, 4, 384], BF16)
                sgs = K.sb(p0, "gl_sgs", [96, 4, 512], BF16)
                stg_t = Tok()
                cosb = K.sb(p0, "gl_cos", [128, 512], F32)
                sinb = K.sb(p0, "gl_sin", [128, 512], F32)
                cs_t = Tok()
                t1 = [K.sb(p0, "gl_t1%d" % i, [128, 512], F32) for i in range(2)]
                t2 = [K.sb(p0, "gl_t2%d" % i, [128, 512], F32) for i in range(2)]
                r_t = [Tok(), Tok()]
                cnt = 0
                for b5 in range(5):
                    c0, n = blk_cols(b5)
                    lat = b5 < 4
                    if lat:
                        K.dma("sp", cosb[:, :n], cos_d[:, c0:c0 + n], [], [cs_t], "cs")
                        K.dma("sp", sinb[:, :n], sin_d[:, c0:c0 + n], [], [cs_t], "cs")
                    lin_fm(None, wga, 0, 32, hT, hT_t, b5, c0, n, 6, wga_t)
                    K.op("act", [pt[6]], [st_t[b5]], lambda h: h.activation(out=aT[:, c0:c0 + n], in_=ps[6][:32, :n], func=AF.Copy))
                    for hp in range(2):
                        for which in range(2):
                            base = which * 512
                            dst = qr if which == 0 else kr
                            u = cnt % 2
                            cnt += 1
                            pa, pbk = (0, 1) if u == 0 else (2, 3)
                            lin_fm(None, wg, base + hp * 128, 128, hT, hT_t, b5, c0, n, pa, w0_t)
                            if lat:
                                lin_fm(None, wg, base + 256 + hp * 128, 128, hT, hT_t, b5, c0, n, pbk, w0_t)
                                K.op("dve", [pt[pa], cs_t], [r_t[u]], lambda h: h.tensor_tensor(out=t1[u][:, :n], in0=ps[pa][:, :n], in1=cosb[:, :n], op=ALU.mult))
                                K.op("dve", [pt[pbk], cs_t, r_t[u]], [r_t[u]], lambda h: h.tensor_tensor(out=t2[u][:, :n], in0=ps[pbk][:, :n], in1=sinb[:, :n], op=ALU.mult))
                                K.op("pool", [r_t[u]], [st_t[b5]], lambda h: h.tensor_tensor(out=dst[:, hp, c0:c0 + n], in0=t1[u][:, :n], in1=t2[u][:, :n], op=ALU.add))
                            else:
                                K.op("act", [pt[pa]], [st_t[b5]], lambda h: h.activation(out=dst[:, hp, c0:c0 + n], in_=ps[pa][:, :n], func=AF.Copy))
                    for tt in range(n // 128):
                        pv = 4 + tt % 2
                        K.mm([wgv_t, hT_t[b5]], [pt[pv]], ps[pv][:, :384],
                             [(hT[:, kc, c0 + tt * 128:c0 + (tt + 1) * 128], wgv[:, kc, :]) for kc in range(8)])
                        if lat:
                            K.op("act", [pt[pv]], [stg_t], lambda h, tt=tt, pv=pv: h.activation(out=gvs[:, tt, :], in_=ps[pv][:, :384], func=AF.Copy))
                        else:
                            K.op("act", [pt[pv]], [gvc_t], lambda h, tt=tt, pv=pv: h.activation(out=gvc[:, tt, :], in_=ps[pv][:, :384], func=AF.Copy))
                    if lat or need_ctx:
                        for hd in range(4):
                            pv = 4 + hd % 2
                            lin_fm(None, wgg, hd * 96, 96, hT, hT_t, b5, c0, n, pv, wgg_t)
                            if lat:
                                K.op("act", [pt[pv]], [stg_t], lambda h, hd=hd, pv=pv: h.activation(out=sgs[:, hd, :n], in_=ps[pv][:96, :n], func=AF.Silu))
                            else:
                                K.op("act", [pt[pv]], [gvc_t], lambda h, hd=hd, pv=pv: h.activation(out=sgc[:, hd, :n], in_=ps[pv][:96, :n], func=AF.Silu))
                    if lat:
                        for tt in range(4):
                            K.op("pool", [stg_t], [hT_t[b5]], lambda h, tt=tt: h.tensor_copy(out=hT[:, tt, c0:c0 + 384], in_=gvs[:, tt, :]))
                        for hd in range(4):
                            K.op("pool", [stg_t], [hT_t[b5]], lambda h, hd=hd: h.tensor_copy(out=hT[:96, 4 + hd, c0:c0 + 512], in_=sgs[:, hd, :]))
                K.barrier()
            oF = K.sb(ph, "gl_oF", [96, 4, NT], BF16)
            oF_t = [Tok() for i in range(5)]
            R = []
            for d in range(2):
                r_ = dict(
                    e1=K.sb(ph, "gl_e1_%d" % d, [128, 512], F32),
                    bpos=K.sb(ph, "gl_bp_%d" % d, [128, 512], F32), eb=K.sb(ph, "gl_eb_%d" % d, [128, 512], F32),
                    qin=K.sb(ph, "gl_qin_%d" % d, [128, 2, 512], BF16), kin=K.sb(ph, "gl_kin_%d" % d, [128, 2, 512], BF16),
                    ktok=K.sb(ph, "gl_ktok_%d" % d, [128, 4, 2, 128], BF16),
                    dec=K.sb(ph, "gl_dec_%d" % d, [128, 2, 4], F32), AM=K.sb(ph, "gl_AM_%d" % d, [128, 4, 128], BF16),
                    S=K.sb(ph, "gl_S_%d" % d, [128, 2, 192], F32), Sb=K.sb(ph, "gl_Sb_%d" % d, [128, 2, 192], BF16),
                    osum=K.sb(ph, "gl_osum_%d" % d, [96, 4, NB], F32),
                    sg_t=Tok(), n_t=Tok(),
                    g_t=Tok(), qk_t=Tok(), ktok_t=Tok(), gv_t=Tok(), dec_t=Tok(), AM_t=Tok(), S_t=Tok(), Sb_t=Tok(), os_t=Tok(),
                    BA=3 * d, BB=3 * d + 1, BO=3 * d + 2)
                R.append(r_)

            def block_prep(d, b5):
                r_ = R[d]
                c0, n = blk_cols(b5)
                nt = n // 128
                BB = r_["BB"]
                e1, bpos, eb = r_["e1"], r_["bpos"], r_["eb"]
                lsb = e1
                enb = eb
                gt = r_["g_t"]
                for hp in range(2):
                    K.mm([st_t[b5], w_t], [pt[BB]], ps[BB][:, :n], [(aw[:, d, hp * 128:(hp + 1) * 128], aT[:, c0:c0 + n])])
                    yield
                    K.op("act", [pt[BB], w_t], [gt], lambda h: h.activation(
                        out=e1[:, :n], in_=ps[BB][:, :n], func=AF.Exp, bias=nab[:, d, hp:hp + 1], scale=-1.0))
                    yield
                    K.op("act", [gt, w_t], [gt], lambda h: h.activation(out=lsb[:, :n], in_=e1[:, :n], func=AF.Ln, bias=oneb[:], scale=1.0))
                    yield
                    K.op("dve", [gt, w_t], [gt], lambda h: h.tensor_tensor_scan(
                        out=bpos[:, :n], data0=mscan[:, :n], data1=lsb[:, :n], initial=0.0, op0=ALU.mult, op1=ALU.add))
                    yield
                    bsel = bpos
                    if d == 1:
                        K.op("dve", [gt], [gt], lambda h: h.tensor_tensor(out=e1[:, :n], in0=lsb[:, :n], in1=bpos[:, :n], op=ALU.subtract))
                        yield
                        K.op("dve", [gt], [gt], lambda h: h.tensor_tensor(
                            out=e1[:, :n].rearrange("p (a b) -> p a b", b=128), in0=e1[:, :n].rearrange("p (a b) -> p a b", b=128),
                            in1=bpos[:, :n].rearrange("p (a b) -> p a b", b=128)[:, :, 127:128].to_broadcast([128, n // 128, 128]), op=ALU.add))
                        yield
                        bsel = e1
                    K.op("act", [gt], [gt], lambda h, bsel=bsel: h.activation(out=eb[:, :n], in_=bsel[:, :n], func=AF.Exp, scale=-1.0 / 16))
                    yield
                    sel = 127 if d == 0 else 0
                    K.op("dve", [gt], [r_["dec_t"]], lambda h: h.tensor_copy(
                        out=r_["dec"][:, hp, 0:nt], in_=eb[:, :n].rearrange("p (a b) -> p a b", b=128)[:, :, sel]))
                    yield
                    K.op("dve", [gt, st_t[b5]], [r_["qk_t"]], lambda h: h.scalar_tensor_tensor(
                        out=r_["qin"][:, hp, :n], in0=qr[:, hp, c0:c0 + n], scalar=QS, in1=eb[:, :n], op0=ALU.mult, op1=ALU.mult))
                    yield
                    K.op("act", [gt, r_["dec_t"], r_["qk_t"]], [gt], lambda h, bsel=bsel: h.activation(out=enb[:, :n], in_=bsel[:, :n], func=AF.Exp, scale=1.0 / 16))
                    yield
                    K.op("dve", [gt, st_t[b5]], [r_["qk_t"]], lambda h: h.tensor_tensor(
                        out=r_["kin"][:, hp, :n], in0=kr[:, hp, c0:c0 + n], in1=enb[:, :n], op=ALU.mult))
                    yield
                    for tt in range(nt):
                        K.op("pe", [r_["qk_t"], t_const], [ptb], lambda h, tt=tt: h.transpose(
                            psb[:, d * 512 + tt * 128:d * 512 + (tt + 1) * 128], r_["kin"][:, hp, tt * 128:(tt + 1) * 128], identb[:]))
                        yield
                    K.op("act", [ptb], [r_["ktok_t"]], lambda h: h.activation(
                        out=r_["ktok"][:, 0:nt, hp, :], in_=psb[:, d * 512:d * 512 + nt * 128].rearrange("p (t f) -> p t f", t=nt), func=AF.Copy))
                    yield

            def scan_tile(d, b5, tt, want_o, second):
                r_ = R[d]
                c0, n = blk_cols(b5)
                tc0 = tt * 128
                BA, BB, BO = r_["BA"], r_["BB"], r_["BO"]
                qin, kin, Sb, S, AM, dec, ktok = r_["qin"], r_["kin"], r_["Sb"], r_["S"], r_["AM"], r_["dec"], r_["ktok"]
                if b5 < 4:
                    gv_tok = hT_t[b5]

                    def gvf(col, w):
                        return hT[:, tt, c0 + col:c0 + col + w]
                else:
                    gv_tok = gvc_t

                    def gvf(col, w):
                        return gvc[:, tt, col:col + w]
                if want_o:
                    for hd in (0, 2, 1, 3):
                        hp, r0 = hd // 2, (hd % 2) * 64
                        pa = BA if hd % 2 == 0 else BB
                        K.mm([r_["qk_t"]], [pt[pa]], ps[pa][:, (hd // 2) * 128:(hd // 2 + 1) * 128],
                             [(kin[r0:r0 + 48, hp, tc0:tc0 + 128], qin[r0:r0 + 48, hp, tc0:tc0 + 128])])
                        yield
                    for par, pa in ((0, BA), (1, BB)):
                        K.op("dve", [pt[pa], w_t], [r_["AM_t"]], lambda h, par=par, pa=pa: h.tensor_tensor(
                            out=AM[:, par:4:2, :], in0=ps[pa][:, 0:256].rearrange("p (a b) -> p a b", b=128),
                            in1=msk[:, d, :, :], op=ALU.mult))
                        yield
                    for hd in range(4):
                        hp, r0 = hd // 2, (hd % 2) * 64
                        K.mm([r_["AM_t"], gv_tok, r_["Sb_t"], r_["qk_t"]], [pt[BO]], ps[BO][:96, hd * 128:(hd + 1) * 128],
                             [(gvf(hd * 96, 96), AM[:, hd, :]),
                              (Sb[r0:r0 + 48, hp, (hd % 2) * 96:(hd % 2 + 1) * 96], qin[r0:r0 + 48, hp, tc0:tc0 + 128])])
                        yield
                    ov = ps[BO][:96, :].rearrange("p (a b) -> p a b", b=128)
                    if not second:
                        K.op("act", [pt[BO]], [oF_t[b5]], lambda h: h.activation(
                            out=oF[:, :, c0 + tc0:c0 + tc0 + 128], in_=ov, func=AF.Copy))
                    else:
                        K.op("dve", [pt[BO], oF_t[b5]], [r_["os_t"]], lambda h: h.tensor_tensor(
                            out=r_["osum"][:, :, (tt % 2) * 128:(tt % 2 + 1) * 128], in0=ov, in1=oF[:, :, c0 + tc0:c0 + tc0 + 128], op=ALU.add))
                    yield
                for hp in range(2):
                    K.mm([r_["ktok_t"], gv_tok], [pt[BA]], ps[BA][:, hp * 192:(hp + 1) * 192],
                         [(ktok[:, tt, hp, :], gvf(hp * 192, 192))])
                    yield
                for hp in range(2):
                    K.op("act", [r_["S_t"], r_["dec_t"]], [r_["S_t"]], lambda h, hp=hp: h.activation(
                        out=S[:, hp, :], in_=S[:, hp, :], func=AF.Identity, scale=dec[:, hp, tt:tt + 1]))
                    yield
                    K.op("dve", [pt[BA], r_["S_t"], r_["dec_t"]], [r_["S_t"]], lambda h, hp=hp: h.scalar_tensor_tensor(
                        out=S[:, hp, :], in0=ps[BA][:, hp * 192:(hp + 1) * 192], scalar=dec[:, hp, tt:tt + 1], in1=S[:, hp, :],
                        op0=ALU.mult, op1=ALU.add))
                    yield
                K.op("act", [r_["S_t"]], [r_["Sb_t"]], lambda h: h.activation(out=Sb[:], in_=S[:], func=AF.Copy))
                yield

            def finalize(d, b5, half):
                r_ = R[d]
                c0 = blk_cols(b5)[0] + half * 256
                n = 256
                r = 0 if b5 < 4 else 1
                osum = r_["osum"]
                BA, BB, BO = r_["BA"], r_["BB"], r_["BO"]
                if b5 < 4:
                    sg_t = hT_t[b5]

                    def sgf(hd):
                        return hT[:96, 4 + hd, c0:c0 + n]
                else:
                    sg_t = gvc_t

                    def sgf(hd):
                        return sgc[:, hd, :n]
                sq = r_["AM"][:96, 0:2, :].rearrange("p a b -> p (a b)")
                am_t = r_["AM_t"]
                rt = r_["e1"]
                n_t, gt = r_["n_t"], r_["g_t"]
                for hd in range(4):
                    K.op("act", [r_["os_t"]], [n_t, am_t], lambda h, hd=hd: h.activation(out=sq[:, :n], in_=osum[:, hd, :n], func=AF.Square))
                    yield
                    K.mm([n_t, am_t, t_const], [pt[BB]], ps[BB][:96, :n], [(onesb[:96, :96], sq[:, :n])])
                    yield
                    K.op("act", [pt[BB], w_t], [n_t, gt], lambda h: h.activation(
                        out=rt[:96, :n], in_=ps[BB][:96, :n], func=AF.Ln, bias=epsb[:96, :], scale=1.0 / 96))
                    yield
                    K.op("act", [n_t], [n_t, gt], lambda h: h.activation(out=rt[:96, :n], in_=rt[:96, :n], func=AF.Exp, scale=-0.5))
                    yield
                    K.op("dve", [n_t, gt, w_t], [r_["os_t"]], lambda h, hd=hd: h.scalar_tensor_tensor(
                        out=osum[:, hd, :n], in0=osum[:, hd, :n], scalar=gon[:, 0:1], in1=rt[:96, :n], op0=ALU.mult, op1=ALU.mult))
                    yield
                    K.op("dve", [r_["os_t"], sg_t], [sg_t], lambda h, hd=hd: h.tensor_tensor(
                        out=sgf(hd), in0=osum[:, hd, :n], in1=sgf(hd), op=ALU.mult))
                    yield
                for dt in range(8):
                    pc = BA if dt % 2 == 0 else BO
                    K.mm([sg_t, w2_t], [pt[pc]], ps[pc][:, :n], [(wo[:, hd, dt * 128:(dt + 1) * 128], sgf(hd)) for hd in range(4)])
                    yield
                    K.op("dve", [pt[pc], t_mod, xT_t[b5]], [xT_t[b5]], lambda h, dt=dt, pc=pc: h.scalar_tensor_tensor(
                        out=xT[:, dt, c0:c0 + n], in0=ps[pc][:, :n], scalar=modT[:, 16 + dt, r:r + 1],
                        in1=xT[:, dt, c0:c0 + n], op0=ALU.mult, op1=ALU.add))
                    yield

            def dir_gen(d):
                r_ = R[d]
                K.op("dve", [], [r_["S_t"]], lambda h: h.memset(r_["S"][:], 0.0))
                K.op("dve", [], [r_["Sb_t"]], lambda h: h.memset(r_["Sb"][:], 0.0))
                order = [4] + ([0, 1, 2, 3] if d == 0 else [3, 2, 1, 0])
                for b5 in order:
                    want_o = (b5 < 4) or need_ctx
                    if b5 == 4:
                        second = (d == 1)
                    else:
                        second = (b5 >= 2) if d == 0 else (b5 <= 1)
                    nt = blk_cols(b5)[1] // 128
                    yield from block_prep(d, b5)
                    tiles = list(range(nt)) if d == 0 else list(range(nt - 1, -1, -1))
                    for i_, tt in enumerate(tiles):
                        yield from scan_tile(d, b5, tt, want_o, second)
                        if want_o and second and i_ % 2 == 1:
                            yield from finalize(d, b5, tt // 2)
                    yield "STEP"

            def run_step(gens):
                active = list(gens)
                while active:
                    for g in list(active):
                        if next(g) == "STEP":
                            active.remove(g)
            gF, gB = dir_gen(0), dir_gen(1)
            run_step([gF])
            run_step([gB])
            for s_ in range(4):
                run_step([gF, gB])
            K.barrier()

    def mixers(l, hT, hT_t, need_ctx):
        nm = lambda: norm_mod(l, "mix", hT, hT_t, range(5))
        with contextlib.ExitStack() as pw:
            wkv = None
            if False:
                wk = K.sb(pw, "na_wk", [128, 8, 384], BF16)
                wv = K.sb(pw, "na_wv", [128, 8, 384], BF16)
                wk_t = Tok()
                K.dma("pool", wk[:], win_d[l][:, :, 640:1024], [], [wk_t], "naw")
                K.dma("pool", wv[:], win_d[l][:, :, 1024:1408], [], [wk_t], "naw")
                wkv = (wk, wv, wk_t)
            if "nofn" not in debug:
                fnet(l, hT, hT_t, need_ctx, pre=nm)
            else:
                nm()
            if "nona" not in debug:
                natt(l, hT, hT_t, need_ctx, wkv=wkv)
            K.barrier()
        if "nogla" not in debug:
            gla(l, hT, hT_t, need_ctx)

    for l in range(2):
        need_ctx = (l == 0)
        modT.t, amix.t, affn.t = modTs[l], amixs[l], affns[l]
        with contextlib.ExitStack() as lay:
            hT = K.sb(lay, "hT", [128, 8, NT], BF16)
            hT_t = [Tok("hT%d" % i) for i in range(5)]
            if "ffn" not in debug:
                mixers(l, hT, hT_t, need_ctx)
            blocks = range(5) if need_ctx else range(4)
            if "noffn" not in debug:
                ffn(l, hT, hT_t, blocks)
            K.barrier()
        if "ffn" in debug or "l0" in debug:
            break

    for name, (buf, shape, dt) in dbg.items():
        dd = K.dram("dbg_" + name, shape, dt, kind="ExternalOutput")
        K.barrier()
        K.dma("sp", dd, buf[:], [], [t_out], "out")
        K.outs["dbg_" + name] = shape

    for b in range(4):
        if b not in out_done:
            emit_out(b)
    S = t_out.dsem["sp"]
    nc.sync.wait_ge(S.sem, S.count * 16)
    es.close()
    return K


_CONSTS = {}


def _consts():
    if _CONSTS:
        return _CONSTS
    bf = ml_dtypes.bfloat16
    t = np.arange(T, dtype=np.int64)
    ang = 2.0 * np.pi * ((t[:, None] * t[None, :]) % T).astype(np.float64) / T
    _CONSTS["c_ct"] = np.ascontiguousarray(np.cos(ang).reshape(16, 128, T).transpose(1, 0, 2)).astype(bf)
    _CONSTS["c_st"] = np.ascontiguousarray(np.sin(ang).reshape(16, 128, T).transpose(1, 0, 2)).astype(bf)
    t2 = np.arange(NCTX, dtype=np.int64)
    ang2 = 2.0 * np.pi * ((t2[:, None] * t2[None, :]) % NCTX).astype(np.float64) / NCTX
    _CONSTS["c_c256"] = np.ascontiguousarray(np.cos(ang2).reshape(2, 128, NCTX).transpose(1, 0, 2)).astype(bf)
    _CONSTS["c_s256"] = np.ascontiguousarray(np.sin(ang2).reshape(2, 128, NCTX).transpose(1, 0, 2)).astype(bf)
    g = np.arange(64)
    a64 = 2.0 * np.pi * ((g[:, None] * g[None, :]) % 64) / 64.0
    c64 = np.zeros((128, 4, 128))
    for blk in range(2):
        sl = slice(64 * blk, 64 * blk + 64)
        c64[sl, 0, sl] = np.cos(a64) / np.sqrt(T * 64.0)
        c64[sl, 1, sl] = -np.sin(a64) / np.sqrt(T * 64.0)
        c64[sl, 2, sl] = np.cos(a64) / np.sqrt(NCTX * 64.0)
        c64[sl, 3, sl] = -np.sin(a64) / np.sqrt(NCTX * 64.0)
    _CONSTS["c_c64"] = c64.astype(bf)
    cosT = np.zeros((128, T), np.float32)
    sinT = np.zeros((128, T), np.float32)
    row = (t // 64).astype(np.float64)
    col = (t % 64).astype(np.float64)
    for p in range(128):
        d = p % 64
        if d >= 48:
            continue
        within = d % 24
        j = within % 12
        inv = 10000.0 ** (-j / 12.0)
        pos = row if d < 24 else col
        cosT[p] = np.cos(pos * inv)
        sinT[p] = np.sin(pos * inv) * (-1.0 if within < 12 else 1.0)
    _CONSTS["c_cos"] = cosT
    _CONSTS["c_sin"] = sinT
    s_ = np.arange(128)
    gm = np.zeros((128, 2, 2, 128), np.float32)
    gm[:, 0, :, :] = (s_[:, None] <= s_[None, :])[:, None, :]
    gm[:, 1, :, :] = (s_[:, None] >= s_[None, :])[:, None, :]
    _CONSTS["c_gmask"] = gm.astype(bf)
    ms = np.ones((128, 512), np.float32)
    ms[:, 0::128] = 0.0
    _CONSTS["c_mscan"] = ms.astype(bf)
    return _CONSTS


def _host_prep(inputs):
    f = lambda a: np.ascontiguousarray(np.asarray(a, dtype=np.float32))
    x, c, ctx, c_ctx = f(inputs["x"]), f(inputs["c"]), f(inputs["ctx"]), f(inputs["c_ctx"])
    shared = {}
    shared["ada_w"] = f(f(inputs["ada_w"]).reshape(2, 8, 128, 6144).transpose(0, 2, 1, 3))
    shared["ada_b"] = f(f(inputs["ada_b"]).reshape(2, 48, 128).transpose(0, 2, 1))
    shared["norm_mix"] = f(f(inputs["norm_mix"]).reshape(2, 8, 128).transpose(0, 2, 1))
    shared["norm_ffn"] = f(f(inputs["norm_ffn"]).reshape(2, 8, 128).transpose(0, 2, 1))
    shared["ident"] = np.eye(128, dtype=np.float32)
    shared["ffn_w1"] = f(f(inputs["ffn_w1"]).reshape(2, 8, 128, DFF).transpose(0, 2, 1, 3))
    shared["ffn_w3"] = f(f(inputs["ffn_w3"]).reshape(2, 8, 128, DFF).transpose(0, 2, 1, 3))
    shared["ffn_w2"] = f(f(inputs["ffn_w2"]).reshape(2, 22, 128, D).transpose(0, 2, 1, 3))
    w_in = f(inputs["w_in"])
    shared["w_in"] = f(w_in.reshape(2, 8, 128, NIN).transpose(0, 2, 1, 3))
    wg = np.zeros((2, D, 1024), np.float32)
    dd = np.arange(48)
    partner = np.where((dd % 24) < 12, dd + 12, dd - 12)
    for hh in range(4):
        wg[:, :, 64 * hh:64 * hh + 48] = w_in[:, :, 1408 + 48 * hh + dd]
        wg[:, :, 256 + 64 * hh:256 + 64 * hh + 48] = w_in[:, :, 1408 + 48 * hh + partner]
        wg[:, :, 512 + 64 * hh:512 + 64 * hh + 48] = w_in[:, :, 1600 + 48 * hh + dd]
        wg[:, :, 768 + 64 * hh:768 + 64 * hh + 48] = w_in[:, :, 1600 + 48 * hh + partner]
    shared["w_g"] = f(wg.reshape(2, 8, 128, 1024).transpose(0, 2, 1, 3))
    shared["fnet_w"] = f(f(inputs["fnet_w"]).reshape(2, 2, 128, 256).transpose(0, 2, 1, 3))
    w_out = f(inputs["w_out"])
    shared["w_out"] = f(w_out.reshape(2, 8, 128, D).transpose(0, 2, 1, 3))
    shared["w_out_g"] = f(w_out[:, 640:, :].reshape(2, 4, 96, D).transpose(0, 2, 1, 3))
    shared["na_q"] = f(np.tile(f(inputs["na_q_norm"]), (1, 2))[:, :, None])
    shared["na_k"] = f(np.tile(f(inputs["na_k_norm"]), (1, 2))[:, :, None])
    rpb = f(inputs["na_rpb"])
    pk = np.arange(128)
    a_, kc_ = pk // 64, pk % 64
    b_, c_ = pk // 64, pk % 64
    dcm = np.clip(kc_[:, None] - c_[None, :], -15, 15) + 15
    cq0 = np.clip(c_ - 8, 0, 48)
    col_ok = (kc_[:, None] >= cq0[None, :]) & (kc_[:, None] < cq0[None, :] + 16)
    bt = np.full((2, 128, 72, 128), NEG, np.float32)
    ents = [(dl, 0) for dl in range(-3, 4)] + [(-2, 1), (-1, 0), (0, 0), (1, 0), (2, 1)]
    for e, (dl, msk_) in enumerate(ents):
        drm = 2 * dl + a_[:, None] - b_[None, :] + 7
        ok = col_ok & (drm >= 0) & (drm <= 14)
        if msk_ and dl == -2:
            ok = ok & ((2 * dl + a_[:, None]) >= (-4 + b_[None, :]))
        if msk_ and dl == 2:
            ok = ok & ((2 * dl + a_[:, None]) <= (3 + b_[None, :]))
        drc = np.clip(drm, 0, 14)
        for hh in range(6):
            vals = rpb[:, hh][:, drc, dcm]
            bt[:, :, hh * 12 + e, :] = np.where(ok[None], vals, NEG)
    shared["na_bt"] = bt
    aw = np.zeros((2, 32, 2, 256), np.float32)
    ab = np.zeros((2, 128, 2, 2), np.float32)
    gaw = f(inputs["gla_alpha_w"]); gab = f(inputs["gla_alpha_b"])
    for dr_ in range(2):
        for hh in range(4):
            aw[:, 16 * dr_:16 * dr_ + 16, dr_, 64 * hh:64 * hh + 48] = gaw[:, dr_, :, 48 * hh:48 * hh + 48]
            ab[:, 64 * (hh % 2):64 * (hh % 2) + 48, dr_, hh // 2] = gab[:, dr_, 48 * hh:48 * hh + 48]
    shared["gla_aw"] = aw
    shared["gla_ab"] = ab
    shared["gla_gon"] = f(f(inputs["gla_o_norm"])[:, :, None])
    shared.update(_consts())
    per_core = []
    for b in range(8):
        m = dict(shared)
        m["x"] = f(x[b].T.reshape(8, 128, T).transpose(1, 0, 2))
        m["ctx"] = f(ctx[b].T.reshape(8, 128, NCTX).transpose(1, 0, 2))
        ccv = np.stack([c[b], c_ctx], axis=-1)
        m["cc"] = f(ccv.reshape(8, 128, 2).transpose(1, 0, 2))
        per_core.append(m)
    return per_core


_DEBUG = tuple(x for x in os.environ.get("KDEBUG", "").split(",") if x)
LAST = {}


def kernel(**inputs):
    in_maps = _host_prep(inputs)
    K = build_program(debug=_DEBUG)
    ncores = int(os.environ.get("KCORES", "8"))
    res = run_bass_kernel_spmd(K.nc, in_maps[:ncores], core_ids=list(range(ncores)))
    LAST["res"] = res
    outs = []
    for r in res.results:
        o = np.asarray(r["out"])
        outs.append(o.transpose(1, 0, 2).reshape(D, T).T)
    return np.ascontiguousarray(np.stack(outs, axis=0)).astype(np.float32)
```
